# Optimizing a Trainium2 kernel written in Bass

```python
import jax, jax.numpy as jnp
from jax import lax
import numpy as np

D_MODEL = 1024
BATCH = 8
SEQ = 4096
DEPTH = 2

POOL_GROUPS = 4
POOL_WINDOWS = (2, 4, 8, 16)
POOL_WIDTH = 3 * D_MODEL // 8
POOL_GROUP_DIM = POOL_WIDTH // POOL_GROUPS
CONV_WIDTH = 3 * D_MODEL // 8
HEAD_DIM = 64
DIL_CONFIGS = ((128, 1), (512, 4), (2048, 16))
HEADS_PER_GROUP = 4
ATTN_HEADS = HEADS_PER_GROUP * len(DIL_CONFIGS)
ATTN_WIDTH = ATTN_HEADS * HEAD_DIM
ATTN_OUT_WIDTH = HEADS_PER_GROUP * HEAD_DIM
ROT_DIM = HEAD_DIM // 4
ROPE_THETA = 500000.0
QUERY_BLOCK = 128
N_BRANCHES = 3
IN_SPLITS = (POOL_WIDTH, CONV_WIDTH, CONV_WIDTH, CONV_WIDTH, ATTN_WIDTH, ATTN_WIDTH, ATTN_WIDTH, N_BRANCHES * D_MODEL)
IN_WIDTH = POOL_WIDTH + 3 * CONV_WIDTH + 3 * ATTN_WIDTH + N_BRANCHES * D_MODEL
MEM_LEN = 256
MEM_HEADS = 4
MEM_HEAD_DIM = D_MODEL // 8
MEM_WIDTH = MEM_HEADS * MEM_HEAD_DIM
D_FF = ((8 * D_MODEL // 3 + 127) // 128) * 128
CONV_K = 3
RMS_EPS = 1e-6

kernel_name = "hybrid_gated_pool_conv_dilattn_block"


def rmsnorm(x, g):
    xf = x.astype(jnp.float32)
    y = xf * lax.rsqrt(jnp.mean(xf * xf, axis=-1, keepdims=True) + RMS_EPS)
    return (y * g.astype(jnp.float32)).astype(x.dtype)


def split_cols(u, sizes):
    outs, start = [], 0
    for s in sizes:
        outs.append(u[..., start:start + s])
        start += s
    return outs


def causal_dwconv(u, w):
    k, c = w.shape
    return lax.conv_general_dilated(
        u, w[:, None, :].astype(u.dtype), window_strides=(1,), padding=((k - 1, 0),),
        dimension_numbers=("NWC", "WIO", "NWC"), feature_group_count=c)


def rope_tables(positions):
    inv = ROPE_THETA ** (-jnp.arange(0, ROT_DIM, 2, dtype=jnp.float32) / ROT_DIM)
    ang = positions.astype(jnp.float32)[..., None] * inv
    return jnp.cos(ang)[:, :, None, :], jnp.sin(ang)[:, :, None, :]


def apply_partial_rope(u, cos, sin):
    half = ROT_DIM // 2
    uf = u[..., :ROT_DIM].astype(jnp.float32)
    u1, u2 = uf[..., :half], uf[..., half:]
    rot = jnp.concatenate([u1 * cos - u2 * sin, u2 * cos + u1 * sin], axis=-1).astype(u.dtype)
    return jnp.concatenate([rot, u[..., ROT_DIM:]], axis=-1)


def multiscale_pool(u, pool_w, pool_scale):
    b, s, _ = u.shape
    ug = u.reshape(b, s, POOL_GROUPS, POOL_GROUP_DIM).astype(jnp.float32)
    cs = jnp.cumsum(ug, axis=1)
    t = jnp.arange(s)
    outs = []
    for g, w in enumerate(POOL_WINDOWS):
        c = cs[:, :, g]
        prev = jnp.pad(c, ((0, 0), (w, 0), (0, 0)))[:, :s]
        cnt = jnp.minimum(t + 1, w).astype(jnp.float32)[None, :, None]
        outs.append((c - prev) / cnt - ug[:, :, g])
    pooled = jnp.stack(outs, axis=2)
    mixed = jnp.einsum("bsgc,gcd->bsgd", pooled, pool_w.astype(jnp.float32))
    return (mixed.reshape(b, s, POOL_WIDTH) * pool_scale.astype(jnp.float32)).astype(u.dtype)


def dilated_attention(q, k, v):
    b, s, _, hd = q.shape
    ng = len(DIL_CONFIGS)
    def grp(u):
        return u.reshape(b, s, ng, HEADS_PER_GROUP, hd).transpose(2, 0, 3, 1, 4)
    qg, kg, vg = grp(q), grp(k), grp(v)
    scale = hd ** -0.5

    def block(bi):
        start = bi * QUERY_BLOCK
        t = start + jnp.arange(QUERY_BLOCK)
        qb = lax.dynamic_slice_in_dim(qg, start, QUERY_BLOCK, axis=3)
        outs, lses = [], []
        for g, (window, dil) in enumerate(DIL_CONFIGS):
            offs = jnp.arange(window // dil + 1) * dil
            idx = t[:, None] - offs[None, :]
            valid = idx >= 0
            idx = jnp.maximum(idx, 0)
            kk = jnp.take(kg[g], idx, axis=2)
            vv = jnp.take(vg[g], idx, axis=2)
            sc = jnp.einsum("bhqd,bhqkd->bhqk", qb[g], kk).astype(jnp.float32) * scale
            sc = jnp.where(valid, sc, -jnp.inf)
            m = jnp.max(sc, axis=-1, keepdims=True)
            p = jnp.exp(sc - m)
            den = jnp.sum(p, axis=-1, keepdims=True)
            o = jnp.einsum("bhqk,bhqkd->bhqd", p, vv.astype(jnp.float32)) / den
            outs.append(o)
            lses.append(m + jnp.log(den))
        wts = jax.nn.softmax(jnp.stack(lses, axis=0), axis=0)
        return jnp.sum(wts * jnp.stack(outs, axis=0), axis=0).astype(q.dtype)

    out = lax.map(block, jnp.arange(s // QUERY_BLOCK))
    return out.transpose(1, 0, 3, 2, 4).reshape(b, s, ATTN_OUT_WIDTH)


def memory_cross_attention(h, mem_n, w_q, w_kv, w_o):
    b, s, _ = h.shape
    q = (h @ w_q).reshape(b, s, MEM_HEADS, MEM_HEAD_DIM)
    kv = mem_n @ w_kv
    k = kv[..., :MEM_WIDTH].reshape(b, -1, MEM_HEADS, MEM_HEAD_DIM)
    v = kv[..., MEM_WIDTH:].reshape(b, -1, MEM_HEADS, MEM_HEAD_DIM)
    sc = jnp.einsum("bshd,bmhd->bhsm", q, k).astype(jnp.float32) * (MEM_HEAD_DIM ** -0.5)
    p = jax.nn.softmax(sc, axis=-1)
    o = jnp.einsum("bhsm,bmhd->bshd", p, v.astype(jnp.float32)).astype(h.dtype)
    return o.reshape(b, s, MEM_WIDTH) @ w_o


def setup_inputs(seed: int = 0) -> dict:
    key = jax.random.key(seed)
    ks = jax.random.split(key, 32)
    f32 = jnp.float32

    def nrm(k, shape, fan_in):
        return jax.random.normal(k, shape, f32) * (fan_in ** -0.5)

    def gain(k, n):
        return 1.0 + 0.05 * jax.random.normal(k, (DEPTH, n), f32)

    return {
        "x": jax.random.normal(ks[0], (BATCH, SEQ, D_MODEL), f32),
        "mem": jax.random.normal(ks[1], (BATCH, MEM_LEN, D_MODEL), f32),
        "positions": jnp.broadcast_to(jnp.arange(SEQ, dtype=jnp.int32), (BATCH, SEQ)),
        "norm_mix_pre": gain(ks[2], D_MODEL),
        "norm_mix_post": gain(ks[3], D_MODEL),
        "w_in": nrm(ks[4], (DEPTH, D_MODEL, IN_WIDTH), D_MODEL),
        "pool_w": nrm(ks[5], (DEPTH, POOL_GROUPS, POOL_GROUP_DIM, POOL_GROUP_DIM), POOL_GROUP_DIM),
        "pool_scale": 1.0 + 0.1 * jax.random.normal(ks[6], (DEPTH, POOL_WIDTH), f32),
        "conv_b_w": nrm(ks[7], (DEPTH, CONV_K, CONV_WIDTH), CONV_K),
        "w_branch_a": nrm(ks[8], (DEPTH, POOL_WIDTH, D_MODEL), POOL_WIDTH),
        "w_branch_b": nrm(ks[9], (DEPTH, CONV_WIDTH, D_MODEL), CONV_WIDTH),
        "w_branch_c": nrm(ks[10], (DEPTH, ATTN_OUT_WIDTH, D_MODEL), ATTN_OUT_WIDTH),
        "w_out": nrm(ks[11], (DEPTH, D_MODEL, D_MODEL), D_MODEL),
        "norm_mem_pre": gain(ks[12], D_MODEL),
        "norm_mem_post": gain(ks[13], D_MODEL),
        "norm_memkv": gain(ks[14], D_MODEL),
        "w_mq": nrm(ks[15], (DEPTH, D_MODEL, MEM_WIDTH), D_MODEL),
        "w_mkv": nrm(ks[16], (DEPTH, D_MODEL, 2 * MEM_WIDTH), D_MODEL),
        "w_mo": nrm(ks[17], (DEPTH, MEM_WIDTH, D_MODEL), MEM_WIDTH),
        "norm_ffn_pre": gain(ks[18], D_MODEL),
        "norm_ffn_post": gain(ks[19], D_MODEL),
        "w_up": nrm(ks[20], (DEPTH, D_MODEL, 2 * D_FF), D_MODEL),
        "conv_ffn_w": nrm(ks[21], (DEPTH, CONV_K, D_FF), CONV_K),
        "w_down": nrm(ks[22], (DEPTH, D_FF, D_MODEL), D_FF),
    }


def reference(x, mem, positions, norm_mix_pre, norm_mix_post, w_in, pool_w, pool_scale, conv_b_w,
              w_branch_a, w_branch_b, w_branch_c, w_out, norm_mem_pre, norm_mem_post, norm_memkv,
              w_mq, w_mkv, w_mo, norm_ffn_pre, norm_ffn_post, w_up, conv_ffn_w, w_down):
    b, s, d = x.shape
    cos, sin = rope_tables(positions)
    cos, sin = cos.astype(x.dtype), sin.astype(x.dtype)
    for l in range(DEPTH):
        h = rmsnorm(x, norm_mix_pre[l])
        a_in, b_x, b_b, b_c, q, k, v, gate_in = split_cols(h @ w_in[l], IN_SPLITS)
        br_a = multiscale_pool(a_in, pool_w[l], pool_scale[l]) @ w_branch_a[l]
        br_b = (b_b * causal_dwconv(b_c * b_x, conv_b_w[l])) @ w_branch_b[l]
        q = apply_partial_rope(q.reshape(b, s, ATTN_HEADS, HEAD_DIM), cos, sin)
        k = apply_partial_rope(k.reshape(b, s, ATTN_HEADS, HEAD_DIM), cos, sin)
        v = v.reshape(b, s, ATTN_HEADS, HEAD_DIM)
        br_c = dilated_attention(q, k, v) @ w_branch_c[l]
        gates = jax.nn.sigmoid(gate_in.astype(jnp.float32)).astype(x.dtype).reshape(b, s, N_BRANCHES, d)
        merged = gates[:, :, 0] * br_a + gates[:, :, 1] * br_b + gates[:, :, 2] * br_c
        x = x + rmsnorm(merged @ w_out[l], norm_mix_post[l])
        h = rmsnorm(x, norm_mem_pre[l])
        mem_n = rmsnorm(mem, norm_memkv[l])
        x = x + rmsnorm(memory_cross_attention(h, mem_n, w_mq[l], w_mkv[l], w_mo[l]), norm_mem_post[l])
        h = rmsnorm(x, norm_ffn_pre[l])
        u = h @ w_up[l]
        ua, ub = u[..., :D_FF], u[..., D_FF:]
        y = (jax.nn.silu(causal_dwconv(ua, conv_ffn_w[l])) * ub) @ w_down[l]
        x = x + rmsnorm(y, norm_ffn_post[l])
    return x
```

```python
import numpy as np
import concourse.bass as bass
import concourse.mybir as mybir

F32 = mybir.dt.float32
BF16 = mybir.dt.bfloat16
I32 = mybir.dt.int32
ALU = mybir.AluOpType
AF = mybir.ActivationFunctionType

EPOCH = 30000


class Sched:
    ENGS = ("pe", "act", "dve", "pool", "sp")

    def __init__(self, nc):
        self.nc = nc
        self.stream = {e: [] for e in self.ENGS}
        self.cnt = {e: 0 for e in self.ENGS}
        self.known = {e: {} for e in self.ENGS}
        self.res = {}
        self.chan_cnt = {}
        self.n_wait = 0

    def _collect(self, eng, reads, writes, waw):
        waits = {}

        def need(k, v):
            if k == ("eng", "pe") and eng == "pe":
                return
            if self.known[eng].get(k, 0) >= v:
                return
            if waits.get(k, 0) < v:
                waits[k] = v

        for r in reads:
            st = self.res.get(r)
            if st:
                for k, v in st["w"].items():
                    need(k, v)
        for w in writes:
            st = self.res.get(w)
            if st:
                for k, v in st["r"].items():
                    need(k, v)
                if waw:
                    for k, v in st["w"].items():
                        need(k, v)
        for k, v in waits.items():
            self.known[eng][k] = v
        return sorted(waits.items(), key=lambda kv: str(kv[0]))

    def _commit(self, ev, reads, writes, waw):
        k, v = ev
        for r in reads:
            st = self.res.setdefault(r, {"w": {}, "r": {}})
            if st["r"].get(k, 0) < v:
                st["r"][k] = v
        for w in writes:
            st = self.res.setdefault(w, {"w": {}, "r": {}})
            if st["r"] or waw:
                st["w"] = {}
            st["r"] = {}
            if st["w"].get(k, 0) < v:
                st["w"][k] = v

    def op(self, eng, fn, reads=(), writes=(), waw=True):
        waits = self._collect(eng, reads, writes, waw)
        self.cnt[eng] += 1
        ev = (("eng", eng), self.cnt[eng])
        self._commit(ev, reads, writes, waw)
        self.stream[eng].append((waits, fn, ev))
        self.n_wait += len(waits)

    def dma(self, eng, fn, chan, reads=(), writes=(), waw=False):
        waits = self._collect(eng, reads, writes, waw)
        self.chan_cnt[chan] = self.chan_cnt.get(chan, 0) + 16
        ev = (("chan", chan), self.chan_cnt[chan])
        self._commit(ev, reads, writes, waw)
        self.stream[eng].append((waits, fn, ev))
        self.n_wait += len(waits)

    def barrier(self, engs=None):
        engs = engs or self.ENGS
        for e in engs:
            waits = {}
            for e2 in self.ENGS:
                if e2 != e and self.cnt[e2] > 0:
                    k = ("eng", e2)
                    if self.known[e].get(k, 0) < self.cnt[e2]:
                        waits[k] = self.cnt[e2]
            for c, v in self.chan_cnt.items():
                k = ("chan", c)
                if self.known[e].get(k, 0) < v:
                    waits[k] = v
            for k, v in waits.items():
                self.known[e][k] = v
            if waits:
                self.stream[e].append((sorted(waits.items(), key=lambda kv: str(kv[0])), None, None))

    def emit(self):
        nc = self.nc
        sems = {}

        def sem_of(k, v):
            if k[0] == "eng":
                ep = (v - 1) // EPOCH
                key = (k, ep)
                val = (v - 1) % EPOCH + 1
            else:
                key = (k, 0)
                val = v
            if key not in sems:
                sems[key] = nc.alloc_semaphore(name=f"s{len(sems)}")
            return sems[key], val

        self.barrier(engs=("sp",))
        for e in self.ENGS:
            for waits, fn, ev in self.stream[e]:
                for k, v in waits:
                    sem_of(k, v)
                if ev is not None:
                    sem_of(*ev)
        eng_map = {"pe": "tensor", "act": "scalar", "dve": "vector", "pool": "gpsimd", "sp": "sync"}
        with nc.Block() as block:
            for e in self.ENGS:
                if not self.stream[e]:
                    continue
                deco = getattr(block, eng_map[e])

                def body(h, e=e):
                    for waits, fn, ev in self.stream[e]:
                        for k, v in waits:
                            s, val = sem_of(k, v)
                            h.wait_ge(s, val)
                        if fn is None:
                            continue
                        ins = fn(h)
                        s, _ = sem_of(*ev)
                        ins.then_inc(s, 16 if ev[0][0] == "chan" else 1)

                deco(body)
        return len(sems)


def sap(t, F, poff, npart, off, dims):
    return bass.AP(t, poff * F + off, [[F, npart]] + [list(d) for d in dims])

from concourse.bass_utils import run_bass_kernel_spmd
from contextlib import ExitStack

D = 1024
SEQ = 4096
T = 512
NT = SEQ // T
KC = 8
DFF = 2816
FC = DFF // 128
NL = 2
EPS = 1e-6
DILS = (1, 4, 16)
OFF_A = 0
OFF_BX = 384
OFF_BB = 768
OFF_BC = 1152
OFF_GA = 1536
OFF_GB = 2560
OFF_GC = 3584
OFF_QKV = 4608
NAB = 3584
V_MIXPRE, V_MIXPOST, V_MEMPRE, V_MEMPOST, V_MEMKV, V_FFNPRE, V_FFNPOST = 0, 8, 16, 24, 32, 40, 48
V_CONVB = 56
V_CONVF = 65
V_PSCALE = 131
V_PER_LAYER = 135
C_ID = 0
C_MASK = 128
C_INV = 384
C_RC = 392
NCONST = 456


def w_in_perm():
    idx = []
    a = 0
    idx += list(range(0, 384))
    idx += list(range(384, 384 + 1152))
    g0 = 384 + 1152 + 3 * 768
    idx += list(range(g0, g0 + 3072))
    q0 = 384 + 1152
    for g in range(3):
        for part in range(3):
            s = q0 + part * 768 + g * 256
            idx += list(range(s, s + 256))
    return np.array(idx, dtype=np.int64)


def build_program(n_layers=NL, dbg=False, opts=()):
    nc = bass.Bass("TRN2", target_bir_lowering=False)
    S = Sched(nc)

    def din(name, shape, dt=F32):
        return nc.dram_tensor(name, list(shape), dt, kind="ExternalInput").ap()

    kind_s = "ExternalOutput" if dbg else "Internal"

    def dscr(name, shape, dt):
        return nc.dram_tensor(name, list(shape), dt, kind=kind_s).ap()

    xT = din("xT", [D, SEQ])
    memT = din("memT", [D, 256])
    pos = din("pos", [32, 128], I32)
    w_in = din("w_in", [NL, D, OFF_QKV])
    w_qkv = din("w_qkv", [NL, 3, D, 1280])
    pool_w = din("pool_w", [NL, 4, 96, 96])
    w_a = din("w_branch_a", [NL, 384, D])
    w_b = din("w_branch_b", [NL, 384, D])
    w_c = din("w_branch_c", [NL, 256, D])
    w_out = din("w_out", [NL, D, D])
    w_mq = din("w_mq", [NL, D, 512])
    w_mkv = din("w_mkv", [NL, D, 1024])
    w_mo = din("w_mo", [NL, 512, D])
    w_up = din("w_up", [NL, D, 2 * DFF])
    w_down = din("w_down", [NL, DFF, D])
    vecs_d = din("vecs", [128, NL * V_PER_LAYER])
    consts_d = din("consts", [128, NCONST])
    outT = nc.dram_tensor("outT", [D, SEQ], F32, kind="ExternalOutput").ap()

    h1T = dscr("h1T", [D, SEQ], BF16)
    h2T = dscr("h2T", [D, SEQ], BF16)
    h3T = dscr("h3T", [D, SEQ], BF16)
    attnT = dscr("attnT", [256, SEQ], BF16)
    mab = dscr("mab", [D, SEQ], F32)
    xs = dscr("xs", [D, SEQ], F32)
    actT = dscr("actT", [DFF, SEQ], BF16)
    rope_d = dscr("rope_tab", [SEQ, 16], F32)

    es = ExitStack()

    def sb(name, shape, dt):
        return es.enter_context(nc.sbuf_tensor("s_" + name, list(shape), dt))

    ps = es.enter_context(nc.psum_tensor("ps", [128, 4096], F32))

    def MM(out, lhsT, rhs, start, stop, R, W):
        S.op("pe", lambda h: h.matmul(out, lhsT=lhsT, rhs=rhs, start=start, stop=stop), reads=R, writes=W)

    def ACT(out, in_, func, R, W, scale=1.0, bias=None, waw=True):
        if bias is None:
            S.op("act", lambda h: h.activation(out=out, in_=in_, func=func, scale=scale), reads=R, writes=W, waw=waw)
        else:
            S.op("act", lambda h: h.activation(out=out, in_=in_, func=func, scale=scale, bias=bias), reads=R, writes=W, waw=waw)

    def TT(eng, out, in0, in1, op, R, W, waw=True):
        S.op(eng, lambda h: h.tensor_tensor(out=out, in0=in0, in1=in1, op=op), reads=R, writes=W, waw=waw)

    def TS(eng, out, in0, s1, s2, op0, op1, R, W, waw=True):
        if s2 is None:
            S.op(eng, lambda h: h.tensor_scalar(out=out, in0=in0, scalar1=s1, scalar2=None, op0=op0), reads=R, writes=W, waw=waw)
        else:
            S.op(eng, lambda h: h.tensor_scalar(out=out, in0=in0, scalar1=s1, scalar2=s2, op0=op0, op1=op1), reads=R, writes=W, waw=waw)

    def STT(eng, out, in0, scalar, in1, op0, op1, R, W, waw=True):
        S.op(eng, lambda h: h.scalar_tensor_tensor(out=out, in0=in0, scalar=scalar, in1=in1, op0=op0, op1=op1), reads=R, writes=W, waw=waw)

    def CP(eng, out, in_, R, W, waw=True):
        if eng == "act":
            ACT(out, in_, AF.Copy, R, W, waw=waw)
        else:
            S.op(eng, lambda h: h.tensor_copy(out=out, in_=in_), reads=R, writes=W, waw=waw)

    def MS(eng, ap, val, W):
        S.op(eng, lambda h: h.memset(ap, val), writes=W)

    def DMA(eng, out, in_, chan, R, W):
        S.dma(eng, lambda h: h.dma_start(out=out, in_=in_), chan, reads=R, writes=W)

    uniq = [0]

    class Ring:
        def __init__(self, name, shape, dt, n, alloc=None):
            alloc = alloc or sb
            uniq[0] += 1
            self.name = name
            self.t = [alloc(f"{name}_{uniq[0]}_{i}", shape, dt) for i in range(n)]
            self.i = 0

        def next(self):
            k = self.i % len(self.t)
            self.i += 1
            return self.t[k], (self.name, k)

    psp = [0]

    def bank(n=1):
        p = psp[0] % 7
        if n == 2:
            if p % 2 == 1:
                psp[0] += 1
                p = psp[0] % 7
            if p == 6:
                psp[0] += 1
                p = 0
        psp[0] += n
        return p

    def PB(b, lo=0, hi=512):
        return ps[:, b * 512 + lo: b * 512 + hi]

    vecs = sb("vecs", [128, NL * V_PER_LAYER], F32)
    consts = sb("consts", [128, NCONST], F32)
    ident = sb("ident", [128, 128], BF16)
    mask2 = sb("mask2", [128, 256], BF16)
    onesm = sb("onesm", [128, 128], BF16)
    ones = sb("ones", [128, 128], BF16)
    onesP = sb("onesP", [128, 2, 128], BF16)
    DMA("sp", vecs[:], vecs_d, "vecs", [], ["vecs"])
    DMA("sp", consts[:], consts_d, "consts", [], ["consts"])
    CP("dve", ident[:], consts[:, C_ID:C_ID + 128], ["consts"], ["cb"])
    CP("dve", mask2[:], consts[:, C_MASK:C_MASK + 256], ["consts"], ["cb"])
    MS("pool", onesm[:], 1.0 / 1024.0, ["cb"])
    MS("pool", ones[:], 1.0, ["cb"])
    MS("pool", onesP[:], 0.0, ["cb"])
    MS("pool", onesP[:, 0, 0:64], 1.0, ["cb"])
    MS("pool", onesP[:, 1, 64:128], 1.0, ["cb"])

    sqring = Ring("sq", [128, T], BF16, 3)
    rstdring = Ring("rstd", [128, T], F32, 2)
    G = {}

    def mk_rings(alloc, x=False, h=False, hin=False, ysb=False):
        if x:
            G["x"] = Ring("xt", [128, KC, T], F32, 2, alloc)
        if h:
            G["h"] = Ring("ht", [128, KC, T], BF16, 2, alloc)
        if hin:
            G["hin"] = Ring("hin", [128, KC, T], BF16, 2, alloc)
        if ysb:
            G["ysb"] = Ring("ysb", [128, KC, T], F32, 1, alloc)
    tmp_ring = Ring("tmpf", [128, T], F32, 3)

    def xview(d, tt):
        return d.rearrange("(c p) t -> p c t", p=128)[:, :, tt * T:(tt + 1) * T]

    def rstd_from_bank(b):
        rstd, rk = rstdring.next()
        ACT(rstd[:], PB(b), AF.Sqrt, [("ps", b)], [rk], bias=EPS)
        S.op("dve", lambda h: h.reciprocal(out=rstd[:], in_=rstd[:]), reads=[rk], writes=[rk])
        return rstd, rk

    def norm_tile(xt, xk, gcol, ht, hk, ncols=T):
        b = bank()
        for c in range(KC):
            sq, sqk = sqring.next()
            ACT(sq[:, 0:ncols], xt[:, c, 0:ncols], AF.Square, [xk], [sqk])
            MM(PB(b, 0, ncols), onesm[:], sq[:, 0:ncols], c == 0, c == KC - 1, [sqk, "cb"], [("ps", b)])
        rstd, rk = rstdring.next()
        ACT(rstd[:, 0:ncols], PB(b, 0, ncols), AF.Sqrt, [("ps", b)], [rk], bias=EPS)
        S.op("dve", lambda h: h.reciprocal(out=rstd[:, 0:ncols], in_=rstd[:, 0:ncols]), reads=[rk], writes=[rk])
        for c in range(KC):
            eng = "dve"
            STT(eng, ht[:, c, 0:ncols], xt[:, c, 0:ncols], vecs[:, gcol + c:gcol + c + 1], rstd[:, 0:ncols],
                ALU.mult, ALU.mult, [xk, rk, "vecs"], [hk], waw=False)

    class PostNorm:
        def __init__(self):
            self.ysb, self.yk = G["ysb"].next()
            self.b = 7

        def chunk(self, mc, b):
            CP("act", self.ysb[:, mc, :], PB(b), [("ps", b)], [self.yk], waw=False)
            sq, sqk = sqring.next()
            TT("dve", sq[:], PB(b), self.ysb[:, mc, :], ALU.mult, [("ps", b), self.yk], [sqk])
            MM(PB(self.b), onesm[:], sq[:], mc == 0, mc == KC - 1, [sqk, "cb"], [("ps", self.b)])

        def finish(self, xt, xk, gcol):
            rstd, rk = rstd_from_bank(self.b)
            for c in range(KC):
                tmp, tk = tmp_ring.next()
                TT("dve", tmp[:], self.ysb[:, c, :], rstd[:], ALU.mult, [self.yk, rk], [tk])
                STT("dve", xt[:, c, :], tmp[:], vecs[:, gcol + c:gcol + c + 1], xt[:, c, :], ALU.mult, ALU.add,
                    [tk, xk, "vecs"], [xk], waw=False)

    def load_w(dst, src2d, ncols, chan, key, rows=128):
        v = src2d.rearrange("(k p) n -> p k n", p=rows)
        for cb in range(0, ncols, 2048):
            w = min(2048, ncols - cb)
            DMA("pool", dst[:, :, cb:cb + w], v[:, :, cb:cb + w], chan, [], [key])

    with ExitStack() as es0:
      if 'norope' not in opts:
          def sb0(name, shape, dt):
              return es0.enter_context(nc.sbuf_tensor(name, list(shape), dt))
          pi_ = sb0("rp_pi", [32, 128], I32)
          pf = sb0("rp_pf", [32, 128], F32)
          ang = sb0("rp_ang", [32, 128, 16], F32)
          a2 = sb0("rp_a2", [32, 128, 16], F32)
          ki = sb0("rp_ki", [32, 128, 16], I32)
          tab = sb0("rp_tab", [32, 128, 16], F32)
          DMA("sp", pi_[:], pos, "rp", [], ["rp_pi"])
          CP("dve", pf[:], pi_[:], ["rp_pi"], ["rp_pf"])
          inv_b = consts[0:32, C_INV:C_INV + 8].unsqueeze(1).to_broadcast([32, 128, 8])
          pf_b = pf[:].unsqueeze(2).to_broadcast([32, 128, 8])
          TT("dve", ang[:, :, 0:8], pf_b, inv_b, ALU.mult, ["rp_pf", "consts"], ["rp_ang"])
          TS("dve", ang[:, :, 8:16], ang[:, :, 0:8], float(np.pi / 2), None, ALU.add, None, ["rp_ang"], ["rp_ang"])
          TS("dve", a2[:], ang[:], float(1.0 / (2 * np.pi)), None, ALU.mult, None, ["rp_ang"], ["rp_a2"])
          CP("dve", ki[:], a2[:], ["rp_a2"], ["rp_ki"])
          CP("dve", a2[:], ki[:], ["rp_ki"], ["rp_a2"])
          STT("dve", ang[:], a2[:], float(-2 * np.pi), ang[:], ALU.mult, ALU.add, ["rp_a2", "rp_ang"], ["rp_ang"])
          TS("dve", a2[:], ang[:], float(np.pi), float(-2 * np.pi), ALU.is_gt, ALU.mult, ["rp_ang"], ["rp_a2"])
          TT("dve", ang[:], ang[:], a2[:], ALU.add, ["rp_ang", "rp_a2"], ["rp_ang"])
          TS("dve", a2[:], ang[:], float(-np.pi), float(2 * np.pi), ALU.is_lt, ALU.mult, ["rp_ang"], ["rp_a2"])
          TT("dve", ang[:], ang[:], a2[:], ALU.add, ["rp_ang", "rp_a2"], ["rp_ang"])
          ACT(tab[:, :, 0:8], ang[:, :, 8:16], AF.Sin, ["rp_ang"], ["rp_tab"])
          ACT(tab[:, :, 8:16], ang[:, :, 0:8], AF.Sin, ["rp_ang"], ["rp_tab"], waw=False)
          DMA("sp", rope_d.rearrange("(b i) f -> b i f", i=128), tab[:], "rp", ["rp_tab"], ["rope_d"])
          S.barrier()

    def phase_norm0(l):
      with ExitStack() as e1:
        mk_rings(lambda n, sh, dt: e1.enter_context(nc.sbuf_tensor(n, list(sh), dt)), x=True, h=True)
        for tt in range(NT):
            xt, xk = G["x"].next()
            DMA("sp", xt[:], xview(xT, tt), xk, [], [xk])
            ht, hk = G["h"].next()
            norm_tile(xt, xk, l * V_PER_LAYER + V_MIXPRE, ht, hk)
            DMA("sp", xview(h1T, tt), ht[:], hk, [hk], [("h1T", tt)])
        S.barrier()

    def phase_attn(l):
        with ExitStack() as e2:
            def sb2(name, shape, dt):
                return e2.enter_context(nc.sbuf_tensor(f"{name}_L{l}", list(shape), dt))
            hT = sb2("at_hT", [128, KC, SEQ], BF16)
            acc = sb2("at_acc", [128, 4, 2048], F32)
            attn_sb = sb2("at_out", [128, 2, 2048], BF16)
            wq = [sb2(f"at_w{i}", [128, KC, 1280], BF16) for i in range(2)]
            cs = [sb2(f"at_cs{g}", [128, 32, 16], F32) for g in range(3)]
            QK = [sb2(f"at_qk{i}", [128, 768], BF16) for i in range(2)]
            QF = [sb2(f"at_qf{i}", [128, 768], F32) for i in range(2)]
            ODT = [sb2(f"at_od{i}", [128, 512], F32) for i in range(2)]
            Vb = [sb2(f"at_v{i}", [128, 4, 128], BF16) for i in range(2)]
            QT = [sb2(f"at_qt{i}", [128, 2, 128], BF16) for i in range(2)]
            KTp = [sb2(f"at_kt{i}", [128, 4, 128], BF16) for i in range(2)]
            PT = [sb2(f"at_pt{i}", [128, 4, 256], BF16) for i in range(2)]
            rt = [sb2(f"at_rt{i}", [128, 4, 8, 8], F32) for i in range(2)]
            for tt in range(NT):
                DMA("sp", hT[:, :, tt * T:(tt + 1) * T], xview(h1T, tt), "at_hT", [("h1T", tt)], ["at_hT"])
            for g in range(3):
                d = DILS[g]
                nb = 32 // d
                for r in range(d):
                    for j in range(nb):
                        src = bass.AP(rope_d.tensor, (r + d * 128 * j) * 16, [[16 * d, 128], [1, 16]])
                        DMA("sp", cs[g][:, r * nb + j, :], src, f"at_cs{g}", ["rope_d"], [("cs", g)])
            wl = [0]
            cnt = {"qk": 0, "pt": 0, "rt": 0, "od": 0}

            def proj_block(g, r, j, need_q, wt, wk):
                d = DILS[g]
                nb = 32 // d
                slot = j % 2
                start = r + d * 128 * j
                sl = slice(start, start + 127 * d + 1, d)
                bq = bank() if need_q else None
                bk = bank()
                bv = bank()
                if need_q:
                    for kc in range(KC):
                        MM(PB(bq, 0, 256), hT[:, kc, sl], wt[:, kc, 0:256], kc == 0, kc == KC - 1, ["at_hT", wk], [("ps", bq)])
                for kc in range(KC):
                    MM(PB(bk), hT[:, kc, sl], wt[:, kc, 256:768], kc == 0, kc == KC - 1, ["at_hT", wk], [("ps", bk)])
                for kc in range(KC):
                    MM(PB(bv), hT[:, kc, sl], wt[:, kc, 768:1280], kc == 0, kc == KC - 1, ["at_hT", wk], [("ps", bv)])
                qi = cnt["qk"] % 2
                cnt["qk"] += 1
                qk, qkk = QK[qi], ("QK", qi)
                qf, qfk = QF[qi], ("QF", qi)
                if need_q:
                    CP("act", qf[:, 0:256], PB(bq, 0, 256), [("ps", bq)], [qfk])
                CP("act", qf[:, 256:768], PB(bk), [("ps", bk)], [qfk], waw=not need_q)
                lo = 0 if need_q else 256
                CP("pool", qk[:, lo:768], qf[:, lo:768], [qfk], [qkk])
                CP("dve", Vb[slot][:].rearrange("p a b -> p (a b)"), PB(bv), [("ps", bv)], [("Vb", slot)])
                if 'pb1' in opts:
                    return
                blk = r * nb + j
                csk = ("cs", g)
                views = []
                if need_q:
                    views.append((qf[:, 0:256].rearrange("p (h e) -> p h e", e=64), qk[:, 0:256].rearrange("p (h e) -> p h e", e=64), [128, 4, 8], 0))
                kfv = bass.AP(qf, 256, [[768, 128], [256, 2], [192, 2], [1, 64]])
                kbv = bass.AP(qk, 256, [[768, 128], [256, 2], [192, 2], [1, 64]])
                views.append((kfv, kbv, [128, 2, 2, 8], 1))
                for (fv, bv_, shp, which) in views:
                    ri = cnt["rt"] % 2
                    cnt["rt"] += 1
                    rtt, rtk = rt[ri], ("rt", ri)
                    if which == 0:
                        cosb = cs[g][:, blk, 0:8].unsqueeze(1).to_broadcast(shp)
                        sinb = cs[g][:, blk, 8:16].unsqueeze(1).to_broadcast(shp)
                        u1, u2 = fv[:, :, 0:8], fv[:, :, 8:16]
                        o1, o2 = bv_[:, :, 0:8], bv_[:, :, 8:16]
                        tv = [rtt[:, k, 0:4, :] for k in range(4)]
                    else:
                        cosb = cs[g][:, blk, 0:8].unsqueeze(1).unsqueeze(1).to_broadcast(shp)
                        sinb = cs[g][:, blk, 8:16].unsqueeze(1).unsqueeze(1).to_broadcast(shp)
                        u1, u2 = fv[:, :, :, 0:8], fv[:, :, :, 8:16]
                        o1, o2 = bv_[:, :, :, 0:8], bv_[:, :, :, 8:16]
                        tv = [rtt[:, k, 0:4, :].rearrange("p (a b) e -> p a b e", a=2) for k in range(4)]
                    TT("dve", tv[0], u1, cosb, ALU.mult, [qfk, csk], [rtk])
                    TT("dve", tv[1], u2, sinb, ALU.mult, [qfk, csk], [rtk], waw=False)
                    TT("dve", tv[2], u2, cosb, ALU.mult, [qfk, csk], [rtk], waw=False)
                    TT("dve", tv[3], u1, sinb, ALU.mult, [qfk, csk], [rtk], waw=False)
                    TT("pool", o1, tv[0], tv[1], ALU.subtract, [rtk], [qkk])
                    TT("pool", o2, tv[2], tv[3], ALU.add, [rtk], [qkk])
                if 'pb2' in opts:
                    return
                if need_q:
                    bt = bank()
                    for ch in range(2):
                        MM(PB(bt, ch * 128, ch * 128 + 128), qk[:, ch * 128:(ch + 1) * 128], ident[:], True, True, [qkk, "cb"], [("ps", bt)])
                    CP("act", QT[slot][:].rearrange("p c t -> p (c t)"), PB(bt, 0, 256), [("ps", bt)], [("QT", slot)])
                bt2 = bank()
                for hh in range(4):
                    MM(PB(bt2, hh * 128, hh * 128 + 128), qk[:, 256 + hh * 128:256 + (hh + 1) * 128], ident[:], True, True, [qkk, "cb"], [("ps", bt2)])
                CP("dve", KTp[slot][:].rearrange("p a b -> p (a b)"), PB(bt2), [("ps", bt2)], [("KTp", slot)])

            def attend(g, r, j, half, first_group):
                d = DILS[g]
                sc, sp_ = j % 2, (j - 1) % 2
                b2 = bank(2)
                for hh in range(4):
                    ch = hh // 2
                    bb = b2 + hh // 2
                    c0 = (hh % 2) * 256
                    if j > 0:
                        MM(PB(bb, c0, c0 + 256), ident[:], mask2[:, 0:256], True, False, ["cb"], [("ps", bb)])
                        MM(PB(bb, c0, c0 + 128), KTp[sp_][:, hh, :], QT[sc][:, ch, :], False, False, [("KTp", sp_), ("QT", sc)], [("ps", bb)])
                    else:
                        MM(PB(bb, c0 + 128, c0 + 256), ident[:], mask2[:, 128:256], True, False, ["cb"], [("ps", bb)])
                    MM(PB(bb, c0 + 128, c0 + 256), KTp[sc][:, hh, :], QT[sc][:, ch, :], False, True, [("KTp", sc), ("QT", sc)], [("ps", bb)])
                pi = cnt["pt"] % 2
                cnt["pt"] += 1
                pt, ptk = PT[pi], ("PT", pi)
                for hb in range(2):
                    bb = b2 + hb
                    if j > 0:
                        ACT(pt[:, 2 * hb:2 * hb + 2, :].rearrange("p a b -> p (a b)"), PB(bb), AF.Exp, [("ps", bb)], [ptk], scale=0.125, waw=False)
                    else:
                        for a in range(2):
                            ACT(pt[:, 2 * hb + a, 128:256], PB(bb, a * 256 + 128, a * 256 + 256), AF.Exp,
                                [("ps", bb)], [ptk], scale=0.125, waw=False)
                b3 = bank()
                kbs = [0, 1] if j > 0 else [1]
                for od in range(2):
                    for pair in range(2):
                        mats = [(hh, kb) for hh in (2 * pair, 2 * pair + 1) for kb in kbs]
                        c0 = od * 256 + pair * 128
                        for i, (hh, kb) in enumerate(mats):
                            vs = sp_ if kb == 0 else sc
                            lhs = Vb[vs][:, hh, :] if od == 0 else onesP[:, hh % 2, :]
                            rd = [ptk, ("Vb", vs)] if od == 0 else [ptk, "cb"]
                            MM(PB(b3, c0, c0 + 128), lhs, pt[:, hh, kb * 128:(kb + 1) * 128], i == 0, i == len(mats) - 1, rd, [("ps", b3)])
                off = r + d * 128 * j - 2048 * half
                av = bass.AP(acc, off, [[4 * 2048, 128], [2048, 4], [d, 128]])
                oi = cnt["od"] % 2
                cnt["od"] += 1
                odt, odk = ODT[oi], ("ODT", oi)
                CP("act", odt[:], PB(b3), [("ps", b3)], [odk])
                pv = odt[:].rearrange("p (a t) -> p a t", t=128)
                if first_group:
                    CP("pool", av, pv, [odk], ["acc"], waw=True)
                else:
                    TT("pool", av, pv, av, ALU.add, [odk, "acc"], ["acc"])

            for half in range(2):
                for g in range(3):
                    d = DILS[g]
                    nb = 32 // d
                    wi = wl[0] % 2
                    wl[0] += 1
                    wt, wk = wq[wi], ("at_w", wi)
                    load_w(wt, w_qkv[l, g], 1280, f"at_w{wi}", wk)
                    jl, jh = half * nb // 2, (half + 1) * nb // 2
                    for r in range(d):
                        if 'at_loads' in opts:
                            continue
                        if jl > 0:
                            proj_block(g, r, jl - 1, False, wt, wk)
                        for j in range(jl, jh):
                            proj_block(g, r, j, True, wt, wk)
                            if 'at_proj' not in opts:
                                attend(g, r, j, half, g == 0)
                for pair in range(2):
                    S.op("dve", lambda h, pair=pair: h.reciprocal(out=acc[:, 2 + pair, :], in_=acc[:, 2 + pair, :]), reads=["acc"], writes=["acc"])
                    TT("dve", attn_sb[:, pair, :], acc[:, pair, :], acc[:, 2 + pair, :], ALU.mult, ["acc"], ["attn_sb"])
                dv = attnT.rearrange("(c p) t -> p c t", p=128)[:, :, half * 2048:(half + 1) * 2048]
                DMA("sp", dv, attn_sb[:], "attn_sb", ["attn_sb"], [("attnT", half)])
            S.barrier()

    def phase_ab(l):
        vb = l * V_PER_LAYER
        with ExitStack() as e3:
            def sb3(name, shape, dt):
                return e3.enter_context(nc.sbuf_tensor(f"{name}_L{l}", list(shape), dt))
            mk_rings(sb3, hin=True)
            wab = sb3("ab_w", [128, KC, NAB], BF16)
            pw = sb3("ab_pw", [96, 4, 96], BF16)
            wa = sb3("ab_wa", [96, 4, D], BF16)
            wb = sb3("ab_wb", [128, 3, D], BF16)
            A = [sb3(f"ab_A{g}", [96, 16 + T], F32) for g in range(4)]
            T2 = sb3("ab_T2", [96, 16 + T], F32)
            T4 = sb3("ab_T4", [96, 16 + T], F32)
            T8 = sb3("ab_T8", [96, 16 + T], F32)
            PL = [sb3(f"ab_PL{g}", [96, T], BF16) for g in range(4)]
            MX = [sb3(f"ab_MX{g}", [96, T], BF16) for g in range(4)]
            BX = sb3("ab_BX", [128, T], F32)
            P = [sb3(f"ab_P{c}", [128, 2 + T], F32) for c in range(3)]
            Y1 = sb3("ab_Y1", [128, T], F32)
            Y2 = sb3("ab_Y2", [128, T], F32)
            Z = [sb3(f"ab_Z{c}", [128, T], BF16) for c in range(3)]
            SG = [sb3(f"ab_SG{i}", [128, T], F32) for i in range(2)]
            MO = [sb3(f"ab_MO{i}", [128, KC, T], F32) for i in range(2)]
            load_w(wab, w_in[l][:, 0:NAB], NAB, "ab_w", "ab_w")
            DMA("pool", pw[:], pool_w[l].rearrange("g c d -> c g d"), "ab_w2", [], ["ab_w2"])
            DMA("pool", wa[:], w_a[l].rearrange("(g c) n -> c g n", c=96), "ab_w2", [], ["ab_w2"])
            DMA("pool", wb[:], w_b[l].rearrange("(k p) n -> p k n", p=128), "ab_w2", [], ["ab_w2"])
            for g in range(4):
                MS("pool", A[g][:, 0:16], 0.0, [("A", g)])
            for c in range(3):
                MS("pool", P[c][:, 0:2], 0.0, [("P", c)])
            sgi = [0]
            for tt in range(NT):
                ht, hk = G["hin"].next()
                DMA("sp", ht[:], xview(h1T, tt), hk, [("h1T", tt)], [hk])
                mo, mok = MO[tt % 2], ("MO", tt % 2)
                for g in range(4):
                    w = (2, 4, 8, 16)[g]
                    Ak = ("A", g)
                    if tt > 0:
                        CP("pool", A[g][:, 0:16], A[g][:, T:T + 16], [Ak], [Ak])
                    b = bank()
                    for kc in range(KC):
                        MM(ps[0:96, b * 512:(b + 1) * 512], wab[:, kc, OFF_A + 96 * g:OFF_A + 96 * (g + 1)], ht[:, kc, :], kc == 0, kc == KC - 1,
                           [hk, "ab_w"], [("ps", b)])
                    CP("act", A[g][:, 16:16 + T], ps[0:96, b * 512:(b + 1) * 512], [("ps", b)], [Ak])
                    src = A[g]
                    srck = Ak
                    n = 1
                    for (dst, dk) in ((T2, "T2"), (T4, "T4"), (T8, "T8"), (None, None)):
                        if n >= w:
                            break
                        if 2 * n == w:
                            tmpw, tmpk = T2 if dst is not T2 and src is not T2 else (T4 if src is not T4 else T8), None
                            tmpw = {1: T2, 2: T4, 4: T8, 8: T2}[n]
                            tmpk = {1: "T2", 2: "T4", 4: "T8", 8: "T2"}[n]
                            TT("pool", tmpw[:, 16:16 + T], src[:, 16:16 + T], src[:, 16 - n:16 - n + T], ALU.add, [srck], [tmpk])
                            STT("dve", PL[g][:], tmpw[:, 16:16 + T], 1.0 / w, A[g][:, 16:16 + T], ALU.mult, ALU.subtract, [tmpk, Ak], [("PL", g)])
                            if tt == 0:
                                tmp, tk = tmp_ring.next()
                                TT("pool", tmp[0:96, 0:16], tmpw[:, 16:32], consts[0:96, C_RC + 16 * g:C_RC + 16 * g + 16], ALU.mult, [tmpk, "consts"], [tk])
                                TT("pool", PL[g][:, 0:16], tmp[0:96, 0:16], A[g][:, 16:32], ALU.subtract, [tk, Ak], [("PL", g)])
                            break
                        TT("pool", dst[:, 2 * n - 1:16 + T], src[:, 2 * n - 1:16 + T], src[:, n - 1:16 + T - n], ALU.add, [srck], [dk])
                        src, srck = dst, dk
                        n *= 2
                    b2 = bank()
                    MM(ps[0:96, b2 * 512:(b2 + 1) * 512], pw[:, g, :], PL[g][:], True, True, [("PL", g), "ab_w2"], [("ps", b2)])
                    ACT(MX[g][:], ps[0:96, b2 * 512:(b2 + 1) * 512], AF.Copy, [("ps", b2), "vecs"], [("MX", g)],
                        scale=vecs[0:96, vb + V_PSCALE + g:vb + V_PSCALE + g + 1])
                for mc in range(KC):
                    b = bank()
                    for g in range(4):
                        MM(PB(b), wa[:, g, mc * 128:(mc + 1) * 128], MX[g][:], g == 0, g == 3, [("MX", g), "ab_w2"], [("ps", b)])
                    bg = bank()
                    for kc in range(KC):
                        MM(PB(bg), wab[:, kc, OFF_GA + mc * 128:OFF_GA + (mc + 1) * 128], ht[:, kc, :], kc == 0, kc == KC - 1, [hk, "ab_w"], [("ps", bg)])
                    sg, sgk = SG[sgi[0] % 2], ("SG", sgi[0] % 2)
                    sgi[0] += 1
                    ACT(sg[:], PB(bg), AF.Sigmoid, [("ps", bg)], [sgk])
                    TT("dve", mo[:, mc, :], PB(b), sg[:], ALU.mult, [("ps", b), sgk], [mok], waw=False)
                for c in range(3):
                    Pk = ("P", c)
                    if tt > 0:
                        CP("pool", P[c][:, 0:2], P[c][:, T:T + 2], [Pk], [Pk])
                    bx, bb_, bc = bank(), bank(), bank()
                    for (bk, off) in ((bx, OFF_BX), (bb_, OFF_BB), (bc, OFF_BC)):
                        for kc in range(KC):
                            MM(PB(bk), wab[:, kc, off + c * 128:off + (c + 1) * 128], ht[:, kc, :], kc == 0, kc == KC - 1, [hk, "ab_w"], [("ps", bk)])
                    CP("act", BX[:], PB(bx), [("ps", bx)], ["BX"])
                    TT("dve", P[c][:, 2:2 + T], PB(bc), BX[:], ALU.mult, [("ps", bc), "BX"], [Pk])
                    cw = vb + V_CONVB
                    TS("pool", Y1[:], P[c][:, 2:2 + T], vecs[:, cw + 2 * 3 + c:cw + 2 * 3 + c + 1], None, ALU.mult, None, [Pk, "vecs"], ["Y1"])
                    STT("dve", Y2[:], P[c][:, 1:1 + T], vecs[:, cw + 1 * 3 + c:cw + 1 * 3 + c + 1], Y1[:], ALU.mult, ALU.add, [Pk, "Y1", "vecs"], ["Y2"])
                    STT("dve", Y1[:], P[c][:, 0:T], vecs[:, cw + 0 * 3 + c:cw + 0 * 3 + c + 1], Y2[:], ALU.mult, ALU.add, [Pk, "Y2", "vecs"], ["Y1"])
                    TT("dve", Z[c][:], PB(bb_), Y1[:], ALU.mult, [("ps", bb_), "Y1"], [("Z", c)])
                for mc in range(KC):
                    b = bank()
                    for c in range(3):
                        MM(PB(b), wb[:, c, mc * 128:(mc + 1) * 128], Z[c][:], c == 0, c == 2, [("Z", c), "ab_w2"], [("ps", b)])
                    bg = bank()
                    for kc in range(KC):
                        MM(PB(bg), wab[:, kc, OFF_GB + mc * 128:OFF_GB + (mc + 1) * 128], ht[:, kc, :], kc == 0, kc == KC - 1, [hk, "ab_w"], [("ps", bg)])
                    sg, sgk = SG[sgi[0] % 2], ("SG", sgi[0] % 2)
                    sgi[0] += 1
                    ACT(sg[:], PB(bg), AF.Sigmoid, [("ps", bg)], [sgk])
                    tmp, tk = tmp_ring.next()
                    TT("dve", tmp[:], PB(b), sg[:], ALU.mult, [("ps", b), sgk], [tk])
                    TT("pool", mo[:, mc, :], mo[:, mc, :], tmp[:], ALU.add, [mok, tk], [mok], waw=False)
                DMA("sp", xview(mab, tt), mo[:], mok, [mok], [("mab", tt)])
            S.barrier()

    def phase_merge(l):
        vb = l * V_PER_LAYER
        xsrc = xT if l == 0 else xs
        with ExitStack() as e4:
            def sb4(name, shape, dt):
                return e4.enter_context(nc.sbuf_tensor(f"{name}_L{l}", list(shape), dt))
            mk_rings(sb4, x=True, h=True, hin=True, ysb=True)
            wgc = sb4("mg_wgc", [128, KC, D], BF16)
            wc = sb4("mg_wc", [128, 2, D], BF16)
            wo = sb4("mg_wo", [128, KC, D], BF16)
            AT = [sb4(f"mg_at{i}", [128, 2, T], BF16) for i in range(2)]
            MI = [sb4(f"mg_mi{i}", [128, KC, T], F32) for i in range(2)]
            MG = sb4("mg_mg", [128, KC, T], BF16)
            SG = [sb4(f"mg_SG{i}", [128, T], F32) for i in range(2)]
            load_w(wgc, w_in[l][:, OFF_GC:OFF_GC + D], D, "mg_w", "mg_w")
            load_w(wc, w_c[l], D, "mg_w", "mg_w")
            load_w(wo, w_out[l], D, "mg_w", "mg_w")
            for tt in range(NT):
                ht, hk = G["hin"].next()
                DMA("sp", ht[:], xview(h1T, tt), hk, [("h1T", tt)], [hk])
                at, atk = AT[tt % 2], ("AT", tt % 2)
                DMA("sp", at[:], attnT.rearrange("(c p) t -> p c t", p=128)[:, :, tt * T:(tt + 1) * T], f"mg_at{tt % 2}", [("attnT", tt // 4)], [atk])
                mi, mik = MI[tt % 2], ("MI", tt % 2)
                DMA("sp", mi[:], xview(mab, tt), f"mg_mi{tt % 2}", [("mab", tt)], [mik])
                xt, xk = G["x"].next()
                DMA("sp", xt[:], xview(xsrc, tt), xk, [("xs", tt)], [xk])
                for mc in range(KC):
                    b = bank()
                    for pr in range(2):
                        MM(PB(b), wc[:, pr, mc * 128:(mc + 1) * 128], at[:, pr, :], pr == 0, pr == 1, [atk, "mg_w"], [("ps", b)])
                    bg = bank()
                    for kc in range(KC):
                        MM(PB(bg), wgc[:, kc, mc * 128:(mc + 1) * 128], ht[:, kc, :], kc == 0, kc == KC - 1, [hk, "mg_w"], [("ps", bg)])
                    sg, sgk = SG[mc % 2], ("SG4", mc % 2)
                    ACT(sg[:], PB(bg), AF.Sigmoid, [("ps", bg)], [sgk])
                    tmp, tk = tmp_ring.next()
                    TT("dve", tmp[:], PB(b), sg[:], ALU.mult, [("ps", b), sgk], [tk])
                    TT("pool", MG[:, mc, :], tmp[:], mi[:, mc, :], ALU.add, [tk, mik], ["MG"], waw=False)
                pn = PostNorm()
                for mc in range(KC):
                    b = bank()
                    for kc in range(KC):
                        MM(PB(b), wo[:, kc, mc * 128:(mc + 1) * 128], MG[:, kc, :], kc == 0, kc == KC - 1, ["MG", "mg_w"], [("ps", b)])
                    pn.chunk(mc, b)
                pn.finish(xt, xk, vb + V_MIXPOST)
                DMA("sp", xview(xs, tt), xt[:], xk, [xk], [("xs", tt)])
                h2, h2k = G["h"].next()
                norm_tile(xt, xk, vb + V_MEMPRE, h2, h2k)
                DMA("sp", xview(h2T, tt), h2[:], h2k, [h2k], [("h2T", tt)])
            S.barrier()

    def phase_mem(l):
        vb = l * V_PER_LAYER
        with ExitStack() as e5:
            def sb5(name, shape, dt):
                return e5.enter_context(nc.sbuf_tensor(f"{name}_L{l}", list(shape), dt))
            mk_rings(sb5, x=True, h=True, hin=True, ysb=True)
            wq_ = sb5("mm_wq", [128, KC, 512], BF16)
            wkv = sb5("mm_wkv", [128, KC, 1024], BF16)
            wo = sb5("mm_wo", [128, 4, D], BF16)
            mt = sb5("mm_mt", [128, KC, T], F32)
            mn = sb5("mm_mn", [128, KC, T], BF16)
            KmT = sb5("mm_KmT", [128, 4, 256], BF16)
            Vm = sb5("mm_Vm", [128, 2, 512], BF16)
            QM = [sb5(f"mm_QM{i}", [128, T], BF16) for i in range(2)]
            PTm = [sb5(f"mm_PT{i}", [128, 2, T], BF16) for i in range(2)]
            DN = [sb5(f"mm_DN{i}", [128, T], F32) for i in range(2)]
            OM = sb5("mm_OM", [128, 4, T], BF16)
            load_w(wq_, w_mq[l], 512, "mm_w", "mm_w")
            load_w(wkv, w_mkv[l], 1024, "mm_w", "mm_w")
            load_w(wo, w_mo[l], D, "mm_w", "mm_w")
            DMA("sp", mt[:, :, 0:256], memT.rearrange("(c p) t -> p c t", p=128), "mm_mt", [], ["mm_mt"])
            norm_tile(mt, "mm_mt", vb + V_MEMKV, mn, "mm_mn", ncols=256)
            for h in range(4):
                b = bank()
                for kc in range(KC):
                    MM(PB(b, 0, 256), wkv[:, kc, h * 128:(h + 1) * 128], mn[:, kc, 0:256], kc == 0, kc == KC - 1, ["mm_mn", "mm_w"], [("ps", b)])
                CP("act", KmT[:, h, :], PB(b, 0, 256), [("ps", b)], ["KmT"], waw=False)
            for mi_ in range(2):
                b = bank()
                for kc in range(KC):
                    MM(PB(b), mn[:, kc, mi_ * 128:(mi_ + 1) * 128], wkv[:, kc, 512:1024], kc == 0, kc == KC - 1, ["mm_mn", "mm_w"], [("ps", b)])
                CP("act", Vm[:, mi_, :], PB(b), [("ps", b)], ["Vm"], waw=False)
            sc = float(128 ** -0.5)
            for tt in range(NT):
                ht, hk = G["hin"].next()
                DMA("sp", ht[:], xview(h2T, tt), hk, [("h2T", tt)], [hk])
                xt, xk = G["x"].next()
                DMA("sp", xt[:], xview(xs, tt), xk, [("xs", tt)], [xk])
                for h in range(4):
                    b = bank()
                    for kc in range(KC):
                        MM(PB(b), wq_[:, kc, h * 128:(h + 1) * 128], ht[:, kc, :], kc == 0, kc == KC - 1, [hk, "mm_w"], [("ps", b)])
                    qm, qmk = QM[h % 2], ("QM", h % 2)
                    CP("act", qm[:], PB(b), [("ps", b)], [qmk])
                    pt, ptk = PTm[h % 2], ("PTm", h % 2)
                    for mi_ in range(2):
                        bs = bank()
                        MM(PB(bs), KmT[:, h, mi_ * 128:(mi_ + 1) * 128], qm[:], True, True, ["KmT", qmk], [("ps", bs)])
                        ACT(pt[:, mi_, :], PB(bs), AF.Exp, [("ps", bs)], [ptk], scale=sc, waw=False)
                    bo, bd = bank(), bank()
                    for mi_ in range(2):
                        MM(PB(bo), Vm[:, mi_, h * 128:(h + 1) * 128], pt[:, mi_, :], mi_ == 0, mi_ == 1, ["Vm", ptk], [("ps", bo)])
                    for mi_ in range(2):
                        MM(PB(bd), ones[:], pt[:, mi_, :], mi_ == 0, mi_ == 1, ["cb", ptk], [("ps", bd)])
                    dn, dnk = DN[h % 2], ("DN", h % 2)
                    S.op("dve", lambda hh, dn=dn, bd=bd: hh.reciprocal(out=dn[:], in_=PB(bd)), reads=[("ps", bd)], writes=[dnk])
                    TT("dve", OM[:, h, :], PB(bo), dn[:], ALU.mult, [("ps", bo), dnk], ["OM"], waw=False)
                pn = PostNorm()
                for mc in range(KC):
                    b = bank()
                    for h in range(4):
                        MM(PB(b), wo[:, h, mc * 128:(mc + 1) * 128], OM[:, h, :], h == 0, h == 3, ["OM", "mm_w"], [("ps", b)])
                    pn.chunk(mc, b)
                pn.finish(xt, xk, vb + V_MEMPOST)
                DMA("sp", xview(xs, tt), xt[:], xk, [xk], [("xs", tt)])
                h3, h3k = G["h"].next()
                norm_tile(xt, xk, vb + V_FFNPRE, h3, h3k)
                DMA("sp", xview(h3T, tt), h3[:], h3k, [h3k], [("h3T", tt)])
            S.barrier()

    def phase_up(l):
        vb = l * V_PER_LAYER
        with ExitStack() as e6:
            def sb6(name, shape, dt):
                return e6.enter_context(nc.sbuf_tensor(f"{name}_L{l}", list(shape), dt))
            mk_rings(sb6, hin=True)
            wu = sb6("up_w", [128, KC, 2 * DFF], BF16)
            H = sb6("up_H", [128, FC, 2], F32)
            UA = [sb6(f"up_UA{i}", [128, 2 + T], F32) for i in range(2)]
            Y1 = [sb6(f"up_Y1{i}", [128, T], F32) for i in range(2)]
            Y2 = [sb6(f"up_Y2{i}", [128, T], F32) for i in range(2)]
            AO = [sb6(f"up_AO{i}", [128, FC, T], BF16) for i in range(2)]
            load_w(wu, w_up[l], 2 * DFF, "up_w", "up_w")
            MS("pool", H[:], 0.0, ["H"])
            cw = vb + V_CONVF
            for tt in range(NT):
                ht, hk = G["hin"].next()
                DMA("sp", ht[:], xview(h3T, tt), hk, [("h3T", tt)], [hk])
                ao, aok = AO[tt % 2], ("AO", tt % 2)
                for c in range(FC):
                    ba, bb_ = bank(), bank()
                    for kc in range(KC):
                        MM(PB(ba), wu[:, kc, c * 128:(c + 1) * 128], ht[:, kc, :], kc == 0, kc == KC - 1, [hk, "up_w"], [("ps", ba)])
                    for kc in range(KC):
                        MM(PB(bb_), wu[:, kc, DFF + c * 128:DFF + (c + 1) * 128], ht[:, kc, :], kc == 0, kc == KC - 1, [hk, "up_w"], [("ps", bb_)])
                    i = c % 2
                    ua, uak = UA[i], ("UA", i)
                    y1, y1k = Y1[i], ("Y1", i)
                    y2, y2k = Y2[i], ("Y2", i)
                    CP("pool", ua[:, 0:2], H[:, c, :], ["H"], [uak])
                    CP("act", ua[:, 2:2 + T], PB(ba), [("ps", ba)], [uak], waw=False)
                    CP("pool", H[:, c, :], ua[:, T:T + 2], [uak], ["H"])
                    TS("pool", y1[:], ua[:, 2:2 + T], vecs[:, cw + 2 * FC + c:cw + 2 * FC + c + 1], None, ALU.mult, None, [uak, "vecs"], [y1k])
                    STT("dve", y2[:], ua[:, 1:1 + T], vecs[:, cw + 1 * FC + c:cw + 1 * FC + c + 1], y1[:], ALU.mult, ALU.add, [uak, y1k, "vecs"], [y2k])
                    STT("dve", y1[:], ua[:, 0:T], vecs[:, cw + 0 * FC + c:cw + 0 * FC + c + 1], y2[:], ALU.mult, ALU.add, [uak, y2k, "vecs"], [y1k])
                    ACT(y2[:], y1[:], AF.Silu, [y1k], [y2k])
                    TT("dve", ao[:, c, :], PB(bb_), y2[:], ALU.mult, [("ps", bb_), y2k], [aok], waw=False)
                DMA("sp", actT.rearrange("(c p) t -> p c t", p=128)[:, :, tt * T:(tt + 1) * T], ao[:], f"up_ao{tt % 2}", [aok], [("actT", tt)])
            S.barrier()

    def phase_down(l, last):
        vb = l * V_PER_LAYER
        with ExitStack() as e7:
            def sb7(name, shape, dt):
                return e7.enter_context(nc.sbuf_tensor(f"{name}_L{l}", list(shape), dt))
            mk_rings(sb7, x=True, h=True, ysb=True)
            wd = sb7("dn_w", [128, FC, D], BF16)
            AI = [sb7(f"dn_AI{i}", [128, FC, T], BF16) for i in range(2)]
            load_w(wd, w_down[l], D, "dn_w", "dn_w")
            for tt in range(NT):
                ai, aik = AI[tt % 2], ("AI", tt % 2)
                DMA("sp", ai[:], actT.rearrange("(c p) t -> p c t", p=128)[:, :, tt * T:(tt + 1) * T], f"dn_ai{tt % 2}", [("actT", tt)], [aik])
                xt, xk = G["x"].next()
                DMA("sp", xt[:], xview(xs, tt), xk, [("xs", tt)], [xk])
                pn = PostNorm()
                for mc in range(KC):
                    b = bank()
                    for c in range(FC):
                        MM(PB(b), wd[:, c, mc * 128:(mc + 1) * 128], ai[:, c, :], c == 0, c == FC - 1, [aik, "dn_w"], [("ps", b)])
                    pn.chunk(mc, b)
                pn.finish(xt, xk, vb + V_FFNPOST)
                if last:
                    DMA("sp", xview(outT, tt), xt[:], xk, [xk], [("outT", tt)])
                else:
                    DMA("sp", xview(xs, tt), xt[:], xk, [xk], [("xs", tt)])
                    h1, h1k = G["h"].next()
                    norm_tile(xt, xk, (l + 1) * V_PER_LAYER + V_MIXPRE, h1, h1k)
                    DMA("sp", xview(h1T, tt), h1[:], h1k, [h1k], [("h1T", tt)])
            S.barrier()

    return dict(nc=nc, S=S, es=es, phases=dict(norm0=phase_norm0, attn=phase_attn, ab=phase_ab, merge=phase_merge,
                                               mem=phase_mem, up=phase_up, down=phase_down))


def build_full(n_layers=NL, dbg=False, stop=None, opts=()):
    P = build_program(n_layers, dbg, opts)
    ph = P["phases"]
    seq = []
    if "nonorm0" not in opts:
        ph["norm0"](0)
    done = stop is not None and stop[1] == "norm0"
    for l in range(n_layers):
        if done:
            break
        for name in ("attn", "ab", "merge", "mem", "up", "down"):
            if name == "down":
                ph[name](l, l == n_layers - 1)
            else:
                ph[name](l)
            if stop is not None and stop == (l, name):
                done = True
                break
        if done:
            break
    nsem = P["S"].emit()
    P["es"].close()
    return P["nc"], P["S"], nsem


def host_inputs(inputs):
    import ml_dtypes
    f32 = np.float32
    perm = w_in_perm()
    wip = np.asarray(inputs["w_in"], f32)[:, :, perm]
    wqkv = np.zeros((NL, 3, D, 1280), f32)
    for g in range(3):
        base = OFF_QKV + 768 * g
        wqkv[:, g, :, 0:256] = wip[:, :, base:base + 256]
        for hh in range(4):
            par = hh % 2
            wqkv[:, g, :, 256 + hh * 128 + 64 * par:256 + hh * 128 + 64 * par + 64] = wip[:, :, base + 256 + 64 * hh:base + 256 + 64 * hh + 64]
            wqkv[:, g, :, 768 + hh * 128 + 64 * par:768 + hh * 128 + 64 * par + 64] = wip[:, :, base + 512 + 64 * hh:base + 512 + 64 * hh + 64]
    shared = {
        "w_in": np.ascontiguousarray(wip[:, :, 0:OFF_QKV]),
        "w_qkv": wqkv,
        "pool_w": np.ascontiguousarray(np.asarray(inputs["pool_w"], f32)),
    }
    for k in ("w_branch_a", "w_branch_b", "w_branch_c", "w_out", "w_mq", "w_mkv", "w_mo", "w_up", "w_down"):
        shared[k] = np.ascontiguousarray(np.asarray(inputs[k], f32))
    vecs = np.zeros((128, NL * V_PER_LAYER), f32)
    for l in range(NL):
        vb = l * V_PER_LAYER
        for name, off in (("norm_mix_pre", V_MIXPRE), ("norm_mix_post", V_MIXPOST), ("norm_mem_pre", V_MEMPRE),
                          ("norm_mem_post", V_MEMPOST), ("norm_memkv", V_MEMKV), ("norm_ffn_pre", V_FFNPRE),
                          ("norm_ffn_post", V_FFNPOST)):
            vecs[:, vb + off:vb + off + 8] = np.asarray(inputs[name], f32)[l].reshape(8, 128).T
        vecs[:, vb + V_CONVB:vb + V_CONVB + 9] = np.asarray(inputs["conv_b_w"], f32)[l].reshape(3, 3, 128).transpose(2, 0, 1).reshape(128, 9)
        vecs[:, vb + V_CONVF:vb + V_CONVF + 66] = np.asarray(inputs["conv_ffn_w"], f32)[l].reshape(3, FC, 128).transpose(2, 0, 1).reshape(128, 66)
        vecs[0:96, vb + V_PSCALE:vb + V_PSCALE + 4] = np.asarray(inputs["pool_scale"], f32)[l].reshape(4, 96).T
    consts = np.zeros((128, NCONST), f32)
    consts[:, C_ID:C_ID + 128] = np.eye(128, dtype=f32)
    k = np.arange(128)[:, None]
    q = np.arange(128)[None, :]
    consts[:, C_MASK:C_MASK + 128] = np.where(k >= q, 0.0, -30000.0)
    consts[:, C_MASK + 128:C_MASK + 256] = np.where(k <= q, 0.0, -30000.0)
    consts[:, C_INV:C_INV + 8] = (f32(500000.0) ** (-np.arange(0, 16, 2, dtype=f32) / f32(16)))[None, :]
    for g, w in enumerate((2, 4, 8, 16)):
        consts[:, C_RC + 16 * g:C_RC + 16 * g + 16] = (1.0 / np.minimum(np.arange(16) + 1, w))[None, :]
    shared["vecs"] = vecs
    shared["consts"] = consts
    x = np.asarray(inputs["x"], f32)
    mem = np.asarray(inputs["mem"], f32)
    posn = np.asarray(inputs["positions"]).astype(np.int32)
    in_maps = []
    for b in range(8):
        m = dict(shared)
        m["xT"] = np.ascontiguousarray(x[b].T)
        m["memT"] = np.ascontiguousarray(mem[b].T)
        m["pos"] = np.ascontiguousarray(posn[b].reshape(32, 128))
        in_maps.append(m)
    return in_maps


_CACHE = {}


def kernel(**inputs):
    in_maps = host_inputs(inputs)
    if "nc" not in _CACHE:
        _CACHE["nc"] = build_full()[0]
    nc = _CACHE["nc"]
    res = run_bass_kernel_spmd(nc, in_maps, core_ids=list(range(8)))
    out = np.stack([np.ascontiguousarray(res.results[b]["outT"].T) for b in range(8)], axis=0)
    return out.astype(np.float32)
```

```python
import numpy as np
import concourse.bass as bass
import concourse.mybir as mybir

F32 = mybir.dt.float32
BF16 = mybir.dt.bfloat16
I32 = mybir.dt.int32
ALU = mybir.AluOpType
AF = mybir.ActivationFunctionType

EPOCH = 30000


class Sched:
    ENGS = ("pe", "act", "dve", "pool", "sp")

    def __init__(self, nc):
        self.nc = nc
        self.stream = {e: [] for e in self.ENGS}
        self.cnt = {e: 0 for e in self.ENGS}
        self.known = {e: {} for e in self.ENGS}
        self.res = {}
        self.chan_cnt = {}
        self.n_wait = 0

    def _collect(self, eng, reads, writes, waw):
        waits = {}

        def need(k, v):
            if k == ("eng", "pe") and eng == "pe":
                return
            if self.known[eng].get(k, 0) >= v:
                return
            if waits.get(k, 0) < v:
                waits[k] = v

        for r in reads:
            st = self.res.get(r)
            if st:
                for k, v in st["w"].items():
                    need(k, v)
        for w in writes:
            st = self.res.get(w)
            if st:
                for k, v in st["r"].items():
                    need(k, v)
                if waw:
                    for k, v in st["w"].items():
                        need(k, v)
        for k, v in waits.items():
            self.known[eng][k] = v
        return sorted(waits.items(), key=lambda kv: str(kv[0]))

    def _commit(self, ev, reads, writes, waw):
        k, v = ev
        for r in reads:
            st = self.res.setdefault(r, {"w": {}, "r": {}})
            if st["r"].get(k, 0) < v:
                st["r"][k] = v
        for w in writes:
            st = self.res.setdefault(w, {"w": {}, "r": {}})
            if st["r"] or waw:
                st["w"] = {}
            st["r"] = {}
            if st["w"].get(k, 0) < v:
                st["w"][k] = v

    def op(self, eng, fn, reads=(), writes=(), waw=True):
        waits = self._collect(eng, reads, writes, waw)
        self.cnt[eng] += 1
        ev = (("eng", eng), self.cnt[eng])
        self._commit(ev, reads, writes, waw)
        self.stream[eng].append((waits, fn, ev))
        self.n_wait += len(waits)

    def dma(self, eng, fn, chan, reads=(), writes=(), waw=False):
        waits = self._collect(eng, reads, writes, waw)
        self.chan_cnt[chan] = self.chan_cnt.get(chan, 0) + 16
        ev = (("chan", chan), self.chan_cnt[chan])
        self._commit(ev, reads, writes, waw)
        self.stream[eng].append((waits, fn, ev))
        self.n_wait += len(waits)

    def barrier(self, engs=None):
        engs = engs or self.ENGS
        for e in engs:
            waits = {}
            for e2 in self.ENGS:
                if e2 != e and self.cnt[e2] > 0:
                    k = ("eng", e2)
                    if self.known[e].get(k, 0) < self.cnt[e2]:
                        waits[k] = self.cnt[e2]
            for c, v in self.chan_cnt.items():
                k = ("chan", c)
                if self.known[e].get(k, 0) < v:
                    waits[k] = v
            for k, v in waits.items():
                self.known[e][k] = v
            if waits:
                self.stream[e].append((sorted(waits.items(), key=lambda kv: str(kv[0])), None, None))

    def emit(self):
        nc = self.nc
        sems = {}

        def sem_of(k, v):
            if k[0] == "eng":
                ep = (v - 1) // EPOCH
                key = (k, ep)
                val = (v - 1) % EPOCH + 1
            else:
                key = (k, 0)
                val = v
            if key not in sems:
                sems[key] = nc.alloc_semaphore(name=f"s{len(sems)}")
            return sems[key], val

        self.barrier(engs=("sp",))
        for e in self.ENGS:
            for waits, fn, ev in self.stream[e]:
                for k, v in waits:
                    sem_of(k, v)
                if ev is not None:
                    sem_of(*ev)
        eng_map = {"pe": "tensor", "act": "scalar", "dve": "vector", "pool": "gpsimd", "sp": "sync"}
        with nc.Block() as block:
            for e in self.ENGS:
                if not self.stream[e]:
                    continue
                deco = getattr(block, eng_map[e])

                def body(h, e=e):
                    for waits, fn, ev in self.stream[e]:
                        for k, v in waits:
                            s, val = sem_of(k, v)
                            h.wait_ge(s, val)
                        if fn is None:
                            continue
                        ins = fn(h)
                        s, _ = sem_of(*ev)
                        ins.then_inc(s, 16 if ev[0][0] == "chan" else 1)

                deco(body)
        return len(sems)


def sap(t, F, poff, npart, off, dims):
    return bass.AP(t, poff * F + off, [[F, npart]] + [list(d) for d in dims])

from concourse.bass_utils import run_bass_kernel_spmd
from contextlib import ExitStack

D = 1024
SEQ = 4096
T = 512
NT = SEQ // T
KC = 8
DFF = 2816
FC = DFF // 128
NL = 2
EPS = 1e-6
DILS = (1, 4, 16)
OFF_A = 0
OFF_BX = 384
OFF_BB = 768
OFF_BC = 1152
OFF_GA = 1536
OFF_GB = 2560
OFF_GC = 3584
OFF_QKV = 4608
NAB = 3584
V_MIXPRE, V_MIXPOST, V_MEMPRE, V_MEMPOST, V_MEMKV, V_FFNPRE, V_FFNPOST = 0, 8, 16, 24, 32, 40, 48
V_CONVB = 56
V_CONVF = 65
V_PSCALE = 131
V_PER_LAYER = 135
C_ID = 0
C_MASK = 128
C_INV = 384
C_RC = 392
NCONST = 456


def w_in_perm():
    idx = []
    a = 0
    idx += list(range(0, 384))
    idx += list(range(384, 384 + 1152))
    g0 = 384 + 1152 + 3 * 768
    idx += list(range(g0, g0 + 3072))
    q0 = 384 + 1152
    for g in range(3):
        for part in range(3):
            s = q0 + part * 768 + g * 256
            idx += list(range(s, s + 256))
    return np.array(idx, dtype=np.int64)


def build_program(n_layers=NL, dbg=False, opts=()):
    nc = bass.Bass("TRN2", target_bir_lowering=False)
    S = Sched(nc)

    def din(name, shape, dt=F32):
        return nc.dram_tensor(name, list(shape), dt, kind="ExternalInput").ap()

    kind_s = "ExternalOutput" if dbg else "Internal"

    def dscr(name, shape, dt):
        return nc.dram_tensor(name, list(shape), dt, kind=kind_s).ap()

    xT = din("xT", [D, SEQ])
    memT = din("memT", [D, 256])
    pos = din("pos", [32, 128], I32)
    w_in = din("w_in", [NL, D, OFF_QKV])
    w_qkv = din("w_qkv", [NL, 3, D, 1280])
    pool_w = din("pool_w", [NL, 4, 96, 96])
    w_a = din("w_branch_a", [NL, 384, D])
    w_b = din("w_branch_b", [NL, 384, D])
    w_c = din("w_branch_c", [NL, 256, D])
    w_out = din("w_out", [NL, D, D])
    w_mq = din("w_mq", [NL, D, 512])
    w_mkv = din("w_mkv", [NL, D, 1024])
    w_mo = din("w_mo", [NL, 512, D])
    w_up = din("w_up", [NL, D, 2 * DFF])
    w_down = din("w_down", [NL, DFF, D])
    vecs_d = din("vecs", [128, NL * V_PER_LAYER])
    consts_d = din("consts", [128, NCONST])
    outT = nc.dram_tensor("outT", [D, SEQ], F32, kind="ExternalOutput").ap()

    h1T = dscr("h1T", [D, SEQ], BF16)
    h2T = dscr("h2T", [D, SEQ], BF16)
    h3T = dscr("h3T", [D, SEQ], BF16)
    attnT = dscr("attnT", [256, SEQ], BF16)
    mab = dscr("mab", [D, SEQ], F32)
    xs = dscr("xs", [D, SEQ], F32)
    actT = dscr("actT", [DFF, SEQ], BF16)
    rope_d = dscr("rope_tab", [SEQ, 16], F32)

    es = ExitStack()

    def sb(name, shape, dt):
        return es.enter_context(nc.sbuf_tensor("s_" + name, list(shape), dt))

    ps = es.enter_context(nc.psum_tensor("ps", [128, 4096], F32))

    def MM(out, lhsT, rhs, start, stop, R, W):
        S.op("pe", lambda h: h.matmul(out, lhsT=lhsT, rhs=rhs, start=start, stop=stop), reads=R, writes=W)

    def ACT(out, in_, func, R, W, scale=1.0, bias=None, waw=True):
        if bias is None:
            S.op("act", lambda h: h.activation(out=out, in_=in_, func=func, scale=scale), reads=R, writes=W, waw=waw)
        else:
            S.op("act", lambda h: h.activation(out=out, in_=in_, func=func, scale=scale, bias=bias), reads=R, writes=W, waw=waw)

    def TT(eng, out, in0, in1, op, R, W, waw=True):
        S.op(eng, lambda h: h.tensor_tensor(out=out, in0=in0, in1=in1, op=op), reads=R, writes=W, waw=waw)

    def TS(eng, out, in0, s1, s2, op0, op1, R, W, waw=True):
        if s2 is None:
            S.op(eng, lambda h: h.tensor_scalar(out=out, in0=in0, scalar1=s1, scalar2=None, op0=op0), reads=R, writes=W, waw=waw)
        else:
            S.op(eng, lambda h: h.tensor_scalar(out=out, in0=in0, scalar1=s1, scalar2=s2, op0=op0, op1=op1), reads=R, writes=W, waw=waw)

    def STT(eng, out, in0, scalar, in1, op0, op1, R, W, waw=True):
        S.op(eng, lambda h: h.scalar_tensor_tensor(out=out, in0=in0, scalar=scalar, in1=in1, op0=op0, op1=op1), reads=R, writes=W, waw=waw)

    def CP(eng, out, in_, R, W, waw=True):
        if eng == "act":
            ACT(out, in_, AF.Copy, R, W, waw=waw)
        else:
            S.op(eng, lambda h: h.tensor_copy(out=out, in_=in_), reads=R, writes=W, waw=waw)

    def MS(eng, ap, val, W):
        S.op(eng, lambda h: h.memset(ap, val), writes=W)

    def DMA(eng, out, in_, chan, R, W):
        S.dma(eng, lambda h: h.dma_start(out=out, in_=in_), chan, reads=R, writes=W)

    uniq = [0]

    class Ring:
        def __init__(self, name, shape, dt, n, alloc=None):
            alloc = alloc or sb
            uniq[0] += 1
            self.name = name
            self.t = [alloc(f"{name}_{uniq[0]}_{i}", shape, dt) for i in range(n)]
            self.i = 0

        def next(self):
            k = self.i % len(self.t)
            self.i += 1
            return self.t[k], (self.name, k)

    psp = [0]

    def bank(n=1):
        p = psp[0] % 7
        if n == 2:
            if p % 2 == 1:
                psp[0] += 1
                p = psp[0] % 7
            if p == 6:
                psp[0] += 1
                p = 0
        psp[0] += n
        return p

    def PB(b, lo=0, hi=512):
        return ps[:, b * 512 + lo: b * 512 + hi]

    vecs = sb("vecs", [128, NL * V_PER_LAYER], F32)
    consts = sb("consts", [128, NCONST], F32)
    ident = sb("ident", [128, 128], BF16)
    mask2 = sb("mask2", [128, 256], BF16)
    onesm = sb("onesm", [128, 128], BF16)
    ones = sb("ones", [128, 128], BF16)
    onesP = sb("onesP", [128, 2, 128], BF16)
    DMA("sp", vecs[:], vecs_d, "vecs", [], ["vecs"])
    DMA("sp", consts[:], consts_d, "consts", [], ["consts"])
    CP("dve", ident[:], consts[:, C_ID:C_ID + 128], ["consts"], ["cb"])
    CP("dve", mask2[:], consts[:, C_MASK:C_MASK + 256], ["consts"], ["cb"])
    MS("pool", onesm[:], 1.0 / 1024.0, ["cb"])
    MS("pool", ones[:], 1.0, ["cb"])
    MS("pool", onesP[:], 0.0, ["cb"])
    MS("pool", onesP[:, 0, 0:64], 1.0, ["cb"])
    MS("pool", onesP[:, 1, 64:128], 1.0, ["cb"])

    sqring = Ring("sq", [128, T], BF16, 3)
    rstdring = Ring("rstd", [128, T], F32, 2)
    G = {}

    def mk_rings(alloc, x=False, h=False, hin=False, ysb=False):
        if x:
            G["x"] = Ring("xt", [128, KC, T], F32, 2, alloc)
        if h:
            G["h"] = Ring("ht", [128, KC, T], BF16, 2, alloc)
        if hin:
            G["hin"] = Ring("hin", [128, KC, T], BF16, 2, alloc)
        if ysb:
            G["ysb"] = Ring("ysb", [128, KC, T], F32, 1, alloc)
    tmp_ring = Ring("tmpf", [128, T], F32, 3)

    def xview(d, tt):
        return d.rearrange("(c p) t -> p c t", p=128)[:, :, tt * T:(tt + 1) * T]

    def rstd_from_bank(b):
        rstd, rk = rstdring.next()
        ACT(rstd[:], PB(b), AF.Ln, [("ps", b)], [rk], bias=EPS)
        ACT(rstd[:], rstd[:], AF.Exp, [rk], [rk], scale=-0.5)
        return rstd, rk

    def norm_tile(xt, xk, gcol, ht, hk, ncols=T):
        b = bank()
        for c in range(KC):
            sq, sqk = sqring.next()
            ACT(sq[:, 0:ncols], xt[:, c, 0:ncols], AF.Square, [xk], [sqk])
            MM(PB(b, 0, ncols), onesm[:], sq[:, 0:ncols], c == 0, c == KC - 1, [sqk, "cb"], [("ps", b)])
        rstd, rk = rstdring.next()
        ACT(rstd[:, 0:ncols], PB(b, 0, ncols), AF.Ln, [("ps", b)], [rk], bias=EPS)
        ACT(rstd[:, 0:ncols], rstd[:, 0:ncols], AF.Exp, [rk], [rk], scale=-0.5)
        for c in range(KC):
            eng = "dve"
            STT(eng, ht[:, c, 0:ncols], xt[:, c, 0:ncols], vecs[:, gcol + c:gcol + c + 1], rstd[:, 0:ncols],
                ALU.mult, ALU.mult, [xk, rk, "vecs"], [hk], waw=False)

    class PostNorm:
        def __init__(self):
            self.ysb, self.yk = G["ysb"].next()
            self.b = 7

        def chunk(self, mc, b):
            CP("act", self.ysb[:, mc, :], PB(b), [("ps", b)], [self.yk], waw=False)
            sq, sqk = sqring.next()
            TT("dve", sq[:], PB(b), self.ysb[:, mc, :], ALU.mult, [("ps", b), self.yk], [sqk])
            MM(PB(self.b), onesm[:], sq[:], mc == 0, mc == KC - 1, [sqk, "cb"], [("ps", self.b)])

        def finish(self, xt, xk, gcol):
            rstd, rk = rstd_from_bank(self.b)
            for c in range(KC):
                tmp, tk = tmp_ring.next()
                TT("dve", tmp[:], self.ysb[:, c, :], rstd[:], ALU.mult, [self.yk, rk], [tk])
                STT("dve", xt[:, c, :], tmp[:], vecs[:, gcol + c:gcol + c + 1], xt[:, c, :], ALU.mult, ALU.add,
                    [tk, xk, "vecs"], [xk], waw=False)

    def load_w(dst, src2d, ncols, chan, key, rows=128):
        v = src2d.rearrange("(k p) n -> p k n", p=rows)
        for cb in range(0, ncols, 2048):
            w = min(2048, ncols - cb)
            DMA("pool", dst[:, :, cb:cb + w], v[:, :, cb:cb + w], chan, [], [key])

    with ExitStack() as es0:
      if 'norope' not in opts:
          def sb0(name, shape, dt):
              return es0.enter_context(nc.sbuf_tensor(name, list(shape), dt))
          pi_ = sb0("rp_pi", [32, 128], I32)
          pf = sb0("rp_pf", [32, 128], F32)
          ang = sb0("rp_ang", [32, 128, 16], F32)
          a2 = sb0("rp_a2", [32, 128, 16], F32)
          ki = sb0("rp_ki", [32, 128, 16], I32)
          tab = sb0("rp_tab", [32, 128, 16], F32)
          DMA("sp", pi_[:], pos, "rp", [], ["rp_pi"])
          CP("dve", pf[:], pi_[:], ["rp_pi"], ["rp_pf"])
          inv_b = consts[0:32, C_INV:C_INV + 8].unsqueeze(1).to_broadcast([32, 128, 8])
          pf_b = pf[:].unsqueeze(2).to_broadcast([32, 128, 8])
          TT("dve", ang[:, :, 0:8], pf_b, inv_b, ALU.mult, ["rp_pf", "consts"], ["rp_ang"])
          TS("dve", ang[:, :, 8:16], ang[:, :, 0:8], float(np.pi / 2), None, ALU.add, None, ["rp_ang"], ["rp_ang"])
          TS("dve", a2[:], ang[:], float(1.0 / (2 * np.pi)), None, ALU.mult, None, ["rp_ang"], ["rp_a2"])
          CP("dve", ki[:], a2[:], ["rp_a2"], ["rp_ki"])
          CP("dve", a2[:], ki[:], ["rp_ki"], ["rp_a2"])
          STT("dve", ang[:], a2[:], float(-2 * np.pi), ang[:], ALU.mult, ALU.add, ["rp_a2", "rp_ang"], ["rp_ang"])
          TS("dve", a2[:], ang[:], float(np.pi), float(-2 * np.pi), ALU.is_gt, ALU.mult, ["rp_ang"], ["rp_a2"])
          TT("dve", ang[:], ang[:], a2[:], ALU.add, ["rp_ang", "rp_a2"], ["rp_ang"])
          TS("dve", a2[:], ang[:], float(-np.pi), float(2 * np.pi), ALU.is_lt, ALU.mult, ["rp_ang"], ["rp_a2"])
          TT("dve", ang[:], ang[:], a2[:], ALU.add, ["rp_ang", "rp_a2"], ["rp_ang"])
          ACT(tab[:, :, 0:8], ang[:, :, 8:16], AF.Sin, ["rp_ang"], ["rp_tab"])
          ACT(tab[:, :, 8:16], ang[:, :, 0:8], AF.Sin, ["rp_ang"], ["rp_tab"], waw=False)
          DMA("sp", rope_d.rearrange("(b i) f -> b i f", i=128), tab[:], "rp", ["rp_tab"], ["rope_d"])
          S.barrier()

    def phase_norm0(l):
      with ExitStack() as e1:
        mk_rings(lambda n, sh, dt: e1.enter_context(nc.sbuf_tensor(n, list(sh), dt)), x=True, h=True)
        for tt in range(NT):
            xt, xk = G["x"].next()
            DMA("sp", xt[:], xview(xT, tt), xk, [], [xk])
            ht, hk = G["h"].next()
            norm_tile(xt, xk, l * V_PER_LAYER + V_MIXPRE, ht, hk)
            DMA("sp", xview(h1T, tt), ht[:], hk, [hk], [("h1T", tt)])
        S.barrier()

    def phase_attn(l):
        with ExitStack() as e2:
            def sb2(name, shape, dt):
                return e2.enter_context(nc.sbuf_tensor(f"{name}_L{l}", list(shape), dt))
            hT = sb2("at_hT", [128, KC, SEQ], BF16)
            acc = sb2("at_acc", [128, 4, 2048], F32)
            attn_sb = sb2("at_out", [128, 2, 2048], BF16)
            wq = [sb2(f"at_w{i}", [128, KC, 1280], BF16) for i in range(2)]
            cs = [sb2(f"at_cs{g}", [128, 32, 16], F32) for g in range(3)]
            QK = [sb2(f"at_qk{i}", [128, 768], BF16) for i in range(2)]
            QF = [sb2(f"at_qf{i}", [128, 768], F32) for i in range(2)]
            ODT = [sb2(f"at_od{i}", [128, 512], F32) for i in range(2)]
            Vb = [sb2(f"at_v{i}", [128, 4, 128], BF16) for i in range(2)]
            QT = [sb2(f"at_qt{i}", [128, 2, 128], BF16) for i in range(2)]
            KTp = [sb2(f"at_kt{i}", [128, 4, 128], BF16) for i in range(2)]
            PT = [sb2(f"at_pt{i}", [128, 4, 256], BF16) for i in range(2)]
            rt = [sb2(f"at_rt{i}", [128, 4, 8, 8], F32) for i in range(2)]
            for tt in range(NT):
                DMA("sp", hT[:, :, tt * T:(tt + 1) * T], xview(h1T, tt), "at_hT", [("h1T", tt)], ["at_hT"])
            for g in range(3):
                d = DILS[g]
                nb = 32 // d
                for r in range(d):
                    for j in range(nb):
                        src = bass.AP(rope_d.tensor, (r + d * 128 * j) * 16, [[16 * d, 128], [1, 16]])
                        DMA("sp", cs[g][:, r * nb + j, :], src, f"at_cs{g}", ["rope_d"], [("cs", g)])
            wl = [0]
            cnt = {"qk": 0, "pt": 0, "rt": 0, "od": 0}

            def proj_block(g, r, j, need_q, wt, wk):
                d = DILS[g]
                nb = 32 // d
                slot = j % 2
                start = r + d * 128 * j
                sl = slice(start, start + 127 * d + 1, d)
                bq = bank() if need_q else None
                bk = bank()
                bv = bank()
                if need_q:
                    for kc in range(KC):
                        MM(PB(bq, 0, 256), hT[:, kc, sl], wt[:, kc, 0:256], kc == 0, kc == KC - 1, ["at_hT", wk], [("ps", bq)])
                for kc in range(KC):
                    MM(PB(bk), hT[:, kc, sl], wt[:, kc, 256:768], kc == 0, kc == KC - 1, ["at_hT", wk], [("ps", bk)])
                for kc in range(KC):
                    MM(PB(bv), hT[:, kc, sl], wt[:, kc, 768:1280], kc == 0, kc == KC - 1, ["at_hT", wk], [("ps", bv)])
                qi = cnt["qk"] % 2
                cnt["qk"] += 1
                qk, qkk = QK[qi], ("QK", qi)
                qf, qfk = QF[qi], ("QF", qi)
                if need_q:
                    CP("act", qf[:, 0:256], PB(bq, 0, 256), [("ps", bq)], [qfk])
                CP("act", qf[:, 256:768], PB(bk), [("ps", bk)], [qfk], waw=not need_q)
                lo = 0 if need_q else 256
                CP("act", qk[:, lo:768], qf[:, lo:768], [qfk], [qkk])
                CP("dve", Vb[slot][:].rearrange("p a b -> p (a b)"), PB(bv), [("ps", bv)], [("Vb", slot)])
                if 'pb1' in opts:
                    return
                blk = r * nb + j
                csk = ("cs", g)
                views = []
                if need_q:
                    views.append((qf[:, 0:256].rearrange("p (h e) -> p h e", e=64), qk[:, 0:256].rearrange("p (h e) -> p h e", e=64), [128, 4, 8], 0))
                kfv = bass.AP(qf, 256, [[768, 128], [256, 2], [192, 2], [1, 64]])
                kbv = bass.AP(qk, 256, [[768, 128], [256, 2], [192, 2], [1, 64]])
                views.append((kfv, kbv, [128, 2, 2, 8], 1))
                for (fv, bv_, shp, which) in views:
                    ri = cnt["rt"] % 2
                    cnt["rt"] += 1
                    rtt, rtk = rt[ri], ("rt", ri)
                    if which == 0:
                        cosb = cs[g][:, blk, 0:8].unsqueeze(1).to_broadcast(shp)
                        sinb = cs[g][:, blk, 8:16].unsqueeze(1).to_broadcast(shp)
                        u1, u2 = fv[:, :, 0:8], fv[:, :, 8:16]
                        o1, o2 = bv_[:, :, 0:8], bv_[:, :, 8:16]
                        tv = [rtt[:, k, 0:4, :] for k in range(4)]
                    else:
                        cosb = cs[g][:, blk, 0:8].unsqueeze(1).unsqueeze(1).to_broadcast(shp)
                        sinb = cs[g][:, blk, 8:16].unsqueeze(1).unsqueeze(1).to_broadcast(shp)
                        u1, u2 = fv[:, :, :, 0:8], fv[:, :, :, 8:16]
                        o1, o2 = bv_[:, :, :, 0:8], bv_[:, :, :, 8:16]
                        tv = [rtt[:, k, 0:4, :].rearrange("p (a b) e -> p a b e", a=2) for k in range(4)]
                    TT("dve", tv[0], u1, cosb, ALU.mult, [qfk, csk], [rtk])
                    TT("dve", tv[1], u2, sinb, ALU.mult, [qfk, csk], [rtk], waw=False)
                    TT("dve", tv[2], u2, cosb, ALU.mult, [qfk, csk], [rtk], waw=False)
                    TT("dve", tv[3], u1, sinb, ALU.mult, [qfk, csk], [rtk], waw=False)
                    TT("pool", o1, tv[0], tv[1], ALU.subtract, [rtk], [qkk])
                    TT("pool", o2, tv[2], tv[3], ALU.add, [rtk], [qkk])
                if 'pb2' in opts:
                    return
                if need_q:
                    bt = bank()
                    for ch in range(2):
                        MM(PB(bt, ch * 128, ch * 128 + 128), qk[:, ch * 128:(ch + 1) * 128], ident[:], True, True, [qkk, "cb"], [("ps", bt)])
                    CP("act", QT[slot][:].rearrange("p c t -> p (c t)"), PB(bt, 0, 256), [("ps", bt)], [("QT", slot)])
                bt2 = bank()
                for hh in range(4):
                    MM(PB(bt2, hh * 128, hh * 128 + 128), qk[:, 256 + hh * 128:256 + (hh + 1) * 128], ident[:], True, True, [qkk, "cb"], [("ps", bt2)])
                CP("dve", KTp[slot][:].rearrange("p a b -> p (a b)"), PB(bt2), [("ps", bt2)], [("KTp", slot)])

            def attend(g, r, j, half, first_group):
                d = DILS[g]
                sc, sp_ = j % 2, (j - 1) % 2
                b2 = bank(2)
                for hh in range(4):
                    ch = hh // 2
                    bb = b2 + hh // 2
                    c0 = (hh % 2) * 256
                    if j > 0:
                        MM(PB(bb, c0, c0 + 256), ident[:], mask2[:, 0:256], True, False, ["cb"], [("ps", bb)])
                        MM(PB(bb, c0, c0 + 128), KTp[sp_][:, hh, :], QT[sc][:, ch, :], False, False, [("KTp", sp_), ("QT", sc)], [("ps", bb)])
                    else:
                        MM(PB(bb, c0 + 128, c0 + 256), ident[:], mask2[:, 128:256], True, False, ["cb"], [("ps", bb)])
                    MM(PB(bb, c0 + 128, c0 + 256), KTp[sc][:, hh, :], QT[sc][:, ch, :], False, True, [("KTp", sc), ("QT", sc)], [("ps", bb)])
                pi = cnt["pt"] % 2
                cnt["pt"] += 1
                pt, ptk = PT[pi], ("PT", pi)
                for hb in range(2):
                    bb = b2 + hb
                    if j > 0:
                        ACT(pt[:, 2 * hb:2 * hb + 2, :].rearrange("p a b -> p (a b)"), PB(bb), AF.Exp, [("ps", bb)], [ptk], scale=0.125, waw=False)
                    else:
                        for a in range(2):
                            ACT(pt[:, 2 * hb + a, 128:256], PB(bb, a * 256 + 128, a * 256 + 256), AF.Exp,
                                [("ps", bb)], [ptk], scale=0.125, waw=False)
                b3 = bank()
                kbs = [0, 1] if j > 0 else [1]
                for od in range(2):
                    for pair in range(2):
                        mats = [(hh, kb) for hh in (2 * pair, 2 * pair + 1) for kb in kbs]
                        c0 = od * 256 + pair * 128
                        for i, (hh, kb) in enumerate(mats):
                            vs = sp_ if kb == 0 else sc
                            lhs = Vb[vs][:, hh, :] if od == 0 else onesP[:, hh % 2, :]
                            rd = [ptk, ("Vb", vs)] if od == 0 else [ptk, "cb"]
                            MM(PB(b3, c0, c0 + 128), lhs, pt[:, hh, kb * 128:(kb + 1) * 128], i == 0, i == len(mats) - 1, rd, [("ps", b3)])
                off = r + d * 128 * j - 2048 * half
                av = bass.AP(acc, off, [[4 * 2048, 128], [2048, 4], [d, 128]])
                oi = cnt["od"] % 2
                cnt["od"] += 1
                odt, odk = ODT[oi], ("ODT", oi)
                CP("act", odt[:], PB(b3), [("ps", b3)], [odk])
                pv = odt[:].rearrange("p (a t) -> p a t", t=128)
                if first_group:
                    CP("pool", av, pv, [odk], ["acc"], waw=True)
                else:
                    TT("pool", av, pv, av, ALU.add, [odk, "acc"], ["acc"])

            for half in range(2):
                for g in range(3):
                    d = DILS[g]
                    nb = 32 // d
                    wi = wl[0] % 2
                    wl[0] += 1
                    wt, wk = wq[wi], ("at_w", wi)
                    load_w(wt, w_qkv[l, g], 1280, f"at_w{wi}", wk)
                    jl, jh = half * nb // 2, (half + 1) * nb // 2
                    for r in range(d):
                        if 'at_loads' in opts:
                            continue
                        if jl > 0:
                            proj_block(g, r, jl - 1, False, wt, wk)
                        for j in range(jl, jh):
                            proj_block(g, r, j, True, wt, wk)
                            if 'at_proj' not in opts:
                                attend(g, r, j, half, g == 0)
                for pair in range(2):
                    ACT(acc[:, 2 + pair, :], acc[:, 2 + pair, :], AF.Ln, ["acc"], ["acc"])
                    ACT(acc[:, 2 + pair, :], acc[:, 2 + pair, :], AF.Exp, ["acc"], ["acc"], scale=-1.0)
                    TT("dve", attn_sb[:, pair, :], acc[:, pair, :], acc[:, 2 + pair, :], ALU.mult, ["acc"], ["attn_sb"])
                dv = attnT.rearrange("(c p) t -> p c t", p=128)[:, :, half * 2048:(half + 1) * 2048]
                DMA("sp", dv, attn_sb[:], "attn_sb", ["attn_sb"], [("attnT", half)])
            S.barrier()

    def phase_ab(l):
        vb = l * V_PER_LAYER
        with ExitStack() as e3:
            def sb3(name, shape, dt):
                return e3.enter_context(nc.sbuf_tensor(f"{name}_L{l}", list(shape), dt))
            mk_rings(sb3, hin=True)
            wab = sb3("ab_w", [128, KC, NAB], BF16)
            pw = sb3("ab_pw", [96, 4, 96], BF16)
            wa = sb3("ab_wa", [96, 4, D], BF16)
            wb = sb3("ab_wb", [128, 3, D], BF16)
            A = [sb3(f"ab_A{g}", [96, 16 + T], F32) for g in range(4)]
            T2 = sb3("ab_T2", [96, 16 + T], F32)
            T4 = sb3("ab_T4", [96, 16 + T], F32)
            T8 = sb3("ab_T8", [96, 16 + T], F32)
            PL = [sb3(f"ab_PL{g}", [96, T], BF16) for g in range(4)]
            MX = [sb3(f"ab_MX{g}", [96, T], BF16) for g in range(4)]
            BX = sb3("ab_BX", [128, T], F32)
            P = [sb3(f"ab_P{c}", [128, 2 + T], F32) for c in range(3)]
            Y1 = sb3("ab_Y1", [128, T], F32)
            Y2 = sb3("ab_Y2", [128, T], F32)
            Z = [sb3(f"ab_Z{c}", [128, T], BF16) for c in range(3)]
            SG = [sb3(f"ab_SG{i}", [128, T], F32) for i in range(2)]
            MO = [sb3(f"ab_MO{i}", [128, KC, T], F32) for i in range(2)]
            load_w(wab, w_in[l][:, 0:NAB], NAB, "ab_w", "ab_w")
            DMA("pool", pw[:], pool_w[l].rearrange("g c d -> c g d"), "ab_w2", [], ["ab_w2"])
            DMA("pool", wa[:], w_a[l].rearrange("(g c) n -> c g n", c=96), "ab_w2", [], ["ab_w2"])
            DMA("pool", wb[:], w_b[l].rearrange("(k p) n -> p k n", p=128), "ab_w2", [], ["ab_w2"])
            for g in range(4):
                MS("pool", A[g][:, 0:16], 0.0, [("A", g)])
            for c in range(3):
                MS("pool", P[c][:, 0:2], 0.0, [("P", c)])
            sgi = [0]
            for tt in range(NT):
                ht, hk = G["hin"].next()
                DMA("sp", ht[:], xview(h1T, tt), hk, [("h1T", tt)], [hk])
                mo, mok = MO[tt % 2], ("MO", tt % 2)
                for g in range(4):
                    w = (2, 4, 8, 16)[g]
                    Ak = ("A", g)
                    if tt > 0:
                        CP("pool", A[g][:, 0:16], A[g][:, T:T + 16], [Ak], [Ak])
                    b = bank()
                    for kc in range(KC):
                        MM(ps[0:96, b * 512:(b + 1) * 512], wab[:, kc, OFF_A + 96 * g:OFF_A + 96 * (g + 1)], ht[:, kc, :], kc == 0, kc == KC - 1,
                           [hk, "ab_w"], [("ps", b)])
                    CP("act", A[g][:, 16:16 + T], ps[0:96, b * 512:(b + 1) * 512], [("ps", b)], [Ak])
                    src = A[g]
                    srck = Ak
                    n = 1
                    for (dst, dk) in ((T2, "T2"), (T4, "T4"), (T8, "T8"), (None, None)):
                        if n >= w:
                            break
                        if 2 * n == w:
                            tmpw, tmpk = T2 if dst is not T2 and src is not T2 else (T4 if src is not T4 else T8), None
                            tmpw = {1: T2, 2: T4, 4: T8, 8: T2}[n]
                            tmpk = {1: "T2", 2: "T4", 4: "T8", 8: "T2"}[n]
                            TT("pool", tmpw[:, 16:16 + T], src[:, 16:16 + T], src[:, 16 - n:16 - n + T], ALU.add, [srck], [tmpk])
                            STT("dve", PL[g][:], tmpw[:, 16:16 + T], 1.0 / w, A[g][:, 16:16 + T], ALU.mult, ALU.subtract, [tmpk, Ak], [("PL", g)])
                            if tt == 0:
                                tmp, tk = tmp_ring.next()
                                TT("pool", tmp[0:96, 0:16], tmpw[:, 16:32], consts[0:96, C_RC + 16 * g:C_RC + 16 * g + 16], ALU.mult, [tmpk, "consts"], [tk])
                                TT("pool", PL[g][:, 0:16], tmp[0:96, 0:16], A[g][:, 16:32], ALU.subtract, [tk, Ak], [("PL", g)])
                            break
                        TT("pool", dst[:, 2 * n - 1:16 + T], src[:, 2 * n - 1:16 + T], src[:, n - 1:16 + T - n], ALU.add, [srck], [dk])
                        src, srck = dst, dk
                        n *= 2
                    b2 = bank()
                    MM(ps[0:96, b2 * 512:(b2 + 1) * 512], pw[:, g, :], PL[g][:], True, True, [("PL", g), "ab_w2"], [("ps", b2)])
                    ACT(MX[g][:], ps[0:96, b2 * 512:(b2 + 1) * 512], AF.Copy, [("ps", b2), "vecs"], [("MX", g)],
                        scale=vecs[0:96, vb + V_PSCALE + g:vb + V_PSCALE + g + 1])
                for mc in range(KC):
                    b = bank()
                    for g in range(4):
                        MM(PB(b), wa[:, g, mc * 128:(mc + 1) * 128], MX[g][:], g == 0, g == 3, [("MX", g), "ab_w2"], [("ps", b)])
                    bg = bank()
                    for kc in range(KC):
                        MM(PB(bg), wab[:, kc, OFF_GA + mc * 128:OFF_GA + (mc + 1) * 128], ht[:, kc, :], kc == 0, kc == KC - 1, [hk, "ab_w"], [("ps", bg)])
                    sg, sgk = SG[sgi[0] % 2], ("SG", sgi[0] % 2)
                    sgi[0] += 1
                    ACT(sg[:], PB(bg), AF.Sigmoid, [("ps", bg)], [sgk])
                    TT("dve", mo[:, mc, :], PB(b), sg[:], ALU.mult, [("ps", b), sgk], [mok], waw=False)
                for c in range(3):
                    Pk = ("P", c)
                    if tt > 0:
                        CP("pool", P[c][:, 0:2], P[c][:, T:T + 2], [Pk], [Pk])
                    bx, bb_, bc = bank(), bank(), bank()
                    for (bk, off) in ((bx, OFF_BX), (bb_, OFF_BB), (bc, OFF_BC)):
                        for kc in range(KC):
                            MM(PB(bk), wab[:, kc, off + c * 128:off + (c + 1) * 128], ht[:, kc, :], kc == 0, kc == KC - 1, [hk, "ab_w"], [("ps", bk)])
                    CP("act", BX[:], PB(bx), [("ps", bx)], ["BX"])
                    TT("dve", P[c][:, 2:2 + T], PB(bc), BX[:], ALU.mult, [("ps", bc), "BX"], [Pk])
                    cw = vb + V_CONVB
                    ACT(Y1[:], P[c][:, 2:2 + T], AF.Copy, [Pk, "vecs"], ["Y1"], scale=vecs[:, cw + 2 * 3 + c:cw + 2 * 3 + c + 1])
                    STT("dve", Y2[:], P[c][:, 1:1 + T], vecs[:, cw + 1 * 3 + c:cw + 1 * 3 + c + 1], Y1[:], ALU.mult, ALU.add, [Pk, "Y1", "vecs"], ["Y2"])
                    STT("dve", Y1[:], P[c][:, 0:T], vecs[:, cw + 0 * 3 + c:cw + 0 * 3 + c + 1], Y2[:], ALU.mult, ALU.add, [Pk, "Y2", "vecs"], ["Y1"])
                    TT("dve", Z[c][:], PB(bb_), Y1[:], ALU.mult, [("ps", bb_), "Y1"], [("Z", c)])
                for mc in range(KC):
                    b = bank()
                    for c in range(3):
                        MM(PB(b), wb[:, c, mc * 128:(mc + 1) * 128], Z[c][:], c == 0, c == 2, [("Z", c), "ab_w2"], [("ps", b)])
                    bg = bank()
                    for kc in range(KC):
                        MM(PB(bg), wab[:, kc, OFF_GB + mc * 128:OFF_GB + (mc + 1) * 128], ht[:, kc, :], kc == 0, kc == KC - 1, [hk, "ab_w"], [("ps", bg)])
                    sg, sgk = SG[sgi[0] % 2], ("SG", sgi[0] % 2)
                    sgi[0] += 1
                    ACT(sg[:], PB(bg), AF.Sigmoid, [("ps", bg)], [sgk])
                    tmp, tk = tmp_ring.next()
                    TT("dve", tmp[:], PB(b), sg[:], ALU.mult, [("ps", b), sgk], [tk])
                    TT("pool", mo[:, mc, :], mo[:, mc, :], tmp[:], ALU.add, [mok, tk], [mok], waw=False)
                DMA("sp", xview(mab, tt), mo[:], mok, [mok], [("mab", tt)])
            S.barrier()

    def phase_merge(l):
        vb = l * V_PER_LAYER
        xsrc = xT if l == 0 else xs
        with ExitStack() as e4:
            def sb4(name, shape, dt):
                return e4.enter_context(nc.sbuf_tensor(f"{name}_L{l}", list(shape), dt))
            mk_rings(sb4, x=True, h=True, hin=True, ysb=True)
            wgc = sb4("mg_wgc", [128, KC, D], BF16)
            wc = sb4("mg_wc", [128, 2, D], BF16)
            wo = sb4("mg_wo", [128, KC, D], BF16)
            AT = [sb4(f"mg_at{i}", [128, 2, T], BF16) for i in range(2)]
            MI = [sb4(f"mg_mi{i}", [128, KC, T], F32) for i in range(2)]
            MG = sb4("mg_mg", [128, KC, T], BF16)
            SG = [sb4(f"mg_SG{i}", [128, T], F32) for i in range(2)]
            load_w(wgc, w_in[l][:, OFF_GC:OFF_GC + D], D, "mg_w", "mg_w")
            load_w(wc, w_c[l], D, "mg_w", "mg_w")
            load_w(wo, w_out[l], D, "mg_w", "mg_w")
            for tt in range(NT):
                ht, hk = G["hin"].next()
                DMA("sp", ht[:], xview(h1T, tt), hk, [("h1T", tt)], [hk])
                at, atk = AT[tt % 2], ("AT", tt % 2)
                DMA("sp", at[:], attnT.rearrange("(c p) t -> p c t", p=128)[:, :, tt * T:(tt + 1) * T], f"mg_at{tt % 2}", [("attnT", tt // 4)], [atk])
                mi, mik = MI[tt % 2], ("MI", tt % 2)
                DMA("sp", mi[:], xview(mab, tt), f"mg_mi{tt % 2}", [("mab", tt)], [mik])
                xt, xk = G["x"].next()
                DMA("sp", xt[:], xview(xsrc, tt), xk, [("xs", tt)], [xk])
                for mc in range(KC):
                    b = bank()
                    for pr in range(2):
                        MM(PB(b), wc[:, pr, mc * 128:(mc + 1) * 128], at[:, pr, :], pr == 0, pr == 1, [atk, "mg_w"], [("ps", b)])
                    bg = bank()
                    for kc in range(KC):
                        MM(PB(bg), wgc[:, kc, mc * 128:(mc + 1) * 128], ht[:, kc, :], kc == 0, kc == KC - 1, [hk, "mg_w"], [("ps", bg)])
                    sg, sgk = SG[mc % 2], ("SG4", mc % 2)
                    ACT(sg[:], PB(bg), AF.Sigmoid, [("ps", bg)], [sgk])
                    tmp, tk = tmp_ring.next()
                    TT("dve", tmp[:], PB(b), sg[:], ALU.mult, [("ps", b), sgk], [tk])
                    TT("pool", MG[:, mc, :], tmp[:], mi[:, mc, :], ALU.add, [tk, mik], ["MG"], waw=False)
                pn = PostNorm()
                for mc in range(KC):
                    b = bank()
                    for kc in range(KC):
                        MM(PB(b), wo[:, kc, mc * 128:(mc + 1) * 128], MG[:, kc, :], kc == 0, kc == KC - 1, ["MG", "mg_w"], [("ps", b)])
                    pn.chunk(mc, b)
                pn.finish(xt, xk, vb + V_MIXPOST)
                DMA("sp", xview(xs, tt), xt[:], xk, [xk], [("xs", tt)])
                h2, h2k = G["h"].next()
                norm_tile(xt, xk, vb + V_MEMPRE, h2, h2k)
                DMA("sp", xview(h2T, tt), h2[:], h2k, [h2k], [("h2T", tt)])
            S.barrier()

    def phase_mem(l):
        vb = l * V_PER_LAYER
        with ExitStack() as e5:
            def sb5(name, shape, dt):
                return e5.enter_context(nc.sbuf_tensor(f"{name}_L{l}", list(shape), dt))
            mk_rings(sb5, x=True, h=True, hin=True, ysb=True)
            wq_ = sb5("mm_wq", [128, KC, 512], BF16)
            wkv = sb5("mm_wkv", [128, KC, 1024], BF16)
            wo = sb5("mm_wo", [128, 4, D], BF16)
            mt = sb5("mm_mt", [128, KC, T], F32)
            mn = sb5("mm_mn", [128, KC, T], BF16)
            KmT = sb5("mm_KmT", [128, 4, 256], BF16)
            Vm = sb5("mm_Vm", [128, 2, 512], BF16)
            QM = [sb5(f"mm_QM{i}", [128, T], BF16) for i in range(2)]
            PTm = [sb5(f"mm_PT{i}", [128, 2, T], BF16) for i in range(2)]
            DN = [sb5(f"mm_DN{i}", [128, T], F32) for i in range(2)]
            OM = sb5("mm_OM", [128, 4, T], BF16)
            load_w(wq_, w_mq[l], 512, "mm_w", "mm_w")
            load_w(wkv, w_mkv[l], 1024, "mm_w", "mm_w")
            load_w(wo, w_mo[l], D, "mm_w", "mm_w")
            DMA("sp", mt[:, :, 0:256], memT.rearrange("(c p) t -> p c t", p=128), "mm_mt", [], ["mm_mt"])
            norm_tile(mt, "mm_mt", vb + V_MEMKV, mn, "mm_mn", ncols=256)
            for h in range(4):
                b = bank()
                for kc in range(KC):
                    MM(PB(b, 0, 256), wkv[:, kc, h * 128:(h + 1) * 128], mn[:, kc, 0:256], kc == 0, kc == KC - 1, ["mm_mn", "mm_w"], [("ps", b)])
                CP("act", KmT[:, h, :], PB(b, 0, 256), [("ps", b)], ["KmT"], waw=False)
            for mi_ in range(2):
                b = bank()
                for kc in range(KC):
                    MM(PB(b), mn[:, kc, mi_ * 128:(mi_ + 1) * 128], wkv[:, kc, 512:1024], kc == 0, kc == KC - 1, ["mm_mn", "mm_w"], [("ps", b)])
                CP("act", Vm[:, mi_, :], PB(b), [("ps", b)], ["Vm"], waw=False)
            sc = float(128 ** -0.5)
            for tt in range(NT):
                ht, hk = G["hin"].next()
                DMA("sp", ht[:], xview(h2T, tt), hk, [("h2T", tt)], [hk])
                xt, xk = G["x"].next()
                DMA("sp", xt[:], xview(xs, tt), xk, [("xs", tt)], [xk])
                for h in range(4):
                    b = bank()
                    for kc in range(KC):
                        MM(PB(b), wq_[:, kc, h * 128:(h + 1) * 128], ht[:, kc, :], kc == 0, kc == KC - 1, [hk, "mm_w"], [("ps", b)])
                    qm, qmk = QM[h % 2], ("QM", h % 2)
                    CP("act", qm[:], PB(b), [("ps", b)], [qmk])
                    pt, ptk = PTm[h % 2], ("PTm", h % 2)
                    for mi_ in range(2):
                        bs = bank()
                        MM(PB(bs), KmT[:, h, mi_ * 128:(mi_ + 1) * 128], qm[:], True, True, ["KmT", qmk], [("ps", bs)])
                        ACT(pt[:, mi_, :], PB(bs), AF.Exp, [("ps", bs)], [ptk], scale=sc, waw=False)
                    bo, bd = bank(), bank()
                    for mi_ in range(2):
                        MM(PB(bo), Vm[:, mi_, h * 128:(h + 1) * 128], pt[:, mi_, :], mi_ == 0, mi_ == 1, ["Vm", ptk], [("ps", bo)])
                    for mi_ in range(2):
                        MM(PB(bd), ones[:], pt[:, mi_, :], mi_ == 0, mi_ == 1, ["cb", ptk], [("ps", bd)])
                    dn, dnk = DN[h % 2], ("DN", h % 2)
                    ACT(dn[:], PB(bd), AF.Ln, [("ps", bd)], [dnk])
                    ACT(dn[:], dn[:], AF.Exp, [dnk], [dnk], scale=-1.0)
                    TT("dve", OM[:, h, :], PB(bo), dn[:], ALU.mult, [("ps", bo), dnk], ["OM"], waw=False)
                pn = PostNorm()
                for mc in range(KC):
                    b = bank()
                    for h in range(4):
                        MM(PB(b), wo[:, h, mc * 128:(mc + 1) * 128], OM[:, h, :], h == 0, h == 3, ["OM", "mm_w"], [("ps", b)])
                    pn.chunk(mc, b)
                pn.finish(xt, xk, vb + V_MEMPOST)
                DMA("sp", xview(xs, tt), xt[:], xk, [xk], [("xs", tt)])
                h3, h3k = G["h"].next()
                norm_tile(xt, xk, vb + V_FFNPRE, h3, h3k)
                DMA("sp", xview(h3T, tt), h3[:], h3k, [h3k], [("h3T", tt)])
            S.barrier()

    def phase_up(l):
        vb = l * V_PER_LAYER
        with ExitStack() as e6:
            def sb6(name, shape, dt):
                return e6.enter_context(nc.sbuf_tensor(f"{name}_L{l}", list(shape), dt))
            mk_rings(sb6, hin=True)
            wu = sb6("up_w", [128, KC, 2 * DFF], BF16)
            H = sb6("up_H", [128, FC, 2], F32)
            UA = [sb6(f"up_UA{i}", [128, 2 + T], F32) for i in range(2)]
            Y1 = [sb6(f"up_Y1{i}", [128, T], F32) for i in range(2)]
            Y2 = [sb6(f"up_Y2{i}", [128, T], F32) for i in range(2)]
            AO = [sb6(f"up_AO{i}", [128, FC, T], BF16) for i in range(2)]
            load_w(wu, w_up[l], 2 * DFF, "up_w", "up_w")
            MS("pool", H[:], 0.0, ["H"])
            cw = vb + V_CONVF
            for tt in range(NT):
                ht, hk = G["hin"].next()
                DMA("sp", ht[:], xview(h3T, tt), hk, [("h3T", tt)], [hk])
                ao, aok = AO[tt % 2], ("AO", tt % 2)
                for c in range(FC):
                    ba, bb_ = bank(), bank()
                    for kc in range(KC):
                        MM(PB(ba), wu[:, kc, c * 128:(c + 1) * 128], ht[:, kc, :], kc == 0, kc == KC - 1, [hk, "up_w"], [("ps", ba)])
                    for kc in range(KC):
                        MM(PB(bb_), wu[:, kc, DFF + c * 128:DFF + (c + 1) * 128], ht[:, kc, :], kc == 0, kc == KC - 1, [hk, "up_w"], [("ps", bb_)])
                    i = c % 2
                    ua, uak = UA[i], ("UA", i)
                    y1, y1k = Y1[i], ("Y1", i)
                    y2, y2k = Y2[i], ("Y2", i)
                    CP("pool", ua[:, 0:2], H[:, c, :], ["H"], [uak])
                    CP("act", ua[:, 2:2 + T], PB(ba), [("ps", ba)], [uak], waw=False)
                    CP("pool", H[:, c, :], ua[:, T:T + 2], [uak], ["H"])
                    ACT(y1[:], ua[:, 2:2 + T], AF.Copy, [uak, "vecs"], [y1k], scale=vecs[:, cw + 2 * FC + c:cw + 2 * FC + c + 1])
                    STT("dve", y2[:], ua[:, 1:1 + T], vecs[:, cw + 1 * FC + c:cw + 1 * FC + c + 1], y1[:], ALU.mult, ALU.add, [uak, y1k, "vecs"], [y2k])
                    STT("dve", y1[:], ua[:, 0:T], vecs[:, cw + 0 * FC + c:cw + 0 * FC + c + 1], y2[:], ALU.mult, ALU.add, [uak, y2k, "vecs"], [y1k])
                    ACT(y2[:], y1[:], AF.Silu, [y1k], [y2k])
                    TT("dve", ao[:, c, :], PB(bb_), y2[:], ALU.mult, [("ps", bb_), y2k], [aok], waw=False)
                DMA("sp", actT.rearrange("(c p) t -> p c t", p=128)[:, :, tt * T:(tt + 1) * T], ao[:], f"up_ao{tt % 2}", [aok], [("actT", tt)])
            S.barrier()

    def phase_down(l, last):
        vb = l * V_PER_LAYER
        with ExitStack() as e7:
            def sb7(name, shape, dt):
                return e7.enter_context(nc.sbuf_tensor(f"{name}_L{l}", list(shape), dt))
            mk_rings(sb7, x=True, h=True, ysb=True)
            wd = sb7("dn_w", [128, FC, D], BF16)
            AI = [sb7(f"dn_AI{i}", [128, FC, T], BF16) for i in range(2)]
            load_w(wd, w_down[l], D, "dn_w", "dn_w")
            for tt in range(NT):
                ai, aik = AI[tt % 2], ("AI", tt % 2)
                DMA("sp", ai[:], actT.rearrange("(c p) t -> p c t", p=128)[:, :, tt * T:(tt + 1) * T], f"dn_ai{tt % 2}", [("actT", tt)], [aik])
                xt, xk = G["x"].next()
                DMA("sp", xt[:], xview(xs, tt), xk, [("xs", tt)], [xk])
                pn = PostNorm()
                for mc in range(KC):
                    b = bank()
                    for c in range(FC):
                        MM(PB(b), wd[:, c, mc * 128:(mc + 1) * 128], ai[:, c, :], c == 0, c == FC - 1, [aik, "dn_w"], [("ps", b)])
                    pn.chunk(mc, b)
                pn.finish(xt, xk, vb + V_FFNPOST)
                if last:
                    DMA("sp", xview(outT, tt), xt[:], xk, [xk], [("outT", tt)])
                else:
                    DMA("sp", xview(xs, tt), xt[:], xk, [xk], [("xs", tt)])
                    h1, h1k = G["h"].next()
                    norm_tile(xt, xk, (l + 1) * V_PER_LAYER + V_MIXPRE, h1, h1k)
                    DMA("sp", xview(h1T, tt), h1[:], h1k, [h1k], [("h1T", tt)])
            S.barrier()

    return dict(nc=nc, S=S, es=es, phases=dict(norm0=phase_norm0, attn=phase_attn, ab=phase_ab, merge=phase_merge,
                                               mem=phase_mem, up=phase_up, down=phase_down))


def build_full(n_layers=NL, dbg=False, stop=None, opts=()):
    P = build_program(n_layers, dbg, opts)
    ph = P["phases"]
    seq = []
    if "nonorm0" not in opts:
        ph["norm0"](0)
    done = stop is not None and stop[1] == "norm0"
    for l in range(n_layers):
        if done:
            break
        for name in ("attn", "ab", "merge", "mem", "up", "down"):
            if name == "down":
                ph[name](l, l == n_layers - 1)
            else:
                ph[name](l)
            if stop is not None and stop == (l, name):
                done = True
                break
        if done:
            break
    nsem = P["S"].emit()
    P["es"].close()
    return P["nc"], P["S"], nsem


def host_inputs(inputs):
    import ml_dtypes
    f32 = np.float32
    perm = w_in_perm()
    wip = np.asarray(inputs["w_in"], f32)[:, :, perm]
    wqkv = np.zeros((NL, 3, D, 1280), f32)
    for g in range(3):
        base = OFF_QKV + 768 * g
        wqkv[:, g, :, 0:256] = wip[:, :, base:base + 256]
        for hh in range(4):
            par = hh % 2
            wqkv[:, g, :, 256 + hh * 128 + 64 * par:256 + hh * 128 + 64 * par + 64] = wip[:, :, base + 256 + 64 * hh:base + 256 + 64 * hh + 64]
            wqkv[:, g, :, 768 + hh * 128 + 64 * par:768 + hh * 128 + 64 * par + 64] = wip[:, :, base + 512 + 64 * hh:base + 512 + 64 * hh + 64]
    shared = {
        "w_in": np.ascontiguousarray(wip[:, :, 0:OFF_QKV]),
        "w_qkv": wqkv,
        "pool_w": np.ascontiguousarray(np.asarray(inputs["pool_w"], f32)),
    }
    for k in ("w_branch_a", "w_branch_b", "w_branch_c", "w_out", "w_mq", "w_mkv", "w_mo", "w_up", "w_down"):
        shared[k] = np.ascontiguousarray(np.asarray(inputs[k], f32))
    vecs = np.zeros((128, NL * V_PER_LAYER), f32)
    for l in range(NL):
        vb = l * V_PER_LAYER
        for name, off in (("norm_mix_pre", V_MIXPRE), ("norm_mix_post", V_MIXPOST), ("norm_mem_pre", V_MEMPRE),
                          ("norm_mem_post", V_MEMPOST), ("norm_memkv", V_MEMKV), ("norm_ffn_pre", V_FFNPRE),
                          ("norm_ffn_post", V_FFNPOST)):
            vecs[:, vb + off:vb + off + 8] = np.asarray(inputs[name], f32)[l].reshape(8, 128).T
        vecs[:, vb + V_CONVB:vb + V_CONVB + 9] = np.asarray(inputs["conv_b_w"], f32)[l].reshape(3, 3, 128).transpose(2, 0, 1).reshape(128, 9)
        vecs[:, vb + V_CONVF:vb + V_CONVF + 66] = np.asarray(inputs["conv_ffn_w"], f32)[l].reshape(3, FC, 128).transpose(2, 0, 1).reshape(128, 66)
        vecs[0:96, vb + V_PSCALE:vb + V_PSCALE + 4] = np.asarray(inputs["pool_scale"], f32)[l].reshape(4, 96).T
    consts = np.zeros((128, NCONST), f32)
    consts[:, C_ID:C_ID + 128] = np.eye(128, dtype=f32)
    k = np.arange(128)[:, None]
    q = np.arange(128)[None, :]
    consts[:, C_MASK:C_MASK + 128] = np.where(k >= q, 0.0, -30000.0)
    consts[:, C_MASK + 128:C_MASK + 256] = np.where(k <= q, 0.0, -30000.0)
    consts[:, C_INV:C_INV + 8] = (f32(500000.0) ** (-np.arange(0, 16, 2, dtype=f32) / f32(16)))[None, :]
    for g, w in enumerate((2, 4, 8, 16)):
        consts[:, C_RC + 16 * g:C_RC + 16 * g + 16] = (1.0 / np.minimum(np.arange(16) + 1, w))[None, :]
    shared["vecs"] = vecs
    shared["consts"] = consts
    x = np.asarray(inputs["x"], f32)
    mem = np.asarray(inputs["mem"], f32)
    posn = np.asarray(inputs["positions"]).astype(np.int32)
    in_maps = []
    for b in range(8):
        m = dict(shared)
        m["xT"] = np.ascontiguousarray(x[b].T)
        m["memT"] = np.ascontiguousarray(mem[b].T)
        m["pos"] = np.ascontiguousarray(posn[b].reshape(32, 128))
        in_maps.append(m)
    return in_maps


_CACHE = {}


def kernel(**inputs):
    in_maps = host_inputs(inputs)
    if "nc" not in _CACHE:
        _CACHE["nc"] = build_full()[0]
    nc = _CACHE["nc"]
    res = run_bass_kernel_spmd(nc, in_maps, core_ids=list(range(8)))
    out = np.stack([np.ascontiguousarray(res.results[b]["outT"].T) for b in range(8)], axis=0)
    return out.astype(np.float32)
```

```python
import numpy as np
import concourse.bass as bass
import concourse.mybir as mybir

F32 = mybir.dt.float32
BF16 = mybir.dt.bfloat16
I32 = mybir.dt.int32
ALU = mybir.AluOpType
AF = mybir.ActivationFunctionType

EPOCH = 30000


class Sched:
    ENGS = ("pe", "act", "dve", "pool", "sp")

    def __init__(self, nc):
        self.nc = nc
        self.stream = {e: [] for e in self.ENGS}
        self.cnt = {e: 0 for e in self.ENGS}
        self.known = {e: {} for e in self.ENGS}
        self.res = {}
        self.chan_cnt = {}
        self.n_wait = 0

    def _collect(self, eng, reads, writes, waw):
        waits = {}

        def need(k, v):
            if k == ("eng", "pe") and eng == "pe":
                return
            if self.known[eng].get(k, 0) >= v:
                return
            if waits.get(k, 0) < v:
                waits[k] = v

        for r in reads:
            st = self.res.get(r)
            if st:
                for k, v in st["w"].items():
                    need(k, v)
        for w in writes:
            st = self.res.get(w)
            if st:
                for k, v in st["r"].items():
                    need(k, v)
                if waw:
                    for k, v in st["w"].items():
                        need(k, v)
        for k, v in waits.items():
            self.known[eng][k] = v
        return sorted(waits.items(), key=lambda kv: str(kv[0]))

    def _commit(self, ev, reads, writes, waw):
        k, v = ev
        for r in reads:
            st = self.res.setdefault(r, {"w": {}, "r": {}})
            if st["r"].get(k, 0) < v:
                st["r"][k] = v
        for w in writes:
            st = self.res.setdefault(w, {"w": {}, "r": {}})
            if st["r"] or waw:
                st["w"] = {}
            st["r"] = {}
            if st["w"].get(k, 0) < v:
                st["w"][k] = v

    def op(self, eng, fn, reads=(), writes=(), waw=True):
        waits = self._collect(eng, reads, writes, waw)
        self.cnt[eng] += 1
        ev = (("eng", eng), self.cnt[eng])
        self._commit(ev, reads, writes, waw)
        self.stream[eng].append((waits, fn, ev))
        self.n_wait += len(waits)

    def dma(self, eng, fn, chan, reads=(), writes=(), waw=False):
        waits = self._collect(eng, reads, writes, waw)
        self.chan_cnt[chan] = self.chan_cnt.get(chan, 0) + 16
        ev = (("chan", chan), self.chan_cnt[chan])
        self._commit(ev, reads, writes, waw)
        self.stream[eng].append((waits, fn, ev))
        self.n_wait += len(waits)

    def barrier(self, engs=None):
        engs = engs or self.ENGS
        for e in engs:
            waits = {}
            for e2 in self.ENGS:
                if e2 != e and self.cnt[e2] > 0:
                    k = ("eng", e2)
                    if self.known[e].get(k, 0) < self.cnt[e2]:
                        waits[k] = self.cnt[e2]
            for c, v in self.chan_cnt.items():
                k = ("chan", c)
                if self.known[e].get(k, 0) < v:
                    waits[k] = v
            for k, v in waits.items():
                self.known[e][k] = v
            if waits:
                self.stream[e].append((sorted(waits.items(), key=lambda kv: str(kv[0])), None, None))

    def emit(self):
        nc = self.nc
        sems = {}

        def sem_of(k, v):
            if k[0] == "eng":
                ep = (v - 1) // EPOCH
                key = (k, ep)
                val = (v - 1) % EPOCH + 1
            else:
                key = (k, 0)
                val = v
            if key not in sems:
                sems[key] = nc.alloc_semaphore(name=f"s{len(sems)}")
            return sems[key], val

        self.barrier(engs=("sp",))
        for e in self.ENGS:
            for waits, fn, ev in self.stream[e]:
                for k, v in waits:
                    sem_of(k, v)
                if ev is not None:
                    sem_of(*ev)
        eng_map = {"pe": "tensor", "act": "scalar", "dve": "vector", "pool": "gpsimd", "sp": "sync"}
        with nc.Block() as block:
            for e in self.ENGS:
                if not self.stream[e]:
                    continue
                deco = getattr(block, eng_map[e])

                def body(h, e=e):
                    for waits, fn, ev in self.stream[e]:
                        for k, v in waits:
                            s, val = sem_of(k, v)
                            h.wait_ge(s, val)
                        if fn is None:
                            continue
                        ins = fn(h)
                        s, _ = sem_of(*ev)
                        ins.then_inc(s, 16 if ev[0][0] == "chan" else 1)

                deco(body)
        return len(sems)


def sap(t, F, poff, npart, off, dims):
    return bass.AP(t, poff * F + off, [[F, npart]] + [list(d) for d in dims])

from concourse.bass_utils import run_bass_kernel_spmd
from contextlib import ExitStack

D = 1024
SEQ = 4096
T = 512
NT = SEQ // T
KC = 8
DFF = 2816
FC = DFF // 128
NL = 2
EPS = 1e-6
DILS = (1, 4, 16)
OFF_A = 0
OFF_BX = 384
OFF_BB = 768
OFF_BC = 1152
OFF_GA = 1536
OFF_GB = 2560
OFF_GC = 3584
OFF_QKV = 4608
NAB = 3584
V_MIXPRE, V_MIXPOST, V_MEMPRE, V_MEMPOST, V_MEMKV, V_FFNPRE, V_FFNPOST = 0, 8, 16, 24, 32, 40, 48
V_CONVB = 56
V_CONVF = 65
V_PSCALE = 131
V_PER_LAYER = 135
C_ID = 0
C_MASK = 128
C_INV = 384
C_RC = 392
NCONST = 456


def w_in_perm():
    idx = []
    a = 0
    idx += list(range(0, 384))
    idx += list(range(384, 384 + 1152))
    g0 = 384 + 1152 + 3 * 768
    idx += list(range(g0, g0 + 3072))
    q0 = 384 + 1152
    for g in range(3):
        for part in range(3):
            s = q0 + part * 768 + g * 256
            idx += list(range(s, s + 256))
    return np.array(idx, dtype=np.int64)


def build_program(n_layers=NL, dbg=False, opts=()):
    nc = bass.Bass("TRN2", target_bir_lowering=False)
    S = Sched(nc)

    def din(name, shape, dt=F32):
        return nc.dram_tensor(name, list(shape), dt, kind="ExternalInput").ap()

    kind_s = "ExternalOutput" if dbg else "Internal"

    def dscr(name, shape, dt):
        return nc.dram_tensor(name, list(shape), dt, kind=kind_s).ap()

    xT = din("xT", [D, SEQ])
    memT = din("memT", [D, 256])
    pos = din("pos", [32, 128], I32)
    w_in = din("w_in", [NL, D, OFF_QKV])
    w_qkv = din("w_qkv", [NL, 3, D, 1280])
    pool_w = din("pool_w", [NL, 4, 96, 96])
    w_a = din("w_branch_a", [NL, 384, D])
    w_b = din("w_branch_b", [NL, 384, D])
    w_c = din("w_branch_c", [NL, 256, D])
    w_out = din("w_out", [NL, D, D])
    w_mq = din("w_mq", [NL, D, 512])
    w_mkv = din("w_mkv", [NL, D, 1024])
    w_mo = din("w_mo", [NL, 512, D])
    w_up = din("w_up", [NL, D, 2 * DFF])
    w_down = din("w_down", [NL, DFF, D])
    vecs_d = din("vecs", [128, NL * V_PER_LAYER])
    consts_d = din("consts", [128, NCONST])
    outT = nc.dram_tensor("outT", [D, SEQ], F32, kind="ExternalOutput").ap()

    h1T = dscr("h1T", [D, SEQ], BF16)
    h2T = dscr("h2T", [D, SEQ], BF16)
    h3T = dscr("h3T", [D, SEQ], BF16)
    attnT = dscr("attnT", [256, SEQ], BF16)
    mab = dscr("mab", [D, SEQ], F32)
    xs = dscr("xs", [D, SEQ], F32)
    actT = dscr("actT", [DFF, SEQ], BF16)
    rope_d = dscr("rope_tab", [SEQ, 16], F32)

    es = ExitStack()

    def sb(name, shape, dt):
        return es.enter_context(nc.sbuf_tensor("s_" + name, list(shape), dt))

    ps = es.enter_context(nc.psum_tensor("ps", [128, 4096], F32))

    def MM(out, lhsT, rhs, start, stop, R, W):
        S.op("pe", lambda h: h.matmul(out, lhsT=lhsT, rhs=rhs, start=start, stop=stop), reads=R, writes=W)

    def ACT(out, in_, func, R, W, scale=1.0, bias=None, waw=True):
        if bias is None:
            S.op("act", lambda h: h.activation(out=out, in_=in_, func=func, scale=scale), reads=R, writes=W, waw=waw)
        else:
            S.op("act", lambda h: h.activation(out=out, in_=in_, func=func, scale=scale, bias=bias), reads=R, writes=W, waw=waw)

    def TT(eng, out, in0, in1, op, R, W, waw=True):
        S.op(eng, lambda h: h.tensor_tensor(out=out, in0=in0, in1=in1, op=op), reads=R, writes=W, waw=waw)

    def TS(eng, out, in0, s1, s2, op0, op1, R, W, waw=True):
        if s2 is None:
            S.op(eng, lambda h: h.tensor_scalar(out=out, in0=in0, scalar1=s1, scalar2=None, op0=op0), reads=R, writes=W, waw=waw)
        else:
            S.op(eng, lambda h: h.tensor_scalar(out=out, in0=in0, scalar1=s1, scalar2=s2, op0=op0, op1=op1), reads=R, writes=W, waw=waw)

    def STT(eng, out, in0, scalar, in1, op0, op1, R, W, waw=True):
        S.op(eng, lambda h: h.scalar_tensor_tensor(out=out, in0=in0, scalar=scalar, in1=in1, op0=op0, op1=op1), reads=R, writes=W, waw=waw)

    def CP(eng, out, in_, R, W, waw=True):
        if eng == "act":
            ACT(out, in_, AF.Copy, R, W, waw=waw)
        else:
            S.op(eng, lambda h: h.tensor_copy(out=out, in_=in_), reads=R, writes=W, waw=waw)

    def MS(eng, ap, val, W):
        S.op(eng, lambda h: h.memset(ap, val), writes=W)

    def DMA(eng, out, in_, chan, R, W):
        S.dma(eng, lambda h: h.dma_start(out=out, in_=in_), chan, reads=R, writes=W)

    uniq = [0]

    class Ring:
        def __init__(self, name, shape, dt, n, alloc=None):
            alloc = alloc or sb
            uniq[0] += 1
            self.name = name
            self.t = [alloc(f"{name}_{uniq[0]}_{i}", shape, dt) for i in range(n)]
            self.i = 0

        def next(self):
            k = self.i % len(self.t)
            self.i += 1
            return self.t[k], (self.name, k)

    psp = [0]

    def bank(n=1):
        if n == 2 and psp[0] % 2 == 1:
            psp[0] += 1
        p = psp[0] % 6
        psp[0] += n
        return p

    def PB(b, lo=0, hi=512):
        return ps[:, b * 512 + lo: b * 512 + hi]

    vecs = sb("vecs", [128, NL * V_PER_LAYER], F32)
    consts = sb("consts", [128, NCONST], F32)
    ident = sb("ident", [128, 128], BF16)
    mask2 = sb("mask2", [128, 256], BF16)
    onesm = sb("onesm", [128, 128], BF16)
    ones = sb("ones", [128, 128], BF16)
    onesP = sb("onesP", [128, 2, 128], BF16)
    DMA("sp", vecs[:], vecs_d, "vecs", [], ["vecs"])
    DMA("sp", consts[:], consts_d, "consts", [], ["consts"])
    CP("dve", ident[:], consts[:, C_ID:C_ID + 128], ["consts"], ["cb"])
    CP("dve", mask2[:], consts[:, C_MASK:C_MASK + 256], ["consts"], ["cb"])
    MS("pool", onesm[:], 1.0 / 1024.0, ["cb"])
    MS("pool", ones[:], 1.0, ["cb"])
    MS("pool", onesP[:], 0.0, ["cb"])
    MS("pool", onesP[:, 0, 0:64], 1.0, ["cb"])
    MS("pool", onesP[:, 1, 64:128], 1.0, ["cb"])

    sqring = Ring("sq", [128, T], BF16, 3)
    rstdring = Ring("rstd", [128, T], F32, 2)
    G = {}

    def mk_rings(alloc, x=False, h=False, hin=False, ysb=False):
        if x:
            G["x"] = Ring("xt", [128, KC, T], F32, 2, alloc)
        if h:
            G["h"] = Ring("ht", [128, KC, T], BF16, 2, alloc)
        if hin:
            G["hin"] = Ring("hin", [128, KC, T], BF16, 2, alloc)
        if ysb:
            G["ysb"] = Ring("ysb", [128, KC, T], F32, 2, alloc)
    tmp_ring = Ring("tmpf", [128, T], F32, 3)

    def xview(d, tt):
        return d.rearrange("(c p) t -> p c t", p=128)[:, :, tt * T:(tt + 1) * T]

    def rstd_from_bank(b):
        rstd, rk = rstdring.next()
        ACT(rstd[:], PB(b), AF.Ln, [("ps", b)], [rk], bias=EPS)
        ACT(rstd[:], rstd[:], AF.Exp, [rk], [rk], scale=-0.5)
        return rstd, rk

    def norm_tile(xt, xk, gcol, ht, hk, ncols=T):
        b = bank()
        for c in range(KC):
            sq, sqk = sqring.next()
            ACT(sq[:, 0:ncols], xt[:, c, 0:ncols], AF.Square, [xk], [sqk])
            MM(PB(b, 0, ncols), onesm[:], sq[:, 0:ncols], c == 0, c == KC - 1, [sqk, "cb"], [("ps", b)])
        rstd, rk = rstdring.next()
        ACT(rstd[:, 0:ncols], PB(b, 0, ncols), AF.Ln, [("ps", b)], [rk], bias=EPS)
        ACT(rstd[:, 0:ncols], rstd[:, 0:ncols], AF.Exp, [rk], [rk], scale=-0.5)
        for c in range(KC):
            eng = "dve"
            STT(eng, ht[:, c, 0:ncols], xt[:, c, 0:ncols], vecs[:, gcol + c:gcol + c + 1], rstd[:, 0:ncols],
                ALU.mult, ALU.mult, [xk, rk, "vecs"], [hk], waw=False)

    pncnt = [0]

    class PostNorm:
        def __init__(self):
            self.ysb, self.yk = G["ysb"].next()
            pncnt[0] += 1
            self.b = 6 + pncnt[0] % 2

        def chunk(self, mc, b):
            CP("act", self.ysb[:, mc, :], PB(b), [("ps", b)], [self.yk], waw=False)
            sq, sqk = sqring.next()
            TT("dve", sq[:], PB(b), self.ysb[:, mc, :], ALU.mult, [("ps", b), self.yk], [sqk])
            MM(PB(self.b), onesm[:], sq[:], mc == 0, mc == KC - 1, [sqk, "cb"], [("ps", self.b)])

        def finish(self, xt, xk, gcol):
            rstd, rk = rstd_from_bank(self.b)
            for c in range(KC):
                tmp, tk = tmp_ring.next()
                TT("dve", tmp[:], self.ysb[:, c, :], rstd[:], ALU.mult, [self.yk, rk], [tk])
                STT("dve", xt[:, c, :], tmp[:], vecs[:, gcol + c:gcol + c + 1], xt[:, c, :], ALU.mult, ALU.add,
                    [tk, xk, "vecs"], [xk], waw=False)

    def load_w(dst, src2d, ncols, chan, key, rows=128):
        v = src2d.rearrange("(k p) n -> p k n", p=rows)
        for cb in range(0, ncols, 2048):
            w = min(2048, ncols - cb)
            DMA("pool", dst[:, :, cb:cb + w], v[:, :, cb:cb + w], chan, [], [key])

    with ExitStack() as es0:
      if 'norope' not in opts:
          def sb0(name, shape, dt):
              return es0.enter_context(nc.sbuf_tensor(name, list(shape), dt))
          pi_ = sb0("rp_pi", [32, 128], I32)
          pf = sb0("rp_pf", [32, 128], F32)
          ang = sb0("rp_ang", [32, 128, 16], F32)
          a2 = sb0("rp_a2", [32, 128, 16], F32)
          ki = sb0("rp_ki", [32, 128, 16], I32)
          tab = sb0("rp_tab", [32, 128, 16], F32)
          DMA("sp", pi_[:], pos, "rp", [], ["rp_pi"])
          CP("dve", pf[:], pi_[:], ["rp_pi"], ["rp_pf"])
          inv_b = consts[0:32, C_INV:C_INV + 8].unsqueeze(1).to_broadcast([32, 128, 8])
          pf_b = pf[:].unsqueeze(2).to_broadcast([32, 128, 8])
          TT("dve", ang[:, :, 0:8], pf_b, inv_b, ALU.mult, ["rp_pf", "consts"], ["rp_ang"])
          TS("dve", ang[:, :, 8:16], ang[:, :, 0:8], float(np.pi / 2), None, ALU.add, None, ["rp_ang"], ["rp_ang"])
          TS("dve", a2[:], ang[:], float(1.0 / (2 * np.pi)), None, ALU.mult, None, ["rp_ang"], ["rp_a2"])
          CP("dve", ki[:], a2[:], ["rp_a2"], ["rp_ki"])
          CP("dve", a2[:], ki[:], ["rp_ki"], ["rp_a2"])
          STT("dve", ang[:], a2[:], float(-2 * np.pi), ang[:], ALU.mult, ALU.add, ["rp_a2", "rp_ang"], ["rp_ang"])
          TS("dve", a2[:], ang[:], float(np.pi), float(-2 * np.pi), ALU.is_gt, ALU.mult, ["rp_ang"], ["rp_a2"])
          TT("dve", ang[:], ang[:], a2[:], ALU.add, ["rp_ang", "rp_a2"], ["rp_ang"])
          TS("dve", a2[:], ang[:], float(-np.pi), float(2 * np.pi), ALU.is_lt, ALU.mult, ["rp_ang"], ["rp_a2"])
          TT("dve", ang[:], ang[:], a2[:], ALU.add, ["rp_ang", "rp_a2"], ["rp_ang"])
          ACT(tab[:, :, 0:8], ang[:, :, 8:16], AF.Sin, ["rp_ang"], ["rp_tab"])
          ACT(tab[:, :, 8:16], ang[:, :, 0:8], AF.Sin, ["rp_ang"], ["rp_tab"], waw=False)
          DMA("sp", rope_d.rearrange("(b i) f -> b i f", i=128), tab[:], "rp", ["rp_tab"], ["rope_d"])
          S.barrier()

    def phase_norm0(l):
      with ExitStack() as e1:
        mk_rings(lambda n, sh, dt: e1.enter_context(nc.sbuf_tensor(n, list(sh), dt)), x=True, h=True)
        for tt in range(NT):
            xt, xk = G["x"].next()
            DMA("sp", xt[:], xview(xT, tt), xk, [], [xk])
            ht, hk = G["h"].next()
            norm_tile(xt, xk, l * V_PER_LAYER + V_MIXPRE, ht, hk)
            DMA("sp", xview(h1T, tt), ht[:], hk, [hk], [("h1T", tt)])
        S.barrier()

    def phase_attn(l):
        with ExitStack() as e2:
            def sb2(name, shape, dt):
                return e2.enter_context(nc.sbuf_tensor(f"{name}_L{l}", list(shape), dt))
            hT = sb2("at_hT", [128, KC, SEQ], BF16)
            acc = sb2("at_acc", [128, 4, 2048], F32)
            attn_sb = sb2("at_out", [128, 2, 2048], BF16)
            wq = [sb2(f"at_w{i}", [128, KC, 1280], BF16) for i in range(2)]
            cs = [sb2(f"at_cs{g}", [128, 32, 16], F32) for g in range(3)]
            QK = [sb2(f"at_qk{i}", [128, 768], BF16) for i in range(2)]
            QF = [sb2(f"at_qf{i}", [128, 768], F32) for i in range(2)]
            ODT = [sb2(f"at_od{i}", [128, 512], F32) for i in range(2)]
            Vb = [sb2(f"at_v{i}", [128, 4, 128], BF16) for i in range(2)]
            QT = [sb2(f"at_qt{i}", [128, 2, 128], BF16) for i in range(2)]
            KTp = [sb2(f"at_kt{i}", [128, 4, 128], BF16) for i in range(2)]
            PT = [sb2(f"at_pt{i}", [128, 4, 256], BF16) for i in range(2)]
            rt = [sb2(f"at_rt{i}", [128, 4, 8, 8], F32) for i in range(2)]
            for tt in range(NT):
                DMA("sp", hT[:, :, tt * T:(tt + 1) * T], xview(h1T, tt), "at_hT", [("h1T", tt)], ["at_hT"])
            for g in range(3):
                d = DILS[g]
                nb = 32 // d
                for r in range(d):
                    for j in range(nb):
                        src = bass.AP(rope_d.tensor, (r + d * 128 * j) * 16, [[16 * d, 128], [1, 16]])
                        DMA("sp", cs[g][:, r * nb + j, :], src, f"at_cs{g}", ["rope_d"], [("cs", g)])
            wl = [0]
            cnt = {"qk": 0, "pt": 0, "rt": 0, "od": 0}

            def proj_block(g, r, j, need_q, wt, wk):
                d = DILS[g]
                nb = 32 // d
                slot = j % 2
                start = r + d * 128 * j
                sl = slice(start, start + 127 * d + 1, d)
                bq = bank() if need_q else None
                bk = bank()
                bv = bank()
                if need_q:
                    for kc in range(KC):
                        MM(PB(bq, 0, 256), hT[:, kc, sl], wt[:, kc, 0:256], kc == 0, kc == KC - 1, ["at_hT", wk], [("ps", bq)])
                for kc in range(KC):
                    MM(PB(bk), hT[:, kc, sl], wt[:, kc, 256:768], kc == 0, kc == KC - 1, ["at_hT", wk], [("ps", bk)])
                for kc in range(KC):
                    MM(PB(bv), hT[:, kc, sl], wt[:, kc, 768:1280], kc == 0, kc == KC - 1, ["at_hT", wk], [("ps", bv)])
                qi = cnt["qk"] % 2
                cnt["qk"] += 1
                qk, qkk = QK[qi], ("QK", qi)
                qf, qfk = QF[qi], ("QF", qi)
                if need_q:
                    CP("act", qf[:, 0:256], PB(bq, 0, 256), [("ps", bq)], [qfk])
                CP("act", qf[:, 256:768], PB(bk), [("ps", bk)], [qfk], waw=not need_q)
                lo = 0 if need_q else 256
                CP("act", qk[:, lo:768], qf[:, lo:768], [qfk], [qkk])
                CP("dve", Vb[slot][:].rearrange("p a b -> p (a b)"), PB(bv), [("ps", bv)], [("Vb", slot)])
                if 'pb1' in opts:
                    return
                blk = r * nb + j
                csk = ("cs", g)
                views = []
                if need_q:
                    views.append((qf[:, 0:256].rearrange("p (h e) -> p h e", e=64), qk[:, 0:256].rearrange("p (h e) -> p h e", e=64), [128, 4, 8], 0))
                kfv = bass.AP(qf, 256, [[768, 128], [256, 2], [192, 2], [1, 64]])
                kbv = bass.AP(qk, 256, [[768, 128], [256, 2], [192, 2], [1, 64]])
                views.append((kfv, kbv, [128, 2, 2, 8], 1))
                for (fv, bv_, shp, which) in views:
                    ri = cnt["rt"] % 2
                    cnt["rt"] += 1
                    rtt, rtk = rt[ri], ("rt", ri)
                    if which == 0:
                        cosb = cs[g][:, blk, 0:8].unsqueeze(1).to_broadcast(shp)
                        sinb = cs[g][:, blk, 8:16].unsqueeze(1).to_broadcast(shp)
                        u1, u2 = fv[:, :, 0:8], fv[:, :, 8:16]
                        o1, o2 = bv_[:, :, 0:8], bv_[:, :, 8:16]
                        tv = [rtt[:, k, 0:4, :] for k in range(4)]
                    else:
                        cosb = cs[g][:, blk, 0:8].unsqueeze(1).unsqueeze(1).to_broadcast(shp)
                        sinb = cs[g][:, blk, 8:16].unsqueeze(1).unsqueeze(1).to_broadcast(shp)
                        u1, u2 = fv[:, :, :, 0:8], fv[:, :, :, 8:16]
                        o1, o2 = bv_[:, :, :, 0:8], bv_[:, :, :, 8:16]
                        tv = [rtt[:, k, 0:4, :].rearrange("p (a b) e -> p a b e", a=2) for k in range(4)]
                    TT("dve", tv[0], u1, cosb, ALU.mult, [qfk, csk], [rtk])
                    TT("dve", tv[1], u2, sinb, ALU.mult, [qfk, csk], [rtk], waw=False)
                    TT("dve", tv[2], u2, cosb, ALU.mult, [qfk, csk], [rtk], waw=False)
                    TT("dve", tv[3], u1, sinb, ALU.mult, [qfk, csk], [rtk], waw=False)
                    TT("pool", o1, tv[0], tv[1], ALU.subtract, [rtk], [qkk])
                    TT("pool", o2, tv[2], tv[3], ALU.add, [rtk], [qkk])
                if 'pb2' in opts:
                    return
                if need_q:
                    bt = bank()
                    for ch in range(2):
                        MM(PB(bt, ch * 128, ch * 128 + 128), qk[:, ch * 128:(ch + 1) * 128], ident[:], True, True, [qkk, "cb"], [("ps", bt)])
                    CP("act", QT[slot][:].rearrange("p c t -> p (c t)"), PB(bt, 0, 256), [("ps", bt)], [("QT", slot)])
                bt2 = bank()
                for hh in range(4):
                    MM(PB(bt2, hh * 128, hh * 128 + 128), qk[:, 256 + hh * 128:256 + (hh + 1) * 128], ident[:], True, True, [qkk, "cb"], [("ps", bt2)])
                CP("dve", KTp[slot][:].rearrange("p a b -> p (a b)"), PB(bt2), [("ps", bt2)], [("KTp", slot)])

            def attend(g, r, j, half, first_group):
                d = DILS[g]
                sc, sp_ = j % 2, (j - 1) % 2
                b2 = bank(2)
                for hh in range(4):
                    ch = hh // 2
                    bb = b2 + hh // 2
                    c0 = (hh % 2) * 256
                    if j > 0:
                        MM(PB(bb, c0, c0 + 256), ident[:], mask2[:, 0:256], True, False, ["cb"], [("ps", bb)])
                        MM(PB(bb, c0, c0 + 128), KTp[sp_][:, hh, :], QT[sc][:, ch, :], False, False, [("KTp", sp_), ("QT", sc)], [("ps", bb)])
                    else:
                        MM(PB(bb, c0 + 128, c0 + 256), ident[:], mask2[:, 128:256], True, False, ["cb"], [("ps", bb)])
                    MM(PB(bb, c0 + 128, c0 + 256), KTp[sc][:, hh, :], QT[sc][:, ch, :], False, True, [("KTp", sc), ("QT", sc)], [("ps", bb)])
                pi = cnt["pt"] % 2
                cnt["pt"] += 1
                pt, ptk = PT[pi], ("PT", pi)
                for hb in range(2):
                    bb = b2 + hb
                    if j > 0:
                        ACT(pt[:, 2 * hb:2 * hb + 2, :].rearrange("p a b -> p (a b)"), PB(bb), AF.Exp, [("ps", bb)], [ptk], scale=0.125, waw=False)
                    else:
                        for a in range(2):
                            ACT(pt[:, 2 * hb + a, 128:256], PB(bb, a * 256 + 128, a * 256 + 256), AF.Exp,
                                [("ps", bb)], [ptk], scale=0.125, waw=False)
                b3 = bank()
                kbs = [0, 1] if j > 0 else [1]
                for od in range(2):
                    for pair in range(2):
                        mats = [(hh, kb) for hh in (2 * pair, 2 * pair + 1) for kb in kbs]
                        c0 = od * 256 + pair * 128
                        for i, (hh, kb) in enumerate(mats):
                            vs = sp_ if kb == 0 else sc
                            lhs = Vb[vs][:, hh, :] if od == 0 else onesP[:, hh % 2, :]
                            rd = [ptk, ("Vb", vs)] if od == 0 else [ptk, "cb"]
                            MM(PB(b3, c0, c0 + 128), lhs, pt[:, hh, kb * 128:(kb + 1) * 128], i == 0, i == len(mats) - 1, rd, [("ps", b3)])
                off = r + d * 128 * j - 2048 * half
                av = bass.AP(acc, off, [[4 * 2048, 128], [2048, 4], [d, 128]])
                oi = cnt["od"] % 2
                cnt["od"] += 1
                odt, odk = ODT[oi], ("ODT", oi)
                CP("act", odt[:], PB(b3), [("ps", b3)], [odk])
                pv = odt[:].rearrange("p (a t) -> p a t", t=128)
                if first_group:
                    CP("pool", av, pv, [odk], ["acc"], waw=True)
                else:
                    TT("pool", av, pv, av, ALU.add, [odk, "acc"], ["acc"])

            for half in range(2):
                for g in range(3):
                    d = DILS[g]
                    nb = 32 // d
                    wi = wl[0] % 2
                    wl[0] += 1
                    wt, wk = wq[wi], ("at_w", wi)
                    load_w(wt, w_qkv[l, g], 1280, f"at_w{wi}", wk)
                    jl, jh = half * nb // 2, (half + 1) * nb // 2
                    for r in range(d):
                        if 'at_loads' in opts:
                            continue
                        if jl > 0:
                            proj_block(g, r, jl - 1, False, wt, wk)
                        for j in range(jl, jh):
                            proj_block(g, r, j, True, wt, wk)
                            if 'at_proj' not in opts:
                                attend(g, r, j, half, g == 0)
                for pair in range(2):
                    ACT(acc[:, 2 + pair, :], acc[:, 2 + pair, :], AF.Ln, ["acc"], ["acc"])
                    ACT(acc[:, 2 + pair, :], acc[:, 2 + pair, :], AF.Exp, ["acc"], ["acc"], scale=-1.0)
                    TT("dve", attn_sb[:, pair, :], acc[:, pair, :], acc[:, 2 + pair, :], ALU.mult, ["acc"], ["attn_sb"])
                dv = attnT.rearrange("(c p) t -> p c t", p=128)[:, :, half * 2048:(half + 1) * 2048]
                DMA("sp", dv, attn_sb[:], "attn_sb", ["attn_sb"], [("attnT", half)])
            S.barrier()

    def phase_ab(l):
        vb = l * V_PER_LAYER
        with ExitStack() as e3:
            def sb3(name, shape, dt):
                return e3.enter_context(nc.sbuf_tensor(f"{name}_L{l}", list(shape), dt))
            mk_rings(sb3, hin=True)
            wab = sb3("ab_w", [128, KC, NAB], BF16)
            pw = sb3("ab_pw", [96, 4, 96], BF16)
            wa = sb3("ab_wa", [96, 4, D], BF16)
            wb = sb3("ab_wb", [128, 3, D], BF16)
            A = [sb3(f"ab_A{g}", [96, 16 + T], F32) for g in range(4)]
            T2 = sb3("ab_T2", [96, 16 + T], F32)
            T4 = sb3("ab_T4", [96, 16 + T], F32)
            T8 = sb3("ab_T8", [96, 16 + T], F32)
            PL = [sb3(f"ab_PL{g}", [96, T], BF16) for g in range(4)]
            MX = [sb3(f"ab_MX{g}", [96, T], BF16) for g in range(4)]
            BX = sb3("ab_BX", [128, T], F32)
            P = [sb3(f"ab_P{c}", [128, 2 + T], F32) for c in range(3)]
            Y1 = sb3("ab_Y1", [128, T], F32)
            Y2 = sb3("ab_Y2", [128, T], F32)
            Z = [sb3(f"ab_Z{c}", [128, T], BF16) for c in range(3)]
            SG = [sb3(f"ab_SG{i}", [128, T], F32) for i in range(2)]
            MO = [sb3(f"ab_MO{i}", [128, KC, T], F32) for i in range(2)]
            load_w(wab, w_in[l][:, 0:NAB], NAB, "ab_w", "ab_w")
            DMA("pool", pw[:], pool_w[l].rearrange("g c d -> c g d"), "ab_w2", [], ["ab_w2"])
            DMA("pool", wa[:], w_a[l].rearrange("(g c) n -> c g n", c=96), "ab_w2", [], ["ab_w2"])
            DMA("pool", wb[:], w_b[l].rearrange("(k p) n -> p k n", p=128), "ab_w2", [], ["ab_w2"])
            for g in range(4):
                MS("pool", A[g][:, 0:16], 0.0, [("A", g)])
            for c in range(3):
                MS("pool", P[c][:, 0:2], 0.0, [("P", c)])
            sgi = [0]
            for tt in range(NT):
                ht, hk = G["hin"].next()
                DMA("sp", ht[:], xview(h1T, tt), hk, [("h1T", tt)], [hk])
                mo, mok = MO[tt % 2], ("MO", tt % 2)
                for g in range(4):
                    w = (2, 4, 8, 16)[g]
                    Ak = ("A", g)
                    if tt > 0:
                        CP("pool", A[g][:, 0:16], A[g][:, T:T + 16], [Ak], [Ak])
                    b = bank()
                    for kc in range(KC):
                        MM(ps[0:96, b * 512:(b + 1) * 512], wab[:, kc, OFF_A + 96 * g:OFF_A + 96 * (g + 1)], ht[:, kc, :], kc == 0, kc == KC - 1,
                           [hk, "ab_w"], [("ps", b)])
                    CP("act", A[g][:, 16:16 + T], ps[0:96, b * 512:(b + 1) * 512], [("ps", b)], [Ak])
                    src = A[g]
                    srck = Ak
                    n = 1
                    for (dst, dk) in ((T2, "T2"), (T4, "T4"), (T8, "T8"), (None, None)):
                        if n >= w:
                            break
                        if 2 * n == w:
                            tmpw, tmpk = T2 if dst is not T2 and src is not T2 else (T4 if src is not T4 else T8), None
                            tmpw = {1: T2, 2: T4, 4: T8, 8: T2}[n]
                            tmpk = {1: "T2", 2: "T4", 4: "T8", 8: "T2"}[n]
                            TT("pool", tmpw[:, 16:16 + T], src[:, 16:16 + T], src[:, 16 - n:16 - n + T], ALU.add, [srck], [tmpk])
                            STT("dve", PL[g][:], tmpw[:, 16:16 + T], 1.0 / w, A[g][:, 16:16 + T], ALU.mult, ALU.subtract, [tmpk, Ak], [("PL", g)])
                            if tt == 0:
                                tmp, tk = tmp_ring.next()
                                TT("pool", tmp[0:96, 0:16], tmpw[:, 16:32], consts[0:96, C_RC + 16 * g:C_RC + 16 * g + 16], ALU.mult, [tmpk, "consts"], [tk])
                                TT("pool", PL[g][:, 0:16], tmp[0:96, 0:16], A[g][:, 16:32], ALU.subtract, [tk, Ak], [("PL", g)])
                            break
                        TT("pool", dst[:, 2 * n - 1:16 + T], src[:, 2 * n - 1:16 + T], src[:, n - 1:16 + T - n], ALU.add, [srck], [dk])
                        src, srck = dst, dk
                        n *= 2
                    b2 = bank()
                    MM(ps[0:96, b2 * 512:(b2 + 1) * 512], pw[:, g, :], PL[g][:], True, True, [("PL", g), "ab_w2"], [("ps", b2)])
                    ACT(MX[g][:], ps[0:96, b2 * 512:(b2 + 1) * 512], AF.Copy, [("ps", b2), "vecs"], [("MX", g)],
                        scale=vecs[0:96, vb + V_PSCALE + g:vb + V_PSCALE + g + 1])
                for mc in range(KC):
                    b = bank()
                    for g in range(4):
                        MM(PB(b), wa[:, g, mc * 128:(mc + 1) * 128], MX[g][:], g == 0, g == 3, [("MX", g), "ab_w2"], [("ps", b)])
                    bg = bank()
                    for kc in range(KC):
                        MM(PB(bg), wab[:, kc, OFF_GA + mc * 128:OFF_GA + (mc + 1) * 128], ht[:, kc, :], kc == 0, kc == KC - 1, [hk, "ab_w"], [("ps", bg)])
                    sg, sgk = SG[sgi[0] % 2], ("SG", sgi[0] % 2)
                    sgi[0] += 1
                    ACT(sg[:], PB(bg), AF.Sigmoid, [("ps", bg)], [sgk])
                    TT("dve", mo[:, mc, :], PB(b), sg[:], ALU.mult, [("ps", b), sgk], [mok], waw=False)
                for c in range(3):
                    Pk = ("P", c)
                    if tt > 0:
                        CP("pool", P[c][:, 0:2], P[c][:, T:T + 2], [Pk], [Pk])
                    bx, bb_, bc = bank(), bank(), bank()
                    for (bk, off) in ((bx, OFF_BX), (bb_, OFF_BB), (bc, OFF_BC)):
                        for kc in range(KC):
                            MM(PB(bk), wab[:, kc, off + c * 128:off + (c + 1) * 128], ht[:, kc, :], kc == 0, kc == KC - 1, [hk, "ab_w"], [("ps", bk)])
                    CP("act", BX[:], PB(bx), [("ps", bx)], ["BX"])
                    TT("dve", P[c][:, 2:2 + T], PB(bc), BX[:], ALU.mult, [("ps", bc), "BX"], [Pk])
                    cw = vb + V_CONVB
                    ACT(Y1[:], P[c][:, 2:2 + T], AF.Copy, [Pk, "vecs"], ["Y1"], scale=vecs[:, cw + 2 * 3 + c:cw + 2 * 3 + c + 1])
                    STT("dve", Y2[:], P[c][:, 1:1 + T], vecs[:, cw + 1 * 3 + c:cw + 1 * 3 + c + 1], Y1[:], ALU.mult, ALU.add, [Pk, "Y1", "vecs"], ["Y2"])
                    STT("dve", Y1[:], P[c][:, 0:T], vecs[:, cw + 0 * 3 + c:cw + 0 * 3 + c + 1], Y2[:], ALU.mult, ALU.add, [Pk, "Y2", "vecs"], ["Y1"])
                    TT("dve", Z[c][:], PB(bb_), Y1[:], ALU.mult, [("ps", bb_), "Y1"], [("Z", c)])
                for mc in range(KC):
                    b = bank()
                    for c in range(3):
                        MM(PB(b), wb[:, c, mc * 128:(mc + 1) * 128], Z[c][:], c == 0, c == 2, [("Z", c), "ab_w2"], [("ps", b)])
                    bg = bank()
                    for kc in range(KC):
                        MM(PB(bg), wab[:, kc, OFF_GB + mc * 128:OFF_GB + (mc + 1) * 128], ht[:, kc, :], kc == 0, kc == KC - 1, [hk, "ab_w"], [("ps", bg)])
                    sg, sgk = SG[sgi[0] % 2], ("SG", sgi[0] % 2)
                    sgi[0] += 1
                    ACT(sg[:], PB(bg), AF.Sigmoid, [("ps", bg)], [sgk])
                    tmp, tk = tmp_ring.next()
                    TT("dve", tmp[:], PB(b), sg[:], ALU.mult, [("ps", b), sgk], [tk])
                    TT("pool", mo[:, mc, :], mo[:, mc, :], tmp[:], ALU.add, [mok, tk], [mok], waw=False)
                DMA("sp", xview(mab, tt), mo[:], mok, [mok], [("mab", tt)])
            S.barrier()

    def phase_merge(l):
        vb = l * V_PER_LAYER
        xsrc = xT if l == 0 else xs
        with ExitStack() as e4:
            def sb4(name, shape, dt):
                return e4.enter_context(nc.sbuf_tensor(f"{name}_L{l}", list(shape), dt))
            mk_rings(sb4, x=True, h=True, hin=True, ysb=True)
            wgc = sb4("mg_wgc", [128, KC, D], BF16)
            wc = sb4("mg_wc", [128, 2, D], BF16)
            wo = sb4("mg_wo", [128, KC, D], BF16)
            AT = [sb4(f"mg_at{i}", [128, 2, T], BF16) for i in range(2)]
            MI = [sb4(f"mg_mi{i}", [128, KC, T], F32) for i in range(2)]
            MGs = [sb4(f"mg_mg{i}", [128, KC, T], BF16) for i in range(2)]
            SG = [sb4(f"mg_SG{i}", [128, T], F32) for i in range(2)]
            load_w(wgc, w_in[l][:, OFF_GC:OFF_GC + D], D, "mg_w", "mg_w")
            load_w(wc, w_c[l], D, "mg_w", "mg_w")
            load_w(wo, w_out[l], D, "mg_wo", "mg_wo")
            st = {}

            def stA(tt):
                ht, hk = G["hin"].next()
                DMA("sp", ht[:], xview(h1T, tt), hk, [("h1T", tt)], [hk])
                at, atk = AT[tt % 2], ("AT", tt % 2)
                DMA("sp", at[:], attnT.rearrange("(c p) t -> p c t", p=128)[:, :, tt * T:(tt + 1) * T], f"mg_at{tt % 2}", [("attnT", tt // 4)], [atk])
                mi, mik = MI[tt % 2], ("MI", tt % 2)
                DMA("sp", mi[:], xview(mab, tt), f"mg_mi{tt % 2}", [("mab", tt)], [mik])
                xt, xk = G["x"].next()
                DMA("sp", xt[:], xview(xsrc, tt), xk, [("xs", tt)], [xk])
                mg, mgk = MGs[tt % 2], ("MG", tt % 2)
                for mc in range(KC):
                    b = bank()
                    for pr in range(2):
                        MM(PB(b), wc[:, pr, mc * 128:(mc + 1) * 128], at[:, pr, :], pr == 0, pr == 1, [atk, "mg_w"], [("ps", b)])
                    bg = bank()
                    for kc in range(KC):
                        MM(PB(bg), wgc[:, kc, mc * 128:(mc + 1) * 128], ht[:, kc, :], kc == 0, kc == KC - 1, [hk, "mg_w"], [("ps", bg)])
                    sg, sgk = SG[mc % 2], ("SG4", mc % 2)
                    ACT(sg[:], PB(bg), AF.Sigmoid, [("ps", bg)], [sgk])
                    tmp, tk = tmp_ring.next()
                    TT("dve", tmp[:], PB(b), sg[:], ALU.mult, [("ps", b), sgk], [tk])
                    TT("pool", mg[:, mc, :], tmp[:], mi[:, mc, :], ALU.add, [tk, mik], [mgk], waw=False)
                st[tt] = dict(xt=xt, xk=xk, mg=mg, mgk=mgk)

            def stW(tt):
                d_ = st[tt]
                pn = PostNorm()
                for mc in range(KC):
                    b = bank()
                    for kc in range(KC):
                        MM(PB(b), wo[:, kc, mc * 128:(mc + 1) * 128], d_["mg"][:, kc, :], kc == 0, kc == KC - 1, [d_["mgk"], "mg_wo"], [("ps", b)])
                    pn.chunk(mc, b)
                d_["pn"] = pn

            def stF(tt):
                d_ = st.pop(tt)
                xt, xk = d_["xt"], d_["xk"]
                d_["pn"].finish(xt, xk, vb + V_MIXPOST)
                DMA("sp", xview(xs, tt), xt[:], xk, [xk], [("xs", tt)])
                h2, h2k = G["h"].next()
                norm_tile(xt, xk, vb + V_MEMPRE, h2, h2k)
                DMA("sp", xview(h2T, tt), h2[:], h2k, [h2k], [("h2T", tt)])

            stA(0)
            stW(0)
            for tt in range(1, NT):
                stA(tt)
                stF(tt - 1)
                stW(tt)
            stF(NT - 1)
            S.barrier()

    def phase_mem(l):
        vb = l * V_PER_LAYER
        with ExitStack() as e5:
            def sb5(name, shape, dt):
                return e5.enter_context(nc.sbuf_tensor(f"{name}_L{l}", list(shape), dt))
            mk_rings(sb5, x=True, h=True, hin=True, ysb=True)
            wq_ = sb5("mm_wq", [128, KC, 512], BF16)
            wkv = sb5("mm_wkv", [128, KC, 1024], BF16)
            wo = sb5("mm_wo", [128, 4, D], BF16)
            mt = sb5("mm_mt", [128, KC, 256], F32)
            mn = sb5("mm_mn", [128, KC, 256], BF16)
            KmT = sb5("mm_KmT", [128, 4, 256], BF16)
            Vm = sb5("mm_Vm", [128, 2, 512], BF16)
            QM = [sb5(f"mm_QM{i}", [128, T], BF16) for i in range(4)]
            PTm = [sb5(f"mm_PT{i}", [128, 2, T], BF16) for i in range(4)]
            DN = [sb5(f"mm_DN{i}", [128, T], F32) for i in range(2)]
            OMs = [sb5(f"mm_OM{i}", [128, 4, T], BF16) for i in range(2)]
            load_w(wkv, w_mkv[l], 1024, "mm_w", "mm_w")
            load_w(wq_, w_mq[l], 512, "mm_w", "mm_w")
            load_w(wo, w_mo[l], D, "mm_wo", "mm_wo")
            DMA("sp", mt[:], memT.rearrange("(c p) t -> p c t", p=128), "mm_mt", [], ["mm_mt"])
            norm_tile(mt, "mm_mt", vb + V_MEMKV, mn, "mm_mn", ncols=256)
            for h in range(4):
                b = bank()
                for kc in range(KC):
                    MM(PB(b, 0, 256), wkv[:, kc, h * 128:(h + 1) * 128], mn[:, kc, 0:256], kc == 0, kc == KC - 1, ["mm_mn", "mm_w"], [("ps", b)])
                CP("act", KmT[:, h, :], PB(b, 0, 256), [("ps", b)], ["KmT"], waw=False)
            for mi_ in range(2):
                b = bank()
                for kc in range(KC):
                    MM(PB(b), mn[:, kc, mi_ * 128:(mi_ + 1) * 128], wkv[:, kc, 512:1024], kc == 0, kc == KC - 1, ["mm_mn", "mm_w"], [("ps", b)])
                CP("act", Vm[:, mi_, :], PB(b), [("ps", b)], ["Vm"], waw=False)
            sc = float(128 ** -0.5)
            st = {}

            def stA(tt):
                ht, hk = G["hin"].next()
                DMA("sp", ht[:], xview(h2T, tt), hk, [("h2T", tt)], [hk])
                xt, xk = G["x"].next()
                DMA("sp", xt[:], xview(xs, tt), xk, [("xs", tt)], [xk])
                om, omk = OMs[tt % 2], ("OM", tt % 2)
                for h in range(4):
                    b = bank()
                    for kc in range(KC):
                        MM(PB(b), wq_[:, kc, h * 128:(h + 1) * 128], ht[:, kc, :], kc == 0, kc == KC - 1, [hk, "mm_w"], [("ps", b)])
                    CP("act", QM[h][:], PB(b), [("ps", b)], [("QM", h)])
                for h in range(4):
                    for mi_ in range(2):
                        bs = bank()
                        MM(PB(bs), KmT[:, h, mi_ * 128:(mi_ + 1) * 128], QM[h][:], True, True, ["KmT", ("QM", h)], [("ps", bs)])
                        ACT(PTm[h][:, mi_, :], PB(bs), AF.Exp, [("ps", bs)], [("PTm", h)], scale=sc, waw=False)
                for h in range(4):
                    pt, ptk = PTm[h], ("PTm", h)
                    bo, bd = bank(), bank()
                    for mi_ in range(2):
                        MM(PB(bo), Vm[:, mi_, h * 128:(h + 1) * 128], pt[:, mi_, :], mi_ == 0, mi_ == 1, ["Vm", ptk], [("ps", bo)])
                    for mi_ in range(2):
                        MM(PB(bd), ones[:], pt[:, mi_, :], mi_ == 0, mi_ == 1, ["cb", ptk], [("ps", bd)])
                    dn, dnk = DN[h % 2], ("DN", h % 2)
                    ACT(dn[:], PB(bd), AF.Ln, [("ps", bd)], [dnk])
                    ACT(dn[:], dn[:], AF.Exp, [dnk], [dnk], scale=-1.0)
                    TT("dve", om[:, h, :], PB(bo), dn[:], ALU.mult, [("ps", bo), dnk], [omk], waw=False)
                st[tt] = dict(xt=xt, xk=xk, om=om, omk=omk)

            def stW(tt):
                d_ = st[tt]
                pn = PostNorm()
                for mc in range(KC):
                    b = bank()
                    for h in range(4):
                        MM(PB(b), wo[:, h, mc * 128:(mc + 1) * 128], d_["om"][:, h, :], h == 0, h == 3, [d_["omk"], "mm_wo"], [("ps", b)])
                    pn.chunk(mc, b)
                d_["pn"] = pn

            def stF(tt):
                d_ = st.pop(tt)
                xt, xk = d_["xt"], d_["xk"]
                d_["pn"].finish(xt, xk, vb + V_MEMPOST)
                DMA("sp", xview(xs, tt), xt[:], xk, [xk], [("xs", tt)])
                h3, h3k = G["h"].next()
                norm_tile(xt, xk, vb + V_FFNPRE, h3, h3k)
                DMA("sp", xview(h3T, tt), h3[:], h3k, [h3k], [("h3T", tt)])

            stA(0)
            stW(0)
            for tt in range(1, NT):
                stA(tt)
                stF(tt - 1)
                stW(tt)
            stF(NT - 1)
            S.barrier()

    def phase_up(l):
        vb = l * V_PER_LAYER
        with ExitStack() as e6:
            def sb6(name, shape, dt):
                return e6.enter_context(nc.sbuf_tensor(f"{name}_L{l}", list(shape), dt))
            mk_rings(sb6, hin=True)
            wu = sb6("up_w", [128, KC, 2 * DFF], BF16)
            H = sb6("up_H", [128, FC, 2], F32)
            UA = [sb6(f"up_UA{i}", [128, 2 + T], F32) for i in range(2)]
            Y1 = [sb6(f"up_Y1{i}", [128, T], F32) for i in range(2)]
            Y2 = [sb6(f"up_Y2{i}", [128, T], F32) for i in range(2)]
            AO = [sb6(f"up_AO{i}", [128, FC, T], BF16) for i in range(2)]
            load_w(wu, w_up[l], 2 * DFF, "up_w", "up_w")
            MS("pool", H[:], 0.0, ["H"])
            cw = vb + V_CONVF
            for tt in range(NT):
                ht, hk = G["hin"].next()
                DMA("sp", ht[:], xview(h3T, tt), hk, [("h3T", tt)], [hk])
                ao, aok = AO[tt % 2], ("AO", tt % 2)
                for c in range(FC):
                    ba, bb_ = bank(), bank()
                    for kc in range(KC):
                        MM(PB(ba), wu[:, kc, c * 128:(c + 1) * 128], ht[:, kc, :], kc == 0, kc == KC - 1, [hk, "up_w"], [("ps", ba)])
                    for kc in range(KC):
                        MM(PB(bb_), wu[:, kc, DFF + c * 128:DFF + (c + 1) * 128], ht[:, kc, :], kc == 0, kc == KC - 1, [hk, "up_w"], [("ps", bb_)])
                    i = c % 2
                    ua, uak = UA[i], ("UA", i)
                    y1, y1k = Y1[i], ("Y1", i)
                    y2, y2k = Y2[i], ("Y2", i)
                    CP("pool", ua[:, 0:2], H[:, c, :], ["H"], [uak])
                    CP("act", ua[:, 2:2 + T], PB(ba), [("ps", ba)], [uak], waw=False)
                    CP("pool", H[:, c, :], ua[:, T:T + 2], [uak], ["H"])
                    ACT(y1[:], ua[:, 2:2 + T], AF.Copy, [uak, "vecs"], [y1k], scale=vecs[:, cw + 2 * FC + c:cw + 2 * FC + c + 1])
                    STT("dve", y2[:], ua[:, 1:1 + T], vecs[:, cw + 1 * FC + c:cw + 1 * FC + c + 1], y1[:], ALU.mult, ALU.add, [uak, y1k, "vecs"], [y2k])
                    STT("dve", y1[:], ua[:, 0:T], vecs[:, cw + 0 * FC + c:cw + 0 * FC + c + 1], y2[:], ALU.mult, ALU.add, [uak, y2k, "vecs"], [y1k])
                    ACT(y2[:], y1[:], AF.Silu, [y1k], [y2k])
                    TT("dve", ao[:, c, :], PB(bb_), y2[:], ALU.mult, [("ps", bb_), y2k], [aok], waw=False)
                DMA("sp", actT.rearrange("(c p) t -> p c t", p=128)[:, :, tt * T:(tt + 1) * T], ao[:], f"up_ao{tt % 2}", [aok], [("actT", tt)])
            S.barrier()

    def phase_down(l, last):
        vb = l * V_PER_LAYER
        with ExitStack() as e7:
            def sb7(name, shape, dt):
                return e7.enter_context(nc.sbuf_tensor(f"{name}_L{l}", list(shape), dt))
            mk_rings(sb7, x=True, h=True, ysb=True)
            wd = sb7("dn_w", [128, FC, D], BF16)
            AI = [sb7(f"dn_AI{i}", [128, FC, T], BF16) for i in range(2)]
            for q4 in range(4):
                DMA("pool", wd[:, :, q4 * 256:(q4 + 1) * 256], w_down[l].rearrange("(k p) n -> p k n", p=128)[:, :, q4 * 256:(q4 + 1) * 256],
                    "dn_w", [], [("dn_w", q4)])
            st = {}

            def stA(tt):
                ai, aik = AI[tt % 2], ("AI", tt % 2)
                DMA("sp", ai[:], actT.rearrange("(c p) t -> p c t", p=128)[:, :, tt * T:(tt + 1) * T], f"dn_ai{tt % 2}", [("actT", tt)], [aik])
                xt, xk = G["x"].next()
                DMA("sp", xt[:], xview(xs, tt), xk, [("xs", tt)], [xk])
                st[tt] = dict(xt=xt, xk=xk, ai=ai, aik=aik, pn=PostNorm())

            def stW(tt, mcs):
                d_ = st[tt]
                for mc in mcs:
                    b = bank()
                    for c in range(FC):
                        MM(PB(b), wd[:, c, mc * 128:(mc + 1) * 128], d_["ai"][:, c, :], c == 0, c == FC - 1, [d_["aik"], ("dn_w", mc // 2)], [("ps", b)])
                    d_["pn"].chunk(mc, b)

            def stF(tt):
                d_ = st.pop(tt)
                xt, xk = d_["xt"], d_["xk"]
                d_["pn"].finish(xt, xk, vb + V_FFNPOST)
                if last:
                    DMA("sp", xview(outT, tt), xt[:], xk, [xk], [("outT", tt)])
                else:
                    DMA("sp", xview(xs, tt), xt[:], xk, [xk], [("xs", tt)])
                    h1, h1k = G["h"].next()
                    norm_tile(xt, xk, (l + 1) * V_PER_LAYER + V_MIXPRE, h1, h1k)
                    DMA("sp", xview(h1T, tt), h1[:], h1k, [h1k], [("h1T", tt)])

            stA(0)
            stW(0, range(KC))
            for tt in range(1, NT):
                stA(tt)
                stW(tt, range(0, 4))
                stF(tt - 1)
                stW(tt, range(4, KC))
            stF(NT - 1)
            S.barrier()


    return dict(nc=nc, S=S, es=es, phases=dict(norm0=phase_norm0, attn=phase_attn, ab=phase_ab, merge=phase_merge,
                                               mem=phase_mem, up=phase_up, down=phase_down))


def build_full(n_layers=NL, dbg=False, stop=None, opts=()):
    P = build_program(n_layers, dbg, opts)
    ph = P["phases"]
    seq = []
    if "nonorm0" not in opts:
        ph["norm0"](0)
    done = stop is not None and stop[1] == "norm0"
    for l in range(n_layers):
        if done:
            break
        for name in ("attn", "ab", "merge", "mem", "up", "down"):
            if name == "down":
                ph[name](l, l == n_layers - 1)
            else:
                ph[name](l)
            if stop is not None and stop == (l, name):
                done = True
                break
        if done:
            break
    nsem = P["S"].emit()
    P["es"].close()
    return P["nc"], P["S"], nsem


def host_inputs(inputs):
    import ml_dtypes
    f32 = np.float32
    perm = w_in_perm()
    wip = np.asarray(inputs["w_in"], f32)[:, :, perm]
    wqkv = np.zeros((NL, 3, D, 1280), f32)
    for g in range(3):
        base = OFF_QKV + 768 * g
        wqkv[:, g, :, 0:256] = wip[:, :, base:base + 256]
        for hh in range(4):
            par = hh % 2
            wqkv[:, g, :, 256 + hh * 128 + 64 * par:256 + hh * 128 + 64 * par + 64] = wip[:, :, base + 256 + 64 * hh:base + 256 + 64 * hh + 64]
            wqkv[:, g, :, 768 + hh * 128 + 64 * par:768 + hh * 128 + 64 * par + 64] = wip[:, :, base + 512 + 64 * hh:base + 512 + 64 * hh + 64]
    shared = {
        "w_in": np.ascontiguousarray(wip[:, :, 0:OFF_QKV]),
        "w_qkv": wqkv,
        "pool_w": np.ascontiguousarray(np.asarray(inputs["pool_w"], f32)),
    }
    for k in ("w_branch_a", "w_branch_b", "w_branch_c", "w_out", "w_mq", "w_mkv", "w_mo", "w_up", "w_down"):
        shared[k] = np.ascontiguousarray(np.asarray(inputs[k], f32))
    vecs = np.zeros((128, NL * V_PER_LAYER), f32)
    for l in range(NL):
        vb = l * V_PER_LAYER
        for name, off in (("norm_mix_pre", V_MIXPRE), ("norm_mix_post", V_MIXPOST), ("norm_mem_pre", V_MEMPRE),
                          ("norm_mem_post", V_MEMPOST), ("norm_memkv", V_MEMKV), ("norm_ffn_pre", V_FFNPRE),
                          ("norm_ffn_post", V_FFNPOST)):
            vecs[:, vb + off:vb + off + 8] = np.asarray(inputs[name], f32)[l].reshape(8, 128).T
        vecs[:, vb + V_CONVB:vb + V_CONVB + 9] = np.asarray(inputs["conv_b_w"], f32)[l].reshape(3, 3, 128).transpose(2, 0, 1).reshape(128, 9)
        vecs[:, vb + V_CONVF:vb + V_CONVF + 66] = np.asarray(inputs["conv_ffn_w"], f32)[l].reshape(3, FC, 128).transpose(2, 0, 1).reshape(128, 66)
        vecs[0:96, vb + V_PSCALE:vb + V_PSCALE + 4] = np.asarray(inputs["pool_scale"], f32)[l].reshape(4, 96).T
    consts = np.zeros((128, NCONST), f32)
    consts[:, C_ID:C_ID + 128] = np.eye(128, dtype=f32)
    k = np.arange(128)[:, None]
    q = np.arange(128)[None, :]
    consts[:, C_MASK:C_MASK + 128] = np.where(k >= q, 0.0, -30000.0)
    consts[:, C_MASK + 128:C_MASK + 256] = np.where(k <= q, 0.0, -30000.0)
    consts[:, C_INV:C_INV + 8] = (f32(500000.0) ** (-np.arange(0, 16, 2, dtype=f32) / f32(16)))[None, :]
    for g, w in enumerate((2, 4, 8, 16)):
        consts[:, C_RC + 16 * g:C_RC + 16 * g + 16] = (1.0 / np.minimum(np.arange(16) + 1, w))[None, :]
    shared["vecs"] = vecs
    shared["consts"] = consts
    x = np.asarray(inputs["x"], f32)
    mem = np.asarray(inputs["mem"], f32)
    posn = np.asarray(inputs["positions"]).astype(np.int32)
    in_maps = []
    for b in range(8):
        m = dict(shared)
        m["xT"] = np.ascontiguousarray(x[b].T)
        m["memT"] = np.ascontiguousarray(mem[b].T)
        m["pos"] = np.ascontiguousarray(posn[b].reshape(32, 128))
        in_maps.append(m)
    return in_maps


_CACHE = {}


def kernel(**inputs):
    in_maps = host_inputs(inputs)
    if "nc" not in _CACHE:
        _CACHE["nc"] = build_full()[0]
    nc = _CACHE["nc"]
    res = run_bass_kernel_spmd(nc, in_maps, core_ids=list(range(8)))
    out = np.stack([np.ascontiguousarray(res.results[b]["outT"].T) for b in range(8)], axis=0)
    return out.astype(np.float32)
```

```python
import numpy as np
import concourse.bass as bass
import concourse.mybir as mybir

F32 = mybir.dt.float32
BF16 = mybir.dt.bfloat16
I32 = mybir.dt.int32
ALU = mybir.AluOpType
AF = mybir.ActivationFunctionType

EPOCH = 30000


class Sched:
    ENGS = ("pe", "act", "dve", "pool", "sp")

    def __init__(self, nc):
        self.nc = nc
        self.stream = {e: [] for e in self.ENGS}
        self.cnt = {e: 0 for e in self.ENGS}
        self.known = {e: {} for e in self.ENGS}
        self.res = {}
        self.chan_cnt = {}
        self.n_wait = 0

    def _collect(self, eng, reads, writes, waw):
        waits = {}

        def need(k, v):
            if k == ("eng", "pe") and eng == "pe":
                return
            if self.known[eng].get(k, 0) >= v:
                return
            if waits.get(k, 0) < v:
                waits[k] = v

        for r in reads:
            st = self.res.get(r)
            if st:
                for k, v in st["w"].items():
                    need(k, v)
        for w in writes:
            st = self.res.get(w)
            if st:
                for k, v in st["r"].items():
                    need(k, v)
                if waw:
                    for k, v in st["w"].items():
                        need(k, v)
        for k, v in waits.items():
            self.known[eng][k] = v
        return sorted(waits.items(), key=lambda kv: str(kv[0]))

    def _commit(self, ev, reads, writes, waw):
        k, v = ev
        for r in reads:
            st = self.res.setdefault(r, {"w": {}, "r": {}})
            if st["r"].get(k, 0) < v:
                st["r"][k] = v
        for w in writes:
            st = self.res.setdefault(w, {"w": {}, "r": {}})
            if st["r"] or waw:
                st["w"] = {}
            st["r"] = {}
            if st["w"].get(k, 0) < v:
                st["w"][k] = v

    def op(self, eng, fn, reads=(), writes=(), waw=True):
        waits = self._collect(eng, reads, writes, waw)
        self.cnt[eng] += 1
        ev = (("eng", eng), self.cnt[eng])
        self._commit(ev, reads, writes, waw)
        self.stream[eng].append((waits, fn, ev))
        self.n_wait += len(waits)

    def dma(self, eng, fn, chan, reads=(), writes=(), waw=False):
        waits = self._collect(eng, reads, writes, waw)
        self.chan_cnt[chan] = self.chan_cnt.get(chan, 0) + 16
        ev = (("chan", chan), self.chan_cnt[chan])
        self._commit(ev, reads, writes, waw)
        self.stream[eng].append((waits, fn, ev))
        self.n_wait += len(waits)

    def barrier(self, engs=None):
        engs = engs or self.ENGS
        for e in engs:
            waits = {}
            for e2 in self.ENGS:
                if e2 != e and self.cnt[e2] > 0:
                    k = ("eng", e2)
                    if self.known[e].get(k, 0) < self.cnt[e2]:
                        waits[k] = self.cnt[e2]
            for c, v in self.chan_cnt.items():
                k = ("chan", c)
                if self.known[e].get(k, 0) < v:
                    waits[k] = v
            for k, v in waits.items():
                self.known[e][k] = v
            if waits:
                self.stream[e].append((sorted(waits.items(), key=lambda kv: str(kv[0])), None, None))

    def emit(self):
        nc = self.nc
        sems = {}

        def sem_of(k, v):
            if k[0] == "eng":
                ep = (v - 1) // EPOCH
                key = (k, ep)
                val = (v - 1) % EPOCH + 1
            else:
                key = (k, 0)
                val = v
            if key not in sems:
                sems[key] = nc.alloc_semaphore(name=f"s{len(sems)}")
            return sems[key], val

        self.barrier(engs=("sp",))
        for e in self.ENGS:
            for waits, fn, ev in self.stream[e]:
                for k, v in waits:
                    sem_of(k, v)
                if ev is not None:
                    sem_of(*ev)
        eng_map = {"pe": "tensor", "act": "scalar", "dve": "vector", "pool": "gpsimd", "sp": "sync"}
        with nc.Block() as block:
            for e in self.ENGS:
                if not self.stream[e]:
                    continue
                deco = getattr(block, eng_map[e])

                def body(h, e=e):
                    for waits, fn, ev in self.stream[e]:
                        for k, v in waits:
                            s, val = sem_of(k, v)
                            h.wait_ge(s, val)
                        if fn is None:
                            continue
                        ins = fn(h)
                        s, _ = sem_of(*ev)
                        ins.then_inc(s, 16 if ev[0][0] == "chan" else 1)

                deco(body)
        return len(sems)


def sap(t, F, poff, npart, off, dims):
    return bass.AP(t, poff * F + off, [[F, npart]] + [list(d) for d in dims])

from concourse.bass_utils import run_bass_kernel_spmd
from contextlib import ExitStack

D = 1024
SEQ = 4096
T = 512
NT = SEQ // T
KC = 8
DFF = 2816
FC = DFF // 128
NL = 2
EPS = 1e-6
DILS = (1, 4, 16)
OFF_A = 0
OFF_BX = 384
OFF_BB = 768
OFF_BC = 1152
OFF_GA = 1536
OFF_GB = 2560
OFF_GC = 3584
OFF_QKV = 4608
NAB = 3584
V_MIXPRE, V_MIXPOST, V_MEMPRE, V_MEMPOST, V_MEMKV, V_FFNPRE, V_FFNPOST = 0, 8, 16, 24, 32, 40, 48
V_CONVB = 56
V_CONVF = 65
V_PSCALE = 131
V_PER_LAYER = 135
C_ID = 0
C_MASK = 128
C_INV = 384
C_RC = 392
NCONST = 456


def w_in_perm():
    idx = []
    a = 0
    idx += list(range(0, 384))
    idx += list(range(384, 384 + 1152))
    g0 = 384 + 1152 + 3 * 768
    idx += list(range(g0, g0 + 3072))
    q0 = 384 + 1152
    for g in range(3):
        for part in range(3):
            s = q0 + part * 768 + g * 256
            idx += list(range(s, s + 256))
    return np.array(idx, dtype=np.int64)


def build_program(n_layers=NL, dbg=False, opts=()):
    nc = bass.Bass("TRN2", target_bir_lowering=False)
    S = Sched(nc)

    def din(name, shape, dt=F32):
        return nc.dram_tensor(name, list(shape), dt, kind="ExternalInput").ap()

    kind_s = "ExternalOutput" if dbg else "Internal"

    def dscr(name, shape, dt):
        return nc.dram_tensor(name, list(shape), dt, kind=kind_s).ap()

    xT = din("xT", [D, SEQ])
    memT = din("memT", [D, 256])
    pos = din("pos", [32, 128], I32)
    w_in = din("w_in", [NL, D, OFF_QKV])
    w_qkv = din("w_qkv", [NL, 3, D, 1280])
    pool_w = din("pool_w", [NL, 4, 96, 96])
    w_a = din("w_branch_a", [NL, 384, D])
    w_b = din("w_branch_b", [NL, 384, D])
    w_c = din("w_branch_c", [NL, 256, D])
    w_out = din("w_out", [NL, D, D])
    w_mq = din("w_mq", [NL, D, 512])
    w_mkv = din("w_mkv", [NL, D, 1024])
    w_mo = din("w_mo", [NL, 512, D])
    w_up = din("w_up", [NL, D, 2 * DFF])
    w_down = din("w_down", [NL, DFF, D])
    vecs_d = din("vecs", [128, NL * V_PER_LAYER])
    consts_d = din("consts", [128, NCONST])
    outT = nc.dram_tensor("outT", [D, SEQ], F32, kind="ExternalOutput").ap()

    h1T = dscr("h1T", [D, SEQ], BF16)
    h2T = dscr("h2T", [D, SEQ], BF16)
    h3T = dscr("h3T", [D, SEQ], BF16)
    attnT = dscr("attnT", [256, SEQ], BF16)
    mab = dscr("mab", [D, SEQ], F32)
    xs = dscr("xs", [D, SEQ], F32)
    actT = dscr("actT", [DFF, SEQ], BF16)
    rope_d = dscr("rope_tab", [SEQ, 16], F32)

    es = ExitStack()

    def sb(name, shape, dt):
        return es.enter_context(nc.sbuf_tensor("s_" + name, list(shape), dt))

    ps = es.enter_context(nc.psum_tensor("ps", [128, 4096], F32))

    def MM(out, lhsT, rhs, start, stop, R, W):
        S.op("pe", lambda h: h.matmul(out, lhsT=lhsT, rhs=rhs, start=start, stop=stop), reads=R, writes=W)

    def ACT(out, in_, func, R, W, scale=1.0, bias=None, waw=True):
        if bias is None:
            S.op("act", lambda h: h.activation(out=out, in_=in_, func=func, scale=scale), reads=R, writes=W, waw=waw)
        else:
            S.op("act", lambda h: h.activation(out=out, in_=in_, func=func, scale=scale, bias=bias), reads=R, writes=W, waw=waw)

    def TT(eng, out, in0, in1, op, R, W, waw=True):
        S.op(eng, lambda h: h.tensor_tensor(out=out, in0=in0, in1=in1, op=op), reads=R, writes=W, waw=waw)

    def TS(eng, out, in0, s1, s2, op0, op1, R, W, waw=True):
        if s2 is None:
            S.op(eng, lambda h: h.tensor_scalar(out=out, in0=in0, scalar1=s1, scalar2=None, op0=op0), reads=R, writes=W, waw=waw)
        else:
            S.op(eng, lambda h: h.tensor_scalar(out=out, in0=in0, scalar1=s1, scalar2=s2, op0=op0, op1=op1), reads=R, writes=W, waw=waw)

    def STT(eng, out, in0, scalar, in1, op0, op1, R, W, waw=True):
        S.op(eng, lambda h: h.scalar_tensor_tensor(out=out, in0=in0, scalar=scalar, in1=in1, op0=op0, op1=op1), reads=R, writes=W, waw=waw)

    def CP(eng, out, in_, R, W, waw=True):
        if eng == "act":
            ACT(out, in_, AF.Copy, R, W, waw=waw)
        else:
            S.op(eng, lambda h: h.tensor_copy(out=out, in_=in_), reads=R, writes=W, waw=waw)

    def MS(eng, ap, val, W):
        S.op(eng, lambda h: h.memset(ap, val), writes=W)

    def DMA(eng, out, in_, chan, R, W):
        S.dma(eng, lambda h: h.dma_start(out=out, in_=in_), chan, reads=R, writes=W)

    uniq = [0]

    class Ring:
        def __init__(self, name, shape, dt, n, alloc=None):
            alloc = alloc or sb
            uniq[0] += 1
            self.name = name
            self.t = [alloc(f"{name}_{uniq[0]}_{i}", shape, dt) for i in range(n)]
            self.i = 0

        def next(self):
            k = self.i % len(self.t)
            self.i += 1
            return self.t[k], (self.name, k)

    psp = [0]

    def bank(n=1):
        if n == 2 and psp[0] % 2 == 1:
            psp[0] += 1
        p = psp[0] % 6
        psp[0] += n
        return p

    def PB(b, lo=0, hi=512):
        return ps[:, b * 512 + lo: b * 512 + hi]

    vecs = sb("vecs", [128, NL * V_PER_LAYER], F32)
    consts = sb("consts", [128, NCONST], F32)
    ident = sb("ident", [128, 128], BF16)
    mask2 = sb("mask2", [128, 256], BF16)
    onesm = sb("onesm", [128, 128], BF16)
    ones = sb("ones", [128, 128], BF16)
    onesP = sb("onesP", [128, 2, 128], BF16)
    DMA("sp", vecs[:], vecs_d, "vecs", [], ["vecs"])
    DMA("sp", consts[:], consts_d, "consts", [], ["consts"])
    CP("dve", ident[:], consts[:, C_ID:C_ID + 128], ["consts"], ["cb"])
    CP("dve", mask2[:], consts[:, C_MASK:C_MASK + 256], ["consts"], ["cb"])
    MS("pool", onesm[:], 1.0 / 1024.0, ["cb"])
    MS("pool", ones[:], 1.0, ["cb"])
    MS("pool", onesP[:], 0.0, ["cb"])
    MS("pool", onesP[:, 0, 0:64], 1.0, ["cb"])
    MS("pool", onesP[:, 1, 64:128], 1.0, ["cb"])

    sqring = Ring("sq", [128, T], BF16, 3)
    rstdring = Ring("rstd", [128, T], F32, 2)
    G = {}

    def mk_rings(alloc, x=False, h=False, hin=False, ysb=False):
        if x:
            G["x"] = Ring("xt", [128, KC, T], F32, 2, alloc)
        if h:
            G["h"] = Ring("ht", [128, KC, T], BF16, 2, alloc)
        if hin:
            G["hin"] = Ring("hin", [128, KC, T], BF16, 2, alloc)
        if ysb:
            G["ysb"] = Ring("ysb", [128, KC, T], F32, 2, alloc)
    tmp_ring = Ring("tmpf", [128, T], F32, 3)

    def xview(d, tt):
        return d.rearrange("(c p) t -> p c t", p=128)[:, :, tt * T:(tt + 1) * T]

    def rstd_from_bank(b):
        rstd, rk = rstdring.next()
        ACT(rstd[:], PB(b), AF.Ln, [("ps", b)], [rk], bias=EPS)
        ACT(rstd[:], rstd[:], AF.Exp, [rk], [rk], scale=-0.5)
        return rstd, rk

    def norm_tile(xt, xk, gcol, ht, hk, ncols=T):
        b = bank()
        for c in range(KC):
            sq, sqk = sqring.next()
            ACT(sq[:, 0:ncols], xt[:, c, 0:ncols], AF.Square, [xk], [sqk])
            MM(PB(b, 0, ncols), onesm[:], sq[:, 0:ncols], c == 0, c == KC - 1, [sqk, "cb"], [("ps", b)])
        rstd, rk = rstdring.next()
        ACT(rstd[:, 0:ncols], PB(b, 0, ncols), AF.Ln, [("ps", b)], [rk], bias=EPS)
        ACT(rstd[:, 0:ncols], rstd[:, 0:ncols], AF.Exp, [rk], [rk], scale=-0.5)
        for c in range(KC):
            eng = "dve"
            STT(eng, ht[:, c, 0:ncols], xt[:, c, 0:ncols], vecs[:, gcol + c:gcol + c + 1], rstd[:, 0:ncols],
                ALU.mult, ALU.mult, [xk, rk, "vecs"], [hk], waw=False)

    pncnt = [0]

    class PostNorm:
        def __init__(self):
            self.ysb, self.yk = G["ysb"].next()
            pncnt[0] += 1
            self.b = 6 + pncnt[0] % 2

        def chunk(self, mc, b):
            CP("act", self.ysb[:, mc, :], PB(b), [("ps", b)], [self.yk], waw=False)
            sq, sqk = sqring.next()
            TT("dve", sq[:], PB(b), self.ysb[:, mc, :], ALU.mult, [("ps", b), self.yk], [sqk])
            MM(PB(self.b), onesm[:], sq[:], mc == 0, mc == KC - 1, [sqk, "cb"], [("ps", self.b)])

        def finish(self, xt, xk, gcol):
            rstd, rk = rstd_from_bank(self.b)
            for c in range(KC):
                tmp, tk = tmp_ring.next()
                TT("dve", tmp[:], self.ysb[:, c, :], rstd[:], ALU.mult, [self.yk, rk], [tk])
                STT("dve", xt[:, c, :], tmp[:], vecs[:, gcol + c:gcol + c + 1], xt[:, c, :], ALU.mult, ALU.add,
                    [tk, xk, "vecs"], [xk], waw=False)

    def load_w(dst, src2d, ncols, chan, key, rows=128):
        v = src2d.rearrange("(k p) n -> p k n", p=rows)
        for cb in range(0, ncols, 2048):
            w = min(2048, ncols - cb)
            DMA("pool", dst[:, :, cb:cb + w], v[:, :, cb:cb + w], chan, [], [key])

    with ExitStack() as es0:
      if 'norope' not in opts:
          def sb0(name, shape, dt):
              return es0.enter_context(nc.sbuf_tensor(name, list(shape), dt))
          pi_ = sb0("rp_pi", [32, 128], I32)
          pf = sb0("rp_pf", [32, 128], F32)
          ang = sb0("rp_ang", [32, 128, 16], F32)
          a2 = sb0("rp_a2", [32, 128, 16], F32)
          ki = sb0("rp_ki", [32, 128, 16], I32)
          tab = sb0("rp_tab", [32, 128, 16], F32)
          DMA("sp", pi_[:], pos, "rp", [], ["rp_pi"])
          CP("dve", pf[:], pi_[:], ["rp_pi"], ["rp_pf"])
          inv_b = consts[0:32, C_INV:C_INV + 8].unsqueeze(1).to_broadcast([32, 128, 8])
          pf_b = pf[:].unsqueeze(2).to_broadcast([32, 128, 8])
          TT("dve", ang[:, :, 0:8], pf_b, inv_b, ALU.mult, ["rp_pf", "consts"], ["rp_ang"])
          TS("dve", ang[:, :, 8:16], ang[:, :, 0:8], float(np.pi / 2), None, ALU.add, None, ["rp_ang"], ["rp_ang"])
          TS("dve", a2[:], ang[:], float(1.0 / (2 * np.pi)), None, ALU.mult, None, ["rp_ang"], ["rp_a2"])
          CP("dve", ki[:], a2[:], ["rp_a2"], ["rp_ki"])
          CP("dve", a2[:], ki[:], ["rp_ki"], ["rp_a2"])
          STT("dve", ang[:], a2[:], float(-2 * np.pi), ang[:], ALU.mult, ALU.add, ["rp_a2", "rp_ang"], ["rp_ang"])
          TS("dve", a2[:], ang[:], float(np.pi), float(-2 * np.pi), ALU.is_gt, ALU.mult, ["rp_ang"], ["rp_a2"])
          TT("dve", ang[:], ang[:], a2[:], ALU.add, ["rp_ang", "rp_a2"], ["rp_ang"])
          TS("dve", a2[:], ang[:], float(-np.pi), float(2 * np.pi), ALU.is_lt, ALU.mult, ["rp_ang"], ["rp_a2"])
          TT("dve", ang[:], ang[:], a2[:], ALU.add, ["rp_ang", "rp_a2"], ["rp_ang"])
          ACT(tab[:, :, 0:8], ang[:, :, 8:16], AF.Sin, ["rp_ang"], ["rp_tab"])
          ACT(tab[:, :, 8:16], ang[:, :, 0:8], AF.Sin, ["rp_ang"], ["rp_tab"], waw=False)
          DMA("sp", rope_d.rearrange("(b i) f -> b i f", i=128), tab[:], "rp", ["rp_tab"], ["rope_d"])
          S.barrier()

    def phase_norm0(l):
      with ExitStack() as e1:
        mk_rings(lambda n, sh, dt: e1.enter_context(nc.sbuf_tensor(n, list(sh), dt)), x=True, h=True)
        for tt in range(NT):
            xt, xk = G["x"].next()
            DMA("sp", xt[:], xview(xT, tt), xk, [], [xk])
            ht, hk = G["h"].next()
            norm_tile(xt, xk, l * V_PER_LAYER + V_MIXPRE, ht, hk)
            DMA("pool", xview(h1T, tt), ht[:], hk, [hk], [("h1T", tt)])
        S.barrier()

    def phase_attn(l):
        with ExitStack() as e2:
            def sb2(name, shape, dt):
                return e2.enter_context(nc.sbuf_tensor(f"{name}_L{l}", list(shape), dt))
            hT = sb2("at_hT", [128, KC, SEQ], BF16)
            acc = sb2("at_acc", [128, 4, 2048], F32)
            attn_sb = sb2("at_out", [128, 2, 2048], BF16)
            wq = [sb2(f"at_w{i}", [128, KC, 1280], BF16) for i in range(2)]
            cs = [sb2(f"at_cs{g}", [128, 32, 16], F32) for g in range(3)]
            QK = [sb2(f"at_qk{i}", [128, 768], BF16) for i in range(2)]
            QF = [sb2(f"at_qf{i}", [128, 768], F32) for i in range(2)]
            ODT = [sb2(f"at_od{i}", [128, 512], F32) for i in range(2)]
            Vb = [sb2(f"at_v{i}", [128, 4, 128], BF16) for i in range(2)]
            QT = [sb2(f"at_qt{i}", [128, 2, 128], BF16) for i in range(2)]
            KTp = [sb2(f"at_kt{i}", [128, 4, 128], BF16) for i in range(2)]
            PT = [sb2(f"at_pt{i}", [128, 4, 256], BF16) for i in range(2)]
            rt = [sb2(f"at_rt{i}", [128, 4, 8, 8], F32) for i in range(2)]
            for tt in range(NT):
                DMA("sp", hT[:, :, tt * T:(tt + 1) * T], xview(h1T, tt), "at_hT", [("h1T", tt)], ["at_hT"])
            for g in range(3):
                d = DILS[g]
                nb = 32 // d
                for r in range(d):
                    for j in range(nb):
                        src = bass.AP(rope_d.tensor, (r + d * 128 * j) * 16, [[16 * d, 128], [1, 16]])
                        DMA("sp", cs[g][:, r * nb + j, :], src, f"at_cs{g}", ["rope_d"], [("cs", g)])
            wl = [0]
            cnt = {"qk": 0, "pt": 0, "rt": 0, "od": 0}

            def proj_block(g, r, j, need_q, wt, wk):
                d = DILS[g]
                nb = 32 // d
                slot = j % 2
                start = r + d * 128 * j
                sl = slice(start, start + 127 * d + 1, d)
                bq = bank() if need_q else None
                bk = bank()
                bv = bank()
                if need_q:
                    for kc in range(KC):
                        MM(PB(bq, 0, 256), hT[:, kc, sl], wt[:, kc, 0:256], kc == 0, kc == KC - 1, ["at_hT", wk], [("ps", bq)])
                for kc in range(KC):
                    MM(PB(bk), hT[:, kc, sl], wt[:, kc, 256:768], kc == 0, kc == KC - 1, ["at_hT", wk], [("ps", bk)])
                for kc in range(KC):
                    MM(PB(bv), hT[:, kc, sl], wt[:, kc, 768:1280], kc == 0, kc == KC - 1, ["at_hT", wk], [("ps", bv)])
                qi = cnt["qk"] % 2
                cnt["qk"] += 1
                qk, qkk = QK[qi], ("QK", qi)
                qf, qfk = QF[qi], ("QF", qi)
                if need_q:
                    CP("act", qf[:, 0:256], PB(bq, 0, 256), [("ps", bq)], [qfk])
                CP("act", qf[:, 256:768], PB(bk), [("ps", bk)], [qfk], waw=not need_q)
                lo = 0 if need_q else 256
                CP("act", qk[:, lo:768], qf[:, lo:768], [qfk], [qkk])
                CP("dve", Vb[slot][:].rearrange("p a b -> p (a b)"), PB(bv), [("ps", bv)], [("Vb", slot)])
                if 'pb1' in opts:
                    return
                blk = r * nb + j
                csk = ("cs", g)
                views = []
                if need_q:
                    views.append((qf[:, 0:256].rearrange("p (h e) -> p h e", e=64), qk[:, 0:256].rearrange("p (h e) -> p h e", e=64), [128, 4, 8], 0))
                kfv = bass.AP(qf, 256, [[768, 128], [256, 2], [192, 2], [1, 64]])
                kbv = bass.AP(qk, 256, [[768, 128], [256, 2], [192, 2], [1, 64]])
                views.append((kfv, kbv, [128, 2, 2, 8], 1))
                for (fv, bv_, shp, which) in views:
                    ri = cnt["rt"] % 2
                    cnt["rt"] += 1
                    rtt, rtk = rt[ri], ("rt", ri)
                    if which == 0:
                        cosb = cs[g][:, blk, 0:8].unsqueeze(1).to_broadcast(shp)
                        sinb = cs[g][:, blk, 8:16].unsqueeze(1).to_broadcast(shp)
                        u1, u2 = fv[:, :, 0:8], fv[:, :, 8:16]
                        o1, o2 = bv_[:, :, 0:8], bv_[:, :, 8:16]
                        tv = [rtt[:, k, 0:4, :] for k in range(4)]
                    else:
                        cosb = cs[g][:, blk, 0:8].unsqueeze(1).unsqueeze(1).to_broadcast(shp)
                        sinb = cs[g][:, blk, 8:16].unsqueeze(1).unsqueeze(1).to_broadcast(shp)
                        u1, u2 = fv[:, :, :, 0:8], fv[:, :, :, 8:16]
                        o1, o2 = bv_[:, :, :, 0:8], bv_[:, :, :, 8:16]
                        tv = [rtt[:, k, 0:4, :].rearrange("p (a b) e -> p a b e", a=2) for k in range(4)]
                    TT("dve", tv[0], u1, cosb, ALU.mult, [qfk, csk], [rtk])
                    TT("dve", tv[1], u2, sinb, ALU.mult, [qfk, csk], [rtk], waw=False)
                    TT("dve", tv[2], u2, cosb, ALU.mult, [qfk, csk], [rtk], waw=False)
                    TT("dve", tv[3], u1, sinb, ALU.mult, [qfk, csk], [rtk], waw=False)
                    TT("pool", o1, tv[0], tv[1], ALU.subtract, [rtk], [qkk])
                    TT("pool", o2, tv[2], tv[3], ALU.add, [rtk], [qkk])
                if 'pb2' in opts:
                    return
                if need_q:
                    bt = bank()
                    for ch in range(2):
                        MM(PB(bt, ch * 128, ch * 128 + 128), qk[:, ch * 128:(ch + 1) * 128], ident[:], True, True, [qkk, "cb"], [("ps", bt)])
                    CP("act", QT[slot][:].rearrange("p c t -> p (c t)"), PB(bt, 0, 256), [("ps", bt)], [("QT", slot)])
                bt2 = bank()
                for hh in range(4):
                    MM(PB(bt2, hh * 128, hh * 128 + 128), qk[:, 256 + hh * 128:256 + (hh + 1) * 128], ident[:], True, True, [qkk, "cb"], [("ps", bt2)])
                CP("dve", KTp[slot][:].rearrange("p a b -> p (a b)"), PB(bt2), [("ps", bt2)], [("KTp", slot)])

            def attend(g, r, j, half, first_group):
                d = DILS[g]
                sc, sp_ = j % 2, (j - 1) % 2
                b2 = bank(2)
                for hh in range(4):
                    ch = hh // 2
                    bb = b2 + hh // 2
                    c0 = (hh % 2) * 256
                    if j > 0:
                        MM(PB(bb, c0, c0 + 256), ident[:], mask2[:, 0:256], True, False, ["cb"], [("ps", bb)])
                        MM(PB(bb, c0, c0 + 128), KTp[sp_][:, hh, :], QT[sc][:, ch, :], False, False, [("KTp", sp_), ("QT", sc)], [("ps", bb)])
                    else:
                        MM(PB(bb, c0 + 128, c0 + 256), ident[:], mask2[:, 128:256], True, False, ["cb"], [("ps", bb)])
                    MM(PB(bb, c0 + 128, c0 + 256), KTp[sc][:, hh, :], QT[sc][:, ch, :], False, True, [("KTp", sc), ("QT", sc)], [("ps", bb)])
                pi = cnt["pt"] % 2
                cnt["pt"] += 1
                pt, ptk = PT[pi], ("PT", pi)
                for hb in range(2):
                    bb = b2 + hb
                    if j > 0:
                        ACT(pt[:, 2 * hb:2 * hb + 2, :].rearrange("p a b -> p (a b)"), PB(bb), AF.Exp, [("ps", bb)], [ptk], scale=0.125, waw=False)
                    else:
                        for a in range(2):
                            ACT(pt[:, 2 * hb + a, 128:256], PB(bb, a * 256 + 128, a * 256 + 256), AF.Exp,
                                [("ps", bb)], [ptk], scale=0.125, waw=False)
                b3 = bank()
                kbs = [0, 1] if j > 0 else [1]
                for od in range(2):
                    for pair in range(2):
                        mats = [(hh, kb) for hh in (2 * pair, 2 * pair + 1) for kb in kbs]
                        c0 = od * 256 + pair * 128
                        for i, (hh, kb) in enumerate(mats):
                            vs = sp_ if kb == 0 else sc
                            lhs = Vb[vs][:, hh, :] if od == 0 else onesP[:, hh % 2, :]
                            rd = [ptk, ("Vb", vs)] if od == 0 else [ptk, "cb"]
                            MM(PB(b3, c0, c0 + 128), lhs, pt[:, hh, kb * 128:(kb + 1) * 128], i == 0, i == len(mats) - 1, rd, [("ps", b3)])
                off = r + d * 128 * j - 2048 * half
                av = bass.AP(acc, off, [[4 * 2048, 128], [2048, 4], [d, 128]])
                oi = cnt["od"] % 2
                cnt["od"] += 1
                odt, odk = ODT[oi], ("ODT", oi)
                CP("act", odt[:], PB(b3), [("ps", b3)], [odk])
                pv = odt[:].rearrange("p (a t) -> p a t", t=128)
                if first_group:
                    CP("pool", av, pv, [odk], ["acc"], waw=True)
                else:
                    TT("pool", av, pv, av, ALU.add, [odk, "acc"], ["acc"])

            for half in range(2):
                for g in range(3):
                    d = DILS[g]
                    nb = 32 // d
                    wi = wl[0] % 2
                    wl[0] += 1
                    wt, wk = wq[wi], ("at_w", wi)
                    load_w(wt, w_qkv[l, g], 1280, f"at_w{wi}", wk)
                    jl, jh = half * nb // 2, (half + 1) * nb // 2
                    for r in range(d):
                        if 'at_loads' in opts:
                            continue
                        if jl > 0:
                            proj_block(g, r, jl - 1, False, wt, wk)
                        for j in range(jl, jh):
                            proj_block(g, r, j, True, wt, wk)
                            if 'at_proj' not in opts:
                                attend(g, r, j, half, g == 0)
                for pair in range(2):
                    ACT(acc[:, 2 + pair, :], acc[:, 2 + pair, :], AF.Ln, ["acc"], ["acc"])
                    ACT(acc[:, 2 + pair, :], acc[:, 2 + pair, :], AF.Exp, ["acc"], ["acc"], scale=-1.0)
                    TT("dve", attn_sb[:, pair, :], acc[:, pair, :], acc[:, 2 + pair, :], ALU.mult, ["acc"], ["attn_sb"])
                dv = attnT.rearrange("(c p) t -> p c t", p=128)[:, :, half * 2048:(half + 1) * 2048]
                DMA("pool", dv, attn_sb[:], "attn_sb", ["attn_sb"], [("attnT", half)])
            S.barrier()

    def phase_ab(l):
        vb = l * V_PER_LAYER
        with ExitStack() as e3:
            def sb3(name, shape, dt):
                return e3.enter_context(nc.sbuf_tensor(f"{name}_L{l}", list(shape), dt))
            mk_rings(sb3, hin=True)
            wab = sb3("ab_w", [128, KC, NAB], BF16)
            pw = sb3("ab_pw", [96, 4, 96], BF16)
            wa = sb3("ab_wa", [96, 4, D], BF16)
            wb = sb3("ab_wb", [128, 3, D], BF16)
            A = [sb3(f"ab_A{g}", [96, 16 + T], F32) for g in range(4)]
            T2 = sb3("ab_T2", [96, 16 + T], F32)
            T4 = sb3("ab_T4", [96, 16 + T], F32)
            T8 = sb3("ab_T8", [96, 16 + T], F32)
            PL = [sb3(f"ab_PL{g}", [96, T], BF16) for g in range(4)]
            MX = [sb3(f"ab_MX{g}", [96, T], BF16) for g in range(4)]
            BX = sb3("ab_BX", [128, T], F32)
            P = [sb3(f"ab_P{c}", [128, 2 + T], F32) for c in range(3)]
            Y1 = sb3("ab_Y1", [128, T], F32)
            Y2 = sb3("ab_Y2", [128, T], F32)
            Z = [sb3(f"ab_Z{c}", [128, T], BF16) for c in range(3)]
            SG = [sb3(f"ab_SG{i}", [128, T], F32) for i in range(2)]
            MO = [sb3(f"ab_MO{i}", [128, KC, T], F32) for i in range(2)]
            load_w(wab, w_in[l][:, 0:NAB], NAB, "ab_w", "ab_w")
            DMA("pool", pw[:], pool_w[l].rearrange("g c d -> c g d"), "ab_w2", [], ["ab_w2"])
            DMA("pool", wa[:], w_a[l].rearrange("(g c) n -> c g n", c=96), "ab_w2", [], ["ab_w2"])
            DMA("pool", wb[:], w_b[l].rearrange("(k p) n -> p k n", p=128), "ab_w2", [], ["ab_w2"])
            for g in range(4):
                MS("pool", A[g][:, 0:16], 0.0, [("A", g)])
            for c in range(3):
                MS("pool", P[c][:, 0:2], 0.0, [("P", c)])
            sgi = [0]
            for tt in range(NT):
                ht, hk = G["hin"].next()
                DMA("sp", ht[:], xview(h1T, tt), hk, [("h1T", tt)], [hk])
                mo, mok = MO[tt % 2], ("MO", tt % 2)
                for g in range(4):
                    w = (2, 4, 8, 16)[g]
                    Ak = ("A", g)
                    if tt > 0:
                        CP("pool", A[g][:, 0:16], A[g][:, T:T + 16], [Ak], [Ak])
                    b = bank()
                    for kc in range(KC):
                        MM(ps[0:96, b * 512:(b + 1) * 512], wab[:, kc, OFF_A + 96 * g:OFF_A + 96 * (g + 1)], ht[:, kc, :], kc == 0, kc == KC - 1,
                           [hk, "ab_w"], [("ps", b)])
                    CP("act", A[g][:, 16:16 + T], ps[0:96, b * 512:(b + 1) * 512], [("ps", b)], [Ak])
                    src = A[g]
                    srck = Ak
                    n = 1
                    for (dst, dk) in ((T2, "T2"), (T4, "T4"), (T8, "T8"), (None, None)):
                        if n >= w:
                            break
                        if 2 * n == w:
                            tmpw, tmpk = T2 if dst is not T2 and src is not T2 else (T4 if src is not T4 else T8), None
                            tmpw = {1: T2, 2: T4, 4: T8, 8: T2}[n]
                            tmpk = {1: "T2", 2: "T4", 4: "T8", 8: "T2"}[n]
                            TT("pool", tmpw[:, 16:16 + T], src[:, 16:16 + T], src[:, 16 - n:16 - n + T], ALU.add, [srck], [tmpk])
                            STT("dve", PL[g][:], tmpw[:, 16:16 + T], 1.0 / w, A[g][:, 16:16 + T], ALU.mult, ALU.subtract, [tmpk, Ak], [("PL", g)])
                            if tt == 0:
                                tmp, tk = tmp_ring.next()
                                TT("pool", tmp[0:96, 0:16], tmpw[:, 16:32], consts[0:96, C_RC + 16 * g:C_RC + 16 * g + 16], ALU.mult, [tmpk, "consts"], [tk])
                                TT("pool", PL[g][:, 0:16], tmp[0:96, 0:16], A[g][:, 16:32], ALU.subtract, [tk, Ak], [("PL", g)])
                            break
                        TT("pool", dst[:, 2 * n - 1:16 + T], src[:, 2 * n - 1:16 + T], src[:, n - 1:16 + T - n], ALU.add, [srck], [dk])
                        src, srck = dst, dk
                        n *= 2
                    b2 = bank()
                    MM(ps[0:96, b2 * 512:(b2 + 1) * 512], pw[:, g, :], PL[g][:], True, True, [("PL", g), "ab_w2"], [("ps", b2)])
                    ACT(MX[g][:], ps[0:96, b2 * 512:(b2 + 1) * 512], AF.Copy, [("ps", b2), "vecs"], [("MX", g)],
                        scale=vecs[0:96, vb + V_PSCALE + g:vb + V_PSCALE + g + 1])
                for mc in range(KC):
                    b = bank()
                    for g in range(4):
                        MM(PB(b), wa[:, g, mc * 128:(mc + 1) * 128], MX[g][:], g == 0, g == 3, [("MX", g), "ab_w2"], [("ps", b)])
                    bg = bank()
                    for kc in range(KC):
                        MM(PB(bg), wab[:, kc, OFF_GA + mc * 128:OFF_GA + (mc + 1) * 128], ht[:, kc, :], kc == 0, kc == KC - 1, [hk, "ab_w"], [("ps", bg)])
                    sg, sgk = SG[sgi[0] % 2], ("SG", sgi[0] % 2)
                    sgi[0] += 1
                    ACT(sg[:], PB(bg), AF.Sigmoid, [("ps", bg)], [sgk])
                    TT("dve", mo[:, mc, :], PB(b), sg[:], ALU.mult, [("ps", b), sgk], [mok], waw=False)
                for c in range(3):
                    Pk = ("P", c)
                    if tt > 0:
                        CP("pool", P[c][:, 0:2], P[c][:, T:T + 2], [Pk], [Pk])
                    bx, bb_, bc = bank(), bank(), bank()
                    for (bk, off) in ((bx, OFF_BX), (bb_, OFF_BB), (bc, OFF_BC)):
                        for kc in range(KC):
                            MM(PB(bk), wab[:, kc, off + c * 128:off + (c + 1) * 128], ht[:, kc, :], kc == 0, kc == KC - 1, [hk, "ab_w"], [("ps", bk)])
                    CP("act", BX[:], PB(bx), [("ps", bx)], ["BX"])
                    TT("dve", P[c][:, 2:2 + T], PB(bc), BX[:], ALU.mult, [("ps", bc), "BX"], [Pk])
                    cw = vb + V_CONVB
                    ACT(Y1[:], P[c][:, 2:2 + T], AF.Copy, [Pk, "vecs"], ["Y1"], scale=vecs[:, cw + 2 * 3 + c:cw + 2 * 3 + c + 1])
                    STT("dve", Y2[:], P[c][:, 1:1 + T], vecs[:, cw + 1 * 3 + c:cw + 1 * 3 + c + 1], Y1[:], ALU.mult, ALU.add, [Pk, "Y1", "vecs"], ["Y2"])
                    STT("dve", Y1[:], P[c][:, 0:T], vecs[:, cw + 0 * 3 + c:cw + 0 * 3 + c + 1], Y2[:], ALU.mult, ALU.add, [Pk, "Y2", "vecs"], ["Y1"])
                    TT("dve", Z[c][:], PB(bb_), Y1[:], ALU.mult, [("ps", bb_), "Y1"], [("Z", c)])
                for mc in range(KC):
                    b = bank()
                    for c in range(3):
                        MM(PB(b), wb[:, c, mc * 128:(mc + 1) * 128], Z[c][:], c == 0, c == 2, [("Z", c), "ab_w2"], [("ps", b)])
                    bg = bank()
                    for kc in range(KC):
                        MM(PB(bg), wab[:, kc, OFF_GB + mc * 128:OFF_GB + (mc + 1) * 128], ht[:, kc, :], kc == 0, kc == KC - 1, [hk, "ab_w"], [("ps", bg)])
                    sg, sgk = SG[sgi[0] % 2], ("SG", sgi[0] % 2)
                    sgi[0] += 1
                    ACT(sg[:], PB(bg), AF.Sigmoid, [("ps", bg)], [sgk])
                    tmp, tk = tmp_ring.next()
                    TT("dve", tmp[:], PB(b), sg[:], ALU.mult, [("ps", b), sgk], [tk])
                    TT("pool", mo[:, mc, :], mo[:, mc, :], tmp[:], ALU.add, [mok, tk], [mok], waw=False)
                DMA("pool", xview(mab, tt), mo[:], mok, [mok], [("mab", tt)])
            S.barrier()

    def phase_merge(l):
        vb = l * V_PER_LAYER
        xsrc = xT if l == 0 else xs
        with ExitStack() as e4:
            def sb4(name, shape, dt):
                return e4.enter_context(nc.sbuf_tensor(f"{name}_L{l}", list(shape), dt))
            mk_rings(sb4, x=True, h=True, hin=True, ysb=True)
            wgc = sb4("mg_wgc", [128, KC, D], BF16)
            wc = sb4("mg_wc", [128, 2, D], BF16)
            wo = sb4("mg_wo", [128, KC, D], BF16)
            AT = [sb4(f"mg_at{i}", [128, 2, T], BF16) for i in range(2)]
            MI = [sb4(f"mg_mi{i}", [128, KC, T], F32) for i in range(2)]
            MGs = [sb4(f"mg_mg{i}", [128, KC, T], BF16) for i in range(2)]
            SG = [sb4(f"mg_SG{i}", [128, T], F32) for i in range(2)]
            load_w(wgc, w_in[l][:, OFF_GC:OFF_GC + D], D, "mg_w", "mg_w")
            load_w(wc, w_c[l], D, "mg_w", "mg_w")
            load_w(wo, w_out[l], D, "mg_wo", "mg_wo")
            st = {}

            def stA(tt):
                ht, hk = G["hin"].next()
                DMA("sp", ht[:], xview(h1T, tt), hk, [("h1T", tt)], [hk])
                at, atk = AT[tt % 2], ("AT", tt % 2)
                DMA("sp", at[:], attnT.rearrange("(c p) t -> p c t", p=128)[:, :, tt * T:(tt + 1) * T], f"mg_at{tt % 2}", [("attnT", tt // 4)], [atk])
                mi, mik = MI[tt % 2], ("MI", tt % 2)
                DMA("sp", mi[:], xview(mab, tt), f"mg_mi{tt % 2}", [("mab", tt)], [mik])
                xt, xk = G["x"].next()
                DMA("sp", xt[:], xview(xsrc, tt), xk, [("xs", tt)], [xk])
                mg, mgk = MGs[tt % 2], ("MG", tt % 2)
                for mc in range(KC):
                    b = bank()
                    for pr in range(2):
                        MM(PB(b), wc[:, pr, mc * 128:(mc + 1) * 128], at[:, pr, :], pr == 0, pr == 1, [atk, "mg_w"], [("ps", b)])
                    bg = bank()
                    for kc in range(KC):
                        MM(PB(bg), wgc[:, kc, mc * 128:(mc + 1) * 128], ht[:, kc, :], kc == 0, kc == KC - 1, [hk, "mg_w"], [("ps", bg)])
                    sg, sgk = SG[mc % 2], ("SG4", mc % 2)
                    ACT(sg[:], PB(bg), AF.Sigmoid, [("ps", bg)], [sgk])
                    tmp, tk = tmp_ring.next()
                    TT("dve", tmp[:], PB(b), sg[:], ALU.mult, [("ps", b), sgk], [tk])
                    TT("pool", mg[:, mc, :], tmp[:], mi[:, mc, :], ALU.add, [tk, mik], [mgk], waw=False)
                st[tt] = dict(xt=xt, xk=xk, mg=mg, mgk=mgk)

            def stW(tt):
                d_ = st[tt]
                pn = PostNorm()
                for mc in range(KC):
                    b = bank()
                    for kc in range(KC):
                        MM(PB(b), wo[:, kc, mc * 128:(mc + 1) * 128], d_["mg"][:, kc, :], kc == 0, kc == KC - 1, [d_["mgk"], "mg_wo"], [("ps", b)])
                    pn.chunk(mc, b)
                d_["pn"] = pn

            def stF(tt):
                d_ = st.pop(tt)
                xt, xk = d_["xt"], d_["xk"]
                d_["pn"].finish(xt, xk, vb + V_MIXPOST)
                DMA("pool", xview(xs, tt), xt[:], xk, [xk], [("xs", tt)])
                h2, h2k = G["h"].next()
                norm_tile(xt, xk, vb + V_MEMPRE, h2, h2k)
                DMA("pool", xview(h2T, tt), h2[:], h2k, [h2k], [("h2T", tt)])

            stA(0)
            stW(0)
            for tt in range(1, NT):
                stA(tt)
                stF(tt - 1)
                stW(tt)
            stF(NT - 1)
            S.barrier()

    def phase_mem(l):
        vb = l * V_PER_LAYER
        with ExitStack() as e5:
            def sb5(name, shape, dt):
                return e5.enter_context(nc.sbuf_tensor(f"{name}_L{l}", list(shape), dt))
            mk_rings(sb5, x=True, h=True, hin=True, ysb=True)
            wq_ = sb5("mm_wq", [128, KC, 512], BF16)
            wkv = sb5("mm_wkv", [128, KC, 1024], BF16)
            wo = sb5("mm_wo", [128, 4, D], BF16)
            mt = sb5("mm_mt", [128, KC, 256], F32)
            mn = sb5("mm_mn", [128, KC, 256], BF16)
            KmT = sb5("mm_KmT", [128, 4, 256], BF16)
            Vm = sb5("mm_Vm", [128, 2, 512], BF16)
            QM = [sb5(f"mm_QM{i}", [128, T], BF16) for i in range(4)]
            PTm = [sb5(f"mm_PT{i}", [128, 2, T], BF16) for i in range(4)]
            DN = [sb5(f"mm_DN{i}", [128, T], F32) for i in range(2)]
            OMs = [sb5(f"mm_OM{i}", [128, 4, T], BF16) for i in range(2)]
            load_w(wkv, w_mkv[l], 1024, "mm_w", "mm_w")
            load_w(wq_, w_mq[l], 512, "mm_w", "mm_w")
            load_w(wo, w_mo[l], D, "mm_wo", "mm_wo")
            DMA("sp", mt[:], memT.rearrange("(c p) t -> p c t", p=128), "mm_mt", [], ["mm_mt"])
            norm_tile(mt, "mm_mt", vb + V_MEMKV, mn, "mm_mn", ncols=256)
            for h in range(4):
                b = bank()
                for kc in range(KC):
                    MM(PB(b, 0, 256), wkv[:, kc, h * 128:(h + 1) * 128], mn[:, kc, 0:256], kc == 0, kc == KC - 1, ["mm_mn", "mm_w"], [("ps", b)])
                CP("act", KmT[:, h, :], PB(b, 0, 256), [("ps", b)], ["KmT"], waw=False)
            for mi_ in range(2):
                b = bank()
                for kc in range(KC):
                    MM(PB(b), mn[:, kc, mi_ * 128:(mi_ + 1) * 128], wkv[:, kc, 512:1024], kc == 0, kc == KC - 1, ["mm_mn", "mm_w"], [("ps", b)])
                CP("act", Vm[:, mi_, :], PB(b), [("ps", b)], ["Vm"], waw=False)
            sc = float(128 ** -0.5)
            st = {}

            def stA(tt):
                ht, hk = G["hin"].next()
                DMA("sp", ht[:], xview(h2T, tt), hk, [("h2T", tt)], [hk])
                xt, xk = G["x"].next()
                DMA("sp", xt[:], xview(xs, tt), xk, [("xs", tt)], [xk])
                om, omk = OMs[tt % 2], ("OM", tt % 2)
                for h in range(4):
                    b = bank()
                    for kc in range(KC):
                        MM(PB(b), wq_[:, kc, h * 128:(h + 1) * 128], ht[:, kc, :], kc == 0, kc == KC - 1, [hk, "mm_w"], [("ps", b)])
                    CP("act", QM[h][:], PB(b), [("ps", b)], [("QM", h)])
                for h in range(4):
                    for mi_ in range(2):
                        bs = bank()
                        MM(PB(bs), KmT[:, h, mi_ * 128:(mi_ + 1) * 128], QM[h][:], True, True, ["KmT", ("QM", h)], [("ps", bs)])
                        ACT(PTm[h][:, mi_, :], PB(bs), AF.Exp, [("ps", bs)], [("PTm", h)], scale=sc, waw=False)
                for h in range(4):
                    pt, ptk = PTm[h], ("PTm", h)
                    bo, bd = bank(), bank()
                    for mi_ in range(2):
                        MM(PB(bo), Vm[:, mi_, h * 128:(h + 1) * 128], pt[:, mi_, :], mi_ == 0, mi_ == 1, ["Vm", ptk], [("ps", bo)])
                    for mi_ in range(2):
                        MM(PB(bd), ones[:], pt[:, mi_, :], mi_ == 0, mi_ == 1, ["cb", ptk], [("ps", bd)])
                    dn, dnk = DN[h % 2], ("DN", h % 2)
                    ACT(dn[:], PB(bd), AF.Ln, [("ps", bd)], [dnk])
                    ACT(dn[:], dn[:], AF.Exp, [dnk], [dnk], scale=-1.0)
                    TT("dve", om[:, h, :], PB(bo), dn[:], ALU.mult, [("ps", bo), dnk], [omk], waw=False)
                st[tt] = dict(xt=xt, xk=xk, om=om, omk=omk)

            def stW(tt):
                d_ = st[tt]
                pn = PostNorm()
                for mc in range(KC):
                    b = bank()
                    for h in range(4):
                        MM(PB(b), wo[:, h, mc * 128:(mc + 1) * 128], d_["om"][:, h, :], h == 0, h == 3, [d_["omk"], "mm_wo"], [("ps", b)])
                    pn.chunk(mc, b)
                d_["pn"] = pn

            def stF(tt):
                d_ = st.pop(tt)
                xt, xk = d_["xt"], d_["xk"]
                d_["pn"].finish(xt, xk, vb + V_MEMPOST)
                DMA("pool", xview(xs, tt), xt[:], xk, [xk], [("xs", tt)])
                h3, h3k = G["h"].next()
                norm_tile(xt, xk, vb + V_FFNPRE, h3, h3k)
                DMA("pool", xview(h3T, tt), h3[:], h3k, [h3k], [("h3T", tt)])

            stA(0)
            stW(0)
            for tt in range(1, NT):
                stA(tt)
                stF(tt - 1)
                stW(tt)
            stF(NT - 1)
            S.barrier()

    def phase_up(l):
        vb = l * V_PER_LAYER
        with ExitStack() as e6:
            def sb6(name, shape, dt):
                return e6.enter_context(nc.sbuf_tensor(f"{name}_L{l}", list(shape), dt))
            mk_rings(sb6, hin=True)
            wu = sb6("up_w", [128, KC, 2 * DFF], BF16)
            H = sb6("up_H", [128, FC, 2], F32)
            UA = [sb6(f"up_UA{i}", [128, 2 + T], F32) for i in range(2)]
            Y1 = [sb6(f"up_Y1{i}", [128, T], F32) for i in range(2)]
            Y2 = [sb6(f"up_Y2{i}", [128, T], F32) for i in range(2)]
            AO = [sb6(f"up_AO{i}", [128, FC, T], BF16) for i in range(2)]
            load_w(wu, w_up[l], 2 * DFF, "up_w", "up_w")
            MS("pool", H[:], 0.0, ["H"])
            cw = vb + V_CONVF
            for tt in range(NT):
                ht, hk = G["hin"].next()
                DMA("sp", ht[:], xview(h3T, tt), hk, [("h3T", tt)], [hk])
                ao, aok = AO[tt % 2], ("AO", tt % 2)
                for c in range(FC):
                    ba, bb_ = bank(), bank()
                    for kc in range(KC):
                        MM(PB(ba), wu[:, kc, c * 128:(c + 1) * 128], ht[:, kc, :], kc == 0, kc == KC - 1, [hk, "up_w"], [("ps", ba)])
                    for kc in range(KC):
                        MM(PB(bb_), wu[:, kc, DFF + c * 128:DFF + (c + 1) * 128], ht[:, kc, :], kc == 0, kc == KC - 1, [hk, "up_w"], [("ps", bb_)])
                    i = c % 2
                    ua, uak = UA[i], ("UA", i)
                    y1, y1k = Y1[i], ("Y1", i)
                    y2, y2k = Y2[i], ("Y2", i)
                    CP("pool", ua[:, 0:2], H[:, c, :], ["H"], [uak])
                    CP("act", ua[:, 2:2 + T], PB(ba), [("ps", ba)], [uak], waw=False)
                    CP("pool", H[:, c, :], ua[:, T:T + 2], [uak], ["H"])
                    ACT(y1[:], ua[:, 2:2 + T], AF.Copy, [uak, "vecs"], [y1k], scale=vecs[:, cw + 2 * FC + c:cw + 2 * FC + c + 1])
                    STT("dve", y2[:], ua[:, 1:1 + T], vecs[:, cw + 1 * FC + c:cw + 1 * FC + c + 1], y1[:], ALU.mult, ALU.add, [uak, y1k, "vecs"], [y2k])
                    STT("dve", y1[:], ua[:, 0:T], vecs[:, cw + 0 * FC + c:cw + 0 * FC + c + 1], y2[:], ALU.mult, ALU.add, [uak, y2k, "vecs"], [y1k])
                    ACT(y2[:], y1[:], AF.Silu, [y1k], [y2k])
                    TT("dve", ao[:, c, :], PB(bb_), y2[:], ALU.mult, [("ps", bb_), y2k], [aok], waw=False)
                DMA("pool", actT.rearrange("(c p) t -> p c t", p=128)[:, :, tt * T:(tt + 1) * T], ao[:], f"up_ao{tt % 2}", [aok], [("actT", tt)])
            S.barrier()

    def phase_down(l, last):
        vb = l * V_PER_LAYER
        with ExitStack() as e7:
            def sb7(name, shape, dt):
                return e7.enter_context(nc.sbuf_tensor(f"{name}_L{l}", list(shape), dt))
            mk_rings(sb7, x=True, h=True, ysb=True)
            wd = sb7("dn_w", [128, FC, D], BF16)
            AI = [sb7(f"dn_AI{i}", [128, FC, T], BF16) for i in range(2)]
            for q4 in range(4):
                DMA("pool", wd[:, :, q4 * 256:(q4 + 1) * 256], w_down[l].rearrange("(k p) n -> p k n", p=128)[:, :, q4 * 256:(q4 + 1) * 256],
                    "dn_w", [], [("dn_w", q4)])
            st = {}

            def stA(tt):
                ai, aik = AI[tt % 2], ("AI", tt % 2)
                DMA("sp", ai[:], actT.rearrange("(c p) t -> p c t", p=128)[:, :, tt * T:(tt + 1) * T], f"dn_ai{tt % 2}", [("actT", tt)], [aik])
                xt, xk = G["x"].next()
                DMA("sp", xt[:], xview(xs, tt), xk, [("xs", tt)], [xk])
                st[tt] = dict(xt=xt, xk=xk, ai=ai, aik=aik, pn=PostNorm())

            def stW(tt, mcs):
                d_ = st[tt]
                for mc in mcs:
                    b = bank()
                    for c in range(FC):
                        MM(PB(b), wd[:, c, mc * 128:(mc + 1) * 128], d_["ai"][:, c, :], c == 0, c == FC - 1, [d_["aik"], ("dn_w", mc // 2)], [("ps", b)])
                    d_["pn"].chunk(mc, b)

            def stF(tt):
                d_ = st.pop(tt)
                xt, xk = d_["xt"], d_["xk"]
                d_["pn"].finish(xt, xk, vb + V_FFNPOST)
                if last:
                    DMA("pool", xview(outT, tt), xt[:], xk, [xk], [("outT", tt)])
                else:
                    DMA("pool", xview(xs, tt), xt[:], xk, [xk], [("xs", tt)])
                    h1, h1k = G["h"].next()
                    norm_tile(xt, xk, (l + 1) * V_PER_LAYER + V_MIXPRE, h1, h1k)
                    DMA("pool", xview(h1T, tt), h1[:], h1k, [h1k], [("h1T", tt)])

            stA(0)
            stW(0, range(KC))
            for tt in range(1, NT):
                stA(tt)
                stW(tt, range(0, 4))
                stF(tt - 1)
                stW(tt, range(4, KC))
            stF(NT - 1)
            S.barrier()


    return dict(nc=nc, S=S, es=es, phases=dict(norm0=phase_norm0, attn=phase_attn, ab=phase_ab, merge=phase_merge,
                                               mem=phase_mem, up=phase_up, down=phase_down))


def build_full(n_layers=NL, dbg=False, stop=None, opts=()):
    P = build_program(n_layers, dbg, opts)
    ph = P["phases"]
    seq = []
    if "nonorm0" not in opts:
        ph["norm0"](0)
    done = stop is not None and stop[1] == "norm0"
    for l in range(n_layers):
        if done:
            break
        for name in ("attn", "ab", "merge", "mem", "up", "down"):
            if name == "down":
                ph[name](l, l == n_layers - 1)
            else:
                ph[name](l)
            if stop is not None and stop == (l, name):
                done = True
                break
        if done:
            break
    nsem = P["S"].emit()
    P["es"].close()
    return P["nc"], P["S"], nsem


def host_inputs(inputs):
    import ml_dtypes
    f32 = np.float32
    perm = w_in_perm()
    wip = np.asarray(inputs["w_in"], f32)[:, :, perm]
    wqkv = np.zeros((NL, 3, D, 1280), f32)
    for g in range(3):
        base = OFF_QKV + 768 * g
        wqkv[:, g, :, 0:256] = wip[:, :, base:base + 256]
        for hh in range(4):
            par = hh % 2
            wqkv[:, g, :, 256 + hh * 128 + 64 * par:256 + hh * 128 + 64 * par + 64] = wip[:, :, base + 256 + 64 * hh:base + 256 + 64 * hh + 64]
            wqkv[:, g, :, 768 + hh * 128 + 64 * par:768 + hh * 128 + 64 * par + 64] = wip[:, :, base + 512 + 64 * hh:base + 512 + 64 * hh + 64]
    shared = {
        "w_in": np.ascontiguousarray(wip[:, :, 0:OFF_QKV]),
        "w_qkv": wqkv,
        "pool_w": np.ascontiguousarray(np.asarray(inputs["pool_w"], f32)),
    }
    for k in ("w_branch_a", "w_branch_b", "w_branch_c", "w_out", "w_mq", "w_mkv", "w_mo", "w_up", "w_down"):
        shared[k] = np.ascontiguousarray(np.asarray(inputs[k], f32))
    vecs = np.zeros((128, NL * V_PER_LAYER), f32)
    for l in range(NL):
        vb = l * V_PER_LAYER
        for name, off in (("norm_mix_pre", V_MIXPRE), ("norm_mix_post", V_MIXPOST), ("norm_mem_pre", V_MEMPRE),
                          ("norm_mem_post", V_MEMPOST), ("norm_memkv", V_MEMKV), ("norm_ffn_pre", V_FFNPRE),
                          ("norm_ffn_post", V_FFNPOST)):
            vecs[:, vb + off:vb + off + 8] = np.asarray(inputs[name], f32)[l].reshape(8, 128).T
        vecs[:, vb + V_CONVB:vb + V_CONVB + 9] = np.asarray(inputs["conv_b_w"], f32)[l].reshape(3, 3, 128).transpose(2, 0, 1).reshape(128, 9)
        vecs[:, vb + V_CONVF:vb + V_CONVF + 66] = np.asarray(inputs["conv_ffn_w"], f32)[l].reshape(3, FC, 128).transpose(2, 0, 1).reshape(128, 66)
        vecs[0:96, vb + V_PSCALE:vb + V_PSCALE + 4] = np.asarray(inputs["pool_scale"], f32)[l].reshape(4, 96).T
    consts = np.zeros((128, NCONST), f32)
    consts[:, C_ID:C_ID + 128] = np.eye(128, dtype=f32)
    k = np.arange(128)[:, None]
    q = np.arange(128)[None, :]
    consts[:, C_MASK:C_MASK + 128] = np.where(k >= q, 0.0, -30000.0)
    consts[:, C_MASK + 128:C_MASK + 256] = np.where(k <= q, 0.0, -30000.0)
    consts[:, C_INV:C_INV + 8] = (f32(500000.0) ** (-np.arange(0, 16, 2, dtype=f32) / f32(16)))[None, :]
    for g, w in enumerate((2, 4, 8, 16)):
        consts[:, C_RC + 16 * g:C_RC + 16 * g + 16] = (1.0 / np.minimum(np.arange(16) + 1, w))[None, :]
    shared["vecs"] = vecs
    shared["consts"] = consts
    x = np.asarray(inputs["x"], f32)
    mem = np.asarray(inputs["mem"], f32)
    posn = np.asarray(inputs["positions"]).astype(np.int32)
    in_maps = []
    for b in range(8):
        m = dict(shared)
        m["xT"] = np.ascontiguousarray(x[b].T)
        m["memT"] = np.ascontiguousarray(mem[b].T)
        m["pos"] = np.ascontiguousarray(posn[b].reshape(32, 128))
        in_maps.append(m)
    return in_maps


_CACHE = {}


def kernel(**inputs):
    in_maps = host_inputs(inputs)
    if "nc" not in _CACHE:
        _CACHE["nc"] = build_full()[0]
    nc = _CACHE["nc"]
    res = run_bass_kernel_spmd(nc, in_maps, core_ids=list(range(8)))
    out = np.stack([np.ascontiguousarray(res.results[b]["outT"].T) for b in range(8)], axis=0)
    return out.astype(np.float32)
```

```python
import numpy as np
import concourse.bass as bass
import concourse.mybir as mybir

F32 = mybir.dt.float32
BF16 = mybir.dt.bfloat16
I32 = mybir.dt.int32
ALU = mybir.AluOpType
AF = mybir.ActivationFunctionType

EPOCH = 30000


class Sched:
    ENGS = ("pe", "act", "dve", "pool", "sp")

    def __init__(self, nc):
        self.nc = nc
        self.stream = {e: [] for e in self.ENGS}
        self.cnt = {e: 0 for e in self.ENGS}
        self.known = {e: {} for e in self.ENGS}
        self.res = {}
        self.chan_cnt = {}
        self.n_wait = 0

    def _collect(self, eng, reads, writes, waw):
        waits = {}

        def need(k, v):
            if k == ("eng", "pe") and eng == "pe":
                return
            if self.known[eng].get(k, 0) >= v:
                return
            if waits.get(k, 0) < v:
                waits[k] = v

        for r in reads:
            st = self.res.get(r)
            if st:
                for k, v in st["w"].items():
                    need(k, v)
        for w in writes:
            st = self.res.get(w)
            if st:
                for k, v in st["r"].items():
                    need(k, v)
                if waw:
                    for k, v in st["w"].items():
                        need(k, v)
        for k, v in waits.items():
            self.known[eng][k] = v
        return sorted(waits.items(), key=lambda kv: str(kv[0]))

    def _commit(self, ev, reads, writes, waw):
        k, v = ev
        for r in reads:
            st = self.res.setdefault(r, {"w": {}, "r": {}})
            if st["r"].get(k, 0) < v:
                st["r"][k] = v
        for w in writes:
            st = self.res.setdefault(w, {"w": {}, "r": {}})
            if st["r"] or waw:
                st["w"] = {}
            st["r"] = {}
            if st["w"].get(k, 0) < v:
                st["w"][k] = v

    def op(self, eng, fn, reads=(), writes=(), waw=True):
        waits = self._collect(eng, reads, writes, waw)
        self.cnt[eng] += 1
        ev = (("eng", eng), self.cnt[eng])
        self._commit(ev, reads, writes, waw)
        self.stream[eng].append((waits, fn, ev))
        self.n_wait += len(waits)

    def dma(self, eng, fn, chan, reads=(), writes=(), waw=False):
        waits = self._collect(eng, reads, writes, waw)
        self.chan_cnt[chan] = self.chan_cnt.get(chan, 0) + 16
        ev = (("chan", chan), self.chan_cnt[chan])
        self._commit(ev, reads, writes, waw)
        self.stream[eng].append((waits, fn, ev))
        self.n_wait += len(waits)

    def barrier(self, engs=None):
        engs = engs or self.ENGS
        for e in engs:
            waits = {}
            for e2 in self.ENGS:
                if e2 != e and self.cnt[e2] > 0:
                    k = ("eng", e2)
                    if self.known[e].get(k, 0) < self.cnt[e2]:
                        waits[k] = self.cnt[e2]
            for c, v in self.chan_cnt.items():
                k = ("chan", c)
                if self.known[e].get(k, 0) < v:
                    waits[k] = v
            for k, v in waits.items():
                self.known[e][k] = v
            if waits:
                self.stream[e].append((sorted(waits.items(), key=lambda kv: str(kv[0])), None, None))

    def emit(self):
        nc = self.nc
        sems = {}

        def sem_of(k, v):
            if k[0] == "eng":
                ep = (v - 1) // EPOCH
                key = (k, ep)
                val = (v - 1) % EPOCH + 1
            else:
                key = (k, 0)
                val = v
            if key not in sems:
                sems[key] = nc.alloc_semaphore(name=f"s{len(sems)}")
            return sems[key], val

        self.barrier(engs=("sp",))
        for e in self.ENGS:
            for waits, fn, ev in self.stream[e]:
                for k, v in waits:
                    sem_of(k, v)
                if ev is not None:
                    sem_of(*ev)
        eng_map = {"pe": "tensor", "act": "scalar", "dve": "vector", "pool": "gpsimd", "sp": "sync"}
        with nc.Block() as block:
            for e in self.ENGS:
                if not self.stream[e]:
                    continue
                deco = getattr(block, eng_map[e])

                def body(h, e=e):
                    for waits, fn, ev in self.stream[e]:
                        for k, v in waits:
                            s, val = sem_of(k, v)
                            h.wait_ge(s, val)
                        if fn is None:
                            continue
                        ins = fn(h)
                        s, _ = sem_of(*ev)
                        ins.then_inc(s, 16 if ev[0][0] == "chan" else 1)

                deco(body)
        return len(sems)


def sap(t, F, poff, npart, off, dims):
    return bass.AP(t, poff * F + off, [[F, npart]] + [list(d) for d in dims])

from concourse.bass_utils import run_bass_kernel_spmd
from contextlib import ExitStack

D = 1024
SEQ = 4096
T = 512
NT = SEQ // T
KC = 8
DFF = 2816
FC = DFF // 128
NL = 2
EPS = 1e-6
DILS = (1, 4, 16)
OFF_A = 0
OFF_BX = 384
OFF_BB = 768
OFF_BC = 1152
OFF_GA = 1536
OFF_GB = 2560
OFF_GC = 3584
OFF_QKV = 4608
NAB = 3584
V_MIXPRE, V_MIXPOST, V_MEMPRE, V_MEMPOST, V_MEMKV, V_FFNPRE, V_FFNPOST = 0, 8, 16, 24, 32, 40, 48
V_CONVB = 56
V_CONVF = 65
V_PSCALE = 131
V_PER_LAYER = 135
C_ID = 0
C_MASK = 128
C_INV = 384
C_RC = 392
NCONST = 456


def w_in_perm():
    idx = []
    a = 0
    idx += list(range(0, 384))
    idx += list(range(384, 384 + 1152))
    g0 = 384 + 1152 + 3 * 768
    idx += list(range(g0, g0 + 3072))
    q0 = 384 + 1152
    for g in range(3):
        for part in range(3):
            s = q0 + part * 768 + g * 256
            idx += list(range(s, s + 256))
    return np.array(idx, dtype=np.int64)


def build_program(n_layers=NL, dbg=False, opts=()):
    nc = bass.Bass("TRN2", target_bir_lowering=False)
    S = Sched(nc)

    def din(name, shape, dt=F32):
        return nc.dram_tensor(name, list(shape), dt, kind="ExternalInput").ap()

    kind_s = "ExternalOutput" if dbg else "Internal"

    def dscr(name, shape, dt):
        return nc.dram_tensor(name, list(shape), dt, kind=kind_s).ap()

    xT = din("xT", [D, SEQ])
    memT = din("memT", [D, 256])
    pos = din("pos", [32, 128], I32)
    w_in = din("w_in", [NL, D, OFF_QKV])
    w_qkv = din("w_qkv", [NL, 3, D, 1280])
    pool_w = din("pool_w", [NL, 4, 96, 96])
    w_a = din("w_branch_a", [NL, 384, D])
    w_b = din("w_branch_b", [NL, 384, D])
    w_c = din("w_branch_c", [NL, 256, D])
    w_out = din("w_out", [NL, D, D])
    w_mq = din("w_mq", [NL, D, 512])
    w_mkv = din("w_mkv", [NL, D, 1024])
    w_mo = din("w_mo", [NL, 512, D])
    w_up = din("w_up", [NL, D, 2 * DFF])
    w_down = din("w_down", [NL, DFF, D])
    vecs_d = din("vecs", [128, NL * V_PER_LAYER])
    consts_d = din("consts", [128, NCONST])
    outT = nc.dram_tensor("outT", [D, SEQ], F32, kind="ExternalOutput").ap()

    h1T = dscr("h1T", [D, SEQ], BF16)
    h2T = dscr("h2T", [D, SEQ], BF16)
    h3T = dscr("h3T", [D, SEQ], BF16)
    attnT = dscr("attnT", [256, SEQ], BF16)
    mab = dscr("mab", [D, SEQ], F32)
    xs = dscr("xs", [D, SEQ], F32)
    actT = dscr("actT", [DFF, SEQ], BF16)
    rope_d = dscr("rope_tab", [SEQ, 16], F32)

    es = ExitStack()

    def sb(name, shape, dt):
        return es.enter_context(nc.sbuf_tensor("s_" + name, list(shape), dt))

    ps = es.enter_context(nc.psum_tensor("ps", [128, 4096], F32))

    def MM(out, lhsT, rhs, start, stop, R, W):
        S.op("pe", lambda h: h.matmul(out, lhsT=lhsT, rhs=rhs, start=start, stop=stop), reads=R, writes=W)

    def ACT(out, in_, func, R, W, scale=1.0, bias=None, waw=True):
        if bias is None:
            S.op("act", lambda h: h.activation(out=out, in_=in_, func=func, scale=scale), reads=R, writes=W, waw=waw)
        else:
            S.op("act", lambda h: h.activation(out=out, in_=in_, func=func, scale=scale, bias=bias), reads=R, writes=W, waw=waw)

    def TT(eng, out, in0, in1, op, R, W, waw=True):
        S.op(eng, lambda h: h.tensor_tensor(out=out, in0=in0, in1=in1, op=op), reads=R, writes=W, waw=waw)

    def TS(eng, out, in0, s1, s2, op0, op1, R, W, waw=True):
        if s2 is None:
            S.op(eng, lambda h: h.tensor_scalar(out=out, in0=in0, scalar1=s1, scalar2=None, op0=op0), reads=R, writes=W, waw=waw)
        else:
            S.op(eng, lambda h: h.tensor_scalar(out=out, in0=in0, scalar1=s1, scalar2=s2, op0=op0, op1=op1), reads=R, writes=W, waw=waw)

    def STT(eng, out, in0, scalar, in1, op0, op1, R, W, waw=True):
        S.op(eng, lambda h: h.scalar_tensor_tensor(out=out, in0=in0, scalar=scalar, in1=in1, op0=op0, op1=op1), reads=R, writes=W, waw=waw)

    def CP(eng, out, in_, R, W, waw=True):
        if eng == "act":
            ACT(out, in_, AF.Copy, R, W, waw=waw)
        else:
            S.op(eng, lambda h: h.tensor_copy(out=out, in_=in_), reads=R, writes=W, waw=waw)

    def MS(eng, ap, val, W):
        S.op(eng, lambda h: h.memset(ap, val), writes=W)

    def DMA(eng, out, in_, chan, R, W):
        S.dma(eng, lambda h: h.dma_start(out=out, in_=in_), chan, reads=R, writes=W)

    uniq = [0]

    class Ring:
        def __init__(self, name, shape, dt, n, alloc=None):
            alloc = alloc or sb
            uniq[0] += 1
            self.name = name
            self.t = [alloc(f"{name}_{uniq[0]}_{i}", shape, dt) for i in range(n)]
            self.i = 0

        def next(self):
            k = self.i % len(self.t)
            self.i += 1
            return self.t[k], (self.name, k)

    psp = [0]

    def bank(n=1):
        if n == 2 and psp[0] % 2 == 1:
            psp[0] += 1
        p = psp[0] % 6
        psp[0] += n
        return p

    def PB(b, lo=0, hi=512):
        return ps[:, b * 512 + lo: b * 512 + hi]

    vecs = sb("vecs", [128, NL * V_PER_LAYER], F32)
    consts = sb("consts", [128, NCONST], F32)
    ident = sb("ident", [128, 128], BF16)
    mask2 = sb("mask2", [128, 256], BF16)
    onesm = sb("onesm", [128, 128], BF16)
    ones = sb("ones", [128, 128], BF16)
    onesP = sb("onesP", [128, 2, 128], BF16)
    DMA("sp", vecs[:], vecs_d, "vecs", [], ["vecs"])
    DMA("sp", consts[:], consts_d, "consts", [], ["consts"])
    CP("dve", ident[:], consts[:, C_ID:C_ID + 128], ["consts"], ["cb"])
    CP("dve", mask2[:], consts[:, C_MASK:C_MASK + 256], ["consts"], ["cb"])
    MS("pool", onesm[:], 1.0 / 1024.0, ["cb"])
    MS("pool", ones[:], 1.0, ["cb"])
    MS("pool", onesP[:], 0.0, ["cb"])
    MS("pool", onesP[:, 0, 0:64], 1.0, ["cb"])
    MS("pool", onesP[:, 1, 64:128], 1.0, ["cb"])

    sqring = Ring("sq", [128, T], BF16, 3)
    rstdring = Ring("rstd", [128, T], F32, 2)
    G = {}

    def mk_rings(alloc, x=False, h=False, hin=False, ysb=False):
        if x:
            G["x"] = Ring("xt", [128, KC, T], F32, 2, alloc)
        if h:
            G["h"] = Ring("ht", [128, KC, T], BF16, 2, alloc)
        if hin:
            G["hin"] = Ring("hin", [128, KC, T], BF16, 2, alloc)
        if ysb:
            G["ysb"] = Ring("ysb", [128, KC, T], F32, 2, alloc)
    tmp_ring = Ring("tmpf", [128, T], F32, 3)

    def xview(d, tt):
        return d.rearrange("(c p) t -> p c t", p=128)[:, :, tt * T:(tt + 1) * T]

    def rstd_from_bank(b):
        rstd, rk = rstdring.next()
        ACT(rstd[:], PB(b), AF.Ln, [("ps", b)], [rk], bias=EPS)
        ACT(rstd[:], rstd[:], AF.Exp, [rk], [rk], scale=-0.5)
        return rstd, rk

    def norm_tile(xt, xk, gcol, ht, hk, ncols=T):
        b = bank()
        for c in range(KC):
            sq, sqk = sqring.next()
            ACT(sq[:, 0:ncols], xt[:, c, 0:ncols], AF.Square, [xk], [sqk])
            MM(PB(b, 0, ncols), onesm[:], sq[:, 0:ncols], c == 0, c == KC - 1, [sqk, "cb"], [("ps", b)])
        rstd, rk = rstdring.next()
        ACT(rstd[:, 0:ncols], PB(b, 0, ncols), AF.Ln, [("ps", b)], [rk], bias=EPS)
        ACT(rstd[:, 0:ncols], rstd[:, 0:ncols], AF.Exp, [rk], [rk], scale=-0.5)
        for c in range(KC):
            eng = "dve"
            STT(eng, ht[:, c, 0:ncols], xt[:, c, 0:ncols], vecs[:, gcol + c:gcol + c + 1], rstd[:, 0:ncols],
                ALU.mult, ALU.mult, [xk, rk, "vecs"], [hk], waw=False)

    pncnt = [0]

    class PostNorm:
        def __init__(self):
            self.ysb, self.yk = G["ysb"].next()
            pncnt[0] += 1
            self.b = 6 + pncnt[0] % 2

        def chunk(self, mc, b):
            CP("act", self.ysb[:, mc, :], PB(b), [("ps", b)], [self.yk], waw=False)
            sq, sqk = sqring.next()
            TT("dve", sq[:], PB(b), self.ysb[:, mc, :], ALU.mult, [("ps", b), self.yk], [sqk])
            MM(PB(self.b), onesm[:], sq[:], mc == 0, mc == KC - 1, [sqk, "cb"], [("ps", self.b)])

        def finish(self, xt, xk, gcol):
            rstd, rk = rstd_from_bank(self.b)
            for c in range(KC):
                tmp, tk = tmp_ring.next()
                TT("pool", tmp[:], self.ysb[:, c, :], rstd[:], ALU.mult, [self.yk, rk], [tk])
                STT("dve", xt[:, c, :], tmp[:], vecs[:, gcol + c:gcol + c + 1], xt[:, c, :], ALU.mult, ALU.add,
                    [tk, xk, "vecs"], [xk], waw=False)

    def load_w(dst, src2d, ncols, chan, key, rows=128):
        v = src2d.rearrange("(k p) n -> p k n", p=rows)
        for cb in range(0, ncols, 2048):
            w = min(2048, ncols - cb)
            DMA("pool", dst[:, :, cb:cb + w], v[:, :, cb:cb + w], chan, [], [key])

    with ExitStack() as es0:
      if 'norope' not in opts:
          def sb0(name, shape, dt):
              return es0.enter_context(nc.sbuf_tensor(name, list(shape), dt))
          pi_ = sb0("rp_pi", [32, 128], I32)
          pf = sb0("rp_pf", [32, 128], F32)
          ang = sb0("rp_ang", [32, 128, 16], F32)
          a2 = sb0("rp_a2", [32, 128, 16], F32)
          ki = sb0("rp_ki", [32, 128, 16], I32)
          tab = sb0("rp_tab", [32, 128, 16], F32)
          DMA("sp", pi_[:], pos, "rp", [], ["rp_pi"])
          CP("dve", pf[:], pi_[:], ["rp_pi"], ["rp_pf"])
          inv_b = consts[0:32, C_INV:C_INV + 8].unsqueeze(1).to_broadcast([32, 128, 8])
          pf_b = pf[:].unsqueeze(2).to_broadcast([32, 128, 8])
          TT("dve", ang[:, :, 0:8], pf_b, inv_b, ALU.mult, ["rp_pf", "consts"], ["rp_ang"])
          TS("dve", ang[:, :, 8:16], ang[:, :, 0:8], float(np.pi / 2), None, ALU.add, None, ["rp_ang"], ["rp_ang"])
          TS("dve", a2[:], ang[:], float(1.0 / (2 * np.pi)), None, ALU.mult, None, ["rp_ang"], ["rp_a2"])
          CP("dve", ki[:], a2[:], ["rp_a2"], ["rp_ki"])
          CP("dve", a2[:], ki[:], ["rp_ki"], ["rp_a2"])
          STT("dve", ang[:], a2[:], float(-2 * np.pi), ang[:], ALU.mult, ALU.add, ["rp_a2", "rp_ang"], ["rp_ang"])
          TS("dve", a2[:], ang[:], float(np.pi), float(-2 * np.pi), ALU.is_gt, ALU.mult, ["rp_ang"], ["rp_a2"])
          TT("dve", ang[:], ang[:], a2[:], ALU.add, ["rp_ang", "rp_a2"], ["rp_ang"])
          TS("dve", a2[:], ang[:], float(-np.pi), float(2 * np.pi), ALU.is_lt, ALU.mult, ["rp_ang"], ["rp_a2"])
          TT("dve", ang[:], ang[:], a2[:], ALU.add, ["rp_ang", "rp_a2"], ["rp_ang"])
          ACT(tab[:, :, 0:8], ang[:, :, 8:16], AF.Sin, ["rp_ang"], ["rp_tab"])
          ACT(tab[:, :, 8:16], ang[:, :, 0:8], AF.Sin, ["rp_ang"], ["rp_tab"], waw=False)
          DMA("sp", rope_d.rearrange("(b i) f -> b i f", i=128), tab[:], "rp", ["rp_tab"], ["rope_d"])
          S.barrier()

    def phase_norm0(l):
      with ExitStack() as e1:
        mk_rings(lambda n, sh, dt: e1.enter_context(nc.sbuf_tensor(n, list(sh), dt)), x=True, h=True)
        for tt in range(NT):
            xt, xk = G["x"].next()
            DMA("sp", xt[:], xview(xT, tt), xk, [], [xk])
            ht, hk = G["h"].next()
            norm_tile(xt, xk, l * V_PER_LAYER + V_MIXPRE, ht, hk)
            DMA("pool", xview(h1T, tt), ht[:], hk, [hk], [("h1T", tt)])
        S.barrier()

    def phase_attn(l):
        with ExitStack() as e2:
            def sb2(name, shape, dt):
                return e2.enter_context(nc.sbuf_tensor(f"{name}_L{l}", list(shape), dt))
            hT = sb2("at_hT", [128, KC, SEQ], BF16)
            acc = sb2("at_acc", [128, 4, 2048], F32)
            attn_sb = sb2("at_out", [128, 2, 2048], BF16)
            wq = [sb2(f"at_w{i}", [128, KC, 1280], BF16) for i in range(2)]
            cs = [sb2(f"at_cs{g}", [128, 32, 16], F32) for g in range(3)]
            QK = [sb2(f"at_qk{i}", [128, 768], BF16) for i in range(2)]
            QF = [sb2(f"at_qf{i}", [128, 768], F32) for i in range(2)]
            ODT = [sb2(f"at_od{i}", [128, 512], F32) for i in range(2)]
            Vb = [sb2(f"at_v{i}", [128, 4, 128], BF16) for i in range(2)]
            QT = [sb2(f"at_qt{i}", [128, 2, 128], BF16) for i in range(2)]
            KTp = [sb2(f"at_kt{i}", [128, 4, 128], BF16) for i in range(2)]
            PT = [sb2(f"at_pt{i}", [128, 4, 256], BF16) for i in range(2)]
            rt = [sb2(f"at_rt{i}", [128, 4, 8, 8], F32) for i in range(2)]
            for tt in range(NT):
                DMA("sp", hT[:, :, tt * T:(tt + 1) * T], xview(h1T, tt), "at_hT", [("h1T", tt)], ["at_hT"])
            for g in range(3):
                d = DILS[g]
                nb = 32 // d
                for r in range(d):
                    for j in range(nb):
                        src = bass.AP(rope_d.tensor, (r + d * 128 * j) * 16, [[16 * d, 128], [1, 16]])
                        DMA("sp", cs[g][:, r * nb + j, :], src, f"at_cs{g}", ["rope_d"], [("cs", g)])
            wl = [0]
            cnt = {"qk": 0, "pt": 0, "rt": 0, "od": 0}

            def proj_block(g, r, j, need_q, wt, wk):
                d = DILS[g]
                nb = 32 // d
                slot = j % 2
                start = r + d * 128 * j
                sl = slice(start, start + 127 * d + 1, d)
                bq = bank() if need_q else None
                bk = bank()
                bv = bank()
                if need_q:
                    for kc in range(KC):
                        MM(PB(bq, 0, 256), hT[:, kc, sl], wt[:, kc, 0:256], kc == 0, kc == KC - 1, ["at_hT", wk], [("ps", bq)])
                for kc in range(KC):
                    MM(PB(bk), hT[:, kc, sl], wt[:, kc, 256:768], kc == 0, kc == KC - 1, ["at_hT", wk], [("ps", bk)])
                for kc in range(KC):
                    MM(PB(bv), hT[:, kc, sl], wt[:, kc, 768:1280], kc == 0, kc == KC - 1, ["at_hT", wk], [("ps", bv)])
                qi = cnt["qk"] % 2
                cnt["qk"] += 1
                qk, qkk = QK[qi], ("QK", qi)
                qf, qfk = QF[qi], ("QF", qi)
                if need_q:
                    CP("act", qf[:, 0:256], PB(bq, 0, 256), [("ps", bq)], [qfk])
                CP("act", qf[:, 256:768], PB(bk), [("ps", bk)], [qfk], waw=not need_q)
                lo = 0 if need_q else 256
                CP("act", qk[:, lo:768], qf[:, lo:768], [qfk], [qkk])
                CP("dve", Vb[slot][:].rearrange("p a b -> p (a b)"), PB(bv), [("ps", bv)], [("Vb", slot)])
                if 'pb1' in opts:
                    return
                blk = r * nb + j
                csk = ("cs", g)
                views = []
                if need_q:
                    views.append((qf[:, 0:256].rearrange("p (h e) -> p h e", e=64), qk[:, 0:256].rearrange("p (h e) -> p h e", e=64), [128, 4, 8], 0))
                kfv = bass.AP(qf, 256, [[768, 128], [256, 2], [192, 2], [1, 64]])
                kbv = bass.AP(qk, 256, [[768, 128], [256, 2], [192, 2], [1, 64]])
                views.append((kfv, kbv, [128, 2, 2, 8], 1))
                for (fv, bv_, shp, which) in views:
                    ri = cnt["rt"] % 2
                    cnt["rt"] += 1
                    rtt, rtk = rt[ri], ("rt", ri)
                    if which == 0:
                        cosb = cs[g][:, blk, 0:8].unsqueeze(1).to_broadcast(shp)
                        sinb = cs[g][:, blk, 8:16].unsqueeze(1).to_broadcast(shp)
                        u1, u2 = fv[:, :, 0:8], fv[:, :, 8:16]
                        o1, o2 = bv_[:, :, 0:8], bv_[:, :, 8:16]
                        tv = [rtt[:, k, 0:4, :] for k in range(4)]
                    else:
                        cosb = cs[g][:, blk, 0:8].unsqueeze(1).unsqueeze(1).to_broadcast(shp)
                        sinb = cs[g][:, blk, 8:16].unsqueeze(1).unsqueeze(1).to_broadcast(shp)
                        u1, u2 = fv[:, :, :, 0:8], fv[:, :, :, 8:16]
                        o1, o2 = bv_[:, :, :, 0:8], bv_[:, :, :, 8:16]
                        tv = [rtt[:, k, 0:4, :].rearrange("p (a b) e -> p a b e", a=2) for k in range(4)]
                    TT("dve", tv[0], u1, cosb, ALU.mult, [qfk, csk], [rtk])
                    TT("dve", tv[1], u2, sinb, ALU.mult, [qfk, csk], [rtk], waw=False)
                    TT("dve", tv[2], u2, cosb, ALU.mult, [qfk, csk], [rtk], waw=False)
                    TT("dve", tv[3], u1, sinb, ALU.mult, [qfk, csk], [rtk], waw=False)
                    TT("pool", o1, tv[0], tv[1], ALU.subtract, [rtk], [qkk])
                    TT("pool", o2, tv[2], tv[3], ALU.add, [rtk], [qkk])
                if 'pb2' in opts:
                    return
                if need_q:
                    bt = bank()
                    for ch in range(2):
                        MM(PB(bt, ch * 128, ch * 128 + 128), qk[:, ch * 128:(ch + 1) * 128], ident[:], True, True, [qkk, "cb"], [("ps", bt)])
                    CP("act", QT[slot][:].rearrange("p c t -> p (c t)"), PB(bt, 0, 256), [("ps", bt)], [("QT", slot)])
                bt2 = bank()
                for hh in range(4):
                    MM(PB(bt2, hh * 128, hh * 128 + 128), qk[:, 256 + hh * 128:256 + (hh + 1) * 128], ident[:], True, True, [qkk, "cb"], [("ps", bt2)])
                CP("dve", KTp[slot][:].rearrange("p a b -> p (a b)"), PB(bt2), [("ps", bt2)], [("KTp", slot)])

            def attend(g, r, j, half, first_group):
                d = DILS[g]
                sc, sp_ = j % 2, (j - 1) % 2
                b2 = bank(2)
                for hh in range(4):
                    ch = hh // 2
                    bb = b2 + hh // 2
                    c0 = (hh % 2) * 256
                    if j > 0:
                        MM(PB(bb, c0, c0 + 256), ident[:], mask2[:, 0:256], True, False, ["cb"], [("ps", bb)])
                        MM(PB(bb, c0, c0 + 128), KTp[sp_][:, hh, :], QT[sc][:, ch, :], False, False, [("KTp", sp_), ("QT", sc)], [("ps", bb)])
                    else:
                        MM(PB(bb, c0 + 128, c0 + 256), ident[:], mask2[:, 128:256], True, False, ["cb"], [("ps", bb)])
                    MM(PB(bb, c0 + 128, c0 + 256), KTp[sc][:, hh, :], QT[sc][:, ch, :], False, True, [("KTp", sc), ("QT", sc)], [("ps", bb)])
                pi = cnt["pt"] % 2
                cnt["pt"] += 1
                pt, ptk = PT[pi], ("PT", pi)
                for hb in range(2):
                    bb = b2 + hb
                    if j > 0:
                        ACT(pt[:, 2 * hb:2 * hb + 2, :].rearrange("p a b -> p (a b)"), PB(bb), AF.Exp, [("ps", bb)], [ptk], scale=0.125, waw=False)
                    else:
                        for a in range(2):
                            ACT(pt[:, 2 * hb + a, 128:256], PB(bb, a * 256 + 128, a * 256 + 256), AF.Exp,
                                [("ps", bb)], [ptk], scale=0.125, waw=False)
                b3 = bank()
                kbs = [0, 1] if j > 0 else [1]
                for od in range(2):
                    for pair in range(2):
                        mats = [(hh, kb) for hh in (2 * pair, 2 * pair + 1) for kb in kbs]
                        c0 = od * 256 + pair * 128
                        for i, (hh, kb) in enumerate(mats):
                            vs = sp_ if kb == 0 else sc
                            lhs = Vb[vs][:, hh, :] if od == 0 else onesP[:, hh % 2, :]
                            rd = [ptk, ("Vb", vs)] if od == 0 else [ptk, "cb"]
                            MM(PB(b3, c0, c0 + 128), lhs, pt[:, hh, kb * 128:(kb + 1) * 128], i == 0, i == len(mats) - 1, rd, [("ps", b3)])
                off = r + d * 128 * j - 2048 * half
                av = bass.AP(acc, off, [[4 * 2048, 128], [2048, 4], [d, 128]])
                oi = cnt["od"] % 2
                cnt["od"] += 1
                odt, odk = ODT[oi], ("ODT", oi)
                CP("act", odt[:], PB(b3), [("ps", b3)], [odk])
                pv = odt[:].rearrange("p (a t) -> p a t", t=128)
                if first_group:
                    CP("pool", av, pv, [odk], ["acc"], waw=True)
                else:
                    TT("pool", av, pv, av, ALU.add, [odk, "acc"], ["acc"])

            for half in range(2):
                for g in range(3):
                    d = DILS[g]
                    nb = 32 // d
                    wi = wl[0] % 2
                    wl[0] += 1
                    wt, wk = wq[wi], ("at_w", wi)
                    load_w(wt, w_qkv[l, g], 1280, f"at_w{wi}", wk)
                    jl, jh = half * nb // 2, (half + 1) * nb // 2
                    for r in range(d):
                        if 'at_loads' in opts:
                            continue
                        if jl > 0:
                            proj_block(g, r, jl - 1, False, wt, wk)
                        for j in range(jl, jh):
                            proj_block(g, r, j, True, wt, wk)
                            if 'at_proj' not in opts:
                                attend(g, r, j, half, g == 0)
                for pair in range(2):
                    ACT(acc[:, 2 + pair, :], acc[:, 2 + pair, :], AF.Ln, ["acc"], ["acc"])
                    ACT(acc[:, 2 + pair, :], acc[:, 2 + pair, :], AF.Exp, ["acc"], ["acc"], scale=-1.0)
                    TT("dve", attn_sb[:, pair, :], acc[:, pair, :], acc[:, 2 + pair, :], ALU.mult, ["acc"], ["attn_sb"])
                dv = attnT.rearrange("(c p) t -> p c t", p=128)[:, :, half * 2048:(half + 1) * 2048]
                DMA("pool", dv, attn_sb[:], "attn_sb", ["attn_sb"], [("attnT", half)])
            S.barrier()

    def phase_ab(l):
        vb = l * V_PER_LAYER
        with ExitStack() as e3:
            def sb3(name, shape, dt):
                return e3.enter_context(nc.sbuf_tensor(f"{name}_L{l}", list(shape), dt))
            mk_rings(sb3, hin=True)
            wab = sb3("ab_w", [128, KC, NAB], BF16)
            pw = sb3("ab_pw", [96, 4, 96], BF16)
            wa = sb3("ab_wa", [96, 4, D], BF16)
            wb = sb3("ab_wb", [128, 3, D], BF16)
            A = [sb3(f"ab_A{g}", [96, 16 + T], F32) for g in range(4)]
            T2 = sb3("ab_T2", [96, 16 + T], F32)
            T4 = sb3("ab_T4", [96, 16 + T], F32)
            T8 = sb3("ab_T8", [96, 16 + T], F32)
            PL = [sb3(f"ab_PL{g}", [96, T], BF16) for g in range(4)]
            MX = [sb3(f"ab_MX{g}", [96, T], BF16) for g in range(4)]
            BX = sb3("ab_BX", [128, T], F32)
            P = [sb3(f"ab_P{c}", [128, 2 + T], F32) for c in range(3)]
            Y1 = sb3("ab_Y1", [128, T], F32)
            Y2 = sb3("ab_Y2", [128, T], F32)
            Z = [sb3(f"ab_Z{c}", [128, T], BF16) for c in range(3)]
            SG = [sb3(f"ab_SG{i}", [128, T], F32) for i in range(2)]
            MO = [sb3(f"ab_MO{i}", [128, KC, T], F32) for i in range(2)]
            load_w(wab, w_in[l][:, 0:NAB], NAB, "ab_w", "ab_w")
            DMA("pool", pw[:], pool_w[l].rearrange("g c d -> c g d"), "ab_w2", [], ["ab_w2"])
            DMA("pool", wa[:], w_a[l].rearrange("(g c) n -> c g n", c=96), "ab_w2", [], ["ab_w2"])
            DMA("pool", wb[:], w_b[l].rearrange("(k p) n -> p k n", p=128), "ab_w2", [], ["ab_w2"])
            for g in range(4):
                MS("pool", A[g][:, 0:16], 0.0, [("A", g)])
            for c in range(3):
                MS("pool", P[c][:, 0:2], 0.0, [("P", c)])
            sgi = [0]
            for tt in range(NT):
                ht, hk = G["hin"].next()
                DMA("sp", ht[:], xview(h1T, tt), hk, [("h1T", tt)], [hk])
                mo, mok = MO[tt % 2], ("MO", tt % 2)
                for g in range(4):
                    w = (2, 4, 8, 16)[g]
                    Ak = ("A", g)
                    if tt > 0:
                        CP("pool", A[g][:, 0:16], A[g][:, T:T + 16], [Ak], [Ak])
                    b = bank()
                    for kc in range(KC):
                        MM(ps[0:96, b * 512:(b + 1) * 512], wab[:, kc, OFF_A + 96 * g:OFF_A + 96 * (g + 1)], ht[:, kc, :], kc == 0, kc == KC - 1,
                           [hk, "ab_w"], [("ps", b)])
                    CP("act", A[g][:, 16:16 + T], ps[0:96, b * 512:(b + 1) * 512], [("ps", b)], [Ak])
                    src = A[g]
                    srck = Ak
                    n = 1
                    for (dst, dk) in ((T2, "T2"), (T4, "T4"), (T8, "T8"), (None, None)):
                        if n >= w:
                            break
                        if 2 * n == w:
                            tmpw, tmpk = T2 if dst is not T2 and src is not T2 else (T4 if src is not T4 else T8), None
                            tmpw = {1: T2, 2: T4, 4: T8, 8: T2}[n]
                            tmpk = {1: "T2", 2: "T4", 4: "T8", 8: "T2"}[n]
                            TT("pool", tmpw[:, 16:16 + T], src[:, 16:16 + T], src[:, 16 - n:16 - n + T], ALU.add, [srck], [tmpk])
                            STT("dve", PL[g][:], tmpw[:, 16:16 + T], 1.0 / w, A[g][:, 16:16 + T], ALU.mult, ALU.subtract, [tmpk, Ak], [("PL", g)])
                            if tt == 0:
                                tmp, tk = tmp_ring.next()
                                TT("pool", tmp[0:96, 0:16], tmpw[:, 16:32], consts[0:96, C_RC + 16 * g:C_RC + 16 * g + 16], ALU.mult, [tmpk, "consts"], [tk])
                                TT("pool", PL[g][:, 0:16], tmp[0:96, 0:16], A[g][:, 16:32], ALU.subtract, [tk, Ak], [("PL", g)])
                            break
                        TT("pool", dst[:, 2 * n - 1:16 + T], src[:, 2 * n - 1:16 + T], src[:, n - 1:16 + T - n], ALU.add, [srck], [dk])
                        src, srck = dst, dk
                        n *= 2
                    b2 = bank()
                    MM(ps[0:96, b2 * 512:(b2 + 1) * 512], pw[:, g, :], PL[g][:], True, True, [("PL", g), "ab_w2"], [("ps", b2)])
                    ACT(MX[g][:], ps[0:96, b2 * 512:(b2 + 1) * 512], AF.Copy, [("ps", b2), "vecs"], [("MX", g)],
                        scale=vecs[0:96, vb + V_PSCALE + g:vb + V_PSCALE + g + 1])
                for mc in range(KC):
                    b = bank()
                    for g in range(4):
                        MM(PB(b), wa[:, g, mc * 128:(mc + 1) * 128], MX[g][:], g == 0, g == 3, [("MX", g), "ab_w2"], [("ps", b)])
                    bg = bank()
                    for kc in range(KC):
                        MM(PB(bg), wab[:, kc, OFF_GA + mc * 128:OFF_GA + (mc + 1) * 128], ht[:, kc, :], kc == 0, kc == KC - 1, [hk, "ab_w"], [("ps", bg)])
                    sg, sgk = SG[sgi[0] % 2], ("SG", sgi[0] % 2)
                    sgi[0] += 1
                    ACT(sg[:], PB(bg), AF.Sigmoid, [("ps", bg)], [sgk])
                    TT("dve", mo[:, mc, :], PB(b), sg[:], ALU.mult, [("ps", b), sgk], [mok], waw=False)
                for c in range(3):
                    Pk = ("P", c)
                    if tt > 0:
                        CP("pool", P[c][:, 0:2], P[c][:, T:T + 2], [Pk], [Pk])
                    bx, bb_, bc = bank(), bank(), bank()
                    for (bk, off) in ((bx, OFF_BX), (bb_, OFF_BB), (bc, OFF_BC)):
                        for kc in range(KC):
                            MM(PB(bk), wab[:, kc, off + c * 128:off + (c + 1) * 128], ht[:, kc, :], kc == 0, kc == KC - 1, [hk, "ab_w"], [("ps", bk)])
                    CP("act", BX[:], PB(bx), [("ps", bx)], ["BX"])
                    TT("dve", P[c][:, 2:2 + T], PB(bc), BX[:], ALU.mult, [("ps", bc), "BX"], [Pk])
                    cw = vb + V_CONVB
                    ACT(Y1[:], P[c][:, 2:2 + T], AF.Copy, [Pk, "vecs"], ["Y1"], scale=vecs[:, cw + 2 * 3 + c:cw + 2 * 3 + c + 1])
                    STT("dve", Y2[:], P[c][:, 1:1 + T], vecs[:, cw + 1 * 3 + c:cw + 1 * 3 + c + 1], Y1[:], ALU.mult, ALU.add, [Pk, "Y1", "vecs"], ["Y2"])
                    STT("dve", Y1[:], P[c][:, 0:T], vecs[:, cw + 0 * 3 + c:cw + 0 * 3 + c + 1], Y2[:], ALU.mult, ALU.add, [Pk, "Y2", "vecs"], ["Y1"])
                    TT("dve", Z[c][:], PB(bb_), Y1[:], ALU.mult, [("ps", bb_), "Y1"], [("Z", c)])
                for mc in range(KC):
                    b = bank()
                    for c in range(3):
                        MM(PB(b), wb[:, c, mc * 128:(mc + 1) * 128], Z[c][:], c == 0, c == 2, [("Z", c), "ab_w2"], [("ps", b)])
                    bg = bank()
                    for kc in range(KC):
                        MM(PB(bg), wab[:, kc, OFF_GB + mc * 128:OFF_GB + (mc + 1) * 128], ht[:, kc, :], kc == 0, kc == KC - 1, [hk, "ab_w"], [("ps", bg)])
                    sg, sgk = SG[sgi[0] % 2], ("SG", sgi[0] % 2)
                    sgi[0] += 1
                    ACT(sg[:], PB(bg), AF.Sigmoid, [("ps", bg)], [sgk])
                    tmp, tk = tmp_ring.next()
                    TT("dve", tmp[:], PB(b), sg[:], ALU.mult, [("ps", b), sgk], [tk])
                    TT("pool", mo[:, mc, :], mo[:, mc, :], tmp[:], ALU.add, [mok, tk], [mok], waw=False)
                DMA("pool", xview(mab, tt), mo[:], mok, [mok], [("mab", tt)])
            S.barrier()

    def phase_merge(l):
        vb = l * V_PER_LAYER
        xsrc = xT if l == 0 else xs
        with ExitStack() as e4:
            def sb4(name, shape, dt):
                return e4.enter_context(nc.sbuf_tensor(f"{name}_L{l}", list(shape), dt))
            mk_rings(sb4, x=True, h=True, hin=True, ysb=True)
            wgc = sb4("mg_wgc", [128, KC, D], BF16)
            wc = sb4("mg_wc", [128, 2, D], BF16)
            wo = sb4("mg_wo", [128, KC, D], BF16)
            AT = [sb4(f"mg_at{i}", [128, 2, T], BF16) for i in range(2)]
            MI = [sb4(f"mg_mi{i}", [128, KC, T], F32) for i in range(2)]
            MGs = [sb4(f"mg_mg{i}", [128, KC, T], BF16) for i in range(2)]
            SG = [sb4(f"mg_SG{i}", [128, T], F32) for i in range(2)]
            load_w(wgc, w_in[l][:, OFF_GC:OFF_GC + D], D, "mg_w", "mg_w")
            load_w(wc, w_c[l], D, "mg_w", "mg_w")
            load_w(wo, w_out[l], D, "mg_wo", "mg_wo")
            st = {}

            def stA(tt):
                ht, hk = G["hin"].next()
                DMA("sp", ht[:], xview(h1T, tt), hk, [("h1T", tt)], [hk])
                at, atk = AT[tt % 2], ("AT", tt % 2)
                DMA("sp", at[:], attnT.rearrange("(c p) t -> p c t", p=128)[:, :, tt * T:(tt + 1) * T], f"mg_at{tt % 2}", [("attnT", tt // 4)], [atk])
                mi, mik = MI[tt % 2], ("MI", tt % 2)
                DMA("sp", mi[:], xview(mab, tt), f"mg_mi{tt % 2}", [("mab", tt)], [mik])
                xt, xk = G["x"].next()
                DMA("sp", xt[:], xview(xsrc, tt), xk, [("xs", tt)], [xk])
                mg, mgk = MGs[tt % 2], ("MG", tt % 2)
                for mc in range(KC):
                    b = bank()
                    for pr in range(2):
                        MM(PB(b), wc[:, pr, mc * 128:(mc + 1) * 128], at[:, pr, :], pr == 0, pr == 1, [atk, "mg_w"], [("ps", b)])
                    bg = bank()
                    for kc in range(KC):
                        MM(PB(bg), wgc[:, kc, mc * 128:(mc + 1) * 128], ht[:, kc, :], kc == 0, kc == KC - 1, [hk, "mg_w"], [("ps", bg)])
                    sg, sgk = SG[mc % 2], ("SG4", mc % 2)
                    ACT(sg[:], PB(bg), AF.Sigmoid, [("ps", bg)], [sgk])
                    tmp, tk = tmp_ring.next()
                    TT("dve", tmp[:], PB(b), sg[:], ALU.mult, [("ps", b), sgk], [tk])
                    TT("pool", mg[:, mc, :], tmp[:], mi[:, mc, :], ALU.add, [tk, mik], [mgk], waw=False)
                st[tt] = dict(xt=xt, xk=xk, mg=mg, mgk=mgk)

            def stW(tt):
                d_ = st[tt]
                pn = PostNorm()
                for mc in range(KC):
                    b = bank()
                    for kc in range(KC):
                        MM(PB(b), wo[:, kc, mc * 128:(mc + 1) * 128], d_["mg"][:, kc, :], kc == 0, kc == KC - 1, [d_["mgk"], "mg_wo"], [("ps", b)])
                    pn.chunk(mc, b)
                d_["pn"] = pn

            def stF1(tt):
                d_ = st[tt]
                xt, xk = d_["xt"], d_["xk"]
                d_["pn"].finish(xt, xk, vb + V_MIXPOST)
                DMA("pool", xview(xs, tt), xt[:], xk, [xk], [("xs", tt)])

            def stF2(tt):
                d_ = st.pop(tt)
                xt, xk = d_["xt"], d_["xk"]
                h2, h2k = G["h"].next()
                norm_tile(xt, xk, vb + V_MEMPRE, h2, h2k)
                DMA("pool", xview(h2T, tt), h2[:], h2k, [h2k], [("h2T", tt)])

            stA(0)
            stW(0)
            for tt in range(1, NT):
                stF1(tt - 1)
                stA(tt)
                stF2(tt - 1)
                stW(tt)
            stF1(NT - 1)
            stF2(NT - 1)
            S.barrier()

    def phase_mem(l):
        vb = l * V_PER_LAYER
        with ExitStack() as e5:
            def sb5(name, shape, dt):
                return e5.enter_context(nc.sbuf_tensor(f"{name}_L{l}", list(shape), dt))
            mk_rings(sb5, x=True, h=True, hin=True, ysb=True)
            wq_ = sb5("mm_wq", [128, KC, 512], BF16)
            wkv = sb5("mm_wkv", [128, KC, 1024], BF16)
            wo = sb5("mm_wo", [128, 4, D], BF16)
            mt = sb5("mm_mt", [128, KC, 256], F32)
            mn = sb5("mm_mn", [128, KC, 256], BF16)
            KmT = sb5("mm_KmT", [128, 4, 256], BF16)
            Vm = sb5("mm_Vm", [128, 2, 512], BF16)
            QM = [sb5(f"mm_QM{i}", [128, T], BF16) for i in range(4)]
            PTm = [sb5(f"mm_PT{i}", [128, 2, T], BF16) for i in range(4)]
            DN = [sb5(f"mm_DN{i}", [128, T], F32) for i in range(2)]
            OMs = [sb5(f"mm_OM{i}", [128, 4, T], BF16) for i in range(2)]
            load_w(wkv, w_mkv[l], 1024, "mm_w", "mm_w")
            load_w(wq_, w_mq[l], 512, "mm_w", "mm_w")
            load_w(wo, w_mo[l], D, "mm_wo", "mm_wo")
            DMA("sp", mt[:], memT.rearrange("(c p) t -> p c t", p=128), "mm_mt", [], ["mm_mt"])
            norm_tile(mt, "mm_mt", vb + V_MEMKV, mn, "mm_mn", ncols=256)
            for h in range(4):
                b = bank()
                for kc in range(KC):
                    MM(PB(b, 0, 256), wkv[:, kc, h * 128:(h + 1) * 128], mn[:, kc, 0:256], kc == 0, kc == KC - 1, ["mm_mn", "mm_w"], [("ps", b)])
                CP("act", KmT[:, h, :], PB(b, 0, 256), [("ps", b)], ["KmT"], waw=False)
            for mi_ in range(2):
                b = bank()
                for kc in range(KC):
                    MM(PB(b), mn[:, kc, mi_ * 128:(mi_ + 1) * 128], wkv[:, kc, 512:1024], kc == 0, kc == KC - 1, ["mm_mn", "mm_w"], [("ps", b)])
                CP("act", Vm[:, mi_, :], PB(b), [("ps", b)], ["Vm"], waw=False)
            sc = float(128 ** -0.5)
            st = {}

            def stA(tt):
                ht, hk = G["hin"].next()
                DMA("sp", ht[:], xview(h2T, tt), hk, [("h2T", tt)], [hk])
                xt, xk = G["x"].next()
                DMA("sp", xt[:], xview(xs, tt), xk, [("xs", tt)], [xk])
                om, omk = OMs[tt % 2], ("OM", tt % 2)
                for h in range(4):
                    b = bank()
                    for kc in range(KC):
                        MM(PB(b), wq_[:, kc, h * 128:(h + 1) * 128], ht[:, kc, :], kc == 0, kc == KC - 1, [hk, "mm_w"], [("ps", b)])
                    CP("act", QM[h][:], PB(b), [("ps", b)], [("QM", h)])
                for h in range(4):
                    for mi_ in range(2):
                        bs = bank()
                        MM(PB(bs), KmT[:, h, mi_ * 128:(mi_ + 1) * 128], QM[h][:], True, True, ["KmT", ("QM", h)], [("ps", bs)])
                        ACT(PTm[h][:, mi_, :], PB(bs), AF.Exp, [("ps", bs)], [("PTm", h)], scale=sc, waw=False)
                for h in range(4):
                    pt, ptk = PTm[h], ("PTm", h)
                    bo, bd = bank(), bank()
                    for mi_ in range(2):
                        MM(PB(bo), Vm[:, mi_, h * 128:(h + 1) * 128], pt[:, mi_, :], mi_ == 0, mi_ == 1, ["Vm", ptk], [("ps", bo)])
                    for mi_ in range(2):
                        MM(PB(bd), ones[:], pt[:, mi_, :], mi_ == 0, mi_ == 1, ["cb", ptk], [("ps", bd)])
                    dn, dnk = DN[h % 2], ("DN", h % 2)
                    ACT(dn[:], PB(bd), AF.Ln, [("ps", bd)], [dnk])
                    ACT(dn[:], dn[:], AF.Exp, [dnk], [dnk], scale=-1.0)
                    TT("dve", om[:, h, :], PB(bo), dn[:], ALU.mult, [("ps", bo), dnk], [omk], waw=False)
                st[tt] = dict(xt=xt, xk=xk, om=om, omk=omk)

            def stW(tt):
                d_ = st[tt]
                pn = PostNorm()
                for mc in range(KC):
                    b = bank()
                    for h in range(4):
                        MM(PB(b), wo[:, h, mc * 128:(mc + 1) * 128], d_["om"][:, h, :], h == 0, h == 3, [d_["omk"], "mm_wo"], [("ps", b)])
                    pn.chunk(mc, b)
                d_["pn"] = pn

            def stF1(tt):
                d_ = st[tt]
                xt, xk = d_["xt"], d_["xk"]
                d_["pn"].finish(xt, xk, vb + V_MEMPOST)
                DMA("pool", xview(xs, tt), xt[:], xk, [xk], [("xs", tt)])

            def stF2(tt):
                d_ = st.pop(tt)
                xt, xk = d_["xt"], d_["xk"]
                h3, h3k = G["h"].next()
                norm_tile(xt, xk, vb + V_FFNPRE, h3, h3k)
                DMA("pool", xview(h3T, tt), h3[:], h3k, [h3k], [("h3T", tt)])

            stA(0)
            stW(0)
            for tt in range(1, NT):
                stF1(tt - 1)
                stA(tt)
                stF2(tt - 1)
                stW(tt)
            stF1(NT - 1)
            stF2(NT - 1)
            S.barrier()

    def phase_up(l):
        vb = l * V_PER_LAYER
        with ExitStack() as e6:
            def sb6(name, shape, dt):
                return e6.enter_context(nc.sbuf_tensor(f"{name}_L{l}", list(shape), dt))
            mk_rings(sb6, hin=True)
            wu = sb6("up_w", [128, KC, 2 * DFF], BF16)
            H = sb6("up_H", [128, FC, 2], F32)
            UA = [sb6(f"up_UA{i}", [128, 2 + T], F32) for i in range(2)]
            Y1 = [sb6(f"up_Y1{i}", [128, T], F32) for i in range(2)]
            Y2 = [sb6(f"up_Y2{i}", [128, T], F32) for i in range(2)]
            AO = [sb6(f"up_AO{i}", [128, FC, T], BF16) for i in range(2)]
            load_w(wu, w_up[l], 2 * DFF, "up_w", "up_w")
            MS("pool", H[:], 0.0, ["H"])
            cw = vb + V_CONVF
            for tt in range(NT):
                ht, hk = G["hin"].next()
                DMA("sp", ht[:], xview(h3T, tt), hk, [("h3T", tt)], [hk])
                ao, aok = AO[tt % 2], ("AO", tt % 2)
                for c in range(FC):
                    ba, bb_ = bank(), bank()
                    for kc in range(KC):
                        MM(PB(ba), wu[:, kc, c * 128:(c + 1) * 128], ht[:, kc, :], kc == 0, kc == KC - 1, [hk, "up_w"], [("ps", ba)])
                    for kc in range(KC):
                        MM(PB(bb_), wu[:, kc, DFF + c * 128:DFF + (c + 1) * 128], ht[:, kc, :], kc == 0, kc == KC - 1, [hk, "up_w"], [("ps", bb_)])
                    i = c % 2
                    ua, uak = UA[i], ("UA", i)
                    y1, y1k = Y1[i], ("Y1", i)
                    y2, y2k = Y2[i], ("Y2", i)
                    CP("pool", ua[:, 0:2], H[:, c, :], ["H"], [uak])
                    CP("act", ua[:, 2:2 + T], PB(ba), [("ps", ba)], [uak], waw=False)
                    CP("pool", H[:, c, :], ua[:, T:T + 2], [uak], ["H"])
                    ACT(y1[:], ua[:, 2:2 + T], AF.Copy, [uak, "vecs"], [y1k], scale=vecs[:, cw + 2 * FC + c:cw + 2 * FC + c + 1])
                    STT("dve", y2[:], ua[:, 1:1 + T], vecs[:, cw + 1 * FC + c:cw + 1 * FC + c + 1], y1[:], ALU.mult, ALU.add, [uak, y1k, "vecs"], [y2k])
                    STT("dve", y1[:], ua[:, 0:T], vecs[:, cw + 0 * FC + c:cw + 0 * FC + c + 1], y2[:], ALU.mult, ALU.add, [uak, y2k, "vecs"], [y1k])
                    ACT(y2[:], y1[:], AF.Silu, [y1k], [y2k])
                    TT("dve", ao[:, c, :], PB(bb_), y2[:], ALU.mult, [("ps", bb_), y2k], [aok], waw=False)
                DMA("pool", actT.rearrange("(c p) t -> p c t", p=128)[:, :, tt * T:(tt + 1) * T], ao[:], f"up_ao{tt % 2}", [aok], [("actT", tt)])
            S.barrier()

    def phase_down(l, last):
        vb = l * V_PER_LAYER
        with ExitStack() as e7:
            def sb7(name, shape, dt):
                return e7.enter_context(nc.sbuf_tensor(f"{name}_L{l}", list(shape), dt))
            mk_rings(sb7, x=True, h=True, ysb=True)
            wd = sb7("dn_w", [128, FC, D], BF16)
            AI = [sb7(f"dn_AI{i}", [128, FC, T], BF16) for i in range(2)]
            for q4 in range(4):
                DMA("pool", wd[:, :, q4 * 256:(q4 + 1) * 256], w_down[l].rearrange("(k p) n -> p k n", p=128)[:, :, q4 * 256:(q4 + 1) * 256],
                    "dn_w", [], [("dn_w", q4)])
            st = {}

            def stA(tt):
                ai, aik = AI[tt % 2], ("AI", tt % 2)
                DMA("sp", ai[:], actT.rearrange("(c p) t -> p c t", p=128)[:, :, tt * T:(tt + 1) * T], f"dn_ai{tt % 2}", [("actT", tt)], [aik])
                xt, xk = G["x"].next()
                DMA("sp", xt[:], xview(xs, tt), xk, [("xs", tt)], [xk])
                st[tt] = dict(xt=xt, xk=xk, ai=ai, aik=aik, pn=PostNorm())

            def stW(tt, mcs):
                d_ = st[tt]
                for mc in mcs:
                    b = bank()
                    for c in range(FC):
                        MM(PB(b), wd[:, c, mc * 128:(mc + 1) * 128], d_["ai"][:, c, :], c == 0, c == FC - 1, [d_["aik"], ("dn_w", mc // 2)], [("ps", b)])
                    d_["pn"].chunk(mc, b)

            def stF1(tt):
                d_ = st[tt]
                xt, xk = d_["xt"], d_["xk"]
                d_["pn"].finish(xt, xk, vb + V_FFNPOST)
                if last:
                    DMA("pool", xview(outT, tt), xt[:], xk, [xk], [("outT", tt)])
                else:
                    DMA("pool", xview(xs, tt), xt[:], xk, [xk], [("xs", tt)])

            def stF2(tt):
                d_ = st.pop(tt)
                xt, xk = d_["xt"], d_["xk"]
                if not last:
                    h1, h1k = G["h"].next()
                    norm_tile(xt, xk, (l + 1) * V_PER_LAYER + V_MIXPRE, h1, h1k)
                    DMA("pool", xview(h1T, tt), h1[:], h1k, [h1k], [("h1T", tt)])

            stA(0)
            stW(0, range(KC))
            for tt in range(1, NT):
                stF1(tt - 1)
                stA(tt)
                stW(tt, range(0, 4))
                stF2(tt - 1)
                stW(tt, range(4, KC))
            stF1(NT - 1)
            stF2(NT - 1)
            S.barrier()


    return dict(nc=nc, S=S, es=es, phases=dict(norm0=phase_norm0, attn=phase_attn, ab=phase_ab, merge=phase_merge,
                                               mem=phase_mem, up=phase_up, down=phase_down))


def build_full(n_layers=NL, dbg=False, stop=None, opts=()):
    P = build_program(n_layers, dbg, opts)
    ph = P["phases"]
    seq = []
    if "nonorm0" not in opts:
        ph["norm0"](0)
    done = stop is not None and stop[1] == "norm0"
    for l in range(n_layers):
        if done:
            break
        for name in ("attn", "ab", "merge", "mem", "up", "down"):
            if name == "down":
                ph[name](l, l == n_layers - 1)
            else:
                ph[name](l)
            if stop is not None and stop == (l, name):
                done = True
                break
        if done:
            break
    nsem = P["S"].emit()
    P["es"].close()
    return P["nc"], P["S"], nsem


def host_inputs(inputs):
    import ml_dtypes
    f32 = np.float32
    perm = w_in_perm()
    wip = np.asarray(inputs["w_in"], f32)[:, :, perm]
    wqkv = np.zeros((NL, 3, D, 1280), f32)
    for g in range(3):
        base = OFF_QKV + 768 * g
        wqkv[:, g, :, 0:256] = wip[:, :, base:base + 256]
        for hh in range(4):
            par = hh % 2
            wqkv[:, g, :, 256 + hh * 128 + 64 * par:256 + hh * 128 + 64 * par + 64] = wip[:, :, base + 256 + 64 * hh:base + 256 + 64 * hh + 64]
            wqkv[:, g, :, 768 + hh * 128 + 64 * par:768 + hh * 128 + 64 * par + 64] = wip[:, :, base + 512 + 64 * hh:base + 512 + 64 * hh + 64]
    shared = {
        "w_in": np.ascontiguousarray(wip[:, :, 0:OFF_QKV]),
        "w_qkv": wqkv,
        "pool_w": np.ascontiguousarray(np.asarray(inputs["pool_w"], f32)),
    }
    for k in ("w_branch_a", "w_branch_b", "w_branch_c", "w_out", "w_mq", "w_mkv", "w_mo", "w_up", "w_down"):
        shared[k] = np.ascontiguousarray(np.asarray(inputs[k], f32))
    vecs = np.zeros((128, NL * V_PER_LAYER), f32)
    for l in range(NL):
        vb = l * V_PER_LAYER
        for name, off in (("norm_mix_pre", V_MIXPRE), ("norm_mix_post", V_MIXPOST), ("norm_mem_pre", V_MEMPRE),
                          ("norm_mem_post", V_MEMPOST), ("norm_memkv", V_MEMKV), ("norm_ffn_pre", V_FFNPRE),
                          ("norm_ffn_post", V_FFNPOST)):
            vecs[:, vb + off:vb + off + 8] = np.asarray(inputs[name], f32)[l].reshape(8, 128).T
        vecs[:, vb + V_CONVB:vb + V_CONVB + 9] = np.asarray(inputs["conv_b_w"], f32)[l].reshape(3, 3, 128).transpose(2, 0, 1).reshape(128, 9)
        vecs[:, vb + V_CONVF:vb + V_CONVF + 66] = np.asarray(inputs["conv_ffn_w"], f32)[l].reshape(3, FC, 128).transpose(2, 0, 1).reshape(128, 66)
        vecs[0:96, vb + V_PSCALE:vb + V_PSCALE + 4] = np.asarray(inputs["pool_scale"], f32)[l].reshape(4, 96).T
    consts = np.zeros((128, NCONST), f32)
    consts[:, C_ID:C_ID + 128] = np.eye(128, dtype=f32)
    k = np.arange(128)[:, None]
    q = np.arange(128)[None, :]
    consts[:, C_MASK:C_MASK + 128] = np.where(k >= q, 0.0, -30000.0)
    consts[:, C_MASK + 128:C_MASK + 256] = np.where(k <= q, 0.0, -30000.0)
    consts[:, C_INV:C_INV + 8] = (f32(500000.0) ** (-np.arange(0, 16, 2, dtype=f32) / f32(16)))[None, :]
    for g, w in enumerate((2, 4, 8, 16)):
        consts[:, C_RC + 16 * g:C_RC + 16 * g + 16] = (1.0 / np.minimum(np.arange(16) + 1, w))[None, :]
    shared["vecs"] = vecs
    shared["consts"] = consts
    x = np.asarray(inputs["x"], f32)
    mem = np.asarray(inputs["mem"], f32)
    posn = np.asarray(inputs["positions"]).astype(np.int32)
    in_maps = []
    for b in range(8):
        m = dict(shared)
        m["xT"] = np.ascontiguousarray(x[b].T)
        m["memT"] = np.ascontiguousarray(mem[b].T)
        m["pos"] = np.ascontiguousarray(posn[b].reshape(32, 128))
        in_maps.append(m)
    return in_maps


_CACHE = {}


def kernel(**inputs):
    in_maps = host_inputs(inputs)
    if "nc" not in _CACHE:
        _CACHE["nc"] = build_full()[0]
    nc = _CACHE["nc"]
    res = run_bass_kernel_spmd(nc, in_maps, core_ids=list(range(8)))
    out = np.stack([np.ascontiguousarray(res.results[b]["outT"].T) for b in range(8)], axis=0)
    return out.astype(np.float32)
```

```python
import numpy as np
import concourse.bass as bass
import concourse.mybir as mybir

F32 = mybir.dt.float32
BF16 = mybir.dt.bfloat16
I32 = mybir.dt.int32
ALU = mybir.AluOpType
AF = mybir.ActivationFunctionType

EPOCH = 30000


class Sched:
    ENGS = ("pe", "act", "dve", "pool", "sp")

    def __init__(self, nc):
        self.nc = nc
        self.stream = {e: [] for e in self.ENGS}
        self.cnt = {e: 0 for e in self.ENGS}
        self.known = {e: {} for e in self.ENGS}
        self.res = {}
        self.chan_cnt = {}
        self.n_wait = 0

    def _collect(self, eng, reads, writes, waw):
        waits = {}

        def need(k, v):
            if k == ("eng", "pe") and eng == "pe":
                return
            if self.known[eng].get(k, 0) >= v:
                return
            if waits.get(k, 0) < v:
                waits[k] = v

        for r in reads:
            st = self.res.get(r)
            if st:
                for k, v in st["w"].items():
                    need(k, v)
        for w in writes:
            st = self.res.get(w)
            if st:
                for k, v in st["r"].items():
                    need(k, v)
                if waw:
                    for k, v in st["w"].items():
                        need(k, v)
        for k, v in waits.items():
            self.known[eng][k] = v
        return sorted(waits.items(), key=lambda kv: str(kv[0]))

    def _commit(self, ev, reads, writes, waw):
        k, v = ev
        for r in reads:
            st = self.res.setdefault(r, {"w": {}, "r": {}})
            if st["r"].get(k, 0) < v:
                st["r"][k] = v
        for w in writes:
            st = self.res.setdefault(w, {"w": {}, "r": {}})
            if st["r"] or waw:
                st["w"] = {}
            st["r"] = {}
            if st["w"].get(k, 0) < v:
                st["w"][k] = v

    def op(self, eng, fn, reads=(), writes=(), waw=True):
        waits = self._collect(eng, reads, writes, waw)
        self.cnt[eng] += 1
        ev = (("eng", eng), self.cnt[eng])
        self._commit(ev, reads, writes, waw)
        self.stream[eng].append((waits, fn, ev))
        self.n_wait += len(waits)

    def dma(self, eng, fn, chan, reads=(), writes=(), waw=False):
        waits = self._collect(eng, reads, writes, waw)
        self.chan_cnt[chan] = self.chan_cnt.get(chan, 0) + 16
        ev = (("chan", chan), self.chan_cnt[chan])
        self._commit(ev, reads, writes, waw)
        self.stream[eng].append((waits, fn, ev))
        self.n_wait += len(waits)

    def barrier(self, engs=None):
        engs = engs or self.ENGS
        for e in engs:
            waits = {}
            for e2 in self.ENGS:
                if e2 != e and self.cnt[e2] > 0:
                    k = ("eng", e2)
                    if self.known[e].get(k, 0) < self.cnt[e2]:
                        waits[k] = self.cnt[e2]
            for c, v in self.chan_cnt.items():
                k = ("chan", c)
                if self.known[e].get(k, 0) < v:
                    waits[k] = v
            for k, v in waits.items():
                self.known[e][k] = v
            if waits:
                self.stream[e].append((sorted(waits.items(), key=lambda kv: str(kv[0])), None, None))

    def emit(self):
        nc = self.nc
        sems = {}

        def sem_of(k, v):
            if k[0] == "eng":
                ep = (v - 1) // EPOCH
                key = (k, ep)
                val = (v - 1) % EPOCH + 1
            else:
                key = (k, 0)
                val = v
            if key not in sems:
                sems[key] = nc.alloc_semaphore(name=f"s{len(sems)}")
            return sems[key], val

        self.barrier(engs=("sp",))
        for e in self.ENGS:
            for waits, fn, ev in self.stream[e]:
                for k, v in waits:
                    sem_of(k, v)
                if ev is not None:
                    sem_of(*ev)
        eng_map = {"pe": "tensor", "act": "scalar", "dve": "vector", "pool": "gpsimd", "sp": "sync"}
        with nc.Block() as block:
            for e in self.ENGS:
                if not self.stream[e]:
                    continue
                deco = getattr(block, eng_map[e])

                def body(h, e=e):
                    for waits, fn, ev in self.stream[e]:
                        for k, v in waits:
                            s, val = sem_of(k, v)
                            h.wait_ge(s, val)
                        if fn is None:
                            continue
                        ins = fn(h)
                        s, _ = sem_of(*ev)
                        ins.then_inc(s, 16 if ev[0][0] == "chan" else 1)

                deco(body)
        return len(sems)


def sap(t, F, poff, npart, off, dims):
    return bass.AP(t, poff * F + off, [[F, npart]] + [list(d) for d in dims])

from concourse.bass_utils import run_bass_kernel_spmd
from contextlib import ExitStack

D = 1024
SEQ = 4096
T = 512
NT = SEQ // T
KC = 8
DFF = 2816
FC = DFF // 128
NL = 2
EPS = 1e-6
DILS = (1, 4, 16)
OFF_A = 0
OFF_BX = 384
OFF_BB = 768
OFF_BC = 1152
OFF_GA = 1536
OFF_GB = 2560
OFF_GC = 3584
OFF_QKV = 4608
NAB = 3584
V_MIXPRE, V_MIXPOST, V_MEMPRE, V_MEMPOST, V_MEMKV, V_FFNPRE, V_FFNPOST = 0, 8, 16, 24, 32, 40, 48
V_CONVB = 56
V_CONVF = 65
V_PSCALE = 131
V_PER_LAYER = 135
C_ID = 0
C_MASK = 128
C_INV = 384
C_RC = 392
NCONST = 456


def w_in_perm():
    idx = []
    a = 0
    idx += list(range(0, 384))
    idx += list(range(384, 384 + 1152))
    g0 = 384 + 1152 + 3 * 768
    idx += list(range(g0, g0 + 3072))
    q0 = 384 + 1152
    for g in range(3):
        for part in range(3):
            s = q0 + part * 768 + g * 256
            idx += list(range(s, s + 256))
    return np.array(idx, dtype=np.int64)


def build_program(n_layers=NL, dbg=False, opts=()):
    nc = bass.Bass("TRN2", target_bir_lowering=False)
    S = Sched(nc)

    def din(name, shape, dt=F32):
        return nc.dram_tensor(name, list(shape), dt, kind="ExternalInput").ap()

    kind_s = "ExternalOutput" if dbg else "Internal"

    def dscr(name, shape, dt):
        return nc.dram_tensor(name, list(shape), dt, kind=kind_s).ap()

    xT = din("xT", [D, SEQ])
    memT = din("memT", [D, 256])
    pos = din("pos", [32, 128], I32)
    w_in = din("w_in", [NL, D, OFF_QKV])
    w_qkv = din("w_qkv", [NL, 3, D, 1280])
    pool_w = din("pool_w", [NL, 4, 96, 96])
    w_a = din("w_branch_a", [NL, 384, D])
    w_b = din("w_branch_b", [NL, 384, D])
    w_c = din("w_branch_c", [NL, 256, D])
    w_out = din("w_out", [NL, D, D])
    w_mq = din("w_mq", [NL, D, 512])
    w_mkv = din("w_mkv", [NL, D, 1024])
    w_mo = din("w_mo", [NL, 512, D])
    w_up = din("w_up", [NL, D, 2 * DFF])
    w_down = din("w_down", [NL, DFF, D])
    vecs_d = din("vecs", [128, NL * V_PER_LAYER])
    consts_d = din("consts", [128, NCONST])
    outT = nc.dram_tensor("outT", [D, SEQ], F32, kind="ExternalOutput").ap()

    h1T = dscr("h1T", [D, SEQ], BF16)
    h2T = dscr("h2T", [D, SEQ], BF16)
    h3T = dscr("h3T", [D, SEQ], BF16)
    attnT = dscr("attnT", [256, SEQ], BF16)
    mab = dscr("mab", [D, SEQ], F32)
    xs = dscr("xs", [D, SEQ], F32)
    actT = dscr("actT", [DFF, SEQ], BF16)
    rope_d = dscr("rope_tab", [SEQ, 16], F32)

    es = ExitStack()

    def sb(name, shape, dt):
        return es.enter_context(nc.sbuf_tensor("s_" + name, list(shape), dt))

    ps = es.enter_context(nc.psum_tensor("ps", [128, 4096], F32))

    def MM(out, lhsT, rhs, start, stop, R, W):
        S.op("pe", lambda h: h.matmul(out, lhsT=lhsT, rhs=rhs, start=start, stop=stop), reads=R, writes=W)

    def ACT(out, in_, func, R, W, scale=1.0, bias=None, waw=True):
        if bias is None:
            S.op("act", lambda h: h.activation(out=out, in_=in_, func=func, scale=scale), reads=R, writes=W, waw=waw)
        else:
            S.op("act", lambda h: h.activation(out=out, in_=in_, func=func, scale=scale, bias=bias), reads=R, writes=W, waw=waw)

    def TT(eng, out, in0, in1, op, R, W, waw=True):
        S.op(eng, lambda h: h.tensor_tensor(out=out, in0=in0, in1=in1, op=op), reads=R, writes=W, waw=waw)

    def TS(eng, out, in0, s1, s2, op0, op1, R, W, waw=True):
        if s2 is None:
            S.op(eng, lambda h: h.tensor_scalar(out=out, in0=in0, scalar1=s1, scalar2=None, op0=op0), reads=R, writes=W, waw=waw)
        else:
            S.op(eng, lambda h: h.tensor_scalar(out=out, in0=in0, scalar1=s1, scalar2=s2, op0=op0, op1=op1), reads=R, writes=W, waw=waw)

    def STT(eng, out, in0, scalar, in1, op0, op1, R, W, waw=True):
        S.op(eng, lambda h: h.scalar_tensor_tensor(out=out, in0=in0, scalar=scalar, in1=in1, op0=op0, op1=op1), reads=R, writes=W, waw=waw)

    def CP(eng, out, in_, R, W, waw=True):
        if eng == "act":
            ACT(out, in_, AF.Copy, R, W, waw=waw)
        else:
            S.op(eng, lambda h: h.tensor_copy(out=out, in_=in_), reads=R, writes=W, waw=waw)

    def MS(eng, ap, val, W):
        S.op(eng, lambda h: h.memset(ap, val), writes=W)

    def DMA(eng, out, in_, chan, R, W):
        S.dma(eng, lambda h: h.dma_start(out=out, in_=in_), chan, reads=R, writes=W)

    uniq = [0]

    class Ring:
        def __init__(self, name, shape, dt, n, alloc=None):
            alloc = alloc or sb
            uniq[0] += 1
            self.name = name
            self.t = [alloc(f"{name}_{uniq[0]}_{i}", shape, dt) for i in range(n)]
            self.i = 0

        def next(self):
            k = self.i % len(self.t)
            self.i += 1
            return self.t[k], (self.name, k)

    psp = [0]

    def bank(n=1):
        if n == 2 and psp[0] % 2 == 1:
            psp[0] += 1
        p = psp[0] % 6
        psp[0] += n
        return p

    def PB(b, lo=0, hi=512):
        return ps[:, b * 512 + lo: b * 512 + hi]

    vecs = sb("vecs", [128, NL * V_PER_LAYER], F32)
    consts = sb("consts", [128, NCONST], F32)
    ident = sb("ident", [128, 128], BF16)
    mask2 = sb("mask2", [128, 256], BF16)
    onesm = sb("onesm", [128, 128], BF16)
    ones = sb("ones", [128, 128], BF16)
    onesP = sb("onesP", [128, 2, 128], BF16)
    DMA("sp", vecs[:], vecs_d, "vecs", [], ["vecs"])
    DMA("sp", consts[:], consts_d, "consts", [], ["consts"])
    CP("dve", ident[:], consts[:, C_ID:C_ID + 128], ["consts"], ["cb"])
    CP("dve", mask2[:], consts[:, C_MASK:C_MASK + 256], ["consts"], ["cb"])
    MS("pool", onesm[:], 1.0 / 1024.0, ["cb"])
    MS("pool", ones[:], 1.0, ["cb"])
    MS("pool", onesP[:], 0.0, ["cb"])
    MS("pool", onesP[:, 0, 0:64], 1.0, ["cb"])
    MS("pool", onesP[:, 1, 64:128], 1.0, ["cb"])

    sqring = Ring("sq", [128, T], BF16, 3)
    rstdring = Ring("rstd", [128, T], F32, 2)
    G = {}

    def mk_rings(alloc, x=False, h=False, hin=False, ysb=False):
        if x:
            G["x"] = Ring("xt", [128, KC, T], F32, 2, alloc)
        if h:
            G["h"] = Ring("ht", [128, KC, T], BF16, 2, alloc)
        if hin:
            G["hin"] = Ring("hin", [128, KC, T], BF16, 2, alloc)
        if ysb:
            G["ysb"] = Ring("ysb", [128, KC, T], F32, 2, alloc)
    tmp_ring = Ring("tmpf", [128, T], F32, 3)

    def xview(d, tt):
        return d.rearrange("(c p) t -> p c t", p=128)[:, :, tt * T:(tt + 1) * T]

    def rstd_from_bank(b):
        rstd, rk = rstdring.next()
        ACT(rstd[:], PB(b), AF.Ln, [("ps", b)], [rk], bias=EPS)
        ACT(rstd[:], rstd[:], AF.Exp, [rk], [rk], scale=-0.5)
        return rstd, rk

    def norm_tile(xt, xk, gcol, ht, hk, ncols=T):
        b = bank()
        for c in range(KC):
            sq, sqk = sqring.next()
            ACT(sq[:, 0:ncols], xt[:, c, 0:ncols], AF.Square, [xk], [sqk])
            MM(PB(b, 0, ncols), onesm[:], sq[:, 0:ncols], c == 0, c == KC - 1, [sqk, "cb"], [("ps", b)])
        rstd, rk = rstdring.next()
        ACT(rstd[:, 0:ncols], PB(b, 0, ncols), AF.Ln, [("ps", b)], [rk], bias=EPS)
        ACT(rstd[:, 0:ncols], rstd[:, 0:ncols], AF.Exp, [rk], [rk], scale=-0.5)
        for c in range(KC):
            eng = "dve"
            STT(eng, ht[:, c, 0:ncols], xt[:, c, 0:ncols], vecs[:, gcol + c:gcol + c + 1], rstd[:, 0:ncols],
                ALU.mult, ALU.mult, [xk, rk, "vecs"], [hk], waw=False)

    pncnt = [0]

    class PostNorm:
        def __init__(self):
            self.ysb, self.yk = G["ysb"].next()
            pncnt[0] += 1
            self.b = 6 + pncnt[0] % 2
            self.pend = None

        def chunk(self, mc, b):
            CP("act", self.ysb[:, mc, :], PB(b), [("ps", b)], [self.yk], waw=False)
            sq, sqk = sqring.next()
            TT("dve", sq[:], PB(b), self.ysb[:, mc, :], ALU.mult, [("ps", b), self.yk], [sqk])
            if self.pend is not None:
                self.flush()
            self.pend = (mc, sq, sqk)
            if mc == KC - 1:
                self.flush()

        def flush(self):
            mc, sq, sqk = self.pend
            self.pend = None
            MM(PB(self.b), onesm[:], sq[:], mc == 0, mc == KC - 1, [sqk, "cb"], [("ps", self.b)])

        def finish(self, xt, xk, gcol):
            rstd, rk = rstd_from_bank(self.b)
            for c in range(KC):
                tmp, tk = tmp_ring.next()
                TT("pool", tmp[:], self.ysb[:, c, :], rstd[:], ALU.mult, [self.yk, rk], [tk])
                STT("dve", xt[:, c, :], tmp[:], vecs[:, gcol + c:gcol + c + 1], xt[:, c, :], ALU.mult, ALU.add,
                    [tk, xk, "vecs"], [xk], waw=False)

    def load_w(dst, src2d, ncols, chan, key, rows=128):
        v = src2d.rearrange("(k p) n -> p k n", p=rows)
        for cb in range(0, ncols, 2048):
            w = min(2048, ncols - cb)
            DMA("pool", dst[:, :, cb:cb + w], v[:, :, cb:cb + w], chan, [], [key])

    def load_wb(dst, src2d, ncols, name, order=None, blk=512, rows=128):
        v = src2d.rearrange("(k p) n -> p k n", p=rows)
        nb = (ncols + blk - 1) // blk
        for b in (order if order is not None else range(nb)):
            lo = b * blk
            w = min(blk, ncols - lo)
            DMA("pool", dst[:, :, lo:lo + w], v[:, :, lo:lo + w], f"{name}_{b}", [], [(name, b)])

    with ExitStack() as es0:
      if 'norope' not in opts:
          def sb0(name, shape, dt):
              return es0.enter_context(nc.sbuf_tensor(name, list(shape), dt))
          pi_ = sb0("rp_pi", [32, 128], I32)
          pf = sb0("rp_pf", [32, 128], F32)
          ang = sb0("rp_ang", [32, 128, 16], F32)
          a2 = sb0("rp_a2", [32, 128, 16], F32)
          ki = sb0("rp_ki", [32, 128, 16], I32)
          tab = sb0("rp_tab", [32, 128, 16], F32)
          DMA("sp", pi_[:], pos, "rp", [], ["rp_pi"])
          CP("dve", pf[:], pi_[:], ["rp_pi"], ["rp_pf"])
          inv_b = consts[0:32, C_INV:C_INV + 8].unsqueeze(1).to_broadcast([32, 128, 8])
          pf_b = pf[:].unsqueeze(2).to_broadcast([32, 128, 8])
          TT("dve", ang[:, :, 0:8], pf_b, inv_b, ALU.mult, ["rp_pf", "consts"], ["rp_ang"])
          TS("dve", ang[:, :, 8:16], ang[:, :, 0:8], float(np.pi / 2), None, ALU.add, None, ["rp_ang"], ["rp_ang"])
          TS("dve", a2[:], ang[:], float(1.0 / (2 * np.pi)), None, ALU.mult, None, ["rp_ang"], ["rp_a2"])
          CP("dve", ki[:], a2[:], ["rp_a2"], ["rp_ki"])
          CP("dve", a2[:], ki[:], ["rp_ki"], ["rp_a2"])
          STT("dve", ang[:], a2[:], float(-2 * np.pi), ang[:], ALU.mult, ALU.add, ["rp_a2", "rp_ang"], ["rp_ang"])
          TS("dve", a2[:], ang[:], float(np.pi), float(-2 * np.pi), ALU.is_gt, ALU.mult, ["rp_ang"], ["rp_a2"])
          TT("dve", ang[:], ang[:], a2[:], ALU.add, ["rp_ang", "rp_a2"], ["rp_ang"])
          TS("dve", a2[:], ang[:], float(-np.pi), float(2 * np.pi), ALU.is_lt, ALU.mult, ["rp_ang"], ["rp_a2"])
          TT("dve", ang[:], ang[:], a2[:], ALU.add, ["rp_ang", "rp_a2"], ["rp_ang"])
          ACT(tab[:, :, 0:8], ang[:, :, 8:16], AF.Sin, ["rp_ang"], ["rp_tab"])
          ACT(tab[:, :, 8:16], ang[:, :, 0:8], AF.Sin, ["rp_ang"], ["rp_tab"], waw=False)
          DMA("sp", rope_d.rearrange("(b i) f -> b i f", i=128), tab[:], "rp", ["rp_tab"], ["rope_d"])
          S.barrier()

    def phase_norm0(l):
      with ExitStack() as e1:
        mk_rings(lambda n, sh, dt: e1.enter_context(nc.sbuf_tensor(n, list(sh), dt)), x=True, h=True)
        for tt in range(NT):
            xt, xk = G["x"].next()
            DMA("sp", xt[:], xview(xT, tt), xk, [], [xk])
            ht, hk = G["h"].next()
            norm_tile(xt, xk, l * V_PER_LAYER + V_MIXPRE, ht, hk)
            DMA("pool", xview(h1T, tt), ht[:], hk, [hk], [("h1T", tt)])
        S.barrier()

    def phase_attn(l):
        with ExitStack() as e2:
            def sb2(name, shape, dt):
                return e2.enter_context(nc.sbuf_tensor(f"{name}_L{l}", list(shape), dt))
            hT = sb2("at_hT", [128, KC, SEQ], BF16)
            acc = sb2("at_acc", [128, 4, 2048], F32)
            attn_sb = sb2("at_out", [128, 2, 2048], BF16)
            wq = [sb2(f"at_w{i}", [128, KC, 1280], BF16) for i in range(2)]
            cs = [sb2(f"at_cs{g}", [128, 32, 16], F32) for g in range(3)]
            QK = [sb2(f"at_qk{i}", [128, 768], BF16) for i in range(2)]
            QF = [sb2(f"at_qf{i}", [128, 768], F32) for i in range(2)]
            ODT = [sb2(f"at_od{i}", [128, 512], F32) for i in range(2)]
            Vb = [sb2(f"at_v{i}", [128, 4, 128], BF16) for i in range(2)]
            QT = [sb2(f"at_qt{i}", [128, 2, 128], BF16) for i in range(2)]
            KTp = [sb2(f"at_kt{i}", [128, 4, 128], BF16) for i in range(2)]
            PT = [sb2(f"at_pt{i}", [128, 4, 256], BF16) for i in range(2)]
            rt = [sb2(f"at_rt{i}", [128, 4, 8, 8], F32) for i in range(2)]
            for tt in range(NT):
                DMA("sp", hT[:, :, tt * T:(tt + 1) * T], xview(h1T, tt), "at_hT", [("h1T", tt)], ["at_hT"])
            for g in range(3):
                d = DILS[g]
                nb = 32 // d
                for r in range(d):
                    for j in range(nb):
                        src = bass.AP(rope_d.tensor, (r + d * 128 * j) * 16, [[16 * d, 128], [1, 16]])
                        DMA("sp", cs[g][:, r * nb + j, :], src, f"at_cs{g}", ["rope_d"], [("cs", g)])
            wl = [0]
            cnt = {"qk": 0, "pt": 0, "rt": 0, "od": 0}

            def proj_block(g, r, j, need_q, wt, wk):
                d = DILS[g]
                nb = 32 // d
                slot = j % 2
                start = r + d * 128 * j
                sl = slice(start, start + 127 * d + 1, d)
                bq = bank() if need_q else None
                bk = bank()
                bv = bank()
                if need_q:
                    for kc in range(KC):
                        MM(PB(bq, 0, 256), hT[:, kc, sl], wt[:, kc, 0:256], kc == 0, kc == KC - 1, ["at_hT", wk], [("ps", bq)])
                for kc in range(KC):
                    MM(PB(bk), hT[:, kc, sl], wt[:, kc, 256:768], kc == 0, kc == KC - 1, ["at_hT", wk], [("ps", bk)])
                for kc in range(KC):
                    MM(PB(bv), hT[:, kc, sl], wt[:, kc, 768:1280], kc == 0, kc == KC - 1, ["at_hT", wk], [("ps", bv)])
                qi = cnt["qk"] % 2
                cnt["qk"] += 1
                qk, qkk = QK[qi], ("QK", qi)
                qf, qfk = QF[qi], ("QF", qi)
                if need_q:
                    CP("act", qf[:, 0:256], PB(bq, 0, 256), [("ps", bq)], [qfk])
                CP("act", qf[:, 256:768], PB(bk), [("ps", bk)], [qfk], waw=not need_q)
                lo = 0 if need_q else 256
                CP("act", qk[:, lo:768], qf[:, lo:768], [qfk], [qkk])
                CP("dve", Vb[slot][:].rearrange("p a b -> p (a b)"), PB(bv), [("ps", bv)], [("Vb", slot)])
                if 'pb1' in opts:
                    return
                blk = r * nb + j
                csk = ("cs", g)
                views = []
                if need_q:
                    views.append((qf[:, 0:256].rearrange("p (h e) -> p h e", e=64), qk[:, 0:256].rearrange("p (h e) -> p h e", e=64), [128, 4, 8], 0))
                kfv = bass.AP(qf, 256, [[768, 128], [256, 2], [192, 2], [1, 64]])
                kbv = bass.AP(qk, 256, [[768, 128], [256, 2], [192, 2], [1, 64]])
                views.append((kfv, kbv, [128, 2, 2, 8], 1))
                for (fv, bv_, shp, which) in views:
                    ri = cnt["rt"] % 2
                    cnt["rt"] += 1
                    rtt, rtk = rt[ri], ("rt", ri)
                    if which == 0:
                        cosb = cs[g][:, blk, 0:8].unsqueeze(1).to_broadcast(shp)
                        sinb = cs[g][:, blk, 8:16].unsqueeze(1).to_broadcast(shp)
                        u1, u2 = fv[:, :, 0:8], fv[:, :, 8:16]
                        o1, o2 = bv_[:, :, 0:8], bv_[:, :, 8:16]
                        tv = [rtt[:, k, 0:4, :] for k in range(4)]
                    else:
                        cosb = cs[g][:, blk, 0:8].unsqueeze(1).unsqueeze(1).to_broadcast(shp)
                        sinb = cs[g][:, blk, 8:16].unsqueeze(1).unsqueeze(1).to_broadcast(shp)
                        u1, u2 = fv[:, :, :, 0:8], fv[:, :, :, 8:16]
                        o1, o2 = bv_[:, :, :, 0:8], bv_[:, :, :, 8:16]
                        tv = [rtt[:, k, 0:4, :].rearrange("p (a b) e -> p a b e", a=2) for k in range(4)]
                    TT("dve", tv[0], u1, cosb, ALU.mult, [qfk, csk], [rtk])
                    TT("dve", tv[1], u2, sinb, ALU.mult, [qfk, csk], [rtk], waw=False)
                    TT("dve", tv[2], u2, cosb, ALU.mult, [qfk, csk], [rtk], waw=False)
                    TT("dve", tv[3], u1, sinb, ALU.mult, [qfk, csk], [rtk], waw=False)
                    TT("pool", o1, tv[0], tv[1], ALU.subtract, [rtk], [qkk])
                    TT("pool", o2, tv[2], tv[3], ALU.add, [rtk], [qkk])
                if 'pb2' in opts:
                    return
                if need_q:
                    bt = bank()
                    for ch in range(2):
                        MM(PB(bt, ch * 128, ch * 128 + 128), qk[:, ch * 128:(ch + 1) * 128], ident[:], True, True, [qkk, "cb"], [("ps", bt)])
                    CP("act", QT[slot][:].rearrange("p c t -> p (c t)"), PB(bt, 0, 256), [("ps", bt)], [("QT", slot)])
                bt2 = bank()
                for hh in range(4):
                    MM(PB(bt2, hh * 128, hh * 128 + 128), qk[:, 256 + hh * 128:256 + (hh + 1) * 128], ident[:], True, True, [qkk, "cb"], [("ps", bt2)])
                CP("dve", KTp[slot][:].rearrange("p a b -> p (a b)"), PB(bt2), [("ps", bt2)], [("KTp", slot)])

            def attend(g, r, j, half, first_group):
                d = DILS[g]
                sc, sp_ = j % 2, (j - 1) % 2
                b2 = bank(2)
                for hh in range(4):
                    ch = hh // 2
                    bb = b2 + hh // 2
                    c0 = (hh % 2) * 256
                    if j > 0:
                        MM(PB(bb, c0, c0 + 256), ident[:], mask2[:, 0:256], True, False, ["cb"], [("ps", bb)])
                        MM(PB(bb, c0, c0 + 128), KTp[sp_][:, hh, :], QT[sc][:, ch, :], False, False, [("KTp", sp_), ("QT", sc)], [("ps", bb)])
                    else:
                        MM(PB(bb, c0 + 128, c0 + 256), ident[:], mask2[:, 128:256], True, False, ["cb"], [("ps", bb)])
                    MM(PB(bb, c0 + 128, c0 + 256), KTp[sc][:, hh, :], QT[sc][:, ch, :], False, True, [("KTp", sc), ("QT", sc)], [("ps", bb)])
                pi = cnt["pt"] % 2
                cnt["pt"] += 1
                pt, ptk = PT[pi], ("PT", pi)
                for hb in range(2):
                    bb = b2 + hb
                    if j > 0:
                        ACT(pt[:, 2 * hb:2 * hb + 2, :].rearrange("p a b -> p (a b)"), PB(bb), AF.Exp, [("ps", bb)], [ptk], scale=0.125, waw=False)
                    else:
                        for a in range(2):
                            ACT(pt[:, 2 * hb + a, 128:256], PB(bb, a * 256 + 128, a * 256 + 256), AF.Exp,
                                [("ps", bb)], [ptk], scale=0.125, waw=False)
                b3 = bank()
                kbs = [0, 1] if j > 0 else [1]
                for od in range(2):
                    for pair in range(2):
                        mats = [(hh, kb) for hh in (2 * pair, 2 * pair + 1) for kb in kbs]
                        c0 = od * 256 + pair * 128
                        for i, (hh, kb) in enumerate(mats):
                            vs = sp_ if kb == 0 else sc
                            lhs = Vb[vs][:, hh, :] if od == 0 else onesP[:, hh % 2, :]
                            rd = [ptk, ("Vb", vs)] if od == 0 else [ptk, "cb"]
                            MM(PB(b3, c0, c0 + 128), lhs, pt[:, hh, kb * 128:(kb + 1) * 128], i == 0, i == len(mats) - 1, rd, [("ps", b3)])
                off = r + d * 128 * j - 2048 * half
                av = bass.AP(acc, off, [[4 * 2048, 128], [2048, 4], [d, 128]])
                oi = cnt["od"] % 2
                cnt["od"] += 1
                odt, odk = ODT[oi], ("ODT", oi)
                CP("act", odt[:], PB(b3), [("ps", b3)], [odk])
                pv = odt[:].rearrange("p (a t) -> p a t", t=128)
                if first_group:
                    CP("pool", av, pv, [odk], ["acc"], waw=True)
                else:
                    TT("pool", av, pv, av, ALU.add, [odk, "acc"], ["acc"])

            for half in range(2):
                for g in range(3):
                    d = DILS[g]
                    nb = 32 // d
                    wi = wl[0] % 2
                    wl[0] += 1
                    wt, wk = wq[wi], ("at_w", wi)
                    load_w(wt, w_qkv[l, g], 1280, f"at_w{wi}", wk)
                    jl, jh = half * nb // 2, (half + 1) * nb // 2
                    for r in range(d):
                        if 'at_loads' in opts:
                            continue
                        if jl > 0:
                            proj_block(g, r, jl - 1, False, wt, wk)
                        for j in range(jl, jh):
                            proj_block(g, r, j, True, wt, wk)
                            if 'at_proj' not in opts:
                                attend(g, r, j, half, g == 0)
                for pair in range(2):
                    ACT(acc[:, 2 + pair, :], acc[:, 2 + pair, :], AF.Ln, ["acc"], ["acc"])
                    ACT(acc[:, 2 + pair, :], acc[:, 2 + pair, :], AF.Exp, ["acc"], ["acc"], scale=-1.0)
                    TT("dve", attn_sb[:, pair, :], acc[:, pair, :], acc[:, 2 + pair, :], ALU.mult, ["acc"], ["attn_sb"])
                dv = attnT.rearrange("(c p) t -> p c t", p=128)[:, :, half * 2048:(half + 1) * 2048]
                DMA("pool", dv, attn_sb[:], "attn_sb", ["attn_sb"], [("attnT", half)])
            S.barrier()

    def phase_ab(l):
        vb = l * V_PER_LAYER
        with ExitStack() as e3:
            def sb3(name, shape, dt):
                return e3.enter_context(nc.sbuf_tensor(f"{name}_L{l}", list(shape), dt))
            mk_rings(sb3, hin=True)
            wab = sb3("ab_w", [128, KC, NAB], BF16)
            pw = sb3("ab_pw", [96, 4, 96], BF16)
            wa = sb3("ab_wa", [96, 4, D], BF16)
            wb = sb3("ab_wb", [128, 3, D], BF16)
            A = [sb3(f"ab_A{g}", [96, 16 + T], F32) for g in range(4)]
            T2 = sb3("ab_T2", [96, 16 + T], F32)
            T4 = sb3("ab_T4", [96, 16 + T], F32)
            T8 = sb3("ab_T8", [96, 16 + T], F32)
            PL = [sb3(f"ab_PL{g}", [96, T], BF16) for g in range(4)]
            MX = [sb3(f"ab_MX{g}", [96, T], BF16) for g in range(4)]
            BX = sb3("ab_BX", [128, T], F32)
            P = [sb3(f"ab_P{c}", [128, 2 + T], F32) for c in range(3)]
            Y1 = sb3("ab_Y1", [128, T], F32)
            Y2 = sb3("ab_Y2", [128, T], F32)
            Z = [sb3(f"ab_Z{c}", [128, T], BF16) for c in range(3)]
            SG = [sb3(f"ab_SG{i}", [128, T], F32) for i in range(2)]
            MO = [sb3(f"ab_MO{i}", [128, KC, T], F32) for i in range(2)]
            load_wb(wab, w_in[l][:, 0:NAB], NAB, "ab_w", order=[0])
            DMA("pool", pw[:], pool_w[l].rearrange("g c d -> c g d"), "ab_w2", [], ["ab_w2"])
            DMA("pool", wa[:], w_a[l].rearrange("(g c) n -> c g n", c=96), "ab_w2", [], ["ab_w2"])
            load_wb(wab, w_in[l][:, 0:NAB], NAB, "ab_w", order=[3, 4, 1, 2])
            DMA("pool", wb[:], w_b[l].rearrange("(k p) n -> p k n", p=128), "ab_w3", [], ["ab_w3"])
            load_wb(wab, w_in[l][:, 0:NAB], NAB, "ab_w", order=[5, 6])
            for g in range(4):
                MS("pool", A[g][:, 0:16], 0.0, [("A", g)])
            for c in range(3):
                MS("pool", P[c][:, 0:2], 0.0, [("P", c)])
            sgi = [0]
            for tt in range(NT):
                ht, hk = G["hin"].next()
                DMA("sp", ht[:], xview(h1T, tt), hk, [("h1T", tt)], [hk])
                mo, mok = MO[tt % 2], ("MO", tt % 2)
                for g in range(4):
                    w = (2, 4, 8, 16)[g]
                    Ak = ("A", g)
                    if tt > 0:
                        CP("pool", A[g][:, 0:16], A[g][:, T:T + 16], [Ak], [Ak])
                    b = bank()
                    for kc in range(KC):
                        MM(ps[0:96, b * 512:(b + 1) * 512], wab[:, kc, OFF_A + 96 * g:OFF_A + 96 * (g + 1)], ht[:, kc, :], kc == 0, kc == KC - 1,
                           [hk, ("ab_w", 0)], [("ps", b)])
                    CP("act", A[g][:, 16:16 + T], ps[0:96, b * 512:(b + 1) * 512], [("ps", b)], [Ak])
                    src = A[g]
                    srck = Ak
                    n = 1
                    for (dst, dk) in ((T2, "T2"), (T4, "T4"), (T8, "T8"), (None, None)):
                        if n >= w:
                            break
                        if 2 * n == w:
                            tmpw, tmpk = T2 if dst is not T2 and src is not T2 else (T4 if src is not T4 else T8), None
                            tmpw = {1: T2, 2: T4, 4: T8, 8: T2}[n]
                            tmpk = {1: "T2", 2: "T4", 4: "T8", 8: "T2"}[n]
                            TT("pool", tmpw[:, 16:16 + T], src[:, 16:16 + T], src[:, 16 - n:16 - n + T], ALU.add, [srck], [tmpk])
                            STT("dve", PL[g][:], tmpw[:, 16:16 + T], 1.0 / w, A[g][:, 16:16 + T], ALU.mult, ALU.subtract, [tmpk, Ak], [("PL", g)])
                            if tt == 0:
                                tmp, tk = tmp_ring.next()
                                TT("pool", tmp[0:96, 0:16], tmpw[:, 16:32], consts[0:96, C_RC + 16 * g:C_RC + 16 * g + 16], ALU.mult, [tmpk, "consts"], [tk])
                                TT("pool", PL[g][:, 0:16], tmp[0:96, 0:16], A[g][:, 16:32], ALU.subtract, [tk, Ak], [("PL", g)])
                            break
                        TT("pool", dst[:, 2 * n - 1:16 + T], src[:, 2 * n - 1:16 + T], src[:, n - 1:16 + T - n], ALU.add, [srck], [dk])
                        src, srck = dst, dk
                        n *= 2
                    b2 = bank()
                    MM(ps[0:96, b2 * 512:(b2 + 1) * 512], pw[:, g, :], PL[g][:], True, True, [("PL", g), "ab_w2"], [("ps", b2)])
                    ACT(MX[g][:], ps[0:96, b2 * 512:(b2 + 1) * 512], AF.Copy, [("ps", b2), "vecs"], [("MX", g)],
                        scale=vecs[0:96, vb + V_PSCALE + g:vb + V_PSCALE + g + 1])
                for mc in range(KC):
                    b = bank()
                    for g in range(4):
                        MM(PB(b), wa[:, g, mc * 128:(mc + 1) * 128], MX[g][:], g == 0, g == 3, [("MX", g), "ab_w2"], [("ps", b)])
                    bg = bank()
                    for kc in range(KC):
                        MM(PB(bg), wab[:, kc, OFF_GA + mc * 128:OFF_GA + (mc + 1) * 128], ht[:, kc, :], kc == 0, kc == KC - 1, [hk, ("ab_w", (OFF_GA + mc * 128) // 512)], [("ps", bg)])
                    sg, sgk = SG[sgi[0] % 2], ("SG", sgi[0] % 2)
                    sgi[0] += 1
                    ACT(sg[:], PB(bg), AF.Sigmoid, [("ps", bg)], [sgk])
                    TT("dve", mo[:, mc, :], PB(b), sg[:], ALU.mult, [("ps", b), sgk], [mok], waw=False)
                for c in range(3):
                    Pk = ("P", c)
                    if tt > 0:
                        CP("pool", P[c][:, 0:2], P[c][:, T:T + 2], [Pk], [Pk])
                    bx, bb_, bc = bank(), bank(), bank()
                    for (bk, off) in ((bx, OFF_BX), (bb_, OFF_BB), (bc, OFF_BC)):
                        for kc in range(KC):
                            MM(PB(bk), wab[:, kc, off + c * 128:off + (c + 1) * 128], ht[:, kc, :], kc == 0, kc == KC - 1, [hk, ("ab_w", (off + c * 128) // 512)], [("ps", bk)])
                    CP("act", BX[:], PB(bx), [("ps", bx)], ["BX"])
                    TT("dve", P[c][:, 2:2 + T], PB(bc), BX[:], ALU.mult, [("ps", bc), "BX"], [Pk])
                    cw = vb + V_CONVB
                    ACT(Y1[:], P[c][:, 2:2 + T], AF.Copy, [Pk, "vecs"], ["Y1"], scale=vecs[:, cw + 2 * 3 + c:cw + 2 * 3 + c + 1])
                    STT("dve", Y2[:], P[c][:, 1:1 + T], vecs[:, cw + 1 * 3 + c:cw + 1 * 3 + c + 1], Y1[:], ALU.mult, ALU.add, [Pk, "Y1", "vecs"], ["Y2"])
                    STT("dve", Y1[:], P[c][:, 0:T], vecs[:, cw + 0 * 3 + c:cw + 0 * 3 + c + 1], Y2[:], ALU.mult, ALU.add, [Pk, "Y2", "vecs"], ["Y1"])
                    TT("dve", Z[c][:], PB(bb_), Y1[:], ALU.mult, [("ps", bb_), "Y1"], [("Z", c)])
                for mc in range(KC):
                    b = bank()
                    for c in range(3):
                        MM(PB(b), wb[:, c, mc * 128:(mc + 1) * 128], Z[c][:], c == 0, c == 2, [("Z", c), "ab_w3"], [("ps", b)])
                    bg = bank()
                    for kc in range(KC):
                        MM(PB(bg), wab[:, kc, OFF_GB + mc * 128:OFF_GB + (mc + 1) * 128], ht[:, kc, :], kc == 0, kc == KC - 1, [hk, ("ab_w", (OFF_GB + mc * 128) // 512)], [("ps", bg)])
                    sg, sgk = SG[sgi[0] % 2], ("SG", sgi[0] % 2)
                    sgi[0] += 1
                    ACT(sg[:], PB(bg), AF.Sigmoid, [("ps", bg)], [sgk])
                    tmp, tk = tmp_ring.next()
                    TT("dve", tmp[:], PB(b), sg[:], ALU.mult, [("ps", b), sgk], [tk])
                    TT("pool", mo[:, mc, :], mo[:, mc, :], tmp[:], ALU.add, [mok, tk], [mok], waw=False)
                DMA("pool", xview(mab, tt), mo[:], mok, [mok], [("mab", tt)])
            S.barrier()

    def phase_merge(l):
        vb = l * V_PER_LAYER
        xsrc = xT if l == 0 else xs
        with ExitStack() as e4:
            def sb4(name, shape, dt):
                return e4.enter_context(nc.sbuf_tensor(f"{name}_L{l}", list(shape), dt))
            mk_rings(sb4, x=True, h=True, hin=True, ysb=True)
            wgc = sb4("mg_wgc", [128, KC, D], BF16)
            wc = sb4("mg_wc", [128, 2, D], BF16)
            wo = sb4("mg_wo", [128, KC, D], BF16)
            AT = [sb4(f"mg_at{i}", [128, 2, T], BF16) for i in range(2)]
            MI = [sb4(f"mg_mi{i}", [128, KC, T], F32) for i in range(2)]
            MGs = [sb4(f"mg_mg{i}", [128, KC, T], BF16) for i in range(2)]
            SG = [sb4(f"mg_SG{i}", [128, T], F32) for i in range(2)]
            load_w(wc, w_c[l], D, "mg_wc", "mg_wc")
            load_wb(wgc, w_in[l][:, OFF_GC:OFF_GC + D], D, "mg_wg")
            load_wb(wo, w_out[l], D, "mg_wo")
            st = {}

            def stA(tt):
                ht, hk = G["hin"].next()
                DMA("sp", ht[:], xview(h1T, tt), hk, [("h1T", tt)], [hk])
                at, atk = AT[tt % 2], ("AT", tt % 2)
                DMA("sp", at[:], attnT.rearrange("(c p) t -> p c t", p=128)[:, :, tt * T:(tt + 1) * T], f"mg_at{tt % 2}", [("attnT", tt // 4)], [atk])
                mi, mik = MI[tt % 2], ("MI", tt % 2)
                DMA("sp", mi[:], xview(mab, tt), f"mg_mi{tt % 2}", [("mab", tt)], [mik])
                xt, xk = G["x"].next()
                DMA("sp", xt[:], xview(xsrc, tt), xk, [("xs", tt)], [xk])
                mg, mgk = MGs[tt % 2], ("MG", tt % 2)
                for mc in range(KC):
                    b = bank()
                    for pr in range(2):
                        MM(PB(b), wc[:, pr, mc * 128:(mc + 1) * 128], at[:, pr, :], pr == 0, pr == 1, [atk, "mg_wc"], [("ps", b)])
                    bg = bank()
                    for kc in range(KC):
                        MM(PB(bg), wgc[:, kc, mc * 128:(mc + 1) * 128], ht[:, kc, :], kc == 0, kc == KC - 1, [hk, ("mg_wg", mc // 4)], [("ps", bg)])
                    sg, sgk = SG[mc % 2], ("SG4", mc % 2)
                    ACT(sg[:], PB(bg), AF.Sigmoid, [("ps", bg)], [sgk])
                    tmp, tk = tmp_ring.next()
                    TT("dve", tmp[:], PB(b), sg[:], ALU.mult, [("ps", b), sgk], [tk])
                    TT("pool", mg[:, mc, :], tmp[:], mi[:, mc, :], ALU.add, [tk, mik], [mgk], waw=False)
                st[tt] = dict(xt=xt, xk=xk, mg=mg, mgk=mgk)

            def stW(tt):
                d_ = st[tt]
                pn = PostNorm()
                for mc in range(KC):
                    b = bank()
                    for kc in range(KC):
                        MM(PB(b), wo[:, kc, mc * 128:(mc + 1) * 128], d_["mg"][:, kc, :], kc == 0, kc == KC - 1, [d_["mgk"], ("mg_wo", mc // 4)], [("ps", b)])
                    pn.chunk(mc, b)
                d_["pn"] = pn

            def stF1(tt):
                d_ = st[tt]
                xt, xk = d_["xt"], d_["xk"]
                d_["pn"].finish(xt, xk, vb + V_MIXPOST)
                DMA("pool", xview(xs, tt), xt[:], xk, [xk], [("xs", tt)])

            def stF2(tt):
                d_ = st.pop(tt)
                xt, xk = d_["xt"], d_["xk"]
                h2, h2k = G["h"].next()
                norm_tile(xt, xk, vb + V_MEMPRE, h2, h2k)
                DMA("pool", xview(h2T, tt), h2[:], h2k, [h2k], [("h2T", tt)])

            stA(0)
            stW(0)
            for tt in range(1, NT):
                stF1(tt - 1)
                stA(tt)
                stF2(tt - 1)
                stW(tt)
            stF1(NT - 1)
            stF2(NT - 1)
            S.barrier()

    def phase_mem(l):
        vb = l * V_PER_LAYER
        with ExitStack() as e5:
            def sb5(name, shape, dt):
                return e5.enter_context(nc.sbuf_tensor(f"{name}_L{l}", list(shape), dt))
            mk_rings(sb5, x=True, h=True, hin=True, ysb=True)
            wq_ = sb5("mm_wq", [128, KC, 512], BF16)
            wkv = sb5("mm_wkv", [128, KC, 1024], BF16)
            wo = sb5("mm_wo", [128, 4, D], BF16)
            mt = sb5("mm_mt", [128, KC, 256], F32)
            mn = sb5("mm_mn", [128, KC, 256], BF16)
            KmT = sb5("mm_KmT", [128, 4, 256], BF16)
            Vm = sb5("mm_Vm", [128, 2, 512], BF16)
            QM = [sb5(f"mm_QM{i}", [128, T], BF16) for i in range(4)]
            PTm = [sb5(f"mm_PT{i}", [128, 2, T], BF16) for i in range(4)]
            DN = [sb5(f"mm_DN{i}", [128, T], F32) for i in range(2)]
            OMs = [sb5(f"mm_OM{i}", [128, 4, T], BF16) for i in range(2)]
            load_w(wkv, w_mkv[l], 1024, "mm_w", "mm_w")
            load_w(wq_, w_mq[l], 512, "mm_w", "mm_w")
            load_w(wo, w_mo[l], D, "mm_wo", "mm_wo")
            DMA("sp", mt[:], memT.rearrange("(c p) t -> p c t", p=128), "mm_mt", [], ["mm_mt"])
            norm_tile(mt, "mm_mt", vb + V_MEMKV, mn, "mm_mn", ncols=256)
            for h in range(4):
                b = bank()
                for kc in range(KC):
                    MM(PB(b, 0, 256), wkv[:, kc, h * 128:(h + 1) * 128], mn[:, kc, 0:256], kc == 0, kc == KC - 1, ["mm_mn", "mm_w"], [("ps", b)])
                CP("act", KmT[:, h, :], PB(b, 0, 256), [("ps", b)], ["KmT"], waw=False)
            for mi_ in range(2):
                b = bank()
                for kc in range(KC):
                    MM(PB(b), mn[:, kc, mi_ * 128:(mi_ + 1) * 128], wkv[:, kc, 512:1024], kc == 0, kc == KC - 1, ["mm_mn", "mm_w"], [("ps", b)])
                CP("act", Vm[:, mi_, :], PB(b), [("ps", b)], ["Vm"], waw=False)
            sc = float(128 ** -0.5)
            st = {}

            def stA(tt):
                ht, hk = G["hin"].next()
                DMA("sp", ht[:], xview(h2T, tt), hk, [("h2T", tt)], [hk])
                xt, xk = G["x"].next()
                DMA("sp", xt[:], xview(xs, tt), xk, [("xs", tt)], [xk])
                om, omk = OMs[tt % 2], ("OM", tt % 2)
                for h in range(4):
                    b = bank()
                    for kc in range(KC):
                        MM(PB(b), wq_[:, kc, h * 128:(h + 1) * 128], ht[:, kc, :], kc == 0, kc == KC - 1, [hk, "mm_w"], [("ps", b)])
                    CP("act", QM[h][:], PB(b), [("ps", b)], [("QM", h)])
                for h in range(4):
                    for mi_ in range(2):
                        bs = bank()
                        MM(PB(bs), KmT[:, h, mi_ * 128:(mi_ + 1) * 128], QM[h][:], True, True, ["KmT", ("QM", h)], [("ps", bs)])
                        ACT(PTm[h][:, mi_, :], PB(bs), AF.Exp, [("ps", bs)], [("PTm", h)], scale=sc, waw=False)
                for h in range(4):
                    pt, ptk = PTm[h], ("PTm", h)
                    bo, bd = bank(), bank()
                    for mi_ in range(2):
                        MM(PB(bo), Vm[:, mi_, h * 128:(h + 1) * 128], pt[:, mi_, :], mi_ == 0, mi_ == 1, ["Vm", ptk], [("ps", bo)])
                    for mi_ in range(2):
                        MM(PB(bd), ones[:], pt[:, mi_, :], mi_ == 0, mi_ == 1, ["cb", ptk], [("ps", bd)])
                    dn, dnk = DN[h % 2], ("DN", h % 2)
                    ACT(dn[:], PB(bd), AF.Ln, [("ps", bd)], [dnk])
                    ACT(dn[:], dn[:], AF.Exp, [dnk], [dnk], scale=-1.0)
                    TT("dve", om[:, h, :], PB(bo), dn[:], ALU.mult, [("ps", bo), dnk], [omk], waw=False)
                st[tt] = dict(xt=xt, xk=xk, om=om, omk=omk)

            def stW(tt):
                d_ = st[tt]
                pn = PostNorm()
                for mc in range(KC):
                    b = bank()
                    for h in range(4):
                        MM(PB(b), wo[:, h, mc * 128:(mc + 1) * 128], d_["om"][:, h, :], h == 0, h == 3, [d_["omk"], "mm_wo"], [("ps", b)])
                    pn.chunk(mc, b)
                d_["pn"] = pn

            def stF1(tt):
                d_ = st[tt]
                xt, xk = d_["xt"], d_["xk"]
                d_["pn"].finish(xt, xk, vb + V_MEMPOST)
                DMA("pool", xview(xs, tt), xt[:], xk, [xk], [("xs", tt)])

            def stF2(tt):
                d_ = st.pop(tt)
                xt, xk = d_["xt"], d_["xk"]
                h3, h3k = G["h"].next()
                norm_tile(xt, xk, vb + V_FFNPRE, h3, h3k)
                DMA("pool", xview(h3T, tt), h3[:], h3k, [h3k], [("h3T", tt)])

            stA(0)
            stW(0)
            for tt in range(1, NT):
                stF1(tt - 1)
                stA(tt)
                stF2(tt - 1)
                stW(tt)
            stF1(NT - 1)
            stF2(NT - 1)
            S.barrier()

    def phase_up(l):
        vb = l * V_PER_LAYER
        with ExitStack() as e6:
            def sb6(name, shape, dt):
                return e6.enter_context(nc.sbuf_tensor(f"{name}_L{l}", list(shape), dt))
            mk_rings(sb6, hin=True)
            wu = sb6("up_w", [128, KC, 2 * DFF], BF16)
            H = sb6("up_H", [128, FC, 2], F32)
            UA = [sb6(f"up_UA{i}", [128, 2 + T], F32) for i in range(2)]
            Y1 = [sb6(f"up_Y1{i}", [128, T], F32) for i in range(2)]
            Y2 = [sb6(f"up_Y2{i}", [128, T], F32) for i in range(2)]
            AO = [sb6(f"up_AO{i}", [128, FC, T], BF16) for i in range(2)]
            load_wb(wu, w_up[l], 2 * DFF, "up_w", order=[0, 5, 6, 1, 7, 2, 8, 3, 9, 4, 10])
            MS("pool", H[:], 0.0, ["H"])
            cw = vb + V_CONVF
            for tt in range(NT):
                ht, hk = G["hin"].next()
                DMA("sp", ht[:], xview(h3T, tt), hk, [("h3T", tt)], [hk])
                ao, aok = AO[tt % 2], ("AO", tt % 2)
                for c in range(FC):
                    ba, bb_ = bank(), bank()
                    for kc in range(KC):
                        MM(PB(ba), wu[:, kc, c * 128:(c + 1) * 128], ht[:, kc, :], kc == 0, kc == KC - 1, [hk, ("up_w", (c * 128) // 512)], [("ps", ba)])
                    for kc in range(KC):
                        MM(PB(bb_), wu[:, kc, DFF + c * 128:DFF + (c + 1) * 128], ht[:, kc, :], kc == 0, kc == KC - 1, [hk, ("up_w", (DFF + c * 128) // 512)], [("ps", bb_)])
                    i = c % 2
                    ua, uak = UA[i], ("UA", i)
                    y1, y1k = Y1[i], ("Y1", i)
                    y2, y2k = Y2[i], ("Y2", i)
                    CP("pool", ua[:, 0:2], H[:, c, :], ["H"], [uak])
                    CP("act", ua[:, 2:2 + T], PB(ba), [("ps", ba)], [uak], waw=False)
                    CP("pool", H[:, c, :], ua[:, T:T + 2], [uak], ["H"])
                    ACT(y1[:], ua[:, 2:2 + T], AF.Copy, [uak, "vecs"], [y1k], scale=vecs[:, cw + 2 * FC + c:cw + 2 * FC + c + 1])
                    STT("dve", y2[:], ua[:, 1:1 + T], vecs[:, cw + 1 * FC + c:cw + 1 * FC + c + 1], y1[:], ALU.mult, ALU.add, [uak, y1k, "vecs"], [y2k])
                    STT("dve", y1[:], ua[:, 0:T], vecs[:, cw + 0 * FC + c:cw + 0 * FC + c + 1], y2[:], ALU.mult, ALU.add, [uak, y2k, "vecs"], [y1k])
                    ACT(y2[:], y1[:], AF.Silu, [y1k], [y2k])
                    TT("dve", ao[:, c, :], PB(bb_), y2[:], ALU.mult, [("ps", bb_), y2k], [aok], waw=False)
                DMA("pool", actT.rearrange("(c p) t -> p c t", p=128)[:, :, tt * T:(tt + 1) * T], ao[:], f"up_ao{tt % 2}", [aok], [("actT", tt)])
            S.barrier()

    def phase_down(l, last):
        vb = l * V_PER_LAYER
        with ExitStack() as e7:
            def sb7(name, shape, dt):
                return e7.enter_context(nc.sbuf_tensor(f"{name}_L{l}", list(shape), dt))
            mk_rings(sb7, x=True, h=True, ysb=True)
            wd = sb7("dn_w", [128, FC, D], BF16)
            AI = [sb7(f"dn_AI{i}", [128, FC, T], BF16) for i in range(2)]
            for q4 in range(4):
                DMA("pool", wd[:, :, q4 * 256:(q4 + 1) * 256], w_down[l].rearrange("(k p) n -> p k n", p=128)[:, :, q4 * 256:(q4 + 1) * 256],
                    "dn_w", [], [("dn_w", q4)])
            st = {}

            def stA(tt):
                ai, aik = AI[tt % 2], ("AI", tt % 2)
                DMA("sp", ai[:], actT.rearrange("(c p) t -> p c t", p=128)[:, :, tt * T:(tt + 1) * T], f"dn_ai{tt % 2}", [("actT", tt)], [aik])
                xt, xk = G["x"].next()
                DMA("sp", xt[:], xview(xs, tt), xk, [("xs", tt)], [xk])
                st[tt] = dict(xt=xt, xk=xk, ai=ai, aik=aik, pn=PostNorm())

            def stW(tt, mcs):
                d_ = st[tt]
                for mc in mcs:
                    b = bank()
                    for c in range(FC):
                        MM(PB(b), wd[:, c, mc * 128:(mc + 1) * 128], d_["ai"][:, c, :], c == 0, c == FC - 1, [d_["aik"], ("dn_w", mc // 2)], [("ps", b)])
                    d_["pn"].chunk(mc, b)

            def stF1(tt):
                d_ = st[tt]
                xt, xk = d_["xt"], d_["xk"]
                d_["pn"].finish(xt, xk, vb + V_FFNPOST)
                if last:
                    DMA("pool", xview(outT, tt), xt[:], xk, [xk], [("outT", tt)])
                else:
                    DMA("pool", xview(xs, tt), xt[:], xk, [xk], [("xs", tt)])

            def stF2(tt):
                d_ = st.pop(tt)
                xt, xk = d_["xt"], d_["xk"]
                if not last:
                    h1, h1k = G["h"].next()
                    norm_tile(xt, xk, (l + 1) * V_PER_LAYER + V_MIXPRE, h1, h1k)
                    DMA("pool", xview(h1T, tt), h1[:], h1k, [h1k], [("h1T", tt)])

            stA(0)
            stW(0, range(KC))
            for tt in range(1, NT):
                stF1(tt - 1)
                stA(tt)
                stW(tt, range(0, 4))
                stF2(tt - 1)
                stW(tt, range(4, KC))
            stF1(NT - 1)
            stF2(NT - 1)
            S.barrier()


    return dict(nc=nc, S=S, es=es, phases=dict(norm0=phase_norm0, attn=phase_attn, ab=phase_ab, merge=phase_merge,
                                               mem=phase_mem, up=phase_up, down=phase_down))


def build_full(n_layers=NL, dbg=False, stop=None, opts=()):
    P = build_program(n_layers, dbg, opts)
    ph = P["phases"]
    seq = []
    if "nonorm0" not in opts:
        ph["norm0"](0)
    done = stop is not None and stop[1] == "norm0"
    for l in range(n_layers):
        if done:
            break
        for name in ("attn", "ab", "merge", "mem", "up", "down"):
            if name == "down":
                ph[name](l, l == n_layers - 1)
            else:
                ph[name](l)
            if stop is not None and stop == (l, name):
                done = True
                break
        if done:
            break
    nsem = P["S"].emit()
    P["es"].close()
    return P["nc"], P["S"], nsem


def host_inputs(inputs):
    import ml_dtypes
    f32 = np.float32
    perm = w_in_perm()
    wip = np.asarray(inputs["w_in"], f32)[:, :, perm]
    wqkv = np.zeros((NL, 3, D, 1280), f32)
    for g in range(3):
        base = OFF_QKV + 768 * g
        wqkv[:, g, :, 0:256] = wip[:, :, base:base + 256]
        for hh in range(4):
            par = hh % 2
            wqkv[:, g, :, 256 + hh * 128 + 64 * par:256 + hh * 128 + 64 * par + 64] = wip[:, :, base + 256 + 64 * hh:base + 256 + 64 * hh + 64]
            wqkv[:, g, :, 768 + hh * 128 + 64 * par:768 + hh * 128 + 64 * par + 64] = wip[:, :, base + 512 + 64 * hh:base + 512 + 64 * hh + 64]
    shared = {
        "w_in": np.ascontiguousarray(wip[:, :, 0:OFF_QKV]),
        "w_qkv": wqkv,
        "pool_w": np.ascontiguousarray(np.asarray(inputs["pool_w"], f32)),
    }
    for k in ("w_branch_a", "w_branch_b", "w_branch_c", "w_out", "w_mq", "w_mkv", "w_mo", "w_up", "w_down"):
        shared[k] = np.ascontiguousarray(np.asarray(inputs[k], f32))
    vecs = np.zeros((128, NL * V_PER_LAYER), f32)
    for l in range(NL):
        vb = l * V_PER_LAYER
        for name, off in (("norm_mix_pre", V_MIXPRE), ("norm_mix_post", V_MIXPOST), ("norm_mem_pre", V_MEMPRE),
                          ("norm_mem_post", V_MEMPOST), ("norm_memkv", V_MEMKV), ("norm_ffn_pre", V_FFNPRE),
                          ("norm_ffn_post", V_FFNPOST)):
            vecs[:, vb + off:vb + off + 8] = np.asarray(inputs[name], f32)[l].reshape(8, 128).T
        vecs[:, vb + V_CONVB:vb + V_CONVB + 9] = np.asarray(inputs["conv_b_w"], f32)[l].reshape(3, 3, 128).transpose(2, 0, 1).reshape(128, 9)
        vecs[:, vb + V_CONVF:vb + V_CONVF + 66] = np.asarray(inputs["conv_ffn_w"], f32)[l].reshape(3, FC, 128).transpose(2, 0, 1).reshape(128, 66)
        vecs[0:96, vb + V_PSCALE:vb + V_PSCALE + 4] = np.asarray(inputs["pool_scale"], f32)[l].reshape(4, 96).T
    consts = np.zeros((128, NCONST), f32)
    consts[:, C_ID:C_ID + 128] = np.eye(128, dtype=f32)
    k = np.arange(128)[:, None]
    q = np.arange(128)[None, :]
    consts[:, C_MASK:C_MASK + 128] = np.where(k >= q, 0.0, -30000.0)
    consts[:, C_MASK + 128:C_MASK + 256] = np.where(k <= q, 0.0, -30000.0)
    consts[:, C_INV:C_INV + 8] = (f32(500000.0) ** (-np.arange(0, 16, 2, dtype=f32) / f32(16)))[None, :]
    for g, w in enumerate((2, 4, 8, 16)):
        consts[:, C_RC + 16 * g:C_RC + 16 * g + 16] = (1.0 / np.minimum(np.arange(16) + 1, w))[None, :]
    shared["vecs"] = vecs
    shared["consts"] = consts
    x = np.asarray(inputs["x"], f32)
    mem = np.asarray(inputs["mem"], f32)
    posn = np.asarray(inputs["positions"]).astype(np.int32)
    in_maps = []
    for b in range(8):
        m = dict(shared)
        m["xT"] = np.ascontiguousarray(x[b].T)
        m["memT"] = np.ascontiguousarray(mem[b].T)
        m["pos"] = np.ascontiguousarray(posn[b].reshape(32, 128))
        in_maps.append(m)
    return in_maps


_CACHE = {}


def kernel(**inputs):
    in_maps = host_inputs(inputs)
    if "nc" not in _CACHE:
        _CACHE["nc"] = build_full()[0]
    nc = _CACHE["nc"]
    res = run_bass_kernel_spmd(nc, in_maps, core_ids=list(range(8)))
    out = np.stack([np.ascontiguousarray(res.results[b]["outT"].T) for b in range(8)], axis=0)
    return out.astype(np.float32)
```

```python
import numpy as np
import concourse.bass as bass
import concourse.mybir as mybir

F32 = mybir.dt.float32
BF16 = mybir.dt.bfloat16
I32 = mybir.dt.int32
ALU = mybir.AluOpType
AF = mybir.ActivationFunctionType

EPOCH = 30000


class Sched:
    ENGS = ("pe", "act", "dve", "pool", "sp")

    def __init__(self, nc):
        self.nc = nc
        self.stream = {e: [] for e in self.ENGS}
        self.cnt = {e: 0 for e in self.ENGS}
        self.known = {e: {} for e in self.ENGS}
        self.res = {}
        self.chan_cnt = {}
        self.n_wait = 0

    def _collect(self, eng, reads, writes, waw):
        waits = {}

        def need(k, v):
            if k == ("eng", "pe") and eng == "pe":
                return
            if self.known[eng].get(k, 0) >= v:
                return
            if waits.get(k, 0) < v:
                waits[k] = v

        for r in reads:
            st = self.res.get(r)
            if st:
                for k, v in st["w"].items():
                    need(k, v)
        for w in writes:
            st = self.res.get(w)
            if st:
                for k, v in st["r"].items():
                    need(k, v)
                if waw:
                    for k, v in st["w"].items():
                        need(k, v)
        for k, v in waits.items():
            self.known[eng][k] = v
        return sorted(waits.items(), key=lambda kv: str(kv[0]))

    def _commit(self, ev, reads, writes, waw):
        k, v = ev
        for r in reads:
            st = self.res.setdefault(r, {"w": {}, "r": {}})
            if st["r"].get(k, 0) < v:
                st["r"][k] = v
        for w in writes:
            st = self.res.setdefault(w, {"w": {}, "r": {}})
            if st["r"] or waw:
                st["w"] = {}
            st["r"] = {}
            if st["w"].get(k, 0) < v:
                st["w"][k] = v

    def op(self, eng, fn, reads=(), writes=(), waw=True):
        waits = self._collect(eng, reads, writes, waw)
        self.cnt[eng] += 1
        ev = (("eng", eng), self.cnt[eng])
        self._commit(ev, reads, writes, waw)
        self.stream[eng].append((waits, fn, ev))
        self.n_wait += len(waits)

    def dma(self, eng, fn, chan, reads=(), writes=(), waw=False):
        waits = self._collect(eng, reads, writes, waw)
        self.chan_cnt[chan] = self.chan_cnt.get(chan, 0) + 16
        ev = (("chan", chan), self.chan_cnt[chan])
        self._commit(ev, reads, writes, waw)
        self.stream[eng].append((waits, fn, ev))
        self.n_wait += len(waits)

    def barrier(self, engs=None):
        engs = engs or self.ENGS
        for e in engs:
            waits = {}
            for e2 in self.ENGS:
                if e2 != e and self.cnt[e2] > 0:
                    k = ("eng", e2)
                    if self.known[e].get(k, 0) < self.cnt[e2]:
                        waits[k] = self.cnt[e2]
            for c, v in self.chan_cnt.items():
                k = ("chan", c)
                if self.known[e].get(k, 0) < v:
                    waits[k] = v
            for k, v in waits.items():
                self.known[e][k] = v
            if waits:
                self.stream[e].append((sorted(waits.items(), key=lambda kv: str(kv[0])), None, None))

    def emit(self):
        nc = self.nc
        sems = {}

        def sem_of(k, v):
            if k[0] == "eng":
                ep = (v - 1) // EPOCH
                key = (k, ep)
                val = (v - 1) % EPOCH + 1
            else:
                key = (k, 0)
                val = v
            if key not in sems:
                sems[key] = nc.alloc_semaphore(name=f"s{len(sems)}")
            return sems[key], val

        self.barrier(engs=("sp",))
        for e in self.ENGS:
            for waits, fn, ev in self.stream[e]:
                for k, v in waits:
                    sem_of(k, v)
                if ev is not None:
                    sem_of(*ev)
        eng_map = {"pe": "tensor", "act": "scalar", "dve": "vector", "pool": "gpsimd", "sp": "sync"}
        with nc.Block() as block:
            for e in self.ENGS:
                if not self.stream[e]:
                    continue
                deco = getattr(block, eng_map[e])

                def body(h, e=e):
                    for waits, fn, ev in self.stream[e]:
                        for k, v in waits:
                            s, val = sem_of(k, v)
                            h.wait_ge(s, val)
                        if fn is None:
                            continue
                        ins = fn(h)
                        s, _ = sem_of(*ev)
                        ins.then_inc(s, 16 if ev[0][0] == "chan" else 1)

                deco(body)
        return len(sems)


def sap(t, F, poff, npart, off, dims):
    return bass.AP(t, poff * F + off, [[F, npart]] + [list(d) for d in dims])

from concourse.bass_utils import run_bass_kernel_spmd
from contextlib import ExitStack

D = 1024
SEQ = 4096
T = 512
NT = SEQ // T
KC = 8
DFF = 2816
FC = DFF // 128
NL = 2
EPS = 1e-6
DILS = (1, 4, 16)
OFF_A = 0
OFF_BX = 384
OFF_BB = 768
OFF_BC = 1152
OFF_GA = 1536
OFF_GB = 2560
OFF_GC = 3584
OFF_QKV = 4608
NAB = 3584
V_MIXPRE, V_MIXPOST, V_MEMPRE, V_MEMPOST, V_MEMKV, V_FFNPRE, V_FFNPOST = 0, 8, 16, 24, 32, 40, 48
V_CONVB = 56
V_CONVF = 65
V_PSCALE = 131
V_PER_LAYER = 135
C_ID = 0
C_MASK = 128
C_INV = 384
C_RC = 392
NCONST = 456


def w_in_perm():
    idx = []
    a = 0
    idx += list(range(0, 384))
    idx += list(range(384, 384 + 1152))
    g0 = 384 + 1152 + 3 * 768
    idx += list(range(g0, g0 + 3072))
    q0 = 384 + 1152
    for g in range(3):
        for part in range(3):
            s = q0 + part * 768 + g * 256
            idx += list(range(s, s + 256))
    return np.array(idx, dtype=np.int64)


def build_program(n_layers=NL, dbg=False, opts=()):
    nc = bass.Bass("TRN2", target_bir_lowering=False)
    S = Sched(nc)

    def din(name, shape, dt=F32):
        return nc.dram_tensor(name, list(shape), dt, kind="ExternalInput").ap()

    kind_s = "ExternalOutput" if dbg else "Internal"

    def dscr(name, shape, dt):
        return nc.dram_tensor(name, list(shape), dt, kind=kind_s).ap()

    xT = din("xT", [D, SEQ])
    memT = din("memT", [D, 256])
    pos = din("pos", [32, 128], I32)
    w_in = din("w_in", [NL, D, OFF_QKV])
    w_qkv = din("w_qkv", [NL, 3, D, 1280])
    pool_w = din("pool_w", [NL, 4, 96, 96])
    w_a = din("w_branch_a", [NL, 384, D])
    w_b = din("w_branch_b", [NL, 384, D])
    w_c = din("w_branch_c", [NL, 256, D])
    w_out = din("w_out", [NL, D, D])
    w_mq = din("w_mq", [NL, D, 512])
    w_mkv = din("w_mkv", [NL, D, 1024])
    w_mo = din("w_mo", [NL, 512, D])
    w_up = din("w_up", [NL, D, 2 * DFF])
    w_down = din("w_down", [NL, DFF, D])
    vecs_d = din("vecs", [128, NL * V_PER_LAYER])
    consts_d = din("consts", [128, NCONST])
    outT = nc.dram_tensor("outT", [D, SEQ], F32, kind="ExternalOutput").ap()

    h1T = dscr("h1T", [D, SEQ], BF16)
    h2T = dscr("h2T", [D, SEQ], BF16)
    h3T = dscr("h3T", [D, SEQ], BF16)
    attnT = dscr("attnT", [256, SEQ], BF16)
    mab = dscr("mab", [D, SEQ], F32)
    xs = dscr("xs", [D, SEQ], F32)
    actT = dscr("actT", [DFF, SEQ], BF16)
    rope_d = dscr("rope_tab", [SEQ, 16], F32)

    es = ExitStack()

    def sb(name, shape, dt):
        return es.enter_context(nc.sbuf_tensor("s_" + name, list(shape), dt))

    ps = es.enter_context(nc.psum_tensor("ps", [128, 4096], F32))

    def MM(out, lhsT, rhs, start, stop, R, W):
        S.op("pe", lambda h: h.matmul(out, lhsT=lhsT, rhs=rhs, start=start, stop=stop), reads=R, writes=W)

    def ACT(out, in_, func, R, W, scale=1.0, bias=None, waw=True):
        if bias is None:
            S.op("act", lambda h: h.activation(out=out, in_=in_, func=func, scale=scale), reads=R, writes=W, waw=waw)
        else:
            S.op("act", lambda h: h.activation(out=out, in_=in_, func=func, scale=scale, bias=bias), reads=R, writes=W, waw=waw)

    def TT(eng, out, in0, in1, op, R, W, waw=True):
        S.op(eng, lambda h: h.tensor_tensor(out=out, in0=in0, in1=in1, op=op), reads=R, writes=W, waw=waw)

    def TS(eng, out, in0, s1, s2, op0, op1, R, W, waw=True):
        if s2 is None:
            S.op(eng, lambda h: h.tensor_scalar(out=out, in0=in0, scalar1=s1, scalar2=None, op0=op0), reads=R, writes=W, waw=waw)
        else:
            S.op(eng, lambda h: h.tensor_scalar(out=out, in0=in0, scalar1=s1, scalar2=s2, op0=op0, op1=op1), reads=R, writes=W, waw=waw)

    def STT(eng, out, in0, scalar, in1, op0, op1, R, W, waw=True):
        S.op(eng, lambda h: h.scalar_tensor_tensor(out=out, in0=in0, scalar=scalar, in1=in1, op0=op0, op1=op1), reads=R, writes=W, waw=waw)

    def CP(eng, out, in_, R, W, waw=True):
        if eng == "act":
            ACT(out, in_, AF.Copy, R, W, waw=waw)
        else:
            S.op(eng, lambda h: h.tensor_copy(out=out, in_=in_), reads=R, writes=W, waw=waw)

    def MS(eng, ap, val, W):
        S.op(eng, lambda h: h.memset(ap, val), writes=W)

    def DMA(eng, out, in_, chan, R, W):
        if eng == "pool":
            chan = ("swq", chan)
        S.dma(eng, lambda h: h.dma_start(out=out, in_=in_), chan, reads=R, writes=W)

    uniq = [0]

    class Ring:
        def __init__(self, name, shape, dt, n, alloc=None):
            alloc = alloc or sb
            uniq[0] += 1
            self.name = name
            self.t = [alloc(f"{name}_{uniq[0]}_{i}", shape, dt) for i in range(n)]
            self.i = 0

        def next(self):
            k = self.i % len(self.t)
            self.i += 1
            return self.t[k], (self.name, k)

    psp = [0]

    def bank(n=1):
        if n == 2 and psp[0] % 2 == 1:
            psp[0] += 1
        p = psp[0] % 6
        psp[0] += n
        return p

    def PB(b, lo=0, hi=512):
        return ps[:, b * 512 + lo: b * 512 + hi]

    vecs = sb("vecs", [128, NL * V_PER_LAYER], F32)
    consts = sb("consts", [128, NCONST], F32)
    ident = sb("ident", [128, 128], BF16)
    mask2 = sb("mask2", [128, 256], BF16)
    onesm = sb("onesm", [128, 128], BF16)
    ones = sb("ones", [128, 128], BF16)
    onesP = sb("onesP", [128, 2, 128], BF16)
    DMA("sp", vecs[:], vecs_d, "vecs", [], ["vecs"])
    DMA("sp", consts[:], consts_d, "consts", [], ["consts"])
    CP("dve", ident[:], consts[:, C_ID:C_ID + 128], ["consts"], ["cb"])
    CP("dve", mask2[:], consts[:, C_MASK:C_MASK + 256], ["consts"], ["cb"])
    MS("pool", onesm[:], 1.0 / 1024.0, ["cb"])
    MS("pool", ones[:], 1.0, ["cb"])
    MS("pool", onesP[:], 0.0, ["cb"])
    MS("pool", onesP[:, 0, 0:64], 1.0, ["cb"])
    MS("pool", onesP[:, 1, 64:128], 1.0, ["cb"])

    sqring = Ring("sq", [128, T], BF16, 3)
    rstdring = Ring("rstd", [128, T], F32, 2)
    G = {}

    def mk_rings(alloc, x=False, h=False, hin=False, ysb=False):
        if x:
            G["x"] = Ring("xt", [128, KC, T], F32, 2, alloc)
        if h:
            G["h"] = Ring("ht", [128, KC, T], BF16, 2, alloc)
        if hin:
            G["hin"] = Ring("hin", [128, KC, T], BF16, 2, alloc)
        if ysb:
            G["ysb"] = Ring("ysb", [128, KC, T], F32, 2, alloc)
    tmp_ring = Ring("tmpf", [128, T], F32, 3)

    def xview(d, tt):
        return d.rearrange("(c p) t -> p c t", p=128)[:, :, tt * T:(tt + 1) * T]

    def rstd_from_bank(b):
        rstd, rk = rstdring.next()
        ACT(rstd[:], PB(b), AF.Ln, [("ps", b)], [rk], bias=EPS)
        ACT(rstd[:], rstd[:], AF.Exp, [rk], [rk], scale=-0.5)
        return rstd, rk

    def norm_tile(xt, xk, gcol, ht, hk, ncols=T):
        b = bank()
        for c in range(KC):
            sq, sqk = sqring.next()
            ACT(sq[:, 0:ncols], xt[:, c, 0:ncols], AF.Square, [xk], [sqk])
            MM(PB(b, 0, ncols), onesm[:], sq[:, 0:ncols], c == 0, c == KC - 1, [sqk, "cb"], [("ps", b)])
        rstd, rk = rstdring.next()
        ACT(rstd[:, 0:ncols], PB(b, 0, ncols), AF.Ln, [("ps", b)], [rk], bias=EPS)
        ACT(rstd[:, 0:ncols], rstd[:, 0:ncols], AF.Exp, [rk], [rk], scale=-0.5)
        for c in range(KC):
            eng = "dve"
            STT(eng, ht[:, c, 0:ncols], xt[:, c, 0:ncols], vecs[:, gcol + c:gcol + c + 1], rstd[:, 0:ncols],
                ALU.mult, ALU.mult, [xk, rk, "vecs"], [hk], waw=False)

    pncnt = [0]

    class PostNorm:
        def __init__(self):
            self.ysb, self.yk = G["ysb"].next()
            pncnt[0] += 1
            self.b = 6 + pncnt[0] % 2
            self.pend = None

        def chunk(self, mc, b):
            CP("act", self.ysb[:, mc, :], PB(b), [("ps", b)], [self.yk], waw=False)
            sq, sqk = sqring.next()
            TT("dve", sq[:], PB(b), self.ysb[:, mc, :], ALU.mult, [("ps", b), self.yk], [sqk])
            if self.pend is not None:
                self.flush()
            self.pend = (mc, sq, sqk)
            if mc == KC - 1:
                self.flush()

        def flush(self):
            mc, sq, sqk = self.pend
            self.pend = None
            MM(PB(self.b), onesm[:], sq[:], mc == 0, mc == KC - 1, [sqk, "cb"], [("ps", self.b)])

        def finish(self, xt, xk, gcol):
            rstd, rk = rstd_from_bank(self.b)
            for c in range(KC):
                tmp, tk = tmp_ring.next()
                TT("pool", tmp[:], self.ysb[:, c, :], rstd[:], ALU.mult, [self.yk, rk], [tk])
                STT("dve", xt[:, c, :], tmp[:], vecs[:, gcol + c:gcol + c + 1], xt[:, c, :], ALU.mult, ALU.add,
                    [tk, xk, "vecs"], [xk], waw=False)

    def load_w(dst, src2d, ncols, chan, key, rows=128):
        v = src2d.rearrange("(k p) n -> p k n", p=rows)
        for cb in range(0, ncols, 2048):
            w = min(2048, ncols - cb)
            DMA("pool", dst[:, :, cb:cb + w], v[:, :, cb:cb + w], chan, [], [key])

    def load_wb(dst, src2d, ncols, name, order=None, blk=512, rows=128):
        v = src2d.rearrange("(k p) n -> p k n", p=rows)
        nb = (ncols + blk - 1) // blk
        for b in (order if order is not None else range(nb)):
            lo = b * blk
            w = min(blk, ncols - lo)
            DMA("pool", dst[:, :, lo:lo + w], v[:, :, lo:lo + w], f"{name}_{b}", [], [(name, b)])

    with ExitStack() as es0:
      if 'norope' not in opts:
          def sb0(name, shape, dt):
              return es0.enter_context(nc.sbuf_tensor(name, list(shape), dt))
          pi_ = sb0("rp_pi", [32, 128], I32)
          pf = sb0("rp_pf", [32, 128], F32)
          ang = sb0("rp_ang", [32, 128, 16], F32)
          a2 = sb0("rp_a2", [32, 128, 16], F32)
          ki = sb0("rp_ki", [32, 128, 16], I32)
          tab = sb0("rp_tab", [32, 128, 16], F32)
          DMA("sp", pi_[:], pos, "rp", [], ["rp_pi"])
          CP("dve", pf[:], pi_[:], ["rp_pi"], ["rp_pf"])
          inv_b = consts[0:32, C_INV:C_INV + 8].unsqueeze(1).to_broadcast([32, 128, 8])
          pf_b = pf[:].unsqueeze(2).to_broadcast([32, 128, 8])
          TT("dve", ang[:, :, 0:8], pf_b, inv_b, ALU.mult, ["rp_pf", "consts"], ["rp_ang"])
          TS("dve", ang[:, :, 8:16], ang[:, :, 0:8], float(np.pi / 2), None, ALU.add, None, ["rp_ang"], ["rp_ang"])
          TS("dve", a2[:], ang[:], float(1.0 / (2 * np.pi)), None, ALU.mult, None, ["rp_ang"], ["rp_a2"])
          CP("dve", ki[:], a2[:], ["rp_a2"], ["rp_ki"])
          CP("dve", a2[:], ki[:], ["rp_ki"], ["rp_a2"])
          STT("dve", ang[:], a2[:], float(-2 * np.pi), ang[:], ALU.mult, ALU.add, ["rp_a2", "rp_ang"], ["rp_ang"])
          TS("dve", a2[:], ang[:], float(np.pi), float(-2 * np.pi), ALU.is_gt, ALU.mult, ["rp_ang"], ["rp_a2"])
          TT("dve", ang[:], ang[:], a2[:], ALU.add, ["rp_ang", "rp_a2"], ["rp_ang"])
          TS("dve", a2[:], ang[:], float(-np.pi), float(2 * np.pi), ALU.is_lt, ALU.mult, ["rp_ang"], ["rp_a2"])
          TT("dve", ang[:], ang[:], a2[:], ALU.add, ["rp_ang", "rp_a2"], ["rp_ang"])
          ACT(tab[:, :, 0:8], ang[:, :, 8:16], AF.Sin, ["rp_ang"], ["rp_tab"])
          ACT(tab[:, :, 8:16], ang[:, :, 0:8], AF.Sin, ["rp_ang"], ["rp_tab"], waw=False)
          DMA("sp", rope_d.rearrange("(b i) f -> b i f", i=128), tab[:], "rp", ["rp_tab"], ["rope_d"])
          S.barrier()

    def phase_norm0(l):
      with ExitStack() as e1:
        mk_rings(lambda n, sh, dt: e1.enter_context(nc.sbuf_tensor(n, list(sh), dt)), x=True, h=True)
        for tt in range(NT):
            xt, xk = G["x"].next()
            DMA("sp", xt[:], xview(xT, tt), xk, [], [xk])
            ht, hk = G["h"].next()
            norm_tile(xt, xk, l * V_PER_LAYER + V_MIXPRE, ht, hk)
            DMA("pool", xview(h1T, tt), ht[:], hk, [hk], [("h1T", tt)])
        S.barrier()

    def phase_attn(l):
        with ExitStack() as e2:
            def sb2(name, shape, dt):
                return e2.enter_context(nc.sbuf_tensor(f"{name}_L{l}", list(shape), dt))
            hT = sb2("at_hT", [128, KC, SEQ], BF16)
            acc = sb2("at_acc", [128, 4, 2048], F32)
            attn_sb = sb2("at_out", [128, 2, 2048], BF16)
            wq = [sb2(f"at_w{i}", [128, KC, 1280], BF16) for i in range(2)]
            cs = [sb2(f"at_cs{g}", [128, 32, 16], F32) for g in range(3)]
            QK = [sb2(f"at_qk{i}", [128, 768], BF16) for i in range(3)]
            QF = [sb2(f"at_qf{i}", [128, 768], F32) for i in range(3)]
            ODT = [sb2(f"at_od{i}", [128, 512], F32) for i in range(2)]
            Vb = [sb2(f"at_v{i}", [128, 4, 128], BF16) for i in range(4)]
            QT = [sb2(f"at_qt{i}", [128, 2, 128], BF16) for i in range(4)]
            KTp = [sb2(f"at_kt{i}", [128, 4, 128], BF16) for i in range(4)]
            PT = [sb2(f"at_pt{i}", [128, 4, 256], BF16) for i in range(2)]
            rt = [sb2(f"at_rt{i}", [128, 4, 8, 8], F32) for i in range(4)]
            for tt in range(NT):
                DMA("sp", hT[:, :, tt * T:(tt + 1) * T], xview(h1T, tt), "at_hT", [("h1T", tt)], ["at_hT"])
            for g in range(3):
                d = DILS[g]
                nb = 32 // d
                for r in range(d):
                    for j in range(nb):
                        src = bass.AP(rope_d.tensor, (r + d * 128 * j) * 16, [[16 * d, 128], [1, 16]])
                        DMA("sp", cs[g][:, r * nb + j, :], src, f"at_cs{g}", ["rope_d"], [("cs", g)])
            wl = [0]
            cnt = {"rt": 0, "od": 0}
            NS = 4

            def P1(it):
                g, r, j, need_q, wt, wk = it["g"], it["r"], it["j"], it["need_q"], it["wt"], it["wk"]
                d = DILS[g]
                nb = 32 // d
                slot = it["slot"]
                start = r + d * 128 * j
                sl = slice(start, start + 127 * d + 1, d)
                bq = bank() if need_q else None
                bk = bank()
                bv = bank()
                if need_q:
                    for kc in range(KC):
                        MM(PB(bq, 0, 256), hT[:, kc, sl], wt[:, kc, 0:256], kc == 0, kc == KC - 1, ["at_hT", wk], [("ps", bq)])
                for kc in range(KC):
                    MM(PB(bk), hT[:, kc, sl], wt[:, kc, 256:768], kc == 0, kc == KC - 1, ["at_hT", wk], [("ps", bk)])
                for kc in range(KC):
                    MM(PB(bv), hT[:, kc, sl], wt[:, kc, 768:1280], kc == 0, kc == KC - 1, ["at_hT", wk], [("ps", bv)])
                qi = it["idx"] % 3
                qk, qkk = QK[qi], ("QK", qi)
                qf, qfk = QF[qi], ("QF", qi)
                it["qk"], it["qkk"] = qk, qkk
                if need_q:
                    CP("act", qf[:, 0:256], PB(bq, 0, 256), [("ps", bq)], [qfk])
                CP("act", qf[:, 256:768], PB(bk), [("ps", bk)], [qfk], waw=not need_q)
                lo = 0 if need_q else 256
                CP("act", qk[:, lo:768], qf[:, lo:768], [qfk], [qkk])
                CP("dve", Vb[slot][:].rearrange("p a b -> p (a b)"), PB(bv), [("ps", bv)], [("Vb", slot)])
                blk = r * nb + j
                csk = ("cs", g)
                views = []
                if need_q:
                    views.append((qf[:, 0:256].rearrange("p (h e) -> p h e", e=64), qk[:, 0:256].rearrange("p (h e) -> p h e", e=64), [128, 4, 8], 0))
                kfv = bass.AP(qf, 256, [[768, 128], [256, 2], [192, 2], [1, 64]])
                kbv = bass.AP(qk, 256, [[768, 128], [256, 2], [192, 2], [1, 64]])
                views.append((kfv, kbv, [128, 2, 2, 8], 1))
                for (fv, bv_, shp, which) in views:
                    ri = cnt["rt"] % 4
                    cnt["rt"] += 1
                    rtt, rtk = rt[ri], ("rt", ri)
                    if which == 0:
                        cosb = cs[g][:, blk, 0:8].unsqueeze(1).to_broadcast(shp)
                        sinb = cs[g][:, blk, 8:16].unsqueeze(1).to_broadcast(shp)
                        u1, u2 = fv[:, :, 0:8], fv[:, :, 8:16]
                        o1, o2 = bv_[:, :, 0:8], bv_[:, :, 8:16]
                        tv = [rtt[:, k, 0:4, :] for k in range(4)]
                    else:
                        cosb = cs[g][:, blk, 0:8].unsqueeze(1).unsqueeze(1).to_broadcast(shp)
                        sinb = cs[g][:, blk, 8:16].unsqueeze(1).unsqueeze(1).to_broadcast(shp)
                        u1, u2 = fv[:, :, :, 0:8], fv[:, :, :, 8:16]
                        o1, o2 = bv_[:, :, :, 0:8], bv_[:, :, :, 8:16]
                        tv = [rtt[:, k, 0:4, :].rearrange("p (a b) e -> p a b e", a=2) for k in range(4)]
                    TT("dve", tv[0], u1, cosb, ALU.mult, [qfk, csk], [rtk])
                    TT("dve", tv[1], u2, sinb, ALU.mult, [qfk, csk], [rtk], waw=False)
                    TT("dve", tv[2], u2, cosb, ALU.mult, [qfk, csk], [rtk], waw=False)
                    TT("dve", tv[3], u1, sinb, ALU.mult, [qfk, csk], [rtk], waw=False)
                    TT("pool", o1, tv[0], tv[1], ALU.subtract, [rtk], [qkk])
                    TT("pool", o2, tv[2], tv[3], ALU.add, [rtk], [qkk])

            def P2(it):
                need_q, slot, qk, qkk = it["need_q"], it["slot"], it["qk"], it["qkk"]
                if need_q:
                    bt = bank()
                    for ch in range(2):
                        MM(PB(bt, ch * 128, ch * 128 + 128), qk[:, ch * 128:(ch + 1) * 128], ident[:], True, True, [qkk, "cb"], [("ps", bt)])
                    CP("act", QT[slot][:].rearrange("p c t -> p (c t)"), PB(bt, 0, 256), [("ps", bt)], [("QT", slot)])
                bt2 = bank()
                for hh in range(4):
                    MM(PB(bt2, hh * 128, hh * 128 + 128), qk[:, 256 + hh * 128:256 + (hh + 1) * 128], ident[:], True, True, [qkk, "cb"], [("ps", bt2)])
                CP("dve", KTp[slot][:].rearrange("p a b -> p (a b)"), PB(bt2), [("ps", bt2)], [("KTp", slot)])

            def A1(it):
                j = it["j"]
                sc, sp_ = it["slot"], (it["slot"] - 1) % NS
                b2 = bank(2)
                for hh in range(4):
                    ch = hh // 2
                    bb = b2 + hh // 2
                    c0 = (hh % 2) * 256
                    if j > 0:
                        MM(PB(bb, c0, c0 + 256), ident[:], mask2[:, 0:256], True, False, ["cb"], [("ps", bb)])
                        MM(PB(bb, c0, c0 + 128), KTp[sp_][:, hh, :], QT[sc][:, ch, :], False, False, [("KTp", sp_), ("QT", sc)], [("ps", bb)])
                    else:
                        MM(PB(bb, c0 + 128, c0 + 256), ident[:], mask2[:, 128:256], True, False, ["cb"], [("ps", bb)])
                    MM(PB(bb, c0 + 128, c0 + 256), KTp[sc][:, hh, :], QT[sc][:, ch, :], False, True, [("KTp", sc), ("QT", sc)], [("ps", bb)])
                pi = it["idx"] % 2
                pt, ptk = PT[pi], ("PT", pi)
                it["pt"], it["ptk"] = pt, ptk
                for hb in range(2):
                    bb = b2 + hb
                    if j > 0:
                        ACT(pt[:, 2 * hb:2 * hb + 2, :].rearrange("p a b -> p (a b)"), PB(bb), AF.Exp, [("ps", bb)], [ptk], scale=0.125, waw=False)
                    else:
                        for a in range(2):
                            ACT(pt[:, 2 * hb + a, 128:256], PB(bb, a * 256 + 128, a * 256 + 256), AF.Exp,
                                [("ps", bb)], [ptk], scale=0.125, waw=False)

            def A2(it):
                g, r, j, half = it["g"], it["r"], it["j"], it["half"]
                d = DILS[g]
                sc, sp_ = it["slot"], (it["slot"] - 1) % NS
                pt, ptk = it["pt"], it["ptk"]
                b3 = bank()
                kbs = [0, 1] if j > 0 else [1]
                for od in range(2):
                    for pair in range(2):
                        mats = [(hh, kb) for hh in (2 * pair, 2 * pair + 1) for kb in kbs]
                        c0 = od * 256 + pair * 128
                        for i, (hh, kb) in enumerate(mats):
                            vs = sp_ if kb == 0 else sc
                            lhs = Vb[vs][:, hh, :] if od == 0 else onesP[:, hh % 2, :]
                            rd = [ptk, ("Vb", vs)] if od == 0 else [ptk, "cb"]
                            MM(PB(b3, c0, c0 + 128), lhs, pt[:, hh, kb * 128:(kb + 1) * 128], i == 0, i == len(mats) - 1, rd, [("ps", b3)])
                off = r + d * 128 * j - 2048 * half
                av = bass.AP(acc, off, [[4 * 2048, 128], [2048, 4], [d, 128]])
                oi = cnt["od"] % 2
                cnt["od"] += 1
                odt, odk = ODT[oi], ("ODT", oi)
                CP("act", odt[:], PB(b3), [("ps", b3)], [odk])
                pv = odt[:].rearrange("p (a t) -> p a t", t=128)
                if g == 0:
                    CP("pool", av, pv, [odk], ["acc"], waw=True)
                else:
                    TT("pool", av, pv, av, ALU.add, [odk, "acc"], ["acc"])

            gidx = [0]
            allsegs = []
            for half in range(2):
                items = []
                for g in range(3):
                    d = DILS[g]
                    nb = 32 // d
                    jl, jh = half * nb // 2, (half + 1) * nb // 2
                    for r in range(d):
                        if jl > 0:
                            items.append(dict(g=g, r=r, j=jl - 1, need_q=False, att=False, half=half))
                        for j in range(jl, jh):
                            items.append(dict(g=g, r=r, j=j, need_q=True, att=True, half=half))
                for it in items:
                    it["idx"] = gidx[0]
                    it["slot"] = gidx[0] % NS
                    gidx[0] += 1
                segs = []
                for it in items:
                    if not segs or segs[-1][0] != it["g"]:
                        segs.append((it["g"], []))
                    segs[-1][1].append(it)
                allsegs.append(segs)
            flat = [(half, g, its) for half in range(2) for (g, its) in allsegs[half]]
            wbuf = {}

            def issue_w(k):
                half, g, its = flat[k]
                wi = k % 2
                wt, wk = wq[wi], ("at_w", wi)
                v = w_qkv[l, g].rearrange("(k p) n -> p k n", p=128)
                for kc in range(KC):
                    DMA("pool", wt[:, kc:kc + 1, :], v[:, kc:kc + 1, :], f"at_w{wi}", [], [wk])
                for it in its:
                    it["wt"], it["wk"] = wt, wk

            issue_w(0)
            for k, (half, g, its) in enumerate(flat):
                if k + 1 < len(flat):
                    issue_w(k + 1)
                items = its
                n = len(items)
                first_of_half = (k == 0 or flat[k - 1][0] != half)
                last_of_half = (k + 1 == len(flat) or flat[k + 1][0] != half)
                if first_of_half:
                    P1(items[0])
                    prev_last = None
                for i in range(n):
                    nxt = items[i + 1] if i + 1 < n else (flat[k + 1][2][0] if not last_of_half else None)
                    if nxt is not None:
                        P1(nxt)
                    P2(items[i])
                    prv = items[i - 1] if i > 0 else prev_last
                    if prv is not None and prv["att"]:
                        A2(prv)
                    if items[i]["att"]:
                        A1(items[i])
                prev_last = items[n - 1]
                if not last_of_half:
                    continue
                if prev_last["att"]:
                    A2(prev_last)
                prev_last = None
                for pair in range(2):
                    ACT(acc[:, 2 + pair, :], acc[:, 2 + pair, :], AF.Ln, ["acc"], ["acc"])
                    ACT(acc[:, 2 + pair, :], acc[:, 2 + pair, :], AF.Exp, ["acc"], ["acc"], scale=-1.0)
                    TT("dve", attn_sb[:, pair, :], acc[:, pair, :], acc[:, 2 + pair, :], ALU.mult, ["acc"], ["attn_sb"])
                dv = attnT.rearrange("(c p) t -> p c t", p=128)[:, :, half * 2048:(half + 1) * 2048]
                DMA("pool", dv, attn_sb[:], "attn_sb", ["attn_sb"], [("attnT", half)])
            S.barrier()

    def phase_ab(l):
        vb = l * V_PER_LAYER
        with ExitStack() as e3:
            def sb3(name, shape, dt):
                return e3.enter_context(nc.sbuf_tensor(f"{name}_L{l}", list(shape), dt))
            mk_rings(sb3, hin=True)
            wab = sb3("ab_w", [128, KC, NAB], BF16)
            pw = sb3("ab_pw", [96, 4, 96], BF16)
            wa = sb3("ab_wa", [96, 4, D], BF16)
            wb = sb3("ab_wb", [128, 3, D], BF16)
            A = [sb3(f"ab_A{g}", [96, 16 + T], F32) for g in range(4)]
            T2 = sb3("ab_T2", [96, 16 + T], F32)
            T4 = sb3("ab_T4", [96, 16 + T], F32)
            T8 = sb3("ab_T8", [96, 16 + T], F32)
            PL = [sb3(f"ab_PL{g}", [96, T], BF16) for g in range(4)]
            MX = [sb3(f"ab_MX{g}", [96, T], BF16) for g in range(4)]
            BX = sb3("ab_BX", [128, T], F32)
            P = [sb3(f"ab_P{c}", [128, 2 + T], F32) for c in range(3)]
            Y1 = sb3("ab_Y1", [128, T], F32)
            Y2 = sb3("ab_Y2", [128, T], F32)
            Z = [sb3(f"ab_Z{c}", [128, T], BF16) for c in range(3)]
            SG = [sb3(f"ab_SG{i}", [128, T], F32) for i in range(2)]
            MO = [sb3(f"ab_MO{i}", [128, KC, T], F32) for i in range(2)]
            load_wb(wab, w_in[l][:, 0:NAB], NAB, "ab_w", order=[0])
            DMA("pool", pw[:], pool_w[l].rearrange("g c d -> c g d"), "ab_w2", [], ["ab_w2"])
            DMA("pool", wa[:], w_a[l].rearrange("(g c) n -> c g n", c=96), "ab_w2", [], ["ab_w2"])
            load_wb(wab, w_in[l][:, 0:NAB], NAB, "ab_w", order=[1, 2, 3, 4])
            DMA("pool", wb[:], w_b[l].rearrange("(k p) n -> p k n", p=128), "ab_w3", [], ["ab_w3"])
            load_wb(wab, w_in[l][:, 0:NAB], NAB, "ab_w", order=[5, 6])
            for g in range(4):
                MS("pool", A[g][:, 0:16], 0.0, [("A", g)])
            for c in range(3):
                MS("pool", P[c][:, 0:2], 0.0, [("P", c)])
            sgi = [0]
            for tt in range(NT):
                ht, hk = G["hin"].next()
                DMA("sp", ht[:], xview(h1T, tt), hk, [("h1T", tt)], [hk])
                mo, mok = MO[tt % 2], ("MO", tt % 2)
                for g in range(4):
                    w = (2, 4, 8, 16)[g]
                    Ak = ("A", g)
                    if tt > 0:
                        CP("pool", A[g][:, 0:16], A[g][:, T:T + 16], [Ak], [Ak])
                    b = bank()
                    for kc in range(KC):
                        MM(ps[0:96, b * 512:(b + 1) * 512], wab[:, kc, OFF_A + 96 * g:OFF_A + 96 * (g + 1)], ht[:, kc, :], kc == 0, kc == KC - 1,
                           [hk, ("ab_w", 0)], [("ps", b)])
                    CP("act", A[g][:, 16:16 + T], ps[0:96, b * 512:(b + 1) * 512], [("ps", b)], [Ak])
                    src = A[g]
                    srck = Ak
                    n = 1
                    for (dst, dk) in ((T2, "T2"), (T4, "T4"), (T8, "T8"), (None, None)):
                        if n >= w:
                            break
                        if 2 * n == w:
                            tmpw, tmpk = T2 if dst is not T2 and src is not T2 else (T4 if src is not T4 else T8), None
                            tmpw = {1: T2, 2: T4, 4: T8, 8: T2}[n]
                            tmpk = {1: "T2", 2: "T4", 4: "T8", 8: "T2"}[n]
                            TT("pool", tmpw[:, 16:16 + T], src[:, 16:16 + T], src[:, 16 - n:16 - n + T], ALU.add, [srck], [tmpk])
                            STT("dve", PL[g][:], tmpw[:, 16:16 + T], 1.0 / w, A[g][:, 16:16 + T], ALU.mult, ALU.subtract, [tmpk, Ak], [("PL", g)])
                            if tt == 0:
                                tmp, tk = tmp_ring.next()
                                TT("pool", tmp[0:96, 0:16], tmpw[:, 16:32], consts[0:96, C_RC + 16 * g:C_RC + 16 * g + 16], ALU.mult, [tmpk, "consts"], [tk])
                                TT("pool", PL[g][:, 0:16], tmp[0:96, 0:16], A[g][:, 16:32], ALU.subtract, [tk, Ak], [("PL", g)])
                            break
                        TT("pool", dst[:, 2 * n - 1:16 + T], src[:, 2 * n - 1:16 + T], src[:, n - 1:16 + T - n], ALU.add, [srck], [dk])
                        src, srck = dst, dk
                        n *= 2
                for c in range(3):
                    Pk = ("P", c)
                    if tt > 0:
                        CP("pool", P[c][:, 0:2], P[c][:, T:T + 2], [Pk], [Pk])
                    bx, bb_, bc = bank(), bank(), bank()
                    for (bk, off) in ((bx, OFF_BX), (bb_, OFF_BB), (bc, OFF_BC)):
                        for kc in range(KC):
                            MM(PB(bk), wab[:, kc, off + c * 128:off + (c + 1) * 128], ht[:, kc, :], kc == 0, kc == KC - 1, [hk, ("ab_w", (off + c * 128) // 512)], [("ps", bk)])
                    CP("act", BX[:], PB(bx), [("ps", bx)], ["BX"])
                    TT("dve", P[c][:, 2:2 + T], PB(bc), BX[:], ALU.mult, [("ps", bc), "BX"], [Pk])
                    cw = vb + V_CONVB
                    ACT(Y1[:], P[c][:, 2:2 + T], AF.Copy, [Pk, "vecs"], ["Y1"], scale=vecs[:, cw + 2 * 3 + c:cw + 2 * 3 + c + 1])
                    STT("dve", Y2[:], P[c][:, 1:1 + T], vecs[:, cw + 1 * 3 + c:cw + 1 * 3 + c + 1], Y1[:], ALU.mult, ALU.add, [Pk, "Y1", "vecs"], ["Y2"])
                    STT("dve", Y1[:], P[c][:, 0:T], vecs[:, cw + 0 * 3 + c:cw + 0 * 3 + c + 1], Y2[:], ALU.mult, ALU.add, [Pk, "Y2", "vecs"], ["Y1"])
                    TT("dve", Z[c][:], PB(bb_), Y1[:], ALU.mult, [("ps", bb_), "Y1"], [("Z", c)])
                for g in range(4):
                    b2 = bank()
                    MM(ps[0:96, b2 * 512:(b2 + 1) * 512], pw[:, g, :], PL[g][:], True, True, [("PL", g), "ab_w2"], [("ps", b2)])
                    ACT(MX[g][:], ps[0:96, b2 * 512:(b2 + 1) * 512], AF.Copy, [("ps", b2), "vecs"], [("MX", g)],
                        scale=vecs[0:96, vb + V_PSCALE + g:vb + V_PSCALE + g + 1])
                for mc in range(KC):
                    b = bank()
                    for g in range(4):
                        MM(PB(b), wa[:, g, mc * 128:(mc + 1) * 128], MX[g][:], g == 0, g == 3, [("MX", g), "ab_w2"], [("ps", b)])
                    bg = bank()
                    for kc in range(KC):
                        MM(PB(bg), wab[:, kc, OFF_GA + mc * 128:OFF_GA + (mc + 1) * 128], ht[:, kc, :], kc == 0, kc == KC - 1, [hk, ("ab_w", (OFF_GA + mc * 128) // 512)], [("ps", bg)])
                    sg, sgk = SG[sgi[0] % 2], ("SG", sgi[0] % 2)
                    sgi[0] += 1
                    ACT(sg[:], PB(bg), AF.Sigmoid, [("ps", bg)], [sgk])
                    TT("dve", mo[:, mc, :], PB(b), sg[:], ALU.mult, [("ps", b), sgk], [mok], waw=False)
                for mc in range(KC):
                    b = bank()
                    for c in range(3):
                        MM(PB(b), wb[:, c, mc * 128:(mc + 1) * 128], Z[c][:], c == 0, c == 2, [("Z", c), "ab_w3"], [("ps", b)])
                    bg = bank()
                    for kc in range(KC):
                        MM(PB(bg), wab[:, kc, OFF_GB + mc * 128:OFF_GB + (mc + 1) * 128], ht[:, kc, :], kc == 0, kc == KC - 1, [hk, ("ab_w", (OFF_GB + mc * 128) // 512)], [("ps", bg)])
                    sg, sgk = SG[sgi[0] % 2], ("SG", sgi[0] % 2)
                    sgi[0] += 1
                    ACT(sg[:], PB(bg), AF.Sigmoid, [("ps", bg)], [sgk])
                    tmp, tk = tmp_ring.next()
                    TT("dve", tmp[:], PB(b), sg[:], ALU.mult, [("ps", b), sgk], [tk])
                    TT("pool", mo[:, mc, :], mo[:, mc, :], tmp[:], ALU.add, [mok, tk], [mok], waw=False)
                DMA("pool", xview(mab, tt), mo[:], mok, [mok], [("mab", tt)])
            S.barrier()

    def phase_merge(l):
        vb = l * V_PER_LAYER
        xsrc = xT if l == 0 else xs
        with ExitStack() as e4:
            def sb4(name, shape, dt):
                return e4.enter_context(nc.sbuf_tensor(f"{name}_L{l}", list(shape), dt))
            mk_rings(sb4, x=True, h=True, hin=True, ysb=True)
            wgc = sb4("mg_wgc", [128, KC, D], BF16)
            wc = sb4("mg_wc", [128, 2, D], BF16)
            wo = sb4("mg_wo", [128, KC, D], BF16)
            AT = [sb4(f"mg_at{i}", [128, 2, T], BF16) for i in range(2)]
            MI = [sb4(f"mg_mi{i}", [128, KC, T], F32) for i in range(2)]
            MGs = [sb4(f"mg_mg{i}", [128, KC, T], BF16) for i in range(2)]
            SG = [sb4(f"mg_SG{i}", [128, T], F32) for i in range(2)]
            load_w(wc, w_c[l], D, "mg_wc", "mg_wc")
            load_wb(wgc, w_in[l][:, OFF_GC:OFF_GC + D], D, "mg_wg")
            load_wb(wo, w_out[l], D, "mg_wo")
            st = {}

            def stA(tt):
                ht, hk = G["hin"].next()
                DMA("sp", ht[:], xview(h1T, tt), hk, [("h1T", tt)], [hk])
                at, atk = AT[tt % 2], ("AT", tt % 2)
                DMA("sp", at[:], attnT.rearrange("(c p) t -> p c t", p=128)[:, :, tt * T:(tt + 1) * T], f"mg_at{tt % 2}", [("attnT", tt // 4)], [atk])
                mi, mik = MI[tt % 2], ("MI", tt % 2)
                DMA("sp", mi[:], xview(mab, tt), f"mg_mi{tt % 2}", [("mab", tt)], [mik])
                xt, xk = G["x"].next()
                DMA("sp", xt[:], xview(xsrc, tt), xk, [("xs", tt)], [xk])
                mg, mgk = MGs[tt % 2], ("MG", tt % 2)
                for mc in range(KC):
                    b = bank()
                    for pr in range(2):
                        MM(PB(b), wc[:, pr, mc * 128:(mc + 1) * 128], at[:, pr, :], pr == 0, pr == 1, [atk, "mg_wc"], [("ps", b)])
                    bg = bank()
                    for kc in range(KC):
                        MM(PB(bg), wgc[:, kc, mc * 128:(mc + 1) * 128], ht[:, kc, :], kc == 0, kc == KC - 1, [hk, ("mg_wg", mc // 4)], [("ps", bg)])
                    sg, sgk = SG[mc % 2], ("SG4", mc % 2)
                    ACT(sg[:], PB(bg), AF.Sigmoid, [("ps", bg)], [sgk])
                    tmp, tk = tmp_ring.next()
                    TT("dve", tmp[:], PB(b), sg[:], ALU.mult, [("ps", b), sgk], [tk])
                    TT("pool", mg[:, mc, :], tmp[:], mi[:, mc, :], ALU.add, [tk, mik], [mgk], waw=False)
                st[tt] = dict(xt=xt, xk=xk, mg=mg, mgk=mgk)

            def stW(tt):
                d_ = st[tt]
                pn = PostNorm()
                for mc in range(KC):
                    b = bank()
                    for kc in range(KC):
                        MM(PB(b), wo[:, kc, mc * 128:(mc + 1) * 128], d_["mg"][:, kc, :], kc == 0, kc == KC - 1, [d_["mgk"], ("mg_wo", mc // 4)], [("ps", b)])
                    pn.chunk(mc, b)
                d_["pn"] = pn

            def stF1(tt):
                d_ = st[tt]
                xt, xk = d_["xt"], d_["xk"]
                d_["pn"].finish(xt, xk, vb + V_MIXPOST)
                DMA("pool", xview(xs, tt), xt[:], xk, [xk], [("xs", tt)])

            def stF2(tt):
                d_ = st.pop(tt)
                xt, xk = d_["xt"], d_["xk"]
                h2, h2k = G["h"].next()
                norm_tile(xt, xk, vb + V_MEMPRE, h2, h2k)
                DMA("pool", xview(h2T, tt), h2[:], h2k, [h2k], [("h2T", tt)])

            stA(0)
            stW(0)
            for tt in range(1, NT):
                stF1(tt - 1)
                stA(tt)
                stF2(tt - 1)
                stW(tt)
            stF1(NT - 1)
            stF2(NT - 1)
            S.barrier()

    def phase_mem(l):
        vb = l * V_PER_LAYER
        with ExitStack() as e5:
            def sb5(name, shape, dt):
                return e5.enter_context(nc.sbuf_tensor(f"{name}_L{l}", list(shape), dt))
            mk_rings(sb5, x=True, h=True, hin=True, ysb=True)
            wq_ = sb5("mm_wq", [128, KC, 512], BF16)
            wkv = sb5("mm_wkv", [128, KC, 1024], BF16)
            wo = sb5("mm_wo", [128, 4, D], BF16)
            mt = sb5("mm_mt", [128, KC, 256], F32)
            mn = sb5("mm_mn", [128, KC, 256], BF16)
            KmT = sb5("mm_KmT", [128, 4, 256], BF16)
            Vm = sb5("mm_Vm", [128, 2, 512], BF16)
            QM = [sb5(f"mm_QM{i}", [128, T], BF16) for i in range(4)]
            PTm = [sb5(f"mm_PT{i}", [128, 2, T], BF16) for i in range(4)]
            DN = [sb5(f"mm_DN{i}", [128, T], F32) for i in range(2)]
            OMs = [sb5(f"mm_OM{i}", [128, 4, T], BF16) for i in range(2)]
            load_w(wkv, w_mkv[l], 1024, "mm_w", "mm_w")
            load_w(wq_, w_mq[l], 512, "mm_w", "mm_w")
            load_w(wo, w_mo[l], D, "mm_wo", "mm_wo")
            DMA("sp", mt[:], memT.rearrange("(c p) t -> p c t", p=128), "mm_mt", [], ["mm_mt"])
            norm_tile(mt, "mm_mt", vb + V_MEMKV, mn, "mm_mn", ncols=256)
            for h in range(4):
                b = bank()
                for kc in range(KC):
                    MM(PB(b, 0, 256), wkv[:, kc, h * 128:(h + 1) * 128], mn[:, kc, 0:256], kc == 0, kc == KC - 1, ["mm_mn", "mm_w"], [("ps", b)])
                CP("act", KmT[:, h, :], PB(b, 0, 256), [("ps", b)], ["KmT"], waw=False)
            for mi_ in range(2):
                b = bank()
                for kc in range(KC):
                    MM(PB(b), mn[:, kc, mi_ * 128:(mi_ + 1) * 128], wkv[:, kc, 512:1024], kc == 0, kc == KC - 1, ["mm_mn", "mm_w"], [("ps", b)])
                CP("act", Vm[:, mi_, :], PB(b), [("ps", b)], ["Vm"], waw=False)
            sc = float(128 ** -0.5)
            st = {}

            def stA(tt):
                ht, hk = G["hin"].next()
                DMA("sp", ht[:], xview(h2T, tt), hk, [("h2T", tt)], [hk])
                xt, xk = G["x"].next()
                DMA("sp", xt[:], xview(xs, tt), xk, [("xs", tt)], [xk])
                om, omk = OMs[tt % 2], ("OM", tt % 2)
                for h in range(4):
                    b = bank()
                    for kc in range(KC):
                        MM(PB(b), wq_[:, kc, h * 128:(h + 1) * 128], ht[:, kc, :], kc == 0, kc == KC - 1, [hk, "mm_w"], [("ps", b)])
                    CP("act", QM[h][:], PB(b), [("ps", b)], [("QM", h)])
                for h in range(4):
                    for mi_ in range(2):
                        bs = bank()
                        MM(PB(bs), KmT[:, h, mi_ * 128:(mi_ + 1) * 128], QM[h][:], True, True, ["KmT", ("QM", h)], [("ps", bs)])
                        ACT(PTm[h][:, mi_, :], PB(bs), AF.Exp, [("ps", bs)], [("PTm", h)], scale=sc, waw=False)
                for h in range(4):
                    pt, ptk = PTm[h], ("PTm", h)
                    bo, bd = bank(), bank()
                    for mi_ in range(2):
                        MM(PB(bo), Vm[:, mi_, h * 128:(h + 1) * 128], pt[:, mi_, :], mi_ == 0, mi_ == 1, ["Vm", ptk], [("ps", bo)])
                    for mi_ in range(2):
                        MM(PB(bd), ones[:], pt[:, mi_, :], mi_ == 0, mi_ == 1, ["cb", ptk], [("ps", bd)])
                    dn, dnk = DN[h % 2], ("DN", h % 2)
                    ACT(dn[:], PB(bd), AF.Ln, [("ps", bd)], [dnk])
                    ACT(dn[:], dn[:], AF.Exp, [dnk], [dnk], scale=-1.0)
                    TT("dve", om[:, h, :], PB(bo), dn[:], ALU.mult, [("ps", bo), dnk], [omk], waw=False)
                st[tt] = dict(xt=xt, xk=xk, om=om, omk=omk)

            def stW(tt):
                d_ = st[tt]
                pn = PostNorm()
                for mc in range(KC):
                    b = bank()
                    for h in range(4):
                        MM(PB(b), wo[:, h, mc * 128:(mc + 1) * 128], d_["om"][:, h, :], h == 0, h == 3, [d_["omk"], "mm_wo"], [("ps", b)])
                    pn.chunk(mc, b)
                d_["pn"] = pn

            def stF1(tt):
                d_ = st[tt]
                xt, xk = d_["xt"], d_["xk"]
                d_["pn"].finish(xt, xk, vb + V_MEMPOST)
                DMA("pool", xview(xs, tt), xt[:], xk, [xk], [("xs", tt)])

            def stF2(tt):
                d_ = st.pop(tt)
                xt, xk = d_["xt"], d_["xk"]
                h3, h3k = G["h"].next()
                norm_tile(xt, xk, vb + V_FFNPRE, h3, h3k)
                DMA("pool", xview(h3T, tt), h3[:], h3k, [h3k], [("h3T", tt)])

            stA(0)
            stW(0)
            for tt in range(1, NT):
                stF1(tt - 1)
                stA(tt)
                stF2(tt - 1)
                stW(tt)
            stF1(NT - 1)
            stF2(NT - 1)
            S.barrier()

    def phase_up(l):
        vb = l * V_PER_LAYER
        with ExitStack() as e6:
            def sb6(name, shape, dt):
                return e6.enter_context(nc.sbuf_tensor(f"{name}_L{l}", list(shape), dt))
            mk_rings(sb6, hin=True)
            wu = sb6("up_w", [128, KC, 2 * DFF], BF16)
            H = sb6("up_H", [128, FC, 2], F32)
            UA = [sb6(f"up_UA{i}", [128, 2 + T], F32) for i in range(2)]
            Y1 = [sb6(f"up_Y1{i}", [128, T], F32) for i in range(2)]
            Y2 = [sb6(f"up_Y2{i}", [128, T], F32) for i in range(2)]
            AO = [sb6(f"up_AO{i}", [128, FC, T], BF16) for i in range(2)]
            load_wb(wu, w_up[l], 2 * DFF, "up_w", order=[0, 5, 6, 1, 7, 2, 8, 3, 9, 4, 10])
            MS("pool", H[:], 0.0, ["H"])
            cw = vb + V_CONVF
            for tt in range(NT):
                ht, hk = G["hin"].next()
                DMA("sp", ht[:], xview(h3T, tt), hk, [("h3T", tt)], [hk])
                ao, aok = AO[tt % 2], ("AO", tt % 2)
                for c in range(FC):
                    ba, bb_ = bank(), bank()
                    for kc in range(KC):
                        MM(PB(ba), wu[:, kc, c * 128:(c + 1) * 128], ht[:, kc, :], kc == 0, kc == KC - 1, [hk, ("up_w", (c * 128) // 512)], [("ps", ba)])
                    for kc in range(KC):
                        MM(PB(bb_), wu[:, kc, DFF + c * 128:DFF + (c + 1) * 128], ht[:, kc, :], kc == 0, kc == KC - 1, [hk, ("up_w", (DFF + c * 128) // 512)], [("ps", bb_)])
                    i = c % 2
                    ua, uak = UA[i], ("UA", i)
                    y1, y1k = Y1[i], ("Y1", i)
                    y2, y2k = Y2[i], ("Y2", i)
                    CP("pool", ua[:, 0:2], H[:, c, :], ["H"], [uak])
                    CP("act", ua[:, 2:2 + T], PB(ba), [("ps", ba)], [uak], waw=False)
                    CP("pool", H[:, c, :], ua[:, T:T + 2], [uak], ["H"])
                    ACT(y1[:], ua[:, 2:2 + T], AF.Copy, [uak, "vecs"], [y1k], scale=vecs[:, cw + 2 * FC + c:cw + 2 * FC + c + 1])
                    STT("dve", y2[:], ua[:, 1:1 + T], vecs[:, cw + 1 * FC + c:cw + 1 * FC + c + 1], y1[:], ALU.mult, ALU.add, [uak, y1k, "vecs"], [y2k])
                    STT("dve", y1[:], ua[:, 0:T], vecs[:, cw + 0 * FC + c:cw + 0 * FC + c + 1], y2[:], ALU.mult, ALU.add, [uak, y2k, "vecs"], [y1k])
                    ACT(y2[:], y1[:], AF.Silu, [y1k], [y2k])
                    TT("dve", ao[:, c, :], PB(bb_), y2[:], ALU.mult, [("ps", bb_), y2k], [aok], waw=False)
                DMA("pool", actT.rearrange("(c p) t -> p c t", p=128)[:, :, tt * T:(tt + 1) * T], ao[:], f"up_ao{tt % 2}", [aok], [("actT", tt)])
            S.barrier()

    def phase_down(l, last):
        vb = l * V_PER_LAYER
        with ExitStack() as e7:
            def sb7(name, shape, dt):
                return e7.enter_context(nc.sbuf_tensor(f"{name}_L{l}", list(shape), dt))
            mk_rings(sb7, x=True, h=True, ysb=True)
            wd = sb7("dn_w", [128, FC, D], BF16)
            AI = [sb7(f"dn_AI{i}", [128, FC, T], BF16) for i in range(2)]
            for q4 in range(4):
                DMA("pool", wd[:, :, q4 * 256:(q4 + 1) * 256], w_down[l].rearrange("(k p) n -> p k n", p=128)[:, :, q4 * 256:(q4 + 1) * 256],
                    "dn_w", [], [("dn_w", q4)])
            st = {}

            def stA(tt):
                ai, aik = AI[tt % 2], ("AI", tt % 2)
                DMA("sp", ai[:], actT.rearrange("(c p) t -> p c t", p=128)[:, :, tt * T:(tt + 1) * T], f"dn_ai{tt % 2}", [("actT", tt)], [aik])
                xt, xk = G["x"].next()
                DMA("sp", xt[:], xview(xs, tt), xk, [("xs", tt)], [xk])
                st[tt] = dict(xt=xt, xk=xk, ai=ai, aik=aik, pn=PostNorm())

            def stW(tt, mcs):
                d_ = st[tt]
                for mc in mcs:
                    b = bank()
                    for c in range(FC):
                        MM(PB(b), wd[:, c, mc * 128:(mc + 1) * 128], d_["ai"][:, c, :], c == 0, c == FC - 1, [d_["aik"], ("dn_w", mc // 2)], [("ps", b)])
                    d_["pn"].chunk(mc, b)
                if d_["pn"].pend is not None:
                    d_["pn"].flush()

            def stF1(tt):
                d_ = st[tt]
                xt, xk = d_["xt"], d_["xk"]
                d_["pn"].finish(xt, xk, vb + V_FFNPOST)
                if last:
                    DMA("pool", xview(outT, tt), xt[:], xk, [xk], [("outT", tt)])
                else:
                    DMA("pool", xview(xs, tt), xt[:], xk, [xk], [("xs", tt)])

            def stF2(tt):
                d_ = st.pop(tt)
                xt, xk = d_["xt"], d_["xk"]
                if not last:
                    h1, h1k = G["h"].next()
                    norm_tile(xt, xk, (l + 1) * V_PER_LAYER + V_MIXPRE, h1, h1k)
                    DMA("pool", xview(h1T, tt), h1[:], h1k, [h1k], [("h1T", tt)])

            stA(0)
            stW(0, range(KC))
            for tt in range(1, NT):
                stF1(tt - 1)
                stA(tt)
                stW(tt, range(0, 4))
                stF2(tt - 1)
                stW(tt, range(4, KC))
            stF1(NT - 1)
            stF2(NT - 1)
            S.barrier()


    return dict(nc=nc, S=S, es=es, phases=dict(norm0=phase_norm0, attn=phase_attn, ab=phase_ab, merge=phase_merge,
                                               mem=phase_mem, up=phase_up, down=phase_down))


def build_full(n_layers=NL, dbg=False, stop=None, opts=()):
    P = build_program(n_layers, dbg, opts)
    ph = P["phases"]
    seq = []
    if "nonorm0" not in opts:
        ph["norm0"](0)
    done = stop is not None and stop[1] == "norm0"
    for l in range(n_layers):
        if done:
            break
        for name in ("attn", "ab", "merge", "mem", "up", "down"):
            if name == "down":
                ph[name](l, l == n_layers - 1)
            else:
                ph[name](l)
            if stop is not None and stop == (l, name):
                done = True
                break
        if done:
            break
    nsem = P["S"].emit()
    P["es"].close()
    return P["nc"], P["S"], nsem


def host_inputs(inputs):
    import ml_dtypes
    f32 = np.float32
    perm = w_in_perm()
    wip = np.asarray(inputs["w_in"], f32)[:, :, perm]
    wqkv = np.zeros((NL, 3, D, 1280), f32)
    for g in range(3):
        base = OFF_QKV + 768 * g
        wqkv[:, g, :, 0:256] = wip[:, :, base:base + 256]
        for hh in range(4):
            par = hh % 2
            wqkv[:, g, :, 256 + hh * 128 + 64 * par:256 + hh * 128 + 64 * par + 64] = wip[:, :, base + 256 + 64 * hh:base + 256 + 64 * hh + 64]
            wqkv[:, g, :, 768 + hh * 128 + 64 * par:768 + hh * 128 + 64 * par + 64] = wip[:, :, base + 512 + 64 * hh:base + 512 + 64 * hh + 64]
    shared = {
        "w_in": np.ascontiguousarray(wip[:, :, 0:OFF_QKV]),
        "w_qkv": wqkv,
        "pool_w": np.ascontiguousarray(np.asarray(inputs["pool_w"], f32)),
    }
    for k in ("w_branch_a", "w_branch_b", "w_branch_c", "w_out", "w_mq", "w_mkv", "w_mo", "w_up", "w_down"):
        shared[k] = np.ascontiguousarray(np.asarray(inputs[k], f32))
    vecs = np.zeros((128, NL * V_PER_LAYER), f32)
    for l in range(NL):
        vb = l * V_PER_LAYER
        for name, off in (("norm_mix_pre", V_MIXPRE), ("norm_mix_post", V_MIXPOST), ("norm_mem_pre", V_MEMPRE),
                          ("norm_mem_post", V_MEMPOST), ("norm_memkv", V_MEMKV), ("norm_ffn_pre", V_FFNPRE),
                          ("norm_ffn_post", V_FFNPOST)):
            vecs[:, vb + off:vb + off + 8] = np.asarray(inputs[name], f32)[l].reshape(8, 128).T
        vecs[:, vb + V_CONVB:vb + V_CONVB + 9] = np.asarray(inputs["conv_b_w"], f32)[l].reshape(3, 3, 128).transpose(2, 0, 1).reshape(128, 9)
        vecs[:, vb + V_CONVF:vb + V_CONVF + 66] = np.asarray(inputs["conv_ffn_w"], f32)[l].reshape(3, FC, 128).transpose(2, 0, 1).reshape(128, 66)
        vecs[0:96, vb + V_PSCALE:vb + V_PSCALE + 4] = np.asarray(inputs["pool_scale"], f32)[l].reshape(4, 96).T
    consts = np.zeros((128, NCONST), f32)
    consts[:, C_ID:C_ID + 128] = np.eye(128, dtype=f32)
    k = np.arange(128)[:, None]
    q = np.arange(128)[None, :]
    consts[:, C_MASK:C_MASK + 128] = np.where(k >= q, 0.0, -30000.0)
    consts[:, C_MASK + 128:C_MASK + 256] = np.where(k <= q, 0.0, -30000.0)
    consts[:, C_INV:C_INV + 8] = (f32(500000.0) ** (-np.arange(0, 16, 2, dtype=f32) / f32(16)))[None, :]
    for g, w in enumerate((2, 4, 8, 16)):
        consts[:, C_RC + 16 * g:C_RC + 16 * g + 16] = (1.0 / np.minimum(np.arange(16) + 1, w))[None, :]
    shared["vecs"] = vecs
    shared["consts"] = consts
    x = np.asarray(inputs["x"], f32)
    mem = np.asarray(inputs["mem"], f32)
    posn = np.asarray(inputs["positions"]).astype(np.int32)
    in_maps = []
    for b in range(8):
        m = dict(shared)
        m["xT"] = np.ascontiguousarray(x[b].T)
        m["memT"] = np.ascontiguousarray(mem[b].T)
        m["pos"] = np.ascontiguousarray(posn[b].reshape(32, 128))
        in_maps.append(m)
    return in_maps


_CACHE = {}


def kernel(**inputs):
    in_maps = host_inputs(inputs)
    if "nc" not in _CACHE:
        _CACHE["nc"] = build_full()[0]
    nc = _CACHE["nc"]
    res = run_bass_kernel_spmd(nc, in_maps, core_ids=list(range(8)))
    out = np.stack([np.ascontiguousarray(res.results[b]["outT"].T) for b in range(8)], axis=0)
    return out.astype(np.float32)
```

```python
import numpy as np
import concourse.bass as bass
import concourse.mybir as mybir

F32 = mybir.dt.float32
BF16 = mybir.dt.bfloat16
I32 = mybir.dt.int32
ALU = mybir.AluOpType
AF = mybir.ActivationFunctionType

EPOCH = 30000


class Sched:
    ENGS = ("pe", "act", "dve", "pool", "sp")

    def __init__(self, nc):
        self.nc = nc
        self.stream = {e: [] for e in self.ENGS}
        self.cnt = {e: 0 for e in self.ENGS}
        self.known = {e: {} for e in self.ENGS}
        self.res = {}
        self.chan_cnt = {}
        self.n_wait = 0

    def _collect(self, eng, reads, writes, waw):
        waits = {}

        def need(k, v):
            if k == ("eng", "pe") and eng == "pe":
                return
            if self.known[eng].get(k, 0) >= v:
                return
            if waits.get(k, 0) < v:
                waits[k] = v

        for r in reads:
            st = self.res.get(r)
            if st:
                for k, v in st["w"].items():
                    need(k, v)
        for w in writes:
            st = self.res.get(w)
            if st:
                for k, v in st["r"].items():
                    need(k, v)
                if waw:
                    for k, v in st["w"].items():
                        need(k, v)
        for k, v in waits.items():
            self.known[eng][k] = v
        return sorted(waits.items(), key=lambda kv: str(kv[0]))

    def _commit(self, ev, reads, writes, waw):
        k, v = ev
        for r in reads:
            st = self.res.setdefault(r, {"w": {}, "r": {}})
            if st["r"].get(k, 0) < v:
                st["r"][k] = v
        for w in writes:
            st = self.res.setdefault(w, {"w": {}, "r": {}})
            if st["r"] or waw:
                st["w"] = {}
            st["r"] = {}
            if st["w"].get(k, 0) < v:
                st["w"][k] = v

    def op(self, eng, fn, reads=(), writes=(), waw=True):
        waits = self._collect(eng, reads, writes, waw)
        self.cnt[eng] += 1
        ev = (("eng", eng), self.cnt[eng])
        self._commit(ev, reads, writes, waw)
        self.stream[eng].append((waits, fn, ev))
        self.n_wait += len(waits)

    def dma(self, eng, fn, chan, reads=(), writes=(), waw=False):
        waits = self._collect(eng, reads, writes, waw)
        self.chan_cnt[chan] = self.chan_cnt.get(chan, 0) + 16
        ev = (("chan", chan), self.chan_cnt[chan])
        self._commit(ev, reads, writes, waw)
        self.stream[eng].append((waits, fn, ev))
        self.n_wait += len(waits)

    def barrier(self, engs=None):
        engs = engs or self.ENGS
        for e in engs:
            waits = {}
            for e2 in self.ENGS:
                if e2 != e and self.cnt[e2] > 0:
                    k = ("eng", e2)
                    if self.known[e].get(k, 0) < self.cnt[e2]:
                        waits[k] = self.cnt[e2]
            for c, v in self.chan_cnt.items():
                k = ("chan", c)
                if self.known[e].get(k, 0) < v:
                    waits[k] = v
            for k, v in waits.items():
                self.known[e][k] = v
            if waits:
                self.stream[e].append((sorted(waits.items(), key=lambda kv: str(kv[0])), None, None))

    def emit(self):
        nc = self.nc
        sems = {}

        def sem_of(k, v):
            if k[0] == "eng":
                ep = (v - 1) // EPOCH
                key = (k, ep)
                val = (v - 1) % EPOCH + 1
            else:
                key = (k, 0)
                val = v
            if key not in sems:
                sems[key] = nc.alloc_semaphore(name=f"s{len(sems)}")
            return sems[key], val

        self.barrier(engs=("sp",))
        for e in self.ENGS:
            for waits, fn, ev in self.stream[e]:
                for k, v in waits:
                    sem_of(k, v)
                if ev is not None:
                    sem_of(*ev)
        eng_map = {"pe": "tensor", "act": "scalar", "dve": "vector", "pool": "gpsimd", "sp": "sync"}
        with nc.Block() as block:
            for e in self.ENGS:
                if not self.stream[e]:
                    continue
                deco = getattr(block, eng_map[e])

                def body(h, e=e):
                    for waits, fn, ev in self.stream[e]:
                        for k, v in waits:
                            s, val = sem_of(k, v)
                            h.wait_ge(s, val)
                        if fn is None:
                            continue
                        ins = fn(h)
                        s, _ = sem_of(*ev)
                        ins.then_inc(s, 16 if ev[0][0] == "chan" else 1)

                deco(body)
        return len(sems)


def sap(t, F, poff, npart, off, dims):
    return bass.AP(t, poff * F + off, [[F, npart]] + [list(d) for d in dims])

from concourse.bass_utils import run_bass_kernel_spmd
from contextlib import ExitStack

D = 1024
SEQ = 4096
T = 512
NT = SEQ // T
KC = 8
DFF = 2816
FC = DFF // 128
NL = 2
EPS = 1e-6
DILS = (1, 4, 16)
OFF_A = 0
OFF_BX = 384
OFF_BB = 768
OFF_BC = 1152
OFF_GA = 1536
OFF_GB = 2560
OFF_GC = 3584
OFF_QKV = 4608
NAB = 3584
V_MIXPRE, V_MIXPOST, V_MEMPRE, V_MEMPOST, V_MEMKV, V_FFNPRE, V_FFNPOST = 0, 8, 16, 24, 32, 40, 48
V_CONVB = 56
V_CONVF = 65
V_PSCALE = 131
V_PER_LAYER = 135
C_ID = 0
C_MASK = 128
C_INV = 384
C_RC = 392
NCONST = 456


def w_in_perm():
    idx = []
    a = 0
    idx += list(range(0, 384))
    idx += list(range(384, 384 + 1152))
    g0 = 384 + 1152 + 3 * 768
    idx += list(range(g0, g0 + 3072))
    q0 = 384 + 1152
    for g in range(3):
        for part in range(3):
            s = q0 + part * 768 + g * 256
            idx += list(range(s, s + 256))
    return np.array(idx, dtype=np.int64)


def build_program(n_layers=NL, dbg=False, opts=()):
    nc = bass.Bass("TRN2", target_bir_lowering=False)
    S = Sched(nc)

    def din(name, shape, dt=F32):
        return nc.dram_tensor(name, list(shape), dt, kind="ExternalInput").ap()

    kind_s = "ExternalOutput" if dbg else "Internal"

    def dscr(name, shape, dt):
        return nc.dram_tensor(name, list(shape), dt, kind=kind_s).ap()

    xT = din("xT", [D, SEQ])
    memT = din("memT", [D, 256])
    pos = din("pos", [32, 128], I32)
    w_in = din("w_in", [NL, D, OFF_QKV])
    w_qkv = din("w_qkv", [NL, 3, D, 1280])
    pool_w = din("pool_w", [NL, 4, 96, 96])
    w_a = din("w_branch_a", [NL, 384, D])
    w_b = din("w_branch_b", [NL, 384, D])
    w_c = din("w_branch_c", [NL, 256, D])
    w_out = din("w_out", [NL, D, D])
    w_mq = din("w_mq", [NL, D, 512])
    w_mkv = din("w_mkv", [NL, D, 1024])
    w_mo = din("w_mo", [NL, 512, D])
    w_up = din("w_up", [NL, D, 2 * DFF])
    w_down = din("w_down", [NL, DFF, D])
    vecs_d = din("vecs", [128, NL * V_PER_LAYER])
    consts_d = din("consts", [128, NCONST])
    outT = nc.dram_tensor("outT", [D, SEQ], F32, kind="ExternalOutput").ap()

    h1T = dscr("h1T", [D, SEQ], BF16)
    h2T = dscr("h2T", [D, SEQ], BF16)
    h3T = dscr("h3T", [D, SEQ], BF16)
    attnT = dscr("attnT", [256, SEQ], BF16)
    mab = dscr("mab", [D, SEQ], F32)
    xs = dscr("xs", [D, SEQ], F32)
    actT = dscr("actT", [DFF, SEQ], BF16)
    rope_d = dscr("rope_tab", [SEQ, 16], F32)

    es = ExitStack()

    def sb(name, shape, dt):
        return es.enter_context(nc.sbuf_tensor("s_" + name, list(shape), dt))

    ps = es.enter_context(nc.psum_tensor("ps", [128, 4096], F32))

    def MM(out, lhsT, rhs, start, stop, R, W):
        S.op("pe", lambda h: h.matmul(out, lhsT=lhsT, rhs=rhs, start=start, stop=stop), reads=R, writes=W)

    def ACT(out, in_, func, R, W, scale=1.0, bias=None, waw=True):
        if bias is None:
            S.op("act", lambda h: h.activation(out=out, in_=in_, func=func, scale=scale), reads=R, writes=W, waw=waw)
        else:
            S.op("act", lambda h: h.activation(out=out, in_=in_, func=func, scale=scale, bias=bias), reads=R, writes=W, waw=waw)

    def TT(eng, out, in0, in1, op, R, W, waw=True):
        S.op(eng, lambda h: h.tensor_tensor(out=out, in0=in0, in1=in1, op=op), reads=R, writes=W, waw=waw)

    def TS(eng, out, in0, s1, s2, op0, op1, R, W, waw=True):
        if s2 is None:
            S.op(eng, lambda h: h.tensor_scalar(out=out, in0=in0, scalar1=s1, scalar2=None, op0=op0), reads=R, writes=W, waw=waw)
        else:
            S.op(eng, lambda h: h.tensor_scalar(out=out, in0=in0, scalar1=s1, scalar2=s2, op0=op0, op1=op1), reads=R, writes=W, waw=waw)

    def STT(eng, out, in0, scalar, in1, op0, op1, R, W, waw=True):
        S.op(eng, lambda h: h.scalar_tensor_tensor(out=out, in0=in0, scalar=scalar, in1=in1, op0=op0, op1=op1), reads=R, writes=W, waw=waw)

    def CP(eng, out, in_, R, W, waw=True):
        if eng == "act":
            ACT(out, in_, AF.Copy, R, W, waw=waw)
        else:
            S.op(eng, lambda h: h.tensor_copy(out=out, in_=in_), reads=R, writes=W, waw=waw)

    def MS(eng, ap, val, W):
        S.op(eng, lambda h: h.memset(ap, val), writes=W)

    def DMA(eng, out, in_, chan, R, W):
        if eng == "pool":
            chan = ("swq", chan)
        S.dma(eng, lambda h: h.dma_start(out=out, in_=in_), chan, reads=R, writes=W)

    uniq = [0]

    class Ring:
        def __init__(self, name, shape, dt, n, alloc=None):
            alloc = alloc or sb
            uniq[0] += 1
            self.name = name
            self.t = [alloc(f"{name}_{uniq[0]}_{i}", shape, dt) for i in range(n)]
            self.i = 0

        def next(self):
            k = self.i % len(self.t)
            self.i += 1
            return self.t[k], (self.name, k)

    psp = [0]
    nrot = [6]

    def bank(n=1):
        if n == 2 and psp[0] % 2 == 1:
            psp[0] += 1
        p = psp[0] % nrot[0]
        psp[0] += n
        return p

    def PB(b, lo=0, hi=512):
        return ps[:, b * 512 + lo: b * 512 + hi]

    vecs = sb("vecs", [128, NL * V_PER_LAYER], F32)
    consts = sb("consts", [128, NCONST], F32)
    ident = sb("ident", [128, 128], BF16)
    mask2 = sb("mask2", [128, 256], BF16)
    onesm = sb("onesm", [128, 128], BF16)
    ones = sb("ones", [128, 128], BF16)
    onesP = sb("onesP", [128, 2, 128], BF16)
    DMA("sp", vecs[:], vecs_d, "vecs", [], ["vecs"])
    DMA("sp", consts[:], consts_d, "consts", [], ["consts"])
    CP("dve", ident[:], consts[:, C_ID:C_ID + 128], ["consts"], ["cb"])
    CP("dve", mask2[:], consts[:, C_MASK:C_MASK + 256], ["consts"], ["cb"])
    MS("pool", onesm[:], 1.0 / 1024.0, ["cb"])
    MS("pool", ones[:], 1.0, ["cb"])
    MS("pool", onesP[:], 0.0, ["cb"])
    MS("pool", onesP[:, 0, 0:64], 1.0, ["cb"])
    MS("pool", onesP[:, 1, 64:128], 1.0, ["cb"])

    sqring = Ring("sq", [128, T], BF16, 3)
    rstdring = Ring("rstd", [128, T], F32, 2)
    G = {}

    def mk_rings(alloc, x=False, h=False, hin=False, ysb=False):
        if x:
            G["x"] = Ring("xt", [128, KC, T], F32, 2, alloc)
        if h:
            G["h"] = Ring("ht", [128, KC, T], BF16, 2, alloc)
        if hin:
            G["hin"] = Ring("hin", [128, KC, T], BF16, 2, alloc)
        if ysb:
            G["ysb"] = Ring("ysb", [128, KC, T], F32, 2, alloc)
    tmp_ring = Ring("tmpf", [128, T], F32, 3)

    def xview(d, tt):
        return d.rearrange("(c p) t -> p c t", p=128)[:, :, tt * T:(tt + 1) * T]

    def rstd_from_bank(b):
        rstd, rk = rstdring.next()
        ACT(rstd[:], PB(b), AF.Ln, [("ps", b)], [rk], bias=EPS)
        ACT(rstd[:], rstd[:], AF.Exp, [rk], [rk], scale=-0.5)
        return rstd, rk

    def norm_tile(xt, xk, gcol, ht, hk, ncols=T):
        b = bank()
        for c in range(KC):
            sq, sqk = sqring.next()
            ACT(sq[:, 0:ncols], xt[:, c, 0:ncols], AF.Square, [xk], [sqk])
            MM(PB(b, 0, ncols), onesm[:], sq[:, 0:ncols], c == 0, c == KC - 1, [sqk, "cb"], [("ps", b)])
        rstd, rk = rstdring.next()
        ACT(rstd[:, 0:ncols], PB(b, 0, ncols), AF.Ln, [("ps", b)], [rk], bias=EPS)
        ACT(rstd[:, 0:ncols], rstd[:, 0:ncols], AF.Exp, [rk], [rk], scale=-0.5)
        for c in range(KC):
            eng = "dve"
            STT(eng, ht[:, c, 0:ncols], xt[:, c, 0:ncols], vecs[:, gcol + c:gcol + c + 1], rstd[:, 0:ncols],
                ALU.mult, ALU.mult, [xk, rk, "vecs"], [hk], waw=False)

    pncnt = [0]

    class PostNorm:
        def __init__(self):
            self.ysb, self.yk = G["ysb"].next()
            pncnt[0] += 1
            self.b = 6 + pncnt[0] % 2
            self.pend = None

        def chunk(self, mc, b):
            CP("act", self.ysb[:, mc, :], PB(b), [("ps", b)], [self.yk], waw=False)
            sq, sqk = sqring.next()
            TT("dve", sq[:], PB(b), self.ysb[:, mc, :], ALU.mult, [("ps", b), self.yk], [sqk])
            if self.pend is not None:
                self.flush()
            self.pend = (mc, sq, sqk)
            if mc == KC - 1:
                self.flush()

        def flush(self):
            mc, sq, sqk = self.pend
            self.pend = None
            MM(PB(self.b), onesm[:], sq[:], mc == 0, mc == KC - 1, [sqk, "cb"], [("ps", self.b)])

        def finish(self, xt, xk, gcol):
            rstd, rk = rstd_from_bank(self.b)
            for c in range(KC):
                tmp, tk = tmp_ring.next()
                TT("pool", tmp[:], self.ysb[:, c, :], rstd[:], ALU.mult, [self.yk, rk], [tk])
                STT("dve", xt[:, c, :], tmp[:], vecs[:, gcol + c:gcol + c + 1], xt[:, c, :], ALU.mult, ALU.add,
                    [tk, xk, "vecs"], [xk], waw=False)

    def load_w(dst, src2d, ncols, chan, key, rows=128):
        v = src2d.rearrange("(k p) n -> p k n", p=rows)
        for cb in range(0, ncols, 2048):
            w = min(2048, ncols - cb)
            DMA("pool", dst[:, :, cb:cb + w], v[:, :, cb:cb + w], chan, [], [key])

    def load_wb(dst, src2d, ncols, name, order=None, blk=512, rows=128):
        v = src2d.rearrange("(k p) n -> p k n", p=rows)
        nb = (ncols + blk - 1) // blk
        for b in (order if order is not None else range(nb)):
            lo = b * blk
            w = min(blk, ncols - lo)
            DMA("pool", dst[:, :, lo:lo + w], v[:, :, lo:lo + w], f"{name}_{b}", [], [(name, b)])

    with ExitStack() as es0:
      if 'norope' not in opts:
          def sb0(name, shape, dt):
              return es0.enter_context(nc.sbuf_tensor(name, list(shape), dt))
          pi_ = sb0("rp_pi", [32, 128], I32)
          pf = sb0("rp_pf", [32, 128], F32)
          ang = sb0("rp_ang", [32, 128, 16], F32)
          a2 = sb0("rp_a2", [32, 128, 16], F32)
          ki = sb0("rp_ki", [32, 128, 16], I32)
          tab = sb0("rp_tab", [32, 128, 16], F32)
          DMA("sp", pi_[:], pos, "rp", [], ["rp_pi"])
          CP("dve", pf[:], pi_[:], ["rp_pi"], ["rp_pf"])
          inv_b = consts[0:32, C_INV:C_INV + 8].unsqueeze(1).to_broadcast([32, 128, 8])
          pf_b = pf[:].unsqueeze(2).to_broadcast([32, 128, 8])
          TT("dve", ang[:, :, 0:8], pf_b, inv_b, ALU.mult, ["rp_pf", "consts"], ["rp_ang"])
          TS("dve", ang[:, :, 8:16], ang[:, :, 0:8], float(np.pi / 2), None, ALU.add, None, ["rp_ang"], ["rp_ang"])
          TS("dve", a2[:], ang[:], float(1.0 / (2 * np.pi)), None, ALU.mult, None, ["rp_ang"], ["rp_a2"])
          CP("dve", ki[:], a2[:], ["rp_a2"], ["rp_ki"])
          CP("dve", a2[:], ki[:], ["rp_ki"], ["rp_a2"])
          STT("dve", ang[:], a2[:], float(-2 * np.pi), ang[:], ALU.mult, ALU.add, ["rp_a2", "rp_ang"], ["rp_ang"])
          TS("dve", a2[:], ang[:], float(np.pi), float(-2 * np.pi), ALU.is_gt, ALU.mult, ["rp_ang"], ["rp_a2"])
          TT("dve", ang[:], ang[:], a2[:], ALU.add, ["rp_ang", "rp_a2"], ["rp_ang"])
          TS("dve", a2[:], ang[:], float(-np.pi), float(2 * np.pi), ALU.is_lt, ALU.mult, ["rp_ang"], ["rp_a2"])
          TT("dve", ang[:], ang[:], a2[:], ALU.add, ["rp_ang", "rp_a2"], ["rp_ang"])
          ACT(tab[:, :, 0:8], ang[:, :, 8:16], AF.Sin, ["rp_ang"], ["rp_tab"])
          ACT(tab[:, :, 8:16], ang[:, :, 0:8], AF.Sin, ["rp_ang"], ["rp_tab"], waw=False)
          DMA("sp", rope_d.rearrange("(b i) f -> b i f", i=128), tab[:], "rp", ["rp_tab"], ["rope_d"])
          S.barrier()

    def phase_norm0(l):
      nrot[0] = 6
      psp[0] = 0
      with ExitStack() as e1:
        mk_rings(lambda n, sh, dt: e1.enter_context(nc.sbuf_tensor(n, list(sh), dt)), x=True, h=True)
        for tt in range(NT):
            xt, xk = G["x"].next()
            DMA("sp", xt[:], xview(xT, tt), xk, [], [xk])
            ht, hk = G["h"].next()
            norm_tile(xt, xk, l * V_PER_LAYER + V_MIXPRE, ht, hk)
            DMA("pool", xview(h1T, tt), ht[:], hk, [hk], [("h1T", tt)])
        S.barrier()

    def phase_attn(l):
        nrot[0] = 8
        psp[0] = 0
        with ExitStack() as e2:
            def sb2(name, shape, dt):
                return e2.enter_context(nc.sbuf_tensor(f"{name}_L{l}", list(shape), dt))
            hT = sb2("at_hT", [128, KC, SEQ], BF16)
            acc = sb2("at_acc", [128, 4, 2048], F32)
            attn_sb = sb2("at_out", [128, 2, 2048], BF16)
            wq = [sb2(f"at_w{i}", [128, KC, 1280], BF16) for i in range(2)]
            cs = [sb2(f"at_cs{g}", [128, 32, 16], F32) for g in range(3)]
            QK = [sb2(f"at_qk{i}", [128, 768], BF16) for i in range(3)]
            QF = [sb2(f"at_qf{i}", [128, 768], F32) for i in range(3)]
            ODT = [sb2(f"at_od{i}", [128, 512], F32) for i in range(2)]
            Vb = [sb2(f"at_v{i}", [128, 4, 128], BF16) for i in range(4)]
            QT = [sb2(f"at_qt{i}", [128, 2, 128], BF16) for i in range(4)]
            KTp = [sb2(f"at_kt{i}", [128, 4, 128], BF16) for i in range(4)]
            PT = [sb2(f"at_pt{i}", [128, 4, 256], BF16) for i in range(2)]
            rt = [sb2(f"at_rt{i}", [128, 4, 8, 8], F32) for i in range(4)]
            for tt in range(NT):
                DMA("sp", hT[:, :, tt * T:(tt + 1) * T], xview(h1T, tt), f"at_hT{tt}", [("h1T", tt)], [("at_hT", tt)])
            for g in range(3):
                d = DILS[g]
                nb = 32 // d
                for r in range(d):
                    for j in range(nb):
                        src = bass.AP(rope_d.tensor, (r + d * 128 * j) * 16, [[16 * d, 128], [1, 16]])
                        DMA("sp", cs[g][:, r * nb + j, :], src, f"at_cs{g}", ["rope_d"], [("cs", g)])
            wl = [0]
            cnt = {"rt": 0, "od": 0}
            NS = 4

            def P1(it):
                g, r, j, need_q, wt, wk = it["g"], it["r"], it["j"], it["need_q"], it["wt"], it["wk"]
                d = DILS[g]
                nb = 32 // d
                slot = it["slot"]
                start = r + d * 128 * j
                sl = slice(start, start + 127 * d + 1, d)
                hkeys = [("at_hT", t_) for t_ in range(start // T, (start + 127 * d) // T + 1)]
                bq = bank() if need_q else None
                bk = bank()
                bv = bank()
                if need_q:
                    for kc in range(KC):
                        MM(PB(bq, 0, 256), hT[:, kc, sl], wt[:, kc, 0:256], kc == 0, kc == KC - 1, hkeys + [wk], [("ps", bq)])
                for kc in range(KC):
                    MM(PB(bk), hT[:, kc, sl], wt[:, kc, 256:768], kc == 0, kc == KC - 1, hkeys + [wk], [("ps", bk)])
                for kc in range(KC):
                    MM(PB(bv), hT[:, kc, sl], wt[:, kc, 768:1280], kc == 0, kc == KC - 1, hkeys + [wk], [("ps", bv)])
                qi = it["idx"] % 3
                qk, qkk = QK[qi], ("QK", qi)
                qf, qfk = QF[qi], ("QF", qi)
                it["qk"], it["qkk"] = qk, qkk
                if need_q:
                    CP("act", qf[:, 0:256], PB(bq, 0, 256), [("ps", bq)], [qfk])
                CP("act", qf[:, 256:768], PB(bk), [("ps", bk)], [qfk], waw=not need_q)
                lo = 0 if need_q else 256
                CP("act", qk[:, lo:768], qf[:, lo:768], [qfk], [qkk])
                CP("dve", Vb[slot][:].rearrange("p a b -> p (a b)"), PB(bv), [("ps", bv)], [("Vb", slot)])
                blk = r * nb + j
                csk = ("cs", g)
                views = []
                if need_q:
                    views.append((qf[:, 0:256].rearrange("p (h e) -> p h e", e=64), qk[:, 0:256].rearrange("p (h e) -> p h e", e=64), [128, 4, 8], 0))
                kfv = bass.AP(qf, 256, [[768, 128], [256, 2], [192, 2], [1, 64]])
                kbv = bass.AP(qk, 256, [[768, 128], [256, 2], [192, 2], [1, 64]])
                views.append((kfv, kbv, [128, 2, 2, 8], 1))
                for (fv, bv_, shp, which) in views:
                    ri = cnt["rt"] % 4
                    cnt["rt"] += 1
                    rtt, rtk = rt[ri], ("rt", ri)
                    if which == 0:
                        cosb = cs[g][:, blk, 0:8].unsqueeze(1).to_broadcast(shp)
                        sinb = cs[g][:, blk, 8:16].unsqueeze(1).to_broadcast(shp)
                        u1, u2 = fv[:, :, 0:8], fv[:, :, 8:16]
                        o1, o2 = bv_[:, :, 0:8], bv_[:, :, 8:16]
                        tv = [rtt[:, k, 0:4, :] for k in range(4)]
                    else:
                        cosb = cs[g][:, blk, 0:8].unsqueeze(1).unsqueeze(1).to_broadcast(shp)
                        sinb = cs[g][:, blk, 8:16].unsqueeze(1).unsqueeze(1).to_broadcast(shp)
                        u1, u2 = fv[:, :, :, 0:8], fv[:, :, :, 8:16]
                        o1, o2 = bv_[:, :, :, 0:8], bv_[:, :, :, 8:16]
                        tv = [rtt[:, k, 0:4, :].rearrange("p (a b) e -> p a b e", a=2) for k in range(4)]
                    TT("dve", tv[0], u1, cosb, ALU.mult, [qfk, csk], [rtk])
                    TT("dve", tv[1], u2, sinb, ALU.mult, [qfk, csk], [rtk], waw=False)
                    TT("dve", tv[2], u2, cosb, ALU.mult, [qfk, csk], [rtk], waw=False)
                    TT("dve", tv[3], u1, sinb, ALU.mult, [qfk, csk], [rtk], waw=False)
                    TT("pool", o1, tv[0], tv[1], ALU.subtract, [rtk], [qkk])
                    TT("pool", o2, tv[2], tv[3], ALU.add, [rtk], [qkk])

            def P2(it):
                need_q, slot, qk, qkk = it["need_q"], it["slot"], it["qk"], it["qkk"]
                if need_q:
                    bt = bank()
                    for ch in range(2):
                        MM(PB(bt, ch * 128, ch * 128 + 128), qk[:, ch * 128:(ch + 1) * 128], ident[:], True, True, [qkk, "cb"], [("ps", bt)])
                    CP("act", QT[slot][:].rearrange("p c t -> p (c t)"), PB(bt, 0, 256), [("ps", bt)], [("QT", slot)])
                bt2 = bank()
                for hh in range(4):
                    MM(PB(bt2, hh * 128, hh * 128 + 128), qk[:, 256 + hh * 128:256 + (hh + 1) * 128], ident[:], True, True, [qkk, "cb"], [("ps", bt2)])
                CP("dve", KTp[slot][:].rearrange("p a b -> p (a b)"), PB(bt2), [("ps", bt2)], [("KTp", slot)])

            def A1(it):
                j = it["j"]
                sc, sp_ = it["slot"], (it["slot"] - 1) % NS
                b2 = bank(2)
                for hh in range(4):
                    ch = hh // 2
                    bb = b2 + hh // 2
                    c0 = (hh % 2) * 256
                    if j > 0:
                        MM(PB(bb, c0, c0 + 256), ident[:], mask2[:, 0:256], True, False, ["cb"], [("ps", bb)])
                        MM(PB(bb, c0, c0 + 128), KTp[sp_][:, hh, :], QT[sc][:, ch, :], False, False, [("KTp", sp_), ("QT", sc)], [("ps", bb)])
                    else:
                        MM(PB(bb, c0 + 128, c0 + 256), ident[:], mask2[:, 128:256], True, False, ["cb"], [("ps", bb)])
                    MM(PB(bb, c0 + 128, c0 + 256), KTp[sc][:, hh, :], QT[sc][:, ch, :], False, True, [("KTp", sc), ("QT", sc)], [("ps", bb)])
                pi = it["idx"] % 2
                pt, ptk = PT[pi], ("PT", pi)
                it["pt"], it["ptk"] = pt, ptk
                for hb in range(2):
                    bb = b2 + hb
                    if j > 0:
                        ACT(pt[:, 2 * hb:2 * hb + 2, :].rearrange("p a b -> p (a b)"), PB(bb), AF.Exp, [("ps", bb)], [ptk], scale=0.125, waw=False)
                    else:
                        for a in range(2):
                            ACT(pt[:, 2 * hb + a, 128:256], PB(bb, a * 256 + 128, a * 256 + 256), AF.Exp,
                                [("ps", bb)], [ptk], scale=0.125, waw=False)

            def A2(it):
                g, r, j, half = it["g"], it["r"], it["j"], it["half"]
                d = DILS[g]
                sc, sp_ = it["slot"], (it["slot"] - 1) % NS
                pt, ptk = it["pt"], it["ptk"]
                b3 = bank()
                kbs = [0, 1] if j > 0 else [1]
                for od in range(2):
                    for pair in range(2):
                        mats = [(hh, kb) for hh in (2 * pair, 2 * pair + 1) for kb in kbs]
                        c0 = od * 256 + pair * 128
                        for i, (hh, kb) in enumerate(mats):
                            vs = sp_ if kb == 0 else sc
                            lhs = Vb[vs][:, hh, :] if od == 0 else onesP[:, hh % 2, :]
                            rd = [ptk, ("Vb", vs)] if od == 0 else [ptk, "cb"]
                            MM(PB(b3, c0, c0 + 128), lhs, pt[:, hh, kb * 128:(kb + 1) * 128], i == 0, i == len(mats) - 1, rd, [("ps", b3)])
                off = r + d * 128 * j - 2048 * half
                av = bass.AP(acc, off, [[4 * 2048, 128], [2048, 4], [d, 128]])
                oi = cnt["od"] % 2
                cnt["od"] += 1
                odt, odk = ODT[oi], ("ODT", oi)
                CP("act", odt[:], PB(b3), [("ps", b3)], [odk])
                pv = odt[:].rearrange("p (a t) -> p a t", t=128)
                if g == 0:
                    CP("pool", av, pv, [odk], ["acc"], waw=True)
                else:
                    TT("pool", av, pv, av, ALU.add, [odk, "acc"], ["acc"])

            gidx = [0]
            allsegs = []
            for half in range(2):
                items = []
                for g in range(3):
                    d = DILS[g]
                    nb = 32 // d
                    jl, jh = half * nb // 2, (half + 1) * nb // 2
                    for r in range(d):
                        if jl > 0:
                            items.append(dict(g=g, r=r, j=jl - 1, need_q=False, att=False, half=half))
                        for j in range(jl, jh):
                            items.append(dict(g=g, r=r, j=j, need_q=True, att=True, half=half))
                for it in items:
                    it["idx"] = gidx[0]
                    it["slot"] = gidx[0] % NS
                    gidx[0] += 1
                segs = []
                for it in items:
                    if not segs or segs[-1][0] != it["g"]:
                        segs.append((it["g"], []))
                    segs[-1][1].append(it)
                allsegs.append(segs)
            flat = [(half, g, its) for half in range(2) for (g, its) in allsegs[half]]
            wbuf = {}

            def issue_w(k):
                half, g, its = flat[k]
                wi = k % 2
                wt, wk = wq[wi], ("at_w", wi)
                v = w_qkv[l, g].rearrange("(k p) n -> p k n", p=128)
                for kc in range(KC):
                    DMA("pool", wt[:, kc:kc + 1, :], v[:, kc:kc + 1, :], f"at_w{wi}", [], [wk])
                for it in its:
                    it["wt"], it["wk"] = wt, wk

            issue_w(0)
            for k, (half, g, its) in enumerate(flat):
                if k + 1 < len(flat):
                    issue_w(k + 1)
                items = its
                n = len(items)
                first_of_half = (k == 0 or flat[k - 1][0] != half)
                last_of_half = (k + 1 == len(flat) or flat[k + 1][0] != half)
                if first_of_half:
                    P1(items[0])
                    prev_last = None
                for i in range(n):
                    nxt = items[i + 1] if i + 1 < n else (flat[k + 1][2][0] if not last_of_half else None)
                    if nxt is not None:
                        P1(nxt)
                    P2(items[i])
                    prv = items[i - 1] if i > 0 else prev_last
                    if prv is not None and prv["att"]:
                        A2(prv)
                    if items[i]["att"]:
                        A1(items[i])
                prev_last = items[n - 1]
                if not last_of_half:
                    continue
                if prev_last["att"]:
                    A2(prev_last)
                prev_last = None
                for pair in range(2):
                    ACT(acc[:, 2 + pair, :], acc[:, 2 + pair, :], AF.Ln, ["acc"], ["acc"])
                    ACT(acc[:, 2 + pair, :], acc[:, 2 + pair, :], AF.Exp, ["acc"], ["acc"], scale=-1.0)
                    TT("dve", attn_sb[:, pair, :], acc[:, pair, :], acc[:, 2 + pair, :], ALU.mult, ["acc"], ["attn_sb"])
                dv = attnT.rearrange("(c p) t -> p c t", p=128)[:, :, half * 2048:(half + 1) * 2048]
                DMA("pool", dv, attn_sb[:], "attn_sb", ["attn_sb"], [("attnT", half)])
            S.barrier()

    def phase_ab(l):
        nrot[0] = 8
        psp[0] = 0
        vb = l * V_PER_LAYER
        with ExitStack() as e3:
            def sb3(name, shape, dt):
                return e3.enter_context(nc.sbuf_tensor(f"{name}_L{l}", list(shape), dt))
            mk_rings(sb3, hin=True)
            wab = sb3("ab_w", [128, KC, NAB], BF16)
            pw = sb3("ab_pw", [96, 4, 96], BF16)
            wa = sb3("ab_wa", [96, 4, D], BF16)
            wb = sb3("ab_wb", [128, 3, D], BF16)
            A = [sb3(f"ab_A{g}", [96, 16 + T], F32) for g in range(4)]
            T2 = sb3("ab_T2", [96, 16 + T], F32)
            T4 = sb3("ab_T4", [96, 16 + T], F32)
            T8 = sb3("ab_T8", [96, 16 + T], F32)
            PL = [sb3(f"ab_PL{g}", [96, T], BF16) for g in range(4)]
            MX = [sb3(f"ab_MX{g}", [96, T], BF16) for g in range(4)]
            BX = sb3("ab_BX", [128, T], F32)
            P = [sb3(f"ab_P{c}", [128, 2 + T], F32) for c in range(3)]
            Y1 = sb3("ab_Y1", [128, T], F32)
            Y2 = sb3("ab_Y2", [128, T], F32)
            Z = [sb3(f"ab_Z{c}", [128, T], BF16) for c in range(3)]
            SG = [sb3(f"ab_SG{i}", [128, T], F32) for i in range(2)]
            MO = [sb3(f"ab_MO{i}", [128, KC, T], F32) for i in range(2)]
            load_wb(wab, w_in[l][:, 0:NAB], NAB, "ab_w", order=[0])
            DMA("pool", pw[:], pool_w[l].rearrange("g c d -> c g d"), "ab_w2", [], ["ab_w2"])
            DMA("pool", wa[:], w_a[l].rearrange("(g c) n -> c g n", c=96), "ab_w2", [], ["ab_w2"])
            load_wb(wab, w_in[l][:, 0:NAB], NAB, "ab_w", order=[1, 2, 3, 4])
            DMA("pool", wb[:], w_b[l].rearrange("(k p) n -> p k n", p=128), "ab_w3", [], ["ab_w3"])
            load_wb(wab, w_in[l][:, 0:NAB], NAB, "ab_w", order=[5, 6])
            for g in range(4):
                MS("pool", A[g][:, 0:16], 0.0, [("A", g)])
            for c in range(3):
                MS("pool", P[c][:, 0:2], 0.0, [("P", c)])
            sgi = [0]
            for tt in range(NT):
                ht, hk = G["hin"].next()
                DMA("sp", ht[:], xview(h1T, tt), hk, [("h1T", tt)], [hk])
                mo, mok = MO[tt % 2], ("MO", tt % 2)
                for g in range(4):
                    w = (2, 4, 8, 16)[g]
                    Ak = ("A", g)
                    if tt > 0:
                        CP("pool", A[g][:, 0:16], A[g][:, T:T + 16], [Ak], [Ak])
                    b = bank()
                    for kc in range(KC):
                        MM(ps[0:96, b * 512:(b + 1) * 512], wab[:, kc, OFF_A + 96 * g:OFF_A + 96 * (g + 1)], ht[:, kc, :], kc == 0, kc == KC - 1,
                           [hk, ("ab_w", 0)], [("ps", b)])
                    CP("act", A[g][:, 16:16 + T], ps[0:96, b * 512:(b + 1) * 512], [("ps", b)], [Ak])
                    src = A[g]
                    srck = Ak
                    n = 1
                    for (dst, dk) in ((T2, "T2"), (T4, "T4"), (T8, "T8"), (None, None)):
                        if n >= w:
                            break
                        if 2 * n == w:
                            tmpw, tmpk = T2 if dst is not T2 and src is not T2 else (T4 if src is not T4 else T8), None
                            tmpw = {1: T2, 2: T4, 4: T8, 8: T2}[n]
                            tmpk = {1: "T2", 2: "T4", 4: "T8", 8: "T2"}[n]
                            TT("pool", tmpw[:, 16:16 + T], src[:, 16:16 + T], src[:, 16 - n:16 - n + T], ALU.add, [srck], [tmpk])
                            STT("dve", PL[g][:], tmpw[:, 16:16 + T], 1.0 / w, A[g][:, 16:16 + T], ALU.mult, ALU.subtract, [tmpk, Ak], [("PL", g)])
                            if tt == 0:
                                tmp, tk = tmp_ring.next()
                                TT("pool", tmp[0:96, 0:16], tmpw[:, 16:32], consts[0:96, C_RC + 16 * g:C_RC + 16 * g + 16], ALU.mult, [tmpk, "consts"], [tk])
                                TT("pool", PL[g][:, 0:16], tmp[0:96, 0:16], A[g][:, 16:32], ALU.subtract, [tk, Ak], [("PL", g)])
                            break
                        TT("pool", dst[:, 2 * n - 1:16 + T], src[:, 2 * n - 1:16 + T], src[:, n - 1:16 + T - n], ALU.add, [srck], [dk])
                        src, srck = dst, dk
                        n *= 2
                for c in range(3):
                    Pk = ("P", c)
                    if tt > 0:
                        CP("pool", P[c][:, 0:2], P[c][:, T:T + 2], [Pk], [Pk])
                    bx, bb_, bc = bank(), bank(), bank()
                    for (bk, off) in ((bx, OFF_BX), (bb_, OFF_BB), (bc, OFF_BC)):
                        for kc in range(KC):
                            MM(PB(bk), wab[:, kc, off + c * 128:off + (c + 1) * 128], ht[:, kc, :], kc == 0, kc == KC - 1, [hk, ("ab_w", (off + c * 128) // 512)], [("ps", bk)])
                    CP("act", BX[:], PB(bx), [("ps", bx)], ["BX"])
                    TT("dve", P[c][:, 2:2 + T], PB(bc), BX[:], ALU.mult, [("ps", bc), "BX"], [Pk])
                    cw = vb + V_CONVB
                    ACT(Y1[:], P[c][:, 2:2 + T], AF.Copy, [Pk, "vecs"], ["Y1"], scale=vecs[:, cw + 2 * 3 + c:cw + 2 * 3 + c + 1])
                    STT("dve", Y2[:], P[c][:, 1:1 + T], vecs[:, cw + 1 * 3 + c:cw + 1 * 3 + c + 1], Y1[:], ALU.mult, ALU.add, [Pk, "Y1", "vecs"], ["Y2"])
                    STT("dve", Y1[:], P[c][:, 0:T], vecs[:, cw + 0 * 3 + c:cw + 0 * 3 + c + 1], Y2[:], ALU.mult, ALU.add, [Pk, "Y2", "vecs"], ["Y1"])
                    TT("dve", Z[c][:], PB(bb_), Y1[:], ALU.mult, [("ps", bb_), "Y1"], [("Z", c)])
                for g in range(4):
                    b2 = bank()
                    MM(ps[0:96, b2 * 512:(b2 + 1) * 512], pw[:, g, :], PL[g][:], True, True, [("PL", g), "ab_w2"], [("ps", b2)])
                    ACT(MX[g][:], ps[0:96, b2 * 512:(b2 + 1) * 512], AF.Copy, [("ps", b2), "vecs"], [("MX", g)],
                        scale=vecs[0:96, vb + V_PSCALE + g:vb + V_PSCALE + g + 1])
                for mc in range(KC):
                    b = bank()
                    for g in range(4):
                        MM(PB(b), wa[:, g, mc * 128:(mc + 1) * 128], MX[g][:], g == 0, g == 3, [("MX", g), "ab_w2"], [("ps", b)])
                    bg = bank()
                    for kc in range(KC):
                        MM(PB(bg), wab[:, kc, OFF_GA + mc * 128:OFF_GA + (mc + 1) * 128], ht[:, kc, :], kc == 0, kc == KC - 1, [hk, ("ab_w", (OFF_GA + mc * 128) // 512)], [("ps", bg)])
                    sg, sgk = SG[sgi[0] % 2], ("SG", sgi[0] % 2)
                    sgi[0] += 1
                    ACT(sg[:], PB(bg), AF.Sigmoid, [("ps", bg)], [sgk])
                    TT("dve", mo[:, mc, :], PB(b), sg[:], ALU.mult, [("ps", b), sgk], [mok], waw=False)
                for mc in range(KC):
                    b = bank()
                    for c in range(3):
                        MM(PB(b), wb[:, c, mc * 128:(mc + 1) * 128], Z[c][:], c == 0, c == 2, [("Z", c), "ab_w3"], [("ps", b)])
                    bg = bank()
                    for kc in range(KC):
                        MM(PB(bg), wab[:, kc, OFF_GB + mc * 128:OFF_GB + (mc + 1) * 128], ht[:, kc, :], kc == 0, kc == KC - 1, [hk, ("ab_w", (OFF_GB + mc * 128) // 512)], [("ps", bg)])
                    sg, sgk = SG[sgi[0] % 2], ("SG", sgi[0] % 2)
                    sgi[0] += 1
                    ACT(sg[:], PB(bg), AF.Sigmoid, [("ps", bg)], [sgk])
                    tmp, tk = tmp_ring.next()
                    TT("dve", tmp[:], PB(b), sg[:], ALU.mult, [("ps", b), sgk], [tk])
                    TT("pool", mo[:, mc, :], mo[:, mc, :], tmp[:], ALU.add, [mok, tk], [mok], waw=False)
                DMA("pool", xview(mab, tt), mo[:], mok, [mok], [("mab", tt)])
            S.barrier()

    def phase_merge(l):
        nrot[0] = 6
        psp[0] = 0
        vb = l * V_PER_LAYER
        xsrc = xT if l == 0 else xs
        with ExitStack() as e4:
            def sb4(name, shape, dt):
                return e4.enter_context(nc.sbuf_tensor(f"{name}_L{l}", list(shape), dt))
            mk_rings(sb4, x=True, h=True, hin=True, ysb=True)
            wgc = sb4("mg_wgc", [128, KC, D], BF16)
            wc = sb4("mg_wc", [128, 2, D], BF16)
            wo = sb4("mg_wo", [128, KC, D], BF16)
            AT = [sb4(f"mg_at{i}", [128, 2, T], BF16) for i in range(2)]
            MI = [sb4(f"mg_mi{i}", [128, KC, T], F32) for i in range(2)]
            MGs = [sb4(f"mg_mg{i}", [128, KC, T], BF16) for i in range(2)]
            SG = [sb4(f"mg_SG{i}", [128, T], F32) for i in range(2)]
            load_w(wc, w_c[l], D, "mg_wc", "mg_wc")
            load_wb(wgc, w_in[l][:, OFF_GC:OFF_GC + D], D, "mg_wg")
            load_wb(wo, w_out[l], D, "mg_wo")
            st = {}

            def stA(tt):
                ht, hk = G["hin"].next()
                DMA("sp", ht[:], xview(h1T, tt), hk, [("h1T", tt)], [hk])
                at, atk = AT[tt % 2], ("AT", tt % 2)
                DMA("sp", at[:], attnT.rearrange("(c p) t -> p c t", p=128)[:, :, tt * T:(tt + 1) * T], f"mg_at{tt % 2}", [("attnT", tt // 4)], [atk])
                mi, mik = MI[tt % 2], ("MI", tt % 2)
                DMA("sp", mi[:], xview(mab, tt), f"mg_mi{tt % 2}", [("mab", tt)], [mik])
                xt, xk = G["x"].next()
                DMA("sp", xt[:], xview(xsrc, tt), xk, [("xs", tt)], [xk])
                mg, mgk = MGs[tt % 2], ("MG", tt % 2)
                for mc in range(KC):
                    b = bank()
                    for pr in range(2):
                        MM(PB(b), wc[:, pr, mc * 128:(mc + 1) * 128], at[:, pr, :], pr == 0, pr == 1, [atk, "mg_wc"], [("ps", b)])
                    bg = bank()
                    for kc in range(KC):
                        MM(PB(bg), wgc[:, kc, mc * 128:(mc + 1) * 128], ht[:, kc, :], kc == 0, kc == KC - 1, [hk, ("mg_wg", mc // 4)], [("ps", bg)])
                    sg, sgk = SG[mc % 2], ("SG4", mc % 2)
                    ACT(sg[:], PB(bg), AF.Sigmoid, [("ps", bg)], [sgk])
                    tmp, tk = tmp_ring.next()
                    TT("dve", tmp[:], PB(b), sg[:], ALU.mult, [("ps", b), sgk], [tk])
                    TT("pool", mg[:, mc, :], tmp[:], mi[:, mc, :], ALU.add, [tk, mik], [mgk], waw=False)
                st[tt] = dict(xt=xt, xk=xk, mg=mg, mgk=mgk)

            def stW(tt):
                d_ = st[tt]
                pn = PostNorm()
                for mc in range(KC):
                    b = bank()
                    for kc in range(KC):
                        MM(PB(b), wo[:, kc, mc * 128:(mc + 1) * 128], d_["mg"][:, kc, :], kc == 0, kc == KC - 1, [d_["mgk"], ("mg_wo", mc // 4)], [("ps", b)])
                    pn.chunk(mc, b)
                d_["pn"] = pn

            def stF1(tt):
                d_ = st[tt]
                xt, xk = d_["xt"], d_["xk"]
                d_["pn"].finish(xt, xk, vb + V_MIXPOST)
                DMA("pool", xview(xs, tt), xt[:], xk, [xk], [("xs", tt)])

            def stF2(tt):
                d_ = st.pop(tt)
                xt, xk = d_["xt"], d_["xk"]
                h2, h2k = G["h"].next()
                norm_tile(xt, xk, vb + V_MEMPRE, h2, h2k)
                DMA("pool", xview(h2T, tt), h2[:], h2k, [h2k], [("h2T", tt)])

            stA(0)
            stW(0)
            for tt in range(1, NT):
                stF1(tt - 1)
                stA(tt)
                stF2(tt - 1)
                stW(tt)
            stF1(NT - 1)
            stF2(NT - 1)
            S.barrier()

    def phase_mem(l):
        nrot[0] = 6
        psp[0] = 0
        vb = l * V_PER_LAYER
        with ExitStack() as e5:
            def sb5(name, shape, dt):
                return e5.enter_context(nc.sbuf_tensor(f"{name}_L{l}", list(shape), dt))
            mk_rings(sb5, x=True, h=True, hin=True, ysb=True)
            wq_ = sb5("mm_wq", [128, KC, 512], BF16)
            wkv = sb5("mm_wkv", [128, KC, 1024], BF16)
            wo = sb5("mm_wo", [128, 4, D], BF16)
            mt = sb5("mm_mt", [128, KC, 256], F32)
            mn = sb5("mm_mn", [128, KC, 256], BF16)
            KmT = sb5("mm_KmT", [128, 4, 256], BF16)
            Vm = sb5("mm_Vm", [128, 2, 512], BF16)
            QM = [sb5(f"mm_QM{i}", [128, T], BF16) for i in range(4)]
            PTm = [sb5(f"mm_PT{i}", [128, 2, T], BF16) for i in range(4)]
            DN = [sb5(f"mm_DN{i}", [128, T], F32) for i in range(2)]
            OMs = [sb5(f"mm_OM{i}", [128, 4, T], BF16) for i in range(2)]
            load_w(wkv, w_mkv[l], 1024, "mm_w", "mm_w")
            load_w(wq_, w_mq[l], 512, "mm_w", "mm_w")
            load_w(wo, w_mo[l], D, "mm_wo", "mm_wo")
            DMA("sp", mt[:], memT.rearrange("(c p) t -> p c t", p=128), "mm_mt", [], ["mm_mt"])
            norm_tile(mt, "mm_mt", vb + V_MEMKV, mn, "mm_mn", ncols=256)
            for h in range(4):
                b = bank()
                for kc in range(KC):
                    MM(PB(b, 0, 256), wkv[:, kc, h * 128:(h + 1) * 128], mn[:, kc, 0:256], kc == 0, kc == KC - 1, ["mm_mn", "mm_w"], [("ps", b)])
                CP("act", KmT[:, h, :], PB(b, 0, 256), [("ps", b)], ["KmT"], waw=False)
            for mi_ in range(2):
                b = bank()
                for kc in range(KC):
                    MM(PB(b), mn[:, kc, mi_ * 128:(mi_ + 1) * 128], wkv[:, kc, 512:1024], kc == 0, kc == KC - 1, ["mm_mn", "mm_w"], [("ps", b)])
                CP("act", Vm[:, mi_, :], PB(b), [("ps", b)], ["Vm"], waw=False)
            sc = float(128 ** -0.5)
            st = {}

            def stA(tt):
                ht, hk = G["hin"].next()
                DMA("sp", ht[:], xview(h2T, tt), hk, [("h2T", tt)], [hk])
                xt, xk = G["x"].next()
                DMA("sp", xt[:], xview(xs, tt), xk, [("xs", tt)], [xk])
                om, omk = OMs[tt % 2], ("OM", tt % 2)
                for h in range(4):
                    b = bank()
                    for kc in range(KC):
                        MM(PB(b), wq_[:, kc, h * 128:(h + 1) * 128], ht[:, kc, :], kc == 0, kc == KC - 1, [hk, "mm_w"], [("ps", b)])
                    CP("act", QM[h][:], PB(b), [("ps", b)], [("QM", h)])
                for h in range(4):
                    for mi_ in range(2):
                        bs = bank()
                        MM(PB(bs), KmT[:, h, mi_ * 128:(mi_ + 1) * 128], QM[h][:], True, True, ["KmT", ("QM", h)], [("ps", bs)])
                        ACT(PTm[h][:, mi_, :], PB(bs), AF.Exp, [("ps", bs)], [("PTm", h)], scale=sc, waw=False)
                for h in range(4):
                    pt, ptk = PTm[h], ("PTm", h)
                    bo, bd = bank(), bank()
                    for mi_ in range(2):
                        MM(PB(bo), Vm[:, mi_, h * 128:(h + 1) * 128], pt[:, mi_, :], mi_ == 0, mi_ == 1, ["Vm", ptk], [("ps", bo)])
                    for mi_ in range(2):
                        MM(PB(bd), ones[:], pt[:, mi_, :], mi_ == 0, mi_ == 1, ["cb", ptk], [("ps", bd)])
                    dn, dnk = DN[h % 2], ("DN", h % 2)
                    ACT(dn[:], PB(bd), AF.Ln, [("ps", bd)], [dnk])
                    ACT(dn[:], dn[:], AF.Exp, [dnk], [dnk], scale=-1.0)
                    TT("dve", om[:, h, :], PB(bo), dn[:], ALU.mult, [("ps", bo), dnk], [omk], waw=False)
                st[tt] = dict(xt=xt, xk=xk, om=om, omk=omk)

            def stW(tt):
                d_ = st[tt]
                pn = PostNorm()
                for mc in range(KC):
                    b = bank()
                    for h in range(4):
                        MM(PB(b), wo[:, h, mc * 128:(mc + 1) * 128], d_["om"][:, h, :], h == 0, h == 3, [d_["omk"], "mm_wo"], [("ps", b)])
                    pn.chunk(mc, b)
                d_["pn"] = pn

            def stF1(tt):
                d_ = st[tt]
                xt, xk = d_["xt"], d_["xk"]
                d_["pn"].finish(xt, xk, vb + V_MEMPOST)
                DMA("pool", xview(xs, tt), xt[:], xk, [xk], [("xs", tt)])

            def stF2(tt):
                d_ = st.pop(tt)
                xt, xk = d_["xt"], d_["xk"]
                h3, h3k = G["h"].next()
                norm_tile(xt, xk, vb + V_FFNPRE, h3, h3k)
                DMA("pool", xview(h3T, tt), h3[:], h3k, [h3k], [("h3T", tt)])

            stA(0)
            stW(0)
            for tt in range(1, NT):
                stF1(tt - 1)
                stA(tt)
                stF2(tt - 1)
                stW(tt)
            stF1(NT - 1)
            stF2(NT - 1)
            S.barrier()

    def phase_up(l):
        nrot[0] = 8
        psp[0] = 0
        vb = l * V_PER_LAYER
        with ExitStack() as e6:
            def sb6(name, shape, dt):
                return e6.enter_context(nc.sbuf_tensor(f"{name}_L{l}", list(shape), dt))
            mk_rings(sb6, hin=True)
            wu = sb6("up_w", [128, KC, 2 * DFF], BF16)
            H = sb6("up_H", [128, FC, 2], F32)
            UA = [sb6(f"up_UA{i}", [128, 2 + T], F32) for i in range(2)]
            Y1 = [sb6(f"up_Y1{i}", [128, T], F32) for i in range(2)]
            Y2 = [sb6(f"up_Y2{i}", [128, T], F32) for i in range(2)]
            AO = [sb6(f"up_AO{i}", [128, FC, T], BF16) for i in range(2)]
            load_wb(wu, w_up[l], 2 * DFF, "up_w", order=[0, 5, 6, 1, 7, 2, 8, 3, 9, 4, 10])
            MS("pool", H[:], 0.0, ["H"])
            cw = vb + V_CONVF
            for tt in range(NT):
                ht, hk = G["hin"].next()
                DMA("sp", ht[:], xview(h3T, tt), hk, [("h3T", tt)], [hk])
                ao, aok = AO[tt % 2], ("AO", tt % 2)
                for c in range(FC):
                    ba, bb_ = bank(), bank()
                    for kc in range(KC):
                        MM(PB(ba), wu[:, kc, c * 128:(c + 1) * 128], ht[:, kc, :], kc == 0, kc == KC - 1, [hk, ("up_w", (c * 128) // 512)], [("ps", ba)])
                    for kc in range(KC):
                        MM(PB(bb_), wu[:, kc, DFF + c * 128:DFF + (c + 1) * 128], ht[:, kc, :], kc == 0, kc == KC - 1, [hk, ("up_w", (DFF + c * 128) // 512)], [("ps", bb_)])
                    i = c % 2
                    ua, uak = UA[i], ("UA", i)
                    y1, y1k = Y1[i], ("Y1", i)
                    y2, y2k = Y2[i], ("Y2", i)
                    CP("pool", ua[:, 0:2], H[:, c, :], ["H"], [uak])
                    CP("act", ua[:, 2:2 + T], PB(ba), [("ps", ba)], [uak], waw=False)
                    CP("pool", H[:, c, :], ua[:, T:T + 2], [uak], ["H"])
                    ACT(y1[:], ua[:, 2:2 + T], AF.Copy, [uak, "vecs"], [y1k], scale=vecs[:, cw + 2 * FC + c:cw + 2 * FC + c + 1])
                    STT("dve", y2[:], ua[:, 1:1 + T], vecs[:, cw + 1 * FC + c:cw + 1 * FC + c + 1], y1[:], ALU.mult, ALU.add, [uak, y1k, "vecs"], [y2k])
                    STT("dve", y1[:], ua[:, 0:T], vecs[:, cw + 0 * FC + c:cw + 0 * FC + c + 1], y2[:], ALU.mult, ALU.add, [uak, y2k, "vecs"], [y1k])
                    ACT(y2[:], y1[:], AF.Silu, [y1k], [y2k])
                    TT("dve", ao[:, c, :], PB(bb_), y2[:], ALU.mult, [("ps", bb_), y2k], [aok], waw=False)
                DMA("pool", actT.rearrange("(c p) t -> p c t", p=128)[:, :, tt * T:(tt + 1) * T], ao[:], f"up_ao{tt % 2}", [aok], [("actT", tt)])
            S.barrier()

    def phase_down(l, last):
        nrot[0] = 6
        psp[0] = 0
        vb = l * V_PER_LAYER
        with ExitStack() as e7:
            def sb7(name, shape, dt):
                return e7.enter_context(nc.sbuf_tensor(f"{name}_L{l}", list(shape), dt))
            mk_rings(sb7, x=True, h=True, ysb=True)
            wd = sb7("dn_w", [128, FC, D], BF16)
            AI = [sb7(f"dn_AI{i}", [128, FC, T], BF16) for i in range(2)]
            for q4 in range(4):
                DMA("pool", wd[:, :, q4 * 256:(q4 + 1) * 256], w_down[l].rearrange("(k p) n -> p k n", p=128)[:, :, q4 * 256:(q4 + 1) * 256],
                    f"dn_w{q4}", [], [("dn_w", q4)])
            st = {}

            def stA(tt):
                ai, aik = AI[tt % 2], ("AI", tt % 2)
                DMA("sp", ai[:], actT.rearrange("(c p) t -> p c t", p=128)[:, :, tt * T:(tt + 1) * T], f"dn_ai{tt % 2}", [("actT", tt)], [aik])
                xt, xk = G["x"].next()
                DMA("sp", xt[:], xview(xs, tt), xk, [("xs", tt)], [xk])
                st[tt] = dict(xt=xt, xk=xk, ai=ai, aik=aik, pn=PostNorm())

            def stW(tt, mcs):
                d_ = st[tt]
                for mc in mcs:
                    b = bank()
                    for c in range(FC):
                        MM(PB(b), wd[:, c, mc * 128:(mc + 1) * 128], d_["ai"][:, c, :], c == 0, c == FC - 1, [d_["aik"], ("dn_w", mc // 2)], [("ps", b)])
                    d_["pn"].chunk(mc, b)
                if d_["pn"].pend is not None:
                    d_["pn"].flush()

            def stF1(tt):
                d_ = st[tt]
                xt, xk = d_["xt"], d_["xk"]
                d_["pn"].finish(xt, xk, vb + V_FFNPOST)
                if last:
                    DMA("pool", xview(outT, tt), xt[:], xk, [xk], [("outT", tt)])
                else:
                    DMA("pool", xview(xs, tt), xt[:], xk, [xk], [("xs", tt)])

            def stF2(tt):
                d_ = st.pop(tt)
                xt, xk = d_["xt"], d_["xk"]
                if not last:
                    h1, h1k = G["h"].next()
                    norm_tile(xt, xk, (l + 1) * V_PER_LAYER + V_MIXPRE, h1, h1k)
                    DMA("pool", xview(h1T, tt), h1[:], h1k, [h1k], [("h1T", tt)])

            stA(0)
            stW(0, range(KC))
            for tt in range(1, NT):
                stF1(tt - 1)
                stA(tt)
                stW(tt, range(0, 4))
                stF2(tt - 1)
                stW(tt, range(4, KC))
            stF1(NT - 1)
            stF2(NT - 1)
            S.barrier()


    return dict(nc=nc, S=S, es=es, phases=dict(norm0=phase_norm0, attn=phase_attn, ab=phase_ab, merge=phase_merge,
                                               mem=phase_mem, up=phase_up, down=phase_down))


def build_full(n_layers=NL, dbg=False, stop=None, opts=()):
    P = build_program(n_layers, dbg, opts)
    ph = P["phases"]
    seq = []
    if "nonorm0" not in opts:
        ph["norm0"](0)
    done = stop is not None and stop[1] == "norm0"
    for l in range(n_layers):
        if done:
            break
        for name in ("attn", "ab", "merge", "mem", "up", "down"):
            if name == "down":
                ph[name](l, l == n_layers - 1)
            else:
                ph[name](l)
            if stop is not None and stop == (l, name):
                done = True
                break
        if done:
            break
    nsem = P["S"].emit()
    P["es"].close()
    return P["nc"], P["S"], nsem


def host_inputs(inputs):
    import ml_dtypes
    f32 = np.float32
    perm = w_in_perm()
    wip = np.asarray(inputs["w_in"], f32)[:, :, perm]
    wqkv = np.zeros((NL, 3, D, 1280), f32)
    for g in range(3):
        base = OFF_QKV + 768 * g
        wqkv[:, g, :, 0:256] = wip[:, :, base:base + 256]
        for hh in range(4):
            par = hh % 2
            wqkv[:, g, :, 256 + hh * 128 + 64 * par:256 + hh * 128 + 64 * par + 64] = wip[:, :, base + 256 + 64 * hh:base + 256 + 64 * hh + 64]
            wqkv[:, g, :, 768 + hh * 128 + 64 * par:768 + hh * 128 + 64 * par + 64] = wip[:, :, base + 512 + 64 * hh:base + 512 + 64 * hh + 64]
    shared = {
        "w_in": np.ascontiguousarray(wip[:, :, 0:OFF_QKV]),
        "w_qkv": wqkv,
        "pool_w": np.ascontiguousarray(np.asarray(inputs["pool_w"], f32)),
    }
    for k in ("w_branch_a", "w_branch_b", "w_branch_c", "w_out", "w_mq", "w_mkv", "w_mo", "w_up", "w_down"):
        shared[k] = np.ascontiguousarray(np.asarray(inputs[k], f32))
    vecs = np.zeros((128, NL * V_PER_LAYER), f32)
    for l in range(NL):
        vb = l * V_PER_LAYER
        for name, off in (("norm_mix_pre", V_MIXPRE), ("norm_mix_post", V_MIXPOST), ("norm_mem_pre", V_MEMPRE),
                          ("norm_mem_post", V_MEMPOST), ("norm_memkv", V_MEMKV), ("norm_ffn_pre", V_FFNPRE),
                          ("norm_ffn_post", V_FFNPOST)):
            vecs[:, vb + off:vb + off + 8] = np.asarray(inputs[name], f32)[l].reshape(8, 128).T
        vecs[:, vb + V_CONVB:vb + V_CONVB + 9] = np.asarray(inputs["conv_b_w"], f32)[l].reshape(3, 3, 128).transpose(2, 0, 1).reshape(128, 9)
        vecs[:, vb + V_CONVF:vb + V_CONVF + 66] = np.asarray(inputs["conv_ffn_w"], f32)[l].reshape(3, FC, 128).transpose(2, 0, 1).reshape(128, 66)
        vecs[0:96, vb + V_PSCALE:vb + V_PSCALE + 4] = np.asarray(inputs["pool_scale"], f32)[l].reshape(4, 96).T
    consts = np.zeros((128, NCONST), f32)
    consts[:, C_ID:C_ID + 128] = np.eye(128, dtype=f32)
    k = np.arange(128)[:, None]
    q = np.arange(128)[None, :]
    consts[:, C_MASK:C_MASK + 128] = np.where(k >= q, 0.0, -30000.0)
    consts[:, C_MASK + 128:C_MASK + 256] = np.where(k <= q, 0.0, -30000.0)
    consts[:, C_INV:C_INV + 8] = (f32(500000.0) ** (-np.arange(0, 16, 2, dtype=f32) / f32(16)))[None, :]
    for g, w in enumerate((2, 4, 8, 16)):
        consts[:, C_RC + 16 * g:C_RC + 16 * g + 16] = (1.0 / np.minimum(np.arange(16) + 1, w))[None, :]
    shared["vecs"] = vecs
    shared["consts"] = consts
    x = np.asarray(inputs["x"], f32)
    mem = np.asarray(inputs["mem"], f32)
    posn = np.asarray(inputs["positions"]).astype(np.int32)
    in_maps = []
    for b in range(8):
        m = dict(shared)
        m["xT"] = np.ascontiguousarray(x[b].T)
        m["memT"] = np.ascontiguousarray(mem[b].T)
        m["pos"] = np.ascontiguousarray(posn[b].reshape(32, 128))
        in_maps.append(m)
    return in_maps


_CACHE = {}


def kernel(**inputs):
    in_maps = host_inputs(inputs)
    if "nc" not in _CACHE:
        _CACHE["nc"] = build_full()[0]
    nc = _CACHE["nc"]
    res = run_bass_kernel_spmd(nc, in_maps, core_ids=list(range(8)))
    out = np.stack([np.ascontiguousarray(res.results[b]["outT"].T) for b in range(8)], axis=0)
    return out.astype(np.float32)
```

```python
import numpy as np
import concourse.bass as bass
import concourse.mybir as mybir

F32 = mybir.dt.float32
BF16 = mybir.dt.bfloat16
I32 = mybir.dt.int32
ALU = mybir.AluOpType
AF = mybir.ActivationFunctionType

EPOCH = 30000


class Sched:
    ENGS = ("pe", "act", "dve", "pool", "sp")

    def __init__(self, nc):
        self.nc = nc
        self.stream = {e: [] for e in self.ENGS}
        self.cnt = {e: 0 for e in self.ENGS}
        self.known = {e: {} for e in self.ENGS}
        self.res = {}
        self.chan_cnt = {}
        self.n_wait = 0

    def _collect(self, eng, reads, writes, waw):
        waits = {}

        def need(k, v):
            if k == ("eng", "pe") and eng == "pe":
                return
            if self.known[eng].get(k, 0) >= v:
                return
            if waits.get(k, 0) < v:
                waits[k] = v

        for r in reads:
            st = self.res.get(r)
            if st:
                for k, v in st["w"].items():
                    need(k, v)
        for w in writes:
            st = self.res.get(w)
            if st:
                for k, v in st["r"].items():
                    need(k, v)
                if waw:
                    for k, v in st["w"].items():
                        need(k, v)
        for k, v in waits.items():
            self.known[eng][k] = v
        return sorted(waits.items(), key=lambda kv: str(kv[0]))

    def _commit(self, ev, reads, writes, waw):
        k, v = ev
        for r in reads:
            st = self.res.setdefault(r, {"w": {}, "r": {}})
            if st["r"].get(k, 0) < v:
                st["r"][k] = v
        for w in writes:
            st = self.res.setdefault(w, {"w": {}, "r": {}})
            if st["r"] or waw:
                st["w"] = {}
            st["r"] = {}
            if st["w"].get(k, 0) < v:
                st["w"][k] = v

    def op(self, eng, fn, reads=(), writes=(), waw=True):
        waits = self._collect(eng, reads, writes, waw)
        self.cnt[eng] += 1
        ev = (("eng", eng), self.cnt[eng])
        self._commit(ev, reads, writes, waw)
        self.stream[eng].append((waits, fn, ev))
        self.n_wait += len(waits)

    def dma(self, eng, fn, chan, reads=(), writes=(), waw=False):
        waits = self._collect(eng, reads, writes, waw)
        self.chan_cnt[chan] = self.chan_cnt.get(chan, 0) + 16
        ev = (("chan", chan), self.chan_cnt[chan])
        self._commit(ev, reads, writes, waw)
        self.stream[eng].append((waits, fn, ev))
        self.n_wait += len(waits)

    def barrier(self, engs=None):
        engs = engs or self.ENGS
        for e in engs:
            waits = {}
            for e2 in self.ENGS:
                if e2 != e and self.cnt[e2] > 0:
                    k = ("eng", e2)
                    if self.known[e].get(k, 0) < self.cnt[e2]:
                        waits[k] = self.cnt[e2]
            for c, v in self.chan_cnt.items():
                k = ("chan", c)
                if self.known[e].get(k, 0) < v:
                    waits[k] = v
            for k, v in waits.items():
                self.known[e][k] = v
            if waits:
                self.stream[e].append((sorted(waits.items(), key=lambda kv: str(kv[0])), None, None))

    def emit(self):
        nc = self.nc
        sems = {}

        def sem_of(k, v):
            if k[0] == "eng":
                ep = (v - 1) // EPOCH
                key = (k, ep)
                val = (v - 1) % EPOCH + 1
            else:
                key = (k, 0)
                val = v
            if key not in sems:
                sems[key] = nc.alloc_semaphore(name=f"s{len(sems)}")
            return sems[key], val

        self.barrier(engs=("sp",))
        for e in self.ENGS:
            for waits, fn, ev in self.stream[e]:
                for k, v in waits:
                    sem_of(k, v)
                if ev is not None:
                    sem_of(*ev)
        eng_map = {"pe": "tensor", "act": "scalar", "dve": "vector", "pool": "gpsimd", "sp": "sync"}
        with nc.Block() as block:
            for e in self.ENGS:
                if not self.stream[e]:
                    continue
                deco = getattr(block, eng_map[e])

                def body(h, e=e):
                    for waits, fn, ev in self.stream[e]:
                        for k, v in waits:
                            s, val = sem_of(k, v)
                            h.wait_ge(s, val)
                        if fn is None:
                            continue
                        ins = fn(h)
                        s, _ = sem_of(*ev)
                        ins.then_inc(s, 16 if ev[0][0] == "chan" else 1)

                deco(body)
        return len(sems)


def sap(t, F, poff, npart, off, dims):
    return bass.AP(t, poff * F + off, [[F, npart]] + [list(d) for d in dims])

from concourse.bass_utils import run_bass_kernel_spmd
from contextlib import ExitStack

D = 1024
SEQ = 4096
T = 512
NT = SEQ // T
KC = 8
DFF = 2816
FC = DFF // 128
NL = 2
EPS = 1e-6
DILS = (1, 4, 16)
OFF_A = 0
OFF_BX = 384
OFF_BB = 768
OFF_BC = 1152
OFF_GA = 1536
OFF_GB = 2560
OFF_GC = 3584
OFF_QKV = 4608
NAB = 3584
V_MIXPRE, V_MIXPOST, V_MEMPRE, V_MEMPOST, V_MEMKV, V_FFNPRE, V_FFNPOST = 0, 8, 16, 24, 32, 40, 48
V_CONVB = 56
V_CONVF = 65
V_PSCALE = 131
V_PER_LAYER = 135
C_ID = 0
C_MASK = 128
C_INV = 384
C_RC = 392
NCONST = 456


def w_in_perm():
    idx = []
    a = 0
    idx += list(range(0, 384))
    idx += list(range(384, 384 + 1152))
    g0 = 384 + 1152 + 3 * 768
    idx += list(range(g0, g0 + 3072))
    q0 = 384 + 1152
    for g in range(3):
        for part in range(3):
            s = q0 + part * 768 + g * 256
            idx += list(range(s, s + 256))
    return np.array(idx, dtype=np.int64)


def build_program(n_layers=NL, dbg=False, opts=()):
    nc = bass.Bass("TRN2", target_bir_lowering=False)
    S = Sched(nc)

    def din(name, shape, dt=F32):
        return nc.dram_tensor(name, list(shape), dt, kind="ExternalInput").ap()

    kind_s = "ExternalOutput" if dbg else "Internal"

    def dscr(name, shape, dt):
        return nc.dram_tensor(name, list(shape), dt, kind=kind_s).ap()

    xT = din("xT", [D, SEQ])
    memT = din("memT", [D, 256])
    pos = din("pos", [32, 128], I32)
    w_in = din("w_in", [NL, D, OFF_QKV])
    w_qkv = din("w_qkv", [NL, 3, D, 1280])
    pool_w = din("pool_w", [NL, 4, 96, 96])
    w_a = din("w_branch_a", [NL, 384, D])
    w_b = din("w_branch_b", [NL, 384, D])
    w_c = din("w_branch_c", [NL, 256, D])
    w_out = din("w_out", [NL, D, D])
    w_mq = din("w_mq", [NL, D, 512])
    w_mkv = din("w_mkv", [NL, D, 1024])
    w_mo = din("w_mo", [NL, 512, D])
    w_up = din("w_up", [NL, D, 2 * DFF])
    w_down = din("w_down", [NL, DFF, D])
    vecs_d = din("vecs", [128, NL * V_PER_LAYER])
    consts_d = din("consts", [128, NCONST])
    outT = nc.dram_tensor("outT", [D, SEQ], F32, kind="ExternalOutput").ap()

    h1T = dscr("h1T", [D, SEQ], BF16)
    h2T = dscr("h2T", [D, SEQ], BF16)
    h3T = dscr("h3T", [D, SEQ], BF16)
    attnT = dscr("attnT", [256, SEQ], BF16)
    mab = dscr("mab", [D, SEQ], F32)
    xs = dscr("xs", [D, SEQ], F32)
    actT = dscr("actT", [DFF, SEQ], BF16)
    rope_d = dscr("rope_tab", [SEQ, 16], F32)

    es = ExitStack()

    def sb(name, shape, dt):
        return es.enter_context(nc.sbuf_tensor("s_" + name, list(shape), dt))

    ps = es.enter_context(nc.psum_tensor("ps", [128, 4096], F32))

    def MM(out, lhsT, rhs, start, stop, R, W):
        S.op("pe", lambda h: h.matmul(out, lhsT=lhsT, rhs=rhs, start=start, stop=stop), reads=R, writes=W)

    def ACT(out, in_, func, R, W, scale=1.0, bias=None, waw=True):
        if bias is None:
            S.op("act", lambda h: h.activation(out=out, in_=in_, func=func, scale=scale), reads=R, writes=W, waw=waw)
        else:
            S.op("act", lambda h: h.activation(out=out, in_=in_, func=func, scale=scale, bias=bias), reads=R, writes=W, waw=waw)

    def TT(eng, out, in0, in1, op, R, W, waw=True):
        S.op(eng, lambda h: h.tensor_tensor(out=out, in0=in0, in1=in1, op=op), reads=R, writes=W, waw=waw)

    def TS(eng, out, in0, s1, s2, op0, op1, R, W, waw=True):
        if s2 is None:
            S.op(eng, lambda h: h.tensor_scalar(out=out, in0=in0, scalar1=s1, scalar2=None, op0=op0), reads=R, writes=W, waw=waw)
        else:
            S.op(eng, lambda h: h.tensor_scalar(out=out, in0=in0, scalar1=s1, scalar2=s2, op0=op0, op1=op1), reads=R, writes=W, waw=waw)

    def STT(eng, out, in0, scalar, in1, op0, op1, R, W, waw=True):
        S.op(eng, lambda h: h.scalar_tensor_tensor(out=out, in0=in0, scalar=scalar, in1=in1, op0=op0, op1=op1), reads=R, writes=W, waw=waw)

    def CP(eng, out, in_, R, W, waw=True):
        if eng == "act":
            ACT(out, in_, AF.Copy, R, W, waw=waw)
        else:
            S.op(eng, lambda h: h.tensor_copy(out=out, in_=in_), reads=R, writes=W, waw=waw)

    def MS(eng, ap, val, W):
        S.op(eng, lambda h: h.memset(ap, val), writes=W)

    def DMA(eng, out, in_, chan, R, W):
        if eng == "pool":
            chan = ("swq", chan)
        S.dma(eng, lambda h: h.dma_start(out=out, in_=in_), chan, reads=R, writes=W)

    uniq = [0]

    class Ring:
        def __init__(self, name, shape, dt, n, alloc=None):
            alloc = alloc or sb
            uniq[0] += 1
            self.name = name
            self.t = [alloc(f"{name}_{uniq[0]}_{i}", shape, dt) for i in range(n)]
            self.i = 0

        def next(self):
            k = self.i % len(self.t)
            self.i += 1
            return self.t[k], (self.name, k)

    psp = [0]
    nrot = [6]

    def bank(n=1):
        if n == 2 and psp[0] % 2 == 1:
            psp[0] += 1
        p = psp[0] % nrot[0]
        psp[0] += n
        return p

    def PB(b, lo=0, hi=512):
        return ps[:, b * 512 + lo: b * 512 + hi]

    vecs = sb("vecs", [128, NL * V_PER_LAYER], F32)
    consts = sb("consts", [128, NCONST], F32)
    ident = sb("ident", [128, 128], BF16)
    mask2 = sb("mask2", [128, 256], BF16)
    onesm = sb("onesm", [128, 128], BF16)
    ones = sb("ones", [128, 128], BF16)
    onesP = sb("onesP", [128, 2, 128], BF16)
    DMA("sp", vecs[:], vecs_d, "vecs", [], ["vecs"])
    DMA("sp", consts[:], consts_d, "consts", [], ["consts"])
    CP("dve", ident[:], consts[:, C_ID:C_ID + 128], ["consts"], ["cb"])
    CP("dve", mask2[:], consts[:, C_MASK:C_MASK + 256], ["consts"], ["cb"])
    MS("pool", onesm[:], 1.0 / 1024.0, ["cb"])
    MS("pool", ones[:], 1.0, ["cb"])
    MS("pool", onesP[:], 0.0, ["cb"])
    MS("pool", onesP[:, 0, 0:64], 1.0, ["cb"])
    MS("pool", onesP[:, 1, 64:128], 1.0, ["cb"])

    sqring = Ring("sq", [128, T], BF16, 3)
    rstdring = Ring("rstd", [128, T], F32, 2)
    G = {}

    def mk_rings(alloc, x=False, h=False, hin=False, ysb=False):
        if x:
            G["x"] = Ring("xt", [128, KC, T], F32, 2, alloc)
        if h:
            G["h"] = Ring("ht", [128, KC, T], BF16, 2, alloc)
        if hin:
            G["hin"] = Ring("hin", [128, KC, T], BF16, 2, alloc)
        if ysb:
            G["ysb"] = Ring("ysb", [128, KC, T], F32, 2, alloc)
    tmp_ring = Ring("tmpf", [128, T], F32, 3)

    def xview(d, tt):
        return d.rearrange("(c p) t -> p c t", p=128)[:, :, tt * T:(tt + 1) * T]

    def rstd_from_bank(b):
        rstd, rk = rstdring.next()
        ACT(rstd[:], PB(b), AF.Ln, [("ps", b)], [rk], bias=EPS)
        ACT(rstd[:], rstd[:], AF.Exp, [rk], [rk], scale=-0.5)
        return rstd, rk

    def norm_tile(xt, xk, gcol, ht, hk, ncols=T):
        b = bank()
        for c in range(KC):
            sq, sqk = sqring.next()
            ACT(sq[:, 0:ncols], xt[:, c, 0:ncols], AF.Square, [xk], [sqk])
            MM(PB(b, 0, ncols), onesm[:], sq[:, 0:ncols], c == 0, c == KC - 1, [sqk, "cb"], [("ps", b)])
        rstd, rk = rstdring.next()
        ACT(rstd[:, 0:ncols], PB(b, 0, ncols), AF.Ln, [("ps", b)], [rk], bias=EPS)
        ACT(rstd[:, 0:ncols], rstd[:, 0:ncols], AF.Exp, [rk], [rk], scale=-0.5)
        for c in range(KC):
            eng = "dve"
            STT(eng, ht[:, c, 0:ncols], xt[:, c, 0:ncols], vecs[:, gcol + c:gcol + c + 1], rstd[:, 0:ncols],
                ALU.mult, ALU.mult, [xk, rk, "vecs"], [hk], waw=False)

    pncnt = [0]

    class PostNorm:
        def __init__(self):
            self.ysb, self.yk = G["ysb"].next()
            pncnt[0] += 1
            self.b = 6 + pncnt[0] % 2
            self.pend = None

        def chunk(self, mc, b):
            CP("act", self.ysb[:, mc, :], PB(b), [("ps", b)], [self.yk], waw=False)
            sq, sqk = sqring.next()
            TT("dve", sq[:], PB(b), self.ysb[:, mc, :], ALU.mult, [("ps", b), self.yk], [sqk])
            if self.pend is not None:
                self.flush()
            self.pend = (mc, sq, sqk)
            if mc == KC - 1:
                self.flush()

        def flush(self):
            mc, sq, sqk = self.pend
            self.pend = None
            MM(PB(self.b), onesm[:], sq[:], mc == 0, mc == KC - 1, [sqk, "cb"], [("ps", self.b)])

        def finish(self, xt, xk, gcol):
            rstd, rk = rstd_from_bank(self.b)
            for c in range(KC):
                tmp, tk = tmp_ring.next()
                TT("pool", tmp[:], self.ysb[:, c, :], rstd[:], ALU.mult, [self.yk, rk], [tk])
                STT("dve", xt[:, c, :], tmp[:], vecs[:, gcol + c:gcol + c + 1], xt[:, c, :], ALU.mult, ALU.add,
                    [tk, xk, "vecs"], [xk], waw=False)

    def load_w(dst, src2d, ncols, chan, key, rows=128):
        v = src2d.rearrange("(k p) n -> p k n", p=rows)
        for cb in range(0, ncols, 2048):
            w = min(2048, ncols - cb)
            DMA("pool", dst[:, :, cb:cb + w], v[:, :, cb:cb + w], chan, [], [key])

    def load_wb(dst, src2d, ncols, name, order=None, blk=512, rows=128):
        v = src2d.rearrange("(k p) n -> p k n", p=rows)
        nb = (ncols + blk - 1) // blk
        for b in (order if order is not None else range(nb)):
            lo = b * blk
            w = min(blk, ncols - lo)
            DMA("pool", dst[:, :, lo:lo + w], v[:, :, lo:lo + w], f"{name}_{b}", [], [(name, b)])

    with ExitStack() as es0:
      if 'norope' not in opts:
          def sb0(name, shape, dt):
              return es0.enter_context(nc.sbuf_tensor(name, list(shape), dt))
          pi_ = sb0("rp_pi", [32, 128], I32)
          pf = sb0("rp_pf", [32, 128], F32)
          ang = sb0("rp_ang", [32, 128, 16], F32)
          a2 = sb0("rp_a2", [32, 128, 16], F32)
          ki = sb0("rp_ki", [32, 128, 16], I32)
          tab = sb0("rp_tab", [32, 128, 16], F32)
          DMA("sp", pi_[:], pos, "rp", [], ["rp_pi"])
          CP("dve", pf[:], pi_[:], ["rp_pi"], ["rp_pf"])
          inv_b = consts[0:32, C_INV:C_INV + 8].unsqueeze(1).to_broadcast([32, 128, 8])
          pf_b = pf[:].unsqueeze(2).to_broadcast([32, 128, 8])
          TT("dve", ang[:, :, 0:8], pf_b, inv_b, ALU.mult, ["rp_pf", "consts"], ["rp_ang"])
          TS("dve", ang[:, :, 8:16], ang[:, :, 0:8], float(np.pi / 2), None, ALU.add, None, ["rp_ang"], ["rp_ang"])
          TS("dve", a2[:], ang[:], float(1.0 / (2 * np.pi)), None, ALU.mult, None, ["rp_ang"], ["rp_a2"])
          CP("dve", ki[:], a2[:], ["rp_a2"], ["rp_ki"])
          CP("dve", a2[:], ki[:], ["rp_ki"], ["rp_a2"])
          STT("dve", ang[:], a2[:], float(-2 * np.pi), ang[:], ALU.mult, ALU.add, ["rp_a2", "rp_ang"], ["rp_ang"])
          TS("dve", a2[:], ang[:], float(np.pi), float(-2 * np.pi), ALU.is_gt, ALU.mult, ["rp_ang"], ["rp_a2"])
          TT("dve", ang[:], ang[:], a2[:], ALU.add, ["rp_ang", "rp_a2"], ["rp_ang"])
          TS("dve", a2[:], ang[:], float(-np.pi), float(2 * np.pi), ALU.is_lt, ALU.mult, ["rp_ang"], ["rp_a2"])
          TT("dve", ang[:], ang[:], a2[:], ALU.add, ["rp_ang", "rp_a2"], ["rp_ang"])
          ACT(tab[:, :, 0:8], ang[:, :, 8:16], AF.Sin, ["rp_ang"], ["rp_tab"])
          ACT(tab[:, :, 8:16], ang[:, :, 0:8], AF.Sin, ["rp_ang"], ["rp_tab"], waw=False)
          DMA("sp", rope_d.rearrange("(b i) f -> b i f", i=128), tab[:], "rp", ["rp_tab"], ["rope_d"])
          S.barrier()

    def phase_norm0(l):
      nrot[0] = 6
      psp[0] = 0
      with ExitStack() as e1:
        mk_rings(lambda n, sh, dt: e1.enter_context(nc.sbuf_tensor(n, list(sh), dt)), x=True, h=True)
        for tt in range(NT):
            xt, xk = G["x"].next()
            DMA("sp", xt[:], xview(xT, tt), xk, [], [xk])
            ht, hk = G["h"].next()
            norm_tile(xt, xk, l * V_PER_LAYER + V_MIXPRE, ht, hk)
            DMA("pool", xview(h1T, tt), ht[:], hk, [hk], [("h1T", tt)])
        S.barrier()

    def phase_attn(l):
        nrot[0] = 8
        psp[0] = 0
        with ExitStack() as e2:
            def sb2(name, shape, dt):
                return e2.enter_context(nc.sbuf_tensor(f"{name}_L{l}", list(shape), dt))
            hT = sb2("at_hT", [128, KC, SEQ], BF16)
            acc = sb2("at_acc", [128, 4, 2048], F32)
            attn_sb = sb2("at_out", [128, 2, 2048], BF16)
            wq = [sb2(f"at_w{i}", [128, KC, 1280], BF16) for i in range(2)]
            cs = [sb2(f"at_cs{g}", [128, 32, 16], F32) for g in range(3)]
            QK = [sb2(f"at_qk{i}", [128, 768], BF16) for i in range(3)]
            QF = [sb2(f"at_qf{i}", [128, 768], F32) for i in range(3)]
            ODT = [sb2(f"at_od{i}", [128, 512], F32) for i in range(2)]
            Vb = [sb2(f"at_v{i}", [128, 4, 128], BF16) for i in range(4)]
            QT = [sb2(f"at_qt{i}", [128, 2, 128], BF16) for i in range(4)]
            KTp = [sb2(f"at_kt{i}", [128, 4, 128], BF16) for i in range(4)]
            PT = [sb2(f"at_pt{i}", [128, 4, 256], BF16) for i in range(2)]
            rt = [sb2(f"at_rt{i}", [128, 4, 8, 8], F32) for i in range(4)]
            for tt in range(NT):
                DMA("sp", hT[:, :, tt * T:(tt + 1) * T], xview(h1T, tt), f"at_hT{tt}", [("h1T", tt)], [("at_hT", tt)])
            for g in range(3):
                d = DILS[g]
                nb = 32 // d
                for r in range(d):
                    jb = min(nb, 8)
                    for j0 in range(0, nb, jb):
                        src = bass.AP(rope_d.tensor, (r + d * 128 * j0) * 16, [[16 * d, 128], [16 * d * 128, jb], [1, 16]])
                        DMA("sp", cs[g][:, r * nb + j0:r * nb + j0 + jb, :], src, f"at_cs{g}", ["rope_d"], [("cs", g)])
            wl = [0]
            cnt = {"rt": 0, "od": 0}
            NS = 4

            def P1(it):
                g, r, j, need_q, wt, wk = it["g"], it["r"], it["j"], it["need_q"], it["wt"], it["wk"]
                d = DILS[g]
                nb = 32 // d
                slot = it["slot"]
                start = r + d * 128 * j
                sl = slice(start, start + 127 * d + 1, d)
                hkeys = [("at_hT", t_) for t_ in range(start // T, (start + 127 * d) // T + 1)]
                bq = bank() if need_q else None
                bk = bank()
                bv = bank()
                if need_q:
                    for kc in range(KC):
                        MM(PB(bq, 0, 256), hT[:, kc, sl], wt[:, kc, 0:256], kc == 0, kc == KC - 1, hkeys + [wk], [("ps", bq)])
                for kc in range(KC):
                    MM(PB(bk), hT[:, kc, sl], wt[:, kc, 256:768], kc == 0, kc == KC - 1, hkeys + [wk], [("ps", bk)])
                for kc in range(KC):
                    MM(PB(bv), hT[:, kc, sl], wt[:, kc, 768:1280], kc == 0, kc == KC - 1, hkeys + [wk], [("ps", bv)])
                qi = it["idx"] % 3
                qk, qkk = QK[qi], ("QK", qi)
                qf, qfk = QF[qi], ("QF", qi)
                it["qk"], it["qkk"] = qk, qkk
                if need_q:
                    CP("act", qf[:, 0:256], PB(bq, 0, 256), [("ps", bq)], [qfk])
                CP("act", qf[:, 256:768], PB(bk), [("ps", bk)], [qfk], waw=not need_q)
                lo = 0 if need_q else 256
                CP("act", qk[:, lo:768], qf[:, lo:768], [qfk], [qkk])
                CP("dve", Vb[slot][:].rearrange("p a b -> p (a b)"), PB(bv), [("ps", bv)], [("Vb", slot)])
                blk = r * nb + j
                csk = ("cs", g)
                views = []
                if need_q:
                    views.append((qf[:, 0:256].rearrange("p (h e) -> p h e", e=64), qk[:, 0:256].rearrange("p (h e) -> p h e", e=64), [128, 4, 8], 0))
                kfv = bass.AP(qf, 256, [[768, 128], [256, 2], [192, 2], [1, 64]])
                kbv = bass.AP(qk, 256, [[768, 128], [256, 2], [192, 2], [1, 64]])
                views.append((kfv, kbv, [128, 2, 2, 8], 1))
                for (fv, bv_, shp, which) in views:
                    ri = cnt["rt"] % 4
                    cnt["rt"] += 1
                    rtt, rtk = rt[ri], ("rt", ri)
                    if which == 0:
                        cosb = cs[g][:, blk, 0:8].unsqueeze(1).to_broadcast(shp)
                        sinb = cs[g][:, blk, 8:16].unsqueeze(1).to_broadcast(shp)
                        u1, u2 = fv[:, :, 0:8], fv[:, :, 8:16]
                        o1, o2 = bv_[:, :, 0:8], bv_[:, :, 8:16]
                        tv = [rtt[:, k, 0:4, :] for k in range(4)]
                    else:
                        cosb = cs[g][:, blk, 0:8].unsqueeze(1).unsqueeze(1).to_broadcast(shp)
                        sinb = cs[g][:, blk, 8:16].unsqueeze(1).unsqueeze(1).to_broadcast(shp)
                        u1, u2 = fv[:, :, :, 0:8], fv[:, :, :, 8:16]
                        o1, o2 = bv_[:, :, :, 0:8], bv_[:, :, :, 8:16]
                        tv = [rtt[:, k, 0:4, :].rearrange("p (a b) e -> p a b e", a=2) for k in range(4)]
                    TT("dve", tv[0], u1, cosb, ALU.mult, [qfk, csk], [rtk])
                    TT("dve", tv[1], u2, sinb, ALU.mult, [qfk, csk], [rtk], waw=False)
                    TT("dve", tv[2], u2, cosb, ALU.mult, [qfk, csk], [rtk], waw=False)
                    TT("dve", tv[3], u1, sinb, ALU.mult, [qfk, csk], [rtk], waw=False)
                    TT("pool", o1, tv[0], tv[1], ALU.subtract, [rtk], [qkk])
                    TT("pool", o2, tv[2], tv[3], ALU.add, [rtk], [qkk])

            def P2(it):
                need_q, slot, qk, qkk = it["need_q"], it["slot"], it["qk"], it["qkk"]
                if need_q:
                    bt = bank()
                    for ch in range(2):
                        MM(PB(bt, ch * 128, ch * 128 + 128), qk[:, ch * 128:(ch + 1) * 128], ident[:], True, True, [qkk, "cb"], [("ps", bt)])
                    CP("act", QT[slot][:].rearrange("p c t -> p (c t)"), PB(bt, 0, 256), [("ps", bt)], [("QT", slot)])
                bt2 = bank()
                for hh in range(4):
                    MM(PB(bt2, hh * 128, hh * 128 + 128), qk[:, 256 + hh * 128:256 + (hh + 1) * 128], ident[:], True, True, [qkk, "cb"], [("ps", bt2)])
                CP("dve", KTp[slot][:].rearrange("p a b -> p (a b)"), PB(bt2), [("ps", bt2)], [("KTp", slot)])

            def A1(it):
                j = it["j"]
                sc, sp_ = it["slot"], (it["slot"] - 1) % NS
                b2 = bank(2)
                for hh in range(4):
                    ch = hh // 2
                    bb = b2 + hh // 2
                    c0 = (hh % 2) * 256
                    if j > 0:
                        MM(PB(bb, c0, c0 + 256), ident[:], mask2[:, 0:256], True, False, ["cb"], [("ps", bb)])
                        MM(PB(bb, c0, c0 + 128), KTp[sp_][:, hh, :], QT[sc][:, ch, :], False, False, [("KTp", sp_), ("QT", sc)], [("ps", bb)])
                    else:
                        MM(PB(bb, c0 + 128, c0 + 256), ident[:], mask2[:, 128:256], True, False, ["cb"], [("ps", bb)])
                    MM(PB(bb, c0 + 128, c0 + 256), KTp[sc][:, hh, :], QT[sc][:, ch, :], False, True, [("KTp", sc), ("QT", sc)], [("ps", bb)])
                pi = it["idx"] % 2
                pt, ptk = PT[pi], ("PT", pi)
                it["pt"], it["ptk"] = pt, ptk
                for hb in range(2):
                    bb = b2 + hb
                    if j > 0:
                        ACT(pt[:, 2 * hb:2 * hb + 2, :].rearrange("p a b -> p (a b)"), PB(bb), AF.Exp, [("ps", bb)], [ptk], scale=0.125, waw=False)
                    else:
                        for a in range(2):
                            ACT(pt[:, 2 * hb + a, 128:256], PB(bb, a * 256 + 128, a * 256 + 256), AF.Exp,
                                [("ps", bb)], [ptk], scale=0.125, waw=False)

            def A2(it):
                g, r, j, half = it["g"], it["r"], it["j"], it["half"]
                d = DILS[g]
                sc, sp_ = it["slot"], (it["slot"] - 1) % NS
                pt, ptk = it["pt"], it["ptk"]
                b3 = bank()
                kbs = [0, 1] if j > 0 else [1]
                for od in range(2):
                    for pair in range(2):
                        mats = [(hh, kb) for hh in (2 * pair, 2 * pair + 1) for kb in kbs]
                        c0 = od * 256 + pair * 128
                        for i, (hh, kb) in enumerate(mats):
                            vs = sp_ if kb == 0 else sc
                            lhs = Vb[vs][:, hh, :] if od == 0 else onesP[:, hh % 2, :]
                            rd = [ptk, ("Vb", vs)] if od == 0 else [ptk, "cb"]
                            MM(PB(b3, c0, c0 + 128), lhs, pt[:, hh, kb * 128:(kb + 1) * 128], i == 0, i == len(mats) - 1, rd, [("ps", b3)])
                off = r + d * 128 * j - 2048 * half
                av = bass.AP(acc, off, [[4 * 2048, 128], [2048, 4], [d, 128]])
                oi = cnt["od"] % 2
                cnt["od"] += 1
                odt, odk = ODT[oi], ("ODT", oi)
                CP("act", odt[:], PB(b3), [("ps", b3)], [odk])
                pv = odt[:].rearrange("p (a t) -> p a t", t=128)
                if g == 0:
                    CP("pool", av, pv, [odk], ["acc"], waw=True)
                else:
                    TT("pool", av, pv, av, ALU.add, [odk, "acc"], ["acc"])

            gidx = [0]
            allsegs = []
            for half in range(2):
                items = []
                for g in range(3):
                    d = DILS[g]
                    nb = 32 // d
                    jl, jh = half * nb // 2, (half + 1) * nb // 2
                    for r in range(d):
                        if jl > 0:
                            items.append(dict(g=g, r=r, j=jl - 1, need_q=False, att=False, half=half))
                        for j in range(jl, jh):
                            items.append(dict(g=g, r=r, j=j, need_q=True, att=True, half=half))
                for it in items:
                    it["idx"] = gidx[0]
                    it["slot"] = gidx[0] % NS
                    gidx[0] += 1
                segs = []
                for it in items:
                    if not segs or segs[-1][0] != it["g"]:
                        segs.append((it["g"], []))
                    segs[-1][1].append(it)
                allsegs.append(segs)
            flat = [(half, g, its) for half in range(2) for (g, its) in allsegs[half]]
            wbuf = {}

            def issue_w(k):
                half, g, its = flat[k]
                wi = k % 2
                wt, wk = wq[wi], ("at_w", wi)
                v = w_qkv[l, g].rearrange("(k p) n -> p k n", p=128)
                for kc in range(KC):
                    DMA("pool", wt[:, kc:kc + 1, :], v[:, kc:kc + 1, :], f"at_w{wi}", [], [wk])
                for it in its:
                    it["wt"], it["wk"] = wt, wk

            issue_w(0)
            for k, (half, g, its) in enumerate(flat):
                if k + 1 < len(flat):
                    issue_w(k + 1)
                items = its
                n = len(items)
                first_of_half = (k == 0 or flat[k - 1][0] != half)
                last_of_half = (k + 1 == len(flat) or flat[k + 1][0] != half)
                if first_of_half:
                    P1(items[0])
                    prev_last = None
                for i in range(n):
                    nxt = items[i + 1] if i + 1 < n else (flat[k + 1][2][0] if not last_of_half else None)
                    if nxt is not None:
                        P1(nxt)
                    P2(items[i])
                    prv = items[i - 1] if i > 0 else prev_last
                    if prv is not None and prv["att"]:
                        A2(prv)
                    if items[i]["att"]:
                        A1(items[i])
                prev_last = items[n - 1]
                if not last_of_half:
                    continue
                if prev_last["att"]:
                    A2(prev_last)
                prev_last = None
                for pair in range(2):
                    ACT(acc[:, 2 + pair, :], acc[:, 2 + pair, :], AF.Ln, ["acc"], ["acc"])
                    ACT(acc[:, 2 + pair, :], acc[:, 2 + pair, :], AF.Exp, ["acc"], ["acc"], scale=-1.0)
                    TT("dve", attn_sb[:, pair, :], acc[:, pair, :], acc[:, 2 + pair, :], ALU.mult, ["acc"], ["attn_sb"])
                dv = attnT.rearrange("(c p) t -> p c t", p=128)[:, :, half * 2048:(half + 1) * 2048]
                DMA("pool", dv, attn_sb[:], "attn_sb", ["attn_sb"], [("attnT", half)])
            S.barrier()

    def phase_ab(l):
        nrot[0] = 8
        psp[0] = 0
        vb = l * V_PER_LAYER
        with ExitStack() as e3:
            def sb3(name, shape, dt):
                return e3.enter_context(nc.sbuf_tensor(f"{name}_L{l}", list(shape), dt))
            mk_rings(sb3, hin=True)
            wab = sb3("ab_w", [128, KC, NAB], BF16)
            pw = sb3("ab_pw", [96, 4, 96], BF16)
            wa = sb3("ab_wa", [96, 4, D], BF16)
            wb = sb3("ab_wb", [128, 3, D], BF16)
            A = [sb3(f"ab_A{g}", [96, 16 + T], F32) for g in range(4)]
            T2 = sb3("ab_T2", [96, 16 + T], F32)
            T4 = sb3("ab_T4", [96, 16 + T], F32)
            T8 = sb3("ab_T8", [96, 16 + T], F32)
            PL = [sb3(f"ab_PL{g}", [96, T], BF16) for g in range(4)]
            MX = [sb3(f"ab_MX{g}", [96, T], BF16) for g in range(4)]
            BX = sb3("ab_BX", [128, T], F32)
            P = [sb3(f"ab_P{c}", [128, 2 + T], F32) for c in range(3)]
            Y1 = sb3("ab_Y1", [128, T], F32)
            Y2 = sb3("ab_Y2", [128, T], F32)
            Z = [sb3(f"ab_Z{c}", [128, T], BF16) for c in range(3)]
            SG = [sb3(f"ab_SG{i}", [128, T], F32) for i in range(2)]
            MO = [sb3(f"ab_MO{i}", [128, KC, T], F32) for i in range(2)]
            load_wb(wab, w_in[l][:, 0:NAB], NAB, "ab_w", order=[0])
            DMA("pool", pw[:], pool_w[l].rearrange("g c d -> c g d"), "ab_w2", [], ["ab_w2"])
            DMA("pool", wa[:], w_a[l].rearrange("(g c) n -> c g n", c=96), "ab_w2", [], ["ab_w2"])
            load_wb(wab, w_in[l][:, 0:NAB], NAB, "ab_w", order=[1, 2, 3, 4])
            DMA("pool", wb[:], w_b[l].rearrange("(k p) n -> p k n", p=128), "ab_w3", [], ["ab_w3"])
            load_wb(wab, w_in[l][:, 0:NAB], NAB, "ab_w", order=[5, 6])
            for g in range(4):
                MS("pool", A[g][:, 0:16], 0.0, [("A", g)])
            for c in range(3):
                MS("pool", P[c][:, 0:2], 0.0, [("P", c)])
            sgi = [0]
            for tt in range(NT):
                ht, hk = G["hin"].next()
                DMA("sp", ht[:], xview(h1T, tt), hk, [("h1T", tt)], [hk])
                mo, mok = MO[tt % 2], ("MO", tt % 2)
                for g in range(4):
                    w = (2, 4, 8, 16)[g]
                    Ak = ("A", g)
                    if tt > 0:
                        CP("pool", A[g][:, 0:16], A[g][:, T:T + 16], [Ak], [Ak])
                    b = bank()
                    for kc in range(KC):
                        MM(ps[0:96, b * 512:(b + 1) * 512], wab[:, kc, OFF_A + 96 * g:OFF_A + 96 * (g + 1)], ht[:, kc, :], kc == 0, kc == KC - 1,
                           [hk, ("ab_w", 0)], [("ps", b)])
                    CP("act", A[g][:, 16:16 + T], ps[0:96, b * 512:(b + 1) * 512], [("ps", b)], [Ak])
                    src = A[g]
                    srck = Ak
                    n = 1
                    for (dst, dk) in ((T2, "T2"), (T4, "T4"), (T8, "T8"), (None, None)):
                        if n >= w:
                            break
                        if 2 * n == w:
                            tmpw, tmpk = T2 if dst is not T2 and src is not T2 else (T4 if src is not T4 else T8), None
                            tmpw = {1: T2, 2: T4, 4: T8, 8: T2}[n]
                            tmpk = {1: "T2", 2: "T4", 4: "T8", 8: "T2"}[n]
                            TT("pool", tmpw[:, 16:16 + T], src[:, 16:16 + T], src[:, 16 - n:16 - n + T], ALU.add, [srck], [tmpk])
                            STT("dve", PL[g][:], tmpw[:, 16:16 + T], 1.0 / w, A[g][:, 16:16 + T], ALU.mult, ALU.subtract, [tmpk, Ak], [("PL", g)])
                            if tt == 0:
                                tmp, tk = tmp_ring.next()
                                TT("pool", tmp[0:96, 0:16], tmpw[:, 16:32], consts[0:96, C_RC + 16 * g:C_RC + 16 * g + 16], ALU.mult, [tmpk, "consts"], [tk])
                                TT("pool", PL[g][:, 0:16], tmp[0:96, 0:16], A[g][:, 16:32], ALU.subtract, [tk, Ak], [("PL", g)])
                            break
                        TT("pool", dst[:, 2 * n - 1:16 + T], src[:, 2 * n - 1:16 + T], src[:, n - 1:16 + T - n], ALU.add, [srck], [dk])
                        src, srck = dst, dk
                        n *= 2
                for c in range(3):
                    Pk = ("P", c)
                    if tt > 0:
                        CP("pool", P[c][:, 0:2], P[c][:, T:T + 2], [Pk], [Pk])
                    bx, bb_, bc = bank(), bank(), bank()
                    for (bk, off) in ((bx, OFF_BX), (bb_, OFF_BB), (bc, OFF_BC)):
                        for kc in range(KC):
                            MM(PB(bk), wab[:, kc, off + c * 128:off + (c + 1) * 128], ht[:, kc, :], kc == 0, kc == KC - 1, [hk, ("ab_w", (off + c * 128) // 512)], [("ps", bk)])
                    CP("act", BX[:], PB(bx), [("ps", bx)], ["BX"])
                    TT("dve", P[c][:, 2:2 + T], PB(bc), BX[:], ALU.mult, [("ps", bc), "BX"], [Pk])
                    cw = vb + V_CONVB
                    ACT(Y1[:], P[c][:, 2:2 + T], AF.Copy, [Pk, "vecs"], ["Y1"], scale=vecs[:, cw + 2 * 3 + c:cw + 2 * 3 + c + 1])
                    STT("dve", Y2[:], P[c][:, 1:1 + T], vecs[:, cw + 1 * 3 + c:cw + 1 * 3 + c + 1], Y1[:], ALU.mult, ALU.add, [Pk, "Y1", "vecs"], ["Y2"])
                    STT("dve", Y1[:], P[c][:, 0:T], vecs[:, cw + 0 * 3 + c:cw + 0 * 3 + c + 1], Y2[:], ALU.mult, ALU.add, [Pk, "Y2", "vecs"], ["Y1"])
                    TT("dve", Z[c][:], PB(bb_), Y1[:], ALU.mult, [("ps", bb_), "Y1"], [("Z", c)])
                for g in range(4):
                    b2 = bank()
                    MM(ps[0:96, b2 * 512:(b2 + 1) * 512], pw[:, g, :], PL[g][:], True, True, [("PL", g), "ab_w2"], [("ps", b2)])
                    ACT(MX[g][:], ps[0:96, b2 * 512:(b2 + 1) * 512], AF.Copy, [("ps", b2), "vecs"], [("MX", g)],
                        scale=vecs[0:96, vb + V_PSCALE + g:vb + V_PSCALE + g + 1])
                for mc in range(KC):
                    b = bank()
                    for g in range(4):
                        MM(PB(b), wa[:, g, mc * 128:(mc + 1) * 128], MX[g][:], g == 0, g == 3, [("MX", g), "ab_w2"], [("ps", b)])
                    bg = bank()
                    for kc in range(KC):
                        MM(PB(bg), wab[:, kc, OFF_GA + mc * 128:OFF_GA + (mc + 1) * 128], ht[:, kc, :], kc == 0, kc == KC - 1, [hk, ("ab_w", (OFF_GA + mc * 128) // 512)], [("ps", bg)])
                    sg, sgk = SG[sgi[0] % 2], ("SG", sgi[0] % 2)
                    sgi[0] += 1
                    ACT(sg[:], PB(bg), AF.Sigmoid, [("ps", bg)], [sgk])
                    TT("dve", mo[:, mc, :], PB(b), sg[:], ALU.mult, [("ps", b), sgk], [mok], waw=False)
                for mc in range(KC):
                    b = bank()
                    for c in range(3):
                        MM(PB(b), wb[:, c, mc * 128:(mc + 1) * 128], Z[c][:], c == 0, c == 2, [("Z", c), "ab_w3"], [("ps", b)])
                    bg = bank()
                    for kc in range(KC):
                        MM(PB(bg), wab[:, kc, OFF_GB + mc * 128:OFF_GB + (mc + 1) * 128], ht[:, kc, :], kc == 0, kc == KC - 1, [hk, ("ab_w", (OFF_GB + mc * 128) // 512)], [("ps", bg)])
                    sg, sgk = SG[sgi[0] % 2], ("SG", sgi[0] % 2)
                    sgi[0] += 1
                    ACT(sg[:], PB(bg), AF.Sigmoid, [("ps", bg)], [sgk])
                    tmp, tk = tmp_ring.next()
                    TT("dve", tmp[:], PB(b), sg[:], ALU.mult, [("ps", b), sgk], [tk])
                    TT("pool", mo[:, mc, :], mo[:, mc, :], tmp[:], ALU.add, [mok, tk], [mok], waw=False)
                DMA("pool", xview(mab, tt), mo[:], mok, [mok], [("mab", tt)])
            S.barrier()

    def phase_merge(l):
        nrot[0] = 6
        psp[0] = 0
        vb = l * V_PER_LAYER
        xsrc = xT if l == 0 else xs
        with ExitStack() as e4:
            def sb4(name, shape, dt):
                return e4.enter_context(nc.sbuf_tensor(f"{name}_L{l}", list(shape), dt))
            mk_rings(sb4, x=True, h=True, hin=True, ysb=True)
            wgc = sb4("mg_wgc", [128, KC, D], BF16)
            wc = sb4("mg_wc", [128, 2, D], BF16)
            wo = sb4("mg_wo", [128, KC, D], BF16)
            AT = [sb4(f"mg_at{i}", [128, 2, T], BF16) for i in range(2)]
            MI = [sb4(f"mg_mi{i}", [128, KC, T], F32) for i in range(2)]
            MGs = [sb4(f"mg_mg{i}", [128, KC, T], BF16) for i in range(2)]
            SG = [sb4(f"mg_SG{i}", [128, T], F32) for i in range(2)]
            load_w(wc, w_c[l], D, "mg_wc", "mg_wc")
            load_wb(wgc, w_in[l][:, OFF_GC:OFF_GC + D], D, "mg_wg")
            load_wb(wo, w_out[l], D, "mg_wo")
            st = {}

            def stA(tt):
                ht, hk = G["hin"].next()
                DMA("sp", ht[:], xview(h1T, tt), hk, [("h1T", tt)], [hk])
                at, atk = AT[tt % 2], ("AT", tt % 2)
                DMA("sp", at[:], attnT.rearrange("(c p) t -> p c t", p=128)[:, :, tt * T:(tt + 1) * T], f"mg_at{tt % 2}", [("attnT", tt // 4)], [atk])
                mi, mik = MI[tt % 2], ("MI", tt % 2)
                DMA("sp", mi[:], xview(mab, tt), f"mg_mi{tt % 2}", [("mab", tt)], [mik])
                xt, xk = G["x"].next()
                DMA("sp", xt[:], xview(xsrc, tt), xk, [("xs", tt)], [xk])
                mg, mgk = MGs[tt % 2], ("MG", tt % 2)
                for mc in range(KC):
                    b = bank()
                    for pr in range(2):
                        MM(PB(b), wc[:, pr, mc * 128:(mc + 1) * 128], at[:, pr, :], pr == 0, pr == 1, [atk, "mg_wc"], [("ps", b)])
                    bg = bank()
                    for kc in range(KC):
                        MM(PB(bg), wgc[:, kc, mc * 128:(mc + 1) * 128], ht[:, kc, :], kc == 0, kc == KC - 1, [hk, ("mg_wg", mc // 4)], [("ps", bg)])
                    sg, sgk = SG[mc % 2], ("SG4", mc % 2)
                    ACT(sg[:], PB(bg), AF.Sigmoid, [("ps", bg)], [sgk])
                    tmp, tk = tmp_ring.next()
                    TT("dve", tmp[:], PB(b), sg[:], ALU.mult, [("ps", b), sgk], [tk])
                    TT("pool", mg[:, mc, :], tmp[:], mi[:, mc, :], ALU.add, [tk, mik], [mgk], waw=False)
                st[tt] = dict(xt=xt, xk=xk, mg=mg, mgk=mgk)

            def stW(tt):
                d_ = st[tt]
                pn = PostNorm()
                for mc in range(KC):
                    b = bank()
                    for kc in range(KC):
                        MM(PB(b), wo[:, kc, mc * 128:(mc + 1) * 128], d_["mg"][:, kc, :], kc == 0, kc == KC - 1, [d_["mgk"], ("mg_wo", mc // 4)], [("ps", b)])
                    pn.chunk(mc, b)
                d_["pn"] = pn

            def stF1(tt):
                d_ = st[tt]
                xt, xk = d_["xt"], d_["xk"]
                d_["pn"].finish(xt, xk, vb + V_MIXPOST)
                DMA("pool", xview(xs, tt), xt[:], xk, [xk], [("xs", tt)])

            def stF2(tt):
                d_ = st.pop(tt)
                xt, xk = d_["xt"], d_["xk"]
                h2, h2k = G["h"].next()
                norm_tile(xt, xk, vb + V_MEMPRE, h2, h2k)
                DMA("pool", xview(h2T, tt), h2[:], h2k, [h2k], [("h2T", tt)])

            stA(0)
            stW(0)
            for tt in range(1, NT):
                stF1(tt - 1)
                stA(tt)
                stF2(tt - 1)
                stW(tt)
            stF1(NT - 1)
            stF2(NT - 1)
            S.barrier()

    def phase_mem(l):
        nrot[0] = 6
        psp[0] = 0
        vb = l * V_PER_LAYER
        with ExitStack() as e5:
            def sb5(name, shape, dt):
                return e5.enter_context(nc.sbuf_tensor(f"{name}_L{l}", list(shape), dt))
            mk_rings(sb5, x=True, h=True, hin=True, ysb=True)
            wq_ = sb5("mm_wq", [128, KC, 512], BF16)
            wkv = sb5("mm_wkv", [128, KC, 1024], BF16)
            wo = sb5("mm_wo", [128, 4, D], BF16)
            mt = sb5("mm_mt", [128, KC, 256], F32)
            mn = sb5("mm_mn", [128, KC, 256], BF16)
            KmT = sb5("mm_KmT", [128, 4, 256], BF16)
            Vm = sb5("mm_Vm", [128, 2, 512], BF16)
            QM = [sb5(f"mm_QM{i}", [128, T], BF16) for i in range(4)]
            PTm = [sb5(f"mm_PT{i}", [128, 2, T], BF16) for i in range(4)]
            DN = [sb5(f"mm_DN{i}", [128, T], F32) for i in range(2)]
            OMs = [sb5(f"mm_OM{i}", [128, 4, T], BF16) for i in range(2)]
            load_w(wkv, w_mkv[l], 1024, "mm_w", "mm_w")
            load_w(wq_, w_mq[l], 512, "mm_w", "mm_w")
            load_w(wo, w_mo[l], D, "mm_wo", "mm_wo")
            DMA("sp", mt[:], memT.rearrange("(c p) t -> p c t", p=128), "mm_mt", [], ["mm_mt"])
            norm_tile(mt, "mm_mt", vb + V_MEMKV, mn, "mm_mn", ncols=256)
            for h in range(4):
                b = bank()
                for kc in range(KC):
                    MM(PB(b, 0, 256), wkv[:, kc, h * 128:(h + 1) * 128], mn[:, kc, 0:256], kc == 0, kc == KC - 1, ["mm_mn", "mm_w"], [("ps", b)])
                CP("act", KmT[:, h, :], PB(b, 0, 256), [("ps", b)], ["KmT"], waw=False)
            for mi_ in range(2):
                b = bank()
                for kc in range(KC):
                    MM(PB(b), mn[:, kc, mi_ * 128:(mi_ + 1) * 128], wkv[:, kc, 512:1024], kc == 0, kc == KC - 1, ["mm_mn", "mm_w"], [("ps", b)])
                CP("act", Vm[:, mi_, :], PB(b), [("ps", b)], ["Vm"], waw=False)
            sc = float(128 ** -0.5)
            st = {}

            def stA(tt):
                ht, hk = G["hin"].next()
                DMA("sp", ht[:], xview(h2T, tt), hk, [("h2T", tt)], [hk])
                xt, xk = G["x"].next()
                DMA("sp", xt[:], xview(xs, tt), xk, [("xs", tt)], [xk])
                om, omk = OMs[tt % 2], ("OM", tt % 2)
                for h in range(4):
                    b = bank()
                    for kc in range(KC):
                        MM(PB(b), wq_[:, kc, h * 128:(h + 1) * 128], ht[:, kc, :], kc == 0, kc == KC - 1, [hk, "mm_w"], [("ps", b)])
                    CP("act", QM[h][:], PB(b), [("ps", b)], [("QM", h)])
                for h in range(4):
                    for mi_ in range(2):
                        bs = bank()
                        MM(PB(bs), KmT[:, h, mi_ * 128:(mi_ + 1) * 128], QM[h][:], True, True, ["KmT", ("QM", h)], [("ps", bs)])
                        ACT(PTm[h][:, mi_, :], PB(bs), AF.Exp, [("ps", bs)], [("PTm", h)], scale=sc, waw=False)
                for h in range(4):
                    pt, ptk = PTm[h], ("PTm", h)
                    bo, bd = bank(), bank()
                    for mi_ in range(2):
                        MM(PB(bo), Vm[:, mi_, h * 128:(h + 1) * 128], pt[:, mi_, :], mi_ == 0, mi_ == 1, ["Vm", ptk], [("ps", bo)])
                    for mi_ in range(2):
                        MM(PB(bd), ones[:], pt[:, mi_, :], mi_ == 0, mi_ == 1, ["cb", ptk], [("ps", bd)])
                    dn, dnk = DN[h % 2], ("DN", h % 2)
                    ACT(dn[:], PB(bd), AF.Ln, [("ps", bd)], [dnk])
                    ACT(dn[:], dn[:], AF.Exp, [dnk], [dnk], scale=-1.0)
                    TT("dve", om[:, h, :], PB(bo), dn[:], ALU.mult, [("ps", bo), dnk], [omk], waw=False)
                st[tt] = dict(xt=xt, xk=xk, om=om, omk=omk)

            def stW(tt):
                d_ = st[tt]
                pn = PostNorm()
                for mc in range(KC):
                    b = bank()
                    for h in range(4):
                        MM(PB(b), wo[:, h, mc * 128:(mc + 1) * 128], d_["om"][:, h, :], h == 0, h == 3, [d_["omk"], "mm_wo"], [("ps", b)])
                    pn.chunk(mc, b)
                d_["pn"] = pn

            def stF1(tt):
                d_ = st[tt]
                xt, xk = d_["xt"], d_["xk"]
                d_["pn"].finish(xt, xk, vb + V_MEMPOST)
                DMA("pool", xview(xs, tt), xt[:], xk, [xk], [("xs", tt)])

            def stF2(tt):
                d_ = st.pop(tt)
                xt, xk = d_["xt"], d_["xk"]
                h3, h3k = G["h"].next()
                norm_tile(xt, xk, vb + V_FFNPRE, h3, h3k)
                DMA("pool", xview(h3T, tt), h3[:], h3k, [h3k], [("h3T", tt)])

            stA(0)
            stW(0)
            for tt in range(1, NT):
                stF1(tt - 1)
                stA(tt)
                stF2(tt - 1)
                stW(tt)
            stF1(NT - 1)
            stF2(NT - 1)
            S.barrier()

    def phase_up(l):
        nrot[0] = 8
        psp[0] = 0
        vb = l * V_PER_LAYER
        with ExitStack() as e6:
            def sb6(name, shape, dt):
                return e6.enter_context(nc.sbuf_tensor(f"{name}_L{l}", list(shape), dt))
            mk_rings(sb6, hin=True)
            wu = sb6("up_w", [128, KC, 2 * DFF], BF16)
            H = sb6("up_H", [128, FC, 2], F32)
            UA = [sb6(f"up_UA{i}", [128, 2 + T], F32) for i in range(2)]
            Y1 = [sb6(f"up_Y1{i}", [128, T], F32) for i in range(2)]
            Y2 = [sb6(f"up_Y2{i}", [128, T], F32) for i in range(2)]
            AO = [sb6(f"up_AO{i}", [128, FC, T], BF16) for i in range(2)]
            load_wb(wu, w_up[l], 2 * DFF, "up_w", order=[0, 5, 6, 1, 7, 2, 8, 3, 9, 4, 10])
            MS("pool", H[:], 0.0, ["H"])
            cw = vb + V_CONVF
            for tt in range(NT):
                ht, hk = G["hin"].next()
                DMA("sp", ht[:], xview(h3T, tt), hk, [("h3T", tt)], [hk])
                ao, aok = AO[tt % 2], ("AO", tt % 2)
                for c in range(FC):
                    ba, bb_ = bank(), bank()
                    for kc in range(KC):
                        MM(PB(ba), wu[:, kc, c * 128:(c + 1) * 128], ht[:, kc, :], kc == 0, kc == KC - 1, [hk, ("up_w", (c * 128) // 512)], [("ps", ba)])
                    for kc in range(KC):
                        MM(PB(bb_), wu[:, kc, DFF + c * 128:DFF + (c + 1) * 128], ht[:, kc, :], kc == 0, kc == KC - 1, [hk, ("up_w", (DFF + c * 128) // 512)], [("ps", bb_)])
                    i = c % 2
                    ua, uak = UA[i], ("UA", i)
                    y1, y1k = Y1[i], ("Y1", i)
                    y2, y2k = Y2[i], ("Y2", i)
                    CP("pool", ua[:, 0:2], H[:, c, :], ["H"], [uak])
                    CP("act", ua[:, 2:2 + T], PB(ba), [("ps", ba)], [uak], waw=False)
                    CP("pool", H[:, c, :], ua[:, T:T + 2], [uak], ["H"])
                    ACT(y1[:], ua[:, 2:2 + T], AF.Copy, [uak, "vecs"], [y1k], scale=vecs[:, cw + 2 * FC + c:cw + 2 * FC + c + 1])
                    STT("dve", y2[:], ua[:, 1:1 + T], vecs[:, cw + 1 * FC + c:cw + 1 * FC + c + 1], y1[:], ALU.mult, ALU.add, [uak, y1k, "vecs"], [y2k])
                    STT("dve", y1[:], ua[:, 0:T], vecs[:, cw + 0 * FC + c:cw + 0 * FC + c + 1], y2[:], ALU.mult, ALU.add, [uak, y2k, "vecs"], [y1k])
                    ACT(y2[:], y1[:], AF.Silu, [y1k], [y2k])
                    TT("dve", ao[:, c, :], PB(bb_), y2[:], ALU.mult, [("ps", bb_), y2k], [aok], waw=False)
                DMA("pool", actT.rearrange("(c p) t -> p c t", p=128)[:, :, tt * T:(tt + 1) * T], ao[:], f"up_ao{tt % 2}", [aok], [("actT", tt)])
            S.barrier()

    def phase_down(l, last):
        nrot[0] = 6
        psp[0] = 0
        vb = l * V_PER_LAYER
        with ExitStack() as e7:
            def sb7(name, shape, dt):
                return e7.enter_context(nc.sbuf_tensor(f"{name}_L{l}", list(shape), dt))
            mk_rings(sb7, x=True, h=True, ysb=True)
            wd = sb7("dn_w", [128, FC, D], BF16)
            AI = [sb7(f"dn_AI{i}", [128, FC, T], BF16) for i in range(2)]
            for q4 in range(4):
                DMA("pool", wd[:, :, q4 * 256:(q4 + 1) * 256], w_down[l].rearrange("(k p) n -> p k n", p=128)[:, :, q4 * 256:(q4 + 1) * 256],
                    f"dn_w{q4}", [], [("dn_w", q4)])
            st = {}

            def stA(tt):
                ai, aik = AI[tt % 2], ("AI", tt % 2)
                DMA("sp", ai[:], actT.rearrange("(c p) t -> p c t", p=128)[:, :, tt * T:(tt + 1) * T], f"dn_ai{tt % 2}", [("actT", tt)], [aik])
                xt, xk = G["x"].next()
                DMA("sp", xt[:], xview(xs, tt), xk, [("xs", tt)], [xk])
                st[tt] = dict(xt=xt, xk=xk, ai=ai, aik=aik, pn=PostNorm())

            def stW(tt, mcs):
                d_ = st[tt]
                for mc in mcs:
                    b = bank()
                    for c in range(FC):
                        MM(PB(b), wd[:, c, mc * 128:(mc + 1) * 128], d_["ai"][:, c, :], c == 0, c == FC - 1, [d_["aik"], ("dn_w", mc // 2)], [("ps", b)])
                    d_["pn"].chunk(mc, b)
                if d_["pn"].pend is not None:
                    d_["pn"].flush()

            def stF1(tt):
                d_ = st[tt]
                xt, xk = d_["xt"], d_["xk"]
                d_["pn"].finish(xt, xk, vb + V_FFNPOST)
                if last:
                    DMA("pool", xview(outT, tt), xt[:], xk, [xk], [("outT", tt)])
                else:
                    DMA("pool", xview(xs, tt), xt[:], xk, [xk], [("xs", tt)])

            def stF2(tt):
                d_ = st.pop(tt)
                xt, xk = d_["xt"], d_["xk"]
                if not last:
                    h1, h1k = G["h"].next()
                    norm_tile(xt, xk, (l + 1) * V_PER_LAYER + V_MIXPRE, h1, h1k)
                    DMA("pool", xview(h1T, tt), h1[:], h1k, [h1k], [("h1T", tt)])

            stA(0)
            stW(0, range(KC))
            for tt in range(1, NT):
                stF1(tt - 1)
                stA(tt)
                stW(tt, range(0, 4))
                stF2(tt - 1)
                stW(tt, range(4, KC))
            stF1(NT - 1)
            stF2(NT - 1)
            S.barrier()


    return dict(nc=nc, S=S, es=es, phases=dict(norm0=phase_norm0, attn=phase_attn, ab=phase_ab, merge=phase_merge,
                                               mem=phase_mem, up=phase_up, down=phase_down))


def build_full(n_layers=NL, dbg=False, stop=None, opts=()):
    P = build_program(n_layers, dbg, opts)
    ph = P["phases"]
    seq = []
    if "nonorm0" not in opts:
        ph["norm0"](0)
    done = stop is not None and stop[1] == "norm0"
    for l in range(n_layers):
        if done:
            break
        for name in ("attn", "ab", "merge", "mem", "up", "down"):
            if name == "down":
                ph[name](l, l == n_layers - 1)
            else:
                ph[name](l)
            if stop is not None and stop == (l, name):
                done = True
                break
        if done:
            break
    nsem = P["S"].emit()
    P["es"].close()
    return P["nc"], P["S"], nsem


def host_inputs(inputs):
    import ml_dtypes
    f32 = np.float32
    perm = w_in_perm()
    wip = np.asarray(inputs["w_in"], f32)[:, :, perm]
    wqkv = np.zeros((NL, 3, D, 1280), f32)
    for g in range(3):
        base = OFF_QKV + 768 * g
        wqkv[:, g, :, 0:256] = wip[:, :, base:base + 256]
        for hh in range(4):
            par = hh % 2
            wqkv[:, g, :, 256 + hh * 128 + 64 * par:256 + hh * 128 + 64 * par + 64] = wip[:, :, base + 256 + 64 * hh:base + 256 + 64 * hh + 64]
            wqkv[:, g, :, 768 + hh * 128 + 64 * par:768 + hh * 128 + 64 * par + 64] = wip[:, :, base + 512 + 64 * hh:base + 512 + 64 * hh + 64]
    shared = {
        "w_in": np.ascontiguousarray(wip[:, :, 0:OFF_QKV]),
        "w_qkv": wqkv,
        "pool_w": np.ascontiguousarray(np.asarray(inputs["pool_w"], f32)),
    }
    for k in ("w_branch_a", "w_branch_b", "w_branch_c", "w_out", "w_mq", "w_mkv", "w_mo", "w_up", "w_down"):
        shared[k] = np.ascontiguousarray(np.asarray(inputs[k], f32))
    vecs = np.zeros((128, NL * V_PER_LAYER), f32)
    for l in range(NL):
        vb = l * V_PER_LAYER
        for name, off in (("norm_mix_pre", V_MIXPRE), ("norm_mix_post", V_MIXPOST), ("norm_mem_pre", V_MEMPRE),
                          ("norm_mem_post", V_MEMPOST), ("norm_memkv", V_MEMKV), ("norm_ffn_pre", V_FFNPRE),
                          ("norm_ffn_post", V_FFNPOST)):
            vecs[:, vb + off:vb + off + 8] = np.asarray(inputs[name], f32)[l].reshape(8, 128).T
        vecs[:, vb + V_CONVB:vb + V_CONVB + 9] = np.asarray(inputs["conv_b_w"], f32)[l].reshape(3, 3, 128).transpose(2, 0, 1).reshape(128, 9)
        vecs[:, vb + V_CONVF:vb + V_CONVF + 66] = np.asarray(inputs["conv_ffn_w"], f32)[l].reshape(3, FC, 128).transpose(2, 0, 1).reshape(128, 66)
        vecs[0:96, vb + V_PSCALE:vb + V_PSCALE + 4] = np.asarray(inputs["pool_scale"], f32)[l].reshape(4, 96).T
    consts = np.zeros((128, NCONST), f32)
    consts[:, C_ID:C_ID + 128] = np.eye(128, dtype=f32)
    k = np.arange(128)[:, None]
    q = np.arange(128)[None, :]
    consts[:, C_MASK:C_MASK + 128] = np.where(k >= q, 0.0, -30000.0)
    consts[:, C_MASK + 128:C_MASK + 256] = np.where(k <= q, 0.0, -30000.0)
    consts[:, C_INV:C_INV + 8] = (f32(500000.0) ** (-np.arange(0, 16, 2, dtype=f32) / f32(16)))[None, :]
    for g, w in enumerate((2, 4, 8, 16)):
        consts[:, C_RC + 16 * g:C_RC + 16 * g + 16] = (1.0 / np.minimum(np.arange(16) + 1, w))[None, :]
    shared["vecs"] = vecs
    shared["consts"] = consts
    x = np.asarray(inputs["x"], f32)
    mem = np.asarray(inputs["mem"], f32)
    posn = np.asarray(inputs["positions"]).astype(np.int32)
    in_maps = []
    for b in range(8):
        m = dict(shared)
        m["xT"] = np.ascontiguousarray(x[b].T)
        m["memT"] = np.ascontiguousarray(mem[b].T)
        m["pos"] = np.ascontiguousarray(posn[b].reshape(32, 128))
        in_maps.append(m)
    return in_maps


_CACHE = {}


def kernel(**inputs):
    in_maps = host_inputs(inputs)
    if "nc" not in _CACHE:
        _CACHE["nc"] = build_full()[0]
    nc = _CACHE["nc"]
    res = run_bass_kernel_spmd(nc, in_maps, core_ids=list(range(8)))
    out = np.stack([np.ascontiguousarray(res.results[b]["outT"].T) for b in range(8)], axis=0)
    return out.astype(np.float32)
```
